# Optimizing a Trainium2 kernel written in Bass

```python
import math
import jax, jax.numpy as jnp
from jax import lax
import numpy as np

D_MODEL = 1024
BATCH = 4
SEQ = 8192
DEPTH = 2
DEC_BATCH = 32
DEC_SEQ = 64
PAST_LEN = 1024

CHUNK = 64
N_AB = (DEPTH + 1) // 2
N_CD = DEPTH // 2
MIX_WIDTH = D_MODEL
CONV_CH = MIX_WIDTH // 2
CONV_K = 31
DA_HEADS = 4
DA_HEAD_DIM = MIX_WIDTH // 2 // DA_HEADS // 2
DA_QK = DA_HEADS * 2 * DA_HEAD_DIM
Q_BLOCK = 128
GM_WIDTH = MIX_WIDTH // 2
GM_GROUPS = 4
GM_GROUP_CH = GM_WIDTH // GM_GROUPS
GM_CHUNK = 128
SSM_INNER = MIX_WIDTH // 2
SSM_HEAD_DIM = 64
SSM_HEADS = SSM_INNER // SSM_HEAD_DIM
SSM_GROUPS = 2
SSM_STATE = 128
SSM_CONV_K = 4
SSM_CONV_CH = SSM_INNER + 2 * SSM_GROUPS * SSM_STATE
SSD_CHUNK = CHUNK
N_MEM = 256
X_HEADS = 4
X_HEAD_DIM = D_MODEL // X_HEADS
PEER_HEADS = 8
PEER_KEYS = 128
PEER_EXPERTS = PEER_KEYS * PEER_KEYS
PEER_QDIM = 256
PEER_TOPK = 16
PEER_BLOCK = 128

AB_IN = 2 * CONV_CH + 3 * DA_QK
CD_IN = 2 * GM_WIDTH + SSM_INNER + SSM_CONV_CH + SSM_HEADS

kernel_name = 'hybrid_stream_encoder_step'


def _rms(x, g=None, eps=1e-6):
    xf = x.astype(jnp.float32)
    y = xf * lax.rsqrt(jnp.mean(xf * xf, axis=-1, keepdims=True) + eps)
    if g is not None:
        y = y * g.astype(jnp.float32)
    return y.astype(x.dtype)


def _layernorm(x, g, b, eps=1e-5):
    xf = x.astype(jnp.float32)
    xc = xf - jnp.mean(xf, axis=-1, keepdims=True)
    y = xc * lax.rsqrt(jnp.mean(xc * xc, axis=-1, keepdims=True) + eps)
    return (y * g.astype(jnp.float32) + b.astype(jnp.float32)).astype(x.dtype)


def _causal_dwconv(xpad, w, b):
    y = lax.conv_general_dilated(xpad, w[:, None, :].astype(xpad.dtype), window_strides=(1,), padding='VALID',
                                 dimension_numbers=('NWC', 'WIO', 'NWC'), feature_group_count=xpad.shape[-1])
    return y + b


def _diff_core(q, k, v, q_pos, k_pos, lam):
    s = jnp.einsum('bqhtd,bkhtd->bhtqk', q, k).astype(jnp.float32) * (DA_HEAD_DIM ** -0.5)
    allowed = (k_pos[None, :] // CHUNK) <= (q_pos[:, None] // CHUNK)
    p = jax.nn.softmax(jnp.where(allowed, s, -1e30), axis=-1)
    a = p[:, :, 0] - lam * p[:, :, 1]
    return jnp.einsum('bhqk,bkhe->bqhe', a.astype(v.dtype), v)


def _ab_mixer(h, conv_ctx, k_past, v_past, past_len, lam_init, w_in, conv_w, conv_b, ln_g, ln_b,
              lq1, lk1, lq2, lk2, subln_g, w_out):
    b, L, _ = h.shape
    z = h @ w_in
    a_val, a_gate, q, k, v = jnp.split(z, [CONV_CH, 2 * CONV_CH, 2 * CONV_CH + DA_QK, 2 * CONV_CH + 2 * DA_QK], axis=-1)
    glu = a_val * jax.nn.sigmoid(a_gate)
    if conv_ctx is None:
        conv_ctx = jnp.zeros((b, CONV_K - 1, CONV_CH), glu.dtype)
    xpad = jnp.concatenate([conv_ctx.astype(glu.dtype), glu], axis=1)
    ca = jax.nn.silu(_layernorm(_causal_dwconv(xpad, conv_w, conv_b), ln_g, ln_b))
    new_conv = xpad[:, -(CONV_K - 1):]
    q = q.reshape(b, L, DA_HEADS, 2, DA_HEAD_DIM)
    k = k.reshape(b, L, DA_HEADS, 2 * DA_HEAD_DIM)
    v = v.reshape(b, L, DA_HEADS, 2 * DA_HEAD_DIM)
    f32 = jnp.float32
    lam = (jnp.exp(jnp.sum(lq1.astype(f32) * lk1.astype(f32)))
           - jnp.exp(jnp.sum(lq2.astype(f32) * lk2.astype(f32))) + lam_init)
    q_pos = past_len + jnp.arange(L)
    if k_past is None:
        k_pos = jnp.arange(L)
        k5 = k.reshape(b, L, DA_HEADS, 2, DA_HEAD_DIM)
        nb = L // Q_BLOCK
        qb = jnp.swapaxes(q.reshape(b, nb, Q_BLOCK, DA_HEADS, 2, DA_HEAD_DIM), 0, 1)
        pb = q_pos.reshape(nb, Q_BLOCK)
        o = lax.map(lambda qp: _diff_core(qp[0], k5, v, qp[1], k_pos, lam), (qb, pb))
        o = jnp.swapaxes(o, 0, 1).reshape(b, L, DA_HEADS, 2 * DA_HEAD_DIM)
    else:
        kk = jnp.concatenate([k_past.astype(k.dtype), k], axis=1)
        vv = jnp.concatenate([v_past.astype(v.dtype), v], axis=1)
        k_pos = jnp.arange(kk.shape[1])
        o = _diff_core(q, kk.reshape(b, -1, DA_HEADS, 2, DA_HEAD_DIM), vv, q_pos, k_pos, lam)
    o = _rms(o, subln_g) * (1.0 - lam_init)
    out = jnp.concatenate([ca, o.reshape(b, L, DA_QK)], axis=-1) @ w_out
    return out, new_conv, k, v


def _ssd(x, dt, A, Bm, Cm, h0, chunk):
    b, L, H, P = x.shape
    rep = H // Bm.shape[2]
    nc = L // chunk
    f32 = jnp.float32
    x = x.astype(f32).reshape(b, nc, chunk, H, P)
    dt = dt.reshape(b, nc, chunk, H)
    Bh = jnp.repeat(Bm.astype(f32), rep, axis=2).reshape(b, nc, chunk, H, -1)
    Ch = jnp.repeat(Cm.astype(f32), rep, axis=2).reshape(b, nc, chunk, H, -1)
    acs = jnp.cumsum(dt * A, axis=2)
    seg = acs[:, :, :, None, :] - acs[:, :, None, :, :]
    tri = jnp.tril(jnp.ones((chunk, chunk), bool))[None, None, :, :, None]
    lmat = jnp.exp(jnp.where(tri, seg, -jnp.inf))
    xdt = x * dt[..., None]
    scores = jnp.einsum('bcihn,bcjhn->bcijh', Ch, Bh) * lmat
    y_diag = jnp.einsum('bcijh,bcjhp->bcihp', scores, xdt)
    decay = jnp.exp(acs[:, :, -1:, :] - acs)
    states = jnp.einsum('bcjhn,bcjh,bcjhp->bchpn', Bh, decay, xdt)
    chunk_decay = jnp.exp(acs[:, :, -1, :])

    def step(s, inp):
        st, dec = inp
        return s * dec[:, :, None, None] + st, s

    final, prev = lax.scan(step, h0.astype(f32), (jnp.moveaxis(states, 1, 0), jnp.moveaxis(chunk_decay, 1, 0)))
    prev = jnp.moveaxis(prev, 0, 1)
    y_off = jnp.einsum('bcihn,bchpn,bcih->bcihp', Ch, prev, jnp.exp(acs))
    return (y_diag + y_off).reshape(b, L, H, P), final


def _cd_mixer(h, conv_ctx, ssd_state, gm_len, ssd_chunk, w_in, ln_g, ln_b, w_s, b_s, conv_w, conv_b,
              dt_bias, a_log, d_skip, norm_g, w_out):
    b, L, _ = h.shape
    f32 = jnp.float32
    z = h @ w_in
    c_proj, gate, xbc, dt = jnp.split(z, [2 * GM_WIDTH, 2 * GM_WIDTH + SSM_INNER, 2 * GM_WIDTH + SSM_INNER + SSM_CONV_CH], axis=-1)
    u, vv = jnp.split(jax.nn.gelu(c_proj), 2, axis=-1)
    vv = _layernorm(vv, ln_g, ln_b).reshape(b, L // gm_len, gm_len, GM_GROUPS, GM_GROUP_CH)
    w = w_s[:, :gm_len, :gm_len] * jnp.tril(jnp.ones((gm_len, gm_len), w_s.dtype))
    mixed = jnp.einsum('gij,bcjgd->bcigd', w, vv) + jnp.swapaxes(b_s[:, :gm_len], 0, 1)[:, :, None]
    c_out = u * mixed.reshape(b, L, GM_WIDTH)
    new_v = vv.reshape(b, L, GM_GROUPS, GM_GROUP_CH)
    if conv_ctx is None:
        conv_ctx = jnp.zeros((b, SSM_CONV_K - 1, SSM_CONV_CH), xbc.dtype)
    xpad = jnp.concatenate([conv_ctx.astype(xbc.dtype), xbc], axis=1)
    xbc_c = jax.nn.silu(_causal_dwconv(xpad, conv_w, conv_b))
    new_conv = xpad[:, -(SSM_CONV_K - 1):]
    xs, Bm, Cm = jnp.split(xbc_c, [SSM_INNER, SSM_INNER + SSM_GROUPS * SSM_STATE], axis=-1)
    xs = xs.reshape(b, L, SSM_HEADS, SSM_HEAD_DIM)
    Bm = Bm.reshape(b, L, SSM_GROUPS, SSM_STATE)
    Cm = Cm.reshape(b, L, SSM_GROUPS, SSM_STATE)
    dtv = jax.nn.softplus(dt.astype(f32) + dt_bias.astype(f32))
    A = -jnp.exp(a_log.astype(f32))
    if ssd_state is None:
        ssd_state = jnp.zeros((b, SSM_HEADS, SSM_HEAD_DIM, SSM_STATE), f32)
    y, final = _ssd(xs, dtv, A, Bm, Cm, ssd_state, ssd_chunk)
    y = y + d_skip.astype(f32)[:, None] * xs.astype(f32)
    y = y.reshape(b, L, SSM_INNER) * jax.nn.silu(gate.astype(f32))
    y = _rms(y.reshape(b, L, SSM_GROUPS, SSM_INNER // SSM_GROUPS)).reshape(b, L, SSM_INNER) * norm_g.astype(f32)
    out = jnp.concatenate([c_out, y.astype(h.dtype)], axis=-1) @ w_out
    return out, new_v, new_conv, final.astype(h.dtype)


def _mem_kv(mem, w_k, w_v):
    b = mem.shape[0]
    return ((mem @ w_k).reshape(b, N_MEM, X_HEADS, X_HEAD_DIM),
            (mem @ w_v).reshape(b, N_MEM, X_HEADS, X_HEAD_DIM))


def _cross_attn(h, mk, mv, w_q, w_o):
    b, L, _ = h.shape
    q = (h @ w_q).reshape(b, L, X_HEADS, X_HEAD_DIM)
    s = jnp.einsum('bqhd,bkhd->bhqk', q, mk.astype(q.dtype)).astype(jnp.float32) * (X_HEAD_DIM ** -0.5)
    p = jax.nn.softmax(s, axis=-1).astype(q.dtype)
    o = jnp.einsum('bhqk,bkhd->bqhd', p, mv.astype(q.dtype)).reshape(b, L, D_MODEL)
    return o @ w_o


def _peer(h, w_q, sub_keys, eu, ev):
    b, L, D = h.shape
    blk = PEER_BLOCK if L % PEER_BLOCK == 0 else L
    xb = h.reshape(b * L // blk, blk, D)

    def one(x):
        t = x.shape[0]
        q = (x @ w_q).reshape(t, PEER_HEADS, 2, PEER_QDIM // 2)
        s = jnp.einsum('thcd,ckd->thck', q, sub_keys).astype(jnp.float32)
        ts, ti = lax.top_k(s, PEER_TOPK)
        cand_s = (ts[:, :, 0, :, None] + ts[:, :, 1, None, :]).reshape(t, PEER_HEADS, PEER_TOPK * PEER_TOPK)
        cand_i = (ti[:, :, 0, :, None] * PEER_KEYS + ti[:, :, 1, None, :]).reshape(t, PEER_HEADS, PEER_TOPK * PEER_TOPK)
        best_s, best_j = lax.top_k(cand_s, PEER_TOPK)
        idx = jnp.take_along_axis(cand_i, best_j, axis=-1)
        g = jax.nn.softmax(best_s, axis=-1)
        act = jax.nn.gelu(jnp.einsum('thkd,td->thk', eu[idx], x).astype(jnp.float32))
        return jnp.einsum('thk,thkd->td', (g * act).astype(x.dtype), ev[idx])

    return lax.map(one, xb).reshape(b, L, D)


def _trunk(x, mem_k, mem_v, attn_k, attn_v, conv_a, ssd_st, conv_ssm, past_len, gm_len, ssd_chunk, p):
    new_k, new_v, new_ca, new_gv, new_ssd, new_cs = [], [], [], [], [], []
    for l in range(DEPTH):
        i = l // 2
        h = _rms(x, p['norm_mix_g'][l])
        if l % 2 == 0:
            out, c_new, k_new, v_new = _ab_mixer(
                h, None if conv_a is None else conv_a[i], None if attn_k is None else attn_k[i],
                None if attn_v is None else attn_v[i], past_len, 0.8 - 0.6 * math.exp(-0.3 * l),
                p['w_in_ab'][i], p['conv_a_w'][i], p['conv_a_b'][i], p['ln_a_g'][i], p['ln_a_b'][i],
                p['lam_q1'][i], p['lam_k1'][i], p['lam_q2'][i], p['lam_k2'][i], p['subln_g'][i], p['w_out_ab'][i])
            new_k.append(k_new)
            new_v.append(v_new)
            new_ca.append(c_new)
        else:
            out, gv_new, cs_new, st_new = _cd_mixer(
                h, None if conv_ssm is None else conv_ssm[i], None if ssd_st is None else ssd_st[i], gm_len, ssd_chunk,
                p['w_in_cd'][i], p['ln_c_g'][i], p['ln_c_b'][i], p['gm_w_s'][i], p['gm_b_s'][i],
                p['conv_d_w'][i], p['conv_d_b'][i], p['dt_bias'][i], p['a_log'][i], p['d_skip'][i],
                p['norm_d_g'][i], p['w_out_cd'][i])
            new_gv.append(gv_new)
            new_cs.append(cs_new)
            new_ssd.append(st_new)
        x = x + out
        x = x + _cross_attn(_rms(x, p['norm_cross_g'][l]), mem_k[l], mem_v[l], p['w_xq'][l], p['w_xo'][l])
        x = x + _peer(_rms(x, p['norm_ffn_g'][l]), p['w_pq'][l], p['sub_keys'][l], p['expert_u'][l], p['expert_v'][l])
    return _rms(x, p['norm_final_g']), new_k, new_v, new_ca, new_gv, new_ssd, new_cs


def setup_inputs(seed: int = 0) -> dict:
    key = jax.random.key(seed)
    ks = iter(jax.random.split(key, 64))
    f32 = jnp.float32

    def nrm(shape, scale):
        return jax.random.normal(next(ks), shape, f32) * scale

    def gain(shape):
        return 1.0 + nrm(shape, 0.01)

    dt0 = jnp.exp(jax.random.uniform(next(ks), (N_CD, SSM_HEADS), f32, math.log(1e-3), math.log(1e-1)))
    a_log = jnp.log(jax.random.uniform(next(ks), (N_CD, SSM_HEADS), f32, 1.0, 16.0))
    return {
        'x_prompt': nrm((BATCH, SEQ, D_MODEL), 1.0),
        'x_sample': nrm((DEC_BATCH, DEC_SEQ, D_MODEL), 1.0),
        'cache_attn_k': nrm((N_AB, DEC_BATCH, PAST_LEN, DA_HEADS, 2 * DA_HEAD_DIM), 1.0),
        'cache_attn_v': nrm((N_AB, DEC_BATCH, PAST_LEN, DA_HEADS, 2 * DA_HEAD_DIM), 1.0),
        'state_conv_a': nrm((N_AB, DEC_BATCH, CONV_K - 1, CONV_CH), 0.5),
        'state_ssd': nrm((N_CD, DEC_BATCH, SSM_HEADS, SSM_HEAD_DIM, SSM_STATE), 0.1),
        'state_conv_ssm': nrm((N_CD, DEC_BATCH, SSM_CONV_K - 1, SSM_CONV_CH), 1.0),
        'cache_mem_k': nrm((DEPTH, DEC_BATCH, N_MEM, X_HEADS, X_HEAD_DIM), 1.0),
        'cache_mem_v': nrm((DEPTH, DEC_BATCH, N_MEM, X_HEADS, X_HEAD_DIM), 1.0),
        'mem_prompt': nrm((BATCH, N_MEM, D_MODEL), 1.0),
        'norm_mix_g': gain((DEPTH, D_MODEL)),
        'norm_cross_g': gain((DEPTH, D_MODEL)),
        'norm_ffn_g': gain((DEPTH, D_MODEL)),
        'norm_final_g': gain((D_MODEL,)),
        'w_in_ab': nrm((N_AB, D_MODEL, AB_IN), D_MODEL ** -0.5),
        'conv_a_w': nrm((N_AB, CONV_K, CONV_CH), CONV_K ** -0.5),
        'conv_a_b': nrm((N_AB, CONV_CH), 0.01),
        'ln_a_g': gain((N_AB, CONV_CH)),
        'ln_a_b': nrm((N_AB, CONV_CH), 0.01),
        'lam_q1': nrm((N_AB, DA_HEAD_DIM), 0.1),
        'lam_k1': nrm((N_AB, DA_HEAD_DIM), 0.1),
        'lam_q2': nrm((N_AB, DA_HEAD_DIM), 0.1),
        'lam_k2': nrm((N_AB, DA_HEAD_DIM), 0.1),
        'subln_g': gain((N_AB, 2 * DA_HEAD_DIM)),
        'w_out_ab': nrm((N_AB, MIX_WIDTH, D_MODEL), MIX_WIDTH ** -0.5),
        'w_in_cd': nrm((N_CD, D_MODEL, CD_IN), D_MODEL ** -0.5),
        'ln_c_g': gain((N_CD, GM_WIDTH)),
        'ln_c_b': nrm((N_CD, GM_WIDTH), 0.01),
        'gm_w_s': nrm((N_CD, GM_GROUPS, GM_CHUNK, GM_CHUNK), GM_CHUNK ** -0.5),
        'gm_b_s': gain((N_CD, GM_GROUPS, GM_CHUNK)),
        'conv_d_w': nrm((N_CD, SSM_CONV_K, SSM_CONV_CH), SSM_CONV_K ** -0.5),
        'conv_d_b': nrm((N_CD, SSM_CONV_CH), 0.01),
        'dt_bias': dt0 + jnp.log(-jnp.expm1(-dt0)),
        'a_log': a_log,
        'd_skip': gain((N_CD, SSM_HEADS)),
        'norm_d_g': gain((N_CD, SSM_INNER)),
        'w_out_cd': nrm((N_CD, MIX_WIDTH, D_MODEL), MIX_WIDTH ** -0.5),
        'w_xq': nrm((DEPTH, D_MODEL, D_MODEL), D_MODEL ** -0.5),
        'w_xk': nrm((DEPTH, D_MODEL, D_MODEL), D_MODEL ** -0.5),
        'w_xv': nrm((DEPTH, D_MODEL, D_MODEL), D_MODEL ** -0.5),
        'w_xo': nrm((DEPTH, D_MODEL, D_MODEL), D_MODEL ** -0.5),
        'w_pq': nrm((DEPTH, D_MODEL, PEER_HEADS * PEER_QDIM), D_MODEL ** -0.5),
        'sub_keys': nrm((DEPTH, 2, PEER_KEYS, PEER_QDIM // 2), (PEER_QDIM // 2) ** -0.5),
        'expert_u': nrm((DEPTH, PEER_EXPERTS, D_MODEL), D_MODEL ** -0.5),
        'expert_v': nrm((DEPTH, PEER_EXPERTS, D_MODEL), 0.1),
    }


def reference(x_prompt, x_sample, cache_attn_k, cache_attn_v, state_conv_a, state_ssd, state_conv_ssm,
              cache_mem_k, cache_mem_v, mem_prompt,
              norm_mix_g, norm_cross_g, norm_ffn_g, norm_final_g,
              w_in_ab, conv_a_w, conv_a_b, ln_a_g, ln_a_b, lam_q1, lam_k1, lam_q2, lam_k2, subln_g, w_out_ab,
              w_in_cd, ln_c_g, ln_c_b, gm_w_s, gm_b_s, conv_d_w, conv_d_b, dt_bias, a_log, d_skip, norm_d_g, w_out_cd,
              w_xq, w_xk, w_xv, w_xo, w_pq, sub_keys, expert_u, expert_v):
    p = {
        'norm_mix_g': norm_mix_g, 'norm_cross_g': norm_cross_g, 'norm_ffn_g': norm_ffn_g, 'norm_final_g': norm_final_g,
        'w_in_ab': w_in_ab, 'conv_a_w': conv_a_w, 'conv_a_b': conv_a_b, 'ln_a_g': ln_a_g, 'ln_a_b': ln_a_b,
        'lam_q1': lam_q1, 'lam_k1': lam_k1, 'lam_q2': lam_q2, 'lam_k2': lam_k2, 'subln_g': subln_g, 'w_out_ab': w_out_ab,
        'w_in_cd': w_in_cd, 'ln_c_g': ln_c_g, 'ln_c_b': ln_c_b, 'gm_w_s': gm_w_s, 'gm_b_s': gm_b_s,
        'conv_d_w': conv_d_w, 'conv_d_b': conv_d_b, 'dt_bias': dt_bias, 'a_log': a_log, 'd_skip': d_skip,
        'norm_d_g': norm_d_g, 'w_out_cd': w_out_cd, 'w_xq': w_xq, 'w_xo': w_xo,
        'w_pq': w_pq, 'sub_keys': sub_keys, 'expert_u': expert_u, 'expert_v': expert_v,
    }
    mem_kv = [_mem_kv(mem_prompt, w_xk[l], w_xv[l]) for l in range(DEPTH)]
    mem_k_p = jnp.stack([kv[0] for kv in mem_kv])
    mem_v_p = jnp.stack([kv[1] for kv in mem_kv])
    y_prompt, kp, vp, cap, _, ssdp, csp = _trunk(
        x_prompt, mem_k_p, mem_v_p, None, None, None, None, None, 0, GM_CHUNK, SSD_CHUNK, p)
    ds = x_sample.shape[1]
    y_sample, ks, vs, cas, gvs, ssds, css = _trunk(
        x_sample, cache_mem_k, cache_mem_v, cache_attn_k, cache_attn_v, state_conv_a, state_ssd, state_conv_ssm,
        cache_attn_k.shape[2], ds, ds, p)
    return (y_prompt, y_sample,
            jnp.stack(kp), jnp.stack(vp), jnp.stack(cap), jnp.stack(ssdp), jnp.stack(csp), mem_k_p, mem_v_p,
            jnp.stack(ks), jnp.stack(vs), jnp.stack(cas), jnp.stack(gvs), jnp.stack(ssds), jnp.stack(css))
```

```python
import numpy as np
from contextlib import ExitStack
import concourse.bass as bass
import concourse.mybir as mybir
from concourse.bass_utils import run_bass_kernel_spmd

F32 = mybir.dt.float32
BF16 = mybir.dt.bfloat16
I32 = mybir.dt.int32
U32 = mybir.dt.uint32
AF = mybir.ActivationFunctionType
ALU = mybir.AluOpType
AX = mybir.AxisListType
D = 1024
NEG = -1.0e30


class Buf:
    __slots__ = ("name", "w", "r")

    def __init__(self, name):
        self.name = name
        self.w = None
        self.r = {}


class Sched:
    ENGS = ("pe", "act", "dve", "pool", "sp")

    def __init__(self, nc, es):
        self.nc = nc
        self.sems = {}
        self.esem = {}
        for e in ("pe", "act", "dve", "pool"):
            s = es.enter_context(nc.semaphore("es_" + e))
            self.sems["e_" + e] = s
            self.esem[e] = "e_" + e
        self.ecnt = {e: 0 for e in self.esem}
        self.nchan = 40
        for i in range(self.nchan):
            self.sems["c%d" % i] = es.enter_context(nc.semaphore("cs%d" % i))
        self.ccnt = {"c%d" % i: 0 for i in range(self.nchan)}
        self.known = {e: {} for e in self.ENGS}
        self.reset_phase()

    def reset_phase(self):
        self.ops = {e: [] for e in self.ENGS}
        self.chanmap = {}
        self.dbufs = {}

    def chan(self, name):
        if name not in self.chanmap:
            k = "c%d" % len(self.chanmap)
            assert len(self.chanmap) < self.nchan, "too many dma channels"
            self.chanmap[name] = k
        return self.chanmap[name]

    def db(self, name):
        if name not in self.dbufs:
            self.dbufs[name] = Buf(name)
        return self.dbufs[name]

    def op(self, eng, fn, reads=(), writes=(), chan=None):
        need = {}
        own = self.esem.get(eng) if chan is None else None

        def add(sig, raw):
            if sig is None:
                return
            k, v = sig
            if k == own and not raw:
                return
            if need.get(k, 0) < v:
                need[k] = v

        for b in reads:
            add(b.w, True)
        for b in writes:
            add(b.w, False)
            for k, v in b.r.items():
                add((k, v), False)
        kn = self.known[eng]
        waits = []
        for k, v in need.items():
            if kn.get(k, 0) < v:
                waits.append((k, v))
                kn[k] = v
        if chan is not None:
            k = self.chan(chan)
            self.ccnt[k] += 16
            sig = (k, self.ccnt[k])
            amt = 16
        else:
            k = self.esem[eng]
            self.ecnt[eng] += 1
            sig = (k, self.ecnt[eng])
            amt = 1
        self.ops[eng].append((waits, fn, k, amt))
        for b in reads:
            if b.r.get(sig[0], 0) < sig[1]:
                b.r[sig[0]] = sig[1]
        for b in writes:
            b.w = sig
            b.r = {}

    def drain(self):
        waits = []
        kn = self.known["sp"]
        for k, v in list(self.ccnt.items()) + [(self.esem[e], self.ecnt[e]) for e in self.esem]:
            if v > 0 and kn.get(k, 0) < v:
                waits.append((k, v))
                kn[k] = v
        self.ops["sp"].append((waits, None, None, 0))

    def emit(self):
        self.drain()
        nc = self.nc
        with nc.Block() as blk:
            def run(name):
                def f(e):
                    for waits, fn, k, amt in self.ops[name]:
                        for wk, wv in waits:
                            e.wait_ge(self.sems[wk], wv)
                        if fn is not None:
                            fn(e).then_inc(self.sems[k], amt)
                return f
            blk.tensor(run("pe"))
            blk.scalar(run("act"))
            blk.vector(run("dve"))
            blk.gpsimd(run("pool"))
            blk.sync(run("sp"))
        for e in self.ENGS:
            for k, v in list(self.ccnt.items()) + [(self.esem[x], self.ecnt[x]) for x in self.esem]:
                self.known[e][k] = v
        self.reset_phase()


class T:
    def __init__(self, ap, name, nsub=1):
        self.ap = ap
        self.b = Buf(name)
        self.sub = [Buf(name + str(i)) for i in range(nsub)] if nsub > 1 else None

    def __getitem__(self, k):
        return self.ap[k]


def build(NPT, NSS, PL, lam_inits=(0.8 - 0.6, 0.0), stop_after=None):
    nc = bass.Bass("TRN2", target_bir_lowering=False)
    NTILE = NPT + NSS
    NTOK = NPT * 128 + NSS * 64
    NPK = PL // 128
    LAM0 = lam_inits[0]

    def din(name, shape, dt=F32):
        return nc.dram_tensor(name, list(shape), dt, kind="ExternalInput").ap()

    def dout(name, shape, dt=F32):
        return nc.dram_tensor(name, list(shape), dt, kind="ExternalOutput").ap()

    def dscr(name, shape, dt=F32):
        return nc.dram_tensor(name, list(shape), dt, kind="Internal").ap()

    xp = din("x_prompt", [NPT * 128, D])
    xs = din("x_sample", [NSS * 64, D])
    ck = din("cache_attn_k", [NSS, PL, 512])
    cv = din("cache_attn_v", [NSS, PL, 512])
    sca = din("state_conv_a", [NSS, 30, 512])
    sssd = din("state_ssd", [NSS, 8, 64, 128])
    scs = din("state_conv_ssm", [NSS, 3, 1024])
    cmk = din("cache_mem_k", [2, NSS, 256, D])
    cmv = din("cache_mem_v", [2, NSS, 256, D])
    memp = din("mem_prompt", [256, D])
    norm_mix_g = din("norm_mix_g", [2, D])
    norm_cross_g = din("norm_cross_g", [2, D])
    norm_ffn_g = din("norm_ffn_g", [2, D])
    norm_final_g = din("norm_final_g", [1, D])
    w_in_ab = din("w_in_ab", [D, 2560])
    conv_a_w = din("conv_a_w", [31, 512])
    conv_a_b = din("conv_a_b", [1, 512])
    ln_a_g = din("ln_a_g", [1, 512])
    ln_a_b = din("ln_a_b", [1, 512])
    lam_q1 = din("lam_q1", [1, 64])
    lam_k1 = din("lam_k1", [1, 64])
    lam_q2 = din("lam_q2", [1, 64])
    lam_k2 = din("lam_k2", [1, 64])
    subln_g = din("subln_g", [1, 128])
    w_out_ab = din("w_out_ab", [D, D])
    w_in_cd = din("w_in_cd", [D, 2568])
    ln_c_g = din("ln_c_g", [1, 512])
    ln_c_b = din("ln_c_b", [1, 512])
    gm_w_s = din("gm_w_s", [4, 128, 128])
    gm_b_s = din("gm_b_s", [4, 128])
    conv_d_w = din("conv_d_w", [4, 1024])
    conv_d_b = din("conv_d_b", [1, 1024])
    dt_bias = din("dt_bias", [1, 8])
    a_log = din("a_log", [1, 8])
    d_skip = din("d_skip", [1, 8])
    norm_d_g = din("norm_d_g", [1, 512])
    w_out_cd = din("w_out_cd", [D, D])
    w_xq = din("w_xq", [2, D, D])
    w_xk = din("w_xk", [2, D, D])
    w_xv = din("w_xv", [2, D, D])
    w_xo = din("w_xo", [2, D, D])
    w_pq = din("w_pq", [2, D, 2048])
    sub_keys = din("sub_keys", [2, 2, 128, 128])
    expert_u = din("expert_u", [2, 16384, D])
    expert_v = din("expert_v", [2, 16384, D])

    y_p = dout("y_p", [NPT * 128, D])
    y_s = dout("y_s", [NSS * 64, D])
    o_kp = dout("o_kp", [NPT * 128, 512])
    o_vp = dout("o_vp", [NPT * 128, 512])
    o_cap = dout("o_cap", [30, 512])
    o_ssdp = dout("o_ssdp", [8, 64, 128])
    o_csp = dout("o_csp", [3, 1024])
    o_mk = dout("o_mk", [2, 256, D])
    o_mv = dout("o_mv", [2, 256, D])
    o_ks = dout("o_ks", [NSS * 64, 512])
    o_vs = dout("o_vs", [NSS * 64, 512])
    o_cas = dout("o_cas", [NSS, 30, 512])
    o_gvs = dout("o_gvs", [NSS * 64, 512])
    o_ssds = dout("o_ssds", [NSS, 8, 64, 128])
    o_css = dout("o_css", [NSS, 3, 1024])

    XS = dscr("XS", [NTOK, D])
    QTs = dscr("QTs", [NTILE, 128, 512], BF16)
    CATs = dscr("CATs", [NTILE, 128, 512], BF16)
    KTs = dscr("KTs", [128, 4, NTOK], BF16)
    VXs = dscr("VXs", [NTOK, 516], BF16)

    tiles = []
    for i in range(NPT):
        tiles.append(dict(seq=0, prompt=True, T=128, tok0=i * 128, i=i, first=(i == 0), last=(i == NPT - 1), s=-1))
    for s in range(NSS):
        tiles.append(dict(seq=1 + s, prompt=False, T=64, tok0=NPT * 128 + s * 64, i=0, first=True, last=True, s=s))

    def xin_rows(t):
        if t["prompt"]:
            return xp[t["tok0"]:t["tok0"] + 128, :]
        return xs[t["s"] * 64:(t["s"] + 1) * 64, :]

    with ExitStack() as ges:
        S = Sched(nc, ges)
        psf = [T(ges.enter_context(nc.psum_tensor("psf%d" % i, [128, 512], F32)), "psf%d" % i) for i in range(6)]
        psb = [T(ges.enter_context(nc.psum_tensor("psb%d" % i, [128, 1024], BF16)), "psb%d" % i) for i in range(2)]
        identf = T(ges.enter_context(nc.sbuf_tensor("identf", [128, 128], F32)), "identf")
        identb = T(ges.enter_context(nc.sbuf_tensor("identb", [128, 128], BF16)), "identb")
        onesb = T(ges.enter_context(nc.sbuf_tensor("onesb", [128, 128], BF16)), "onesb")
        onesf = T(ges.enter_context(nc.sbuf_tensor("onesf", [128, 128], F32)), "onesf")

        S.op("pool", lambda e: e.memset(identf[:], 1.0), writes=[identf.b])
        S.op("pool", lambda e: e.affine_select(out=identf[:], in_=identf[:], pattern=[[-1, 128]], compare_op=ALU.is_equal,
                                               fill=0.0, base=0, channel_multiplier=1), reads=[identf.b], writes=[identf.b])
        S.op("dve", lambda e: e.tensor_copy(out=identb[:], in_=identf[:]), reads=[identf.b], writes=[identb.b])
        S.op("dve", lambda e: e.memset(onesb[:], 1.0), writes=[onesb.b])
        S.op("dve", lambda e: e.memset(onesf[:], 1.0), writes=[onesf.b])

        rr = {"psf": 0, "psb": 0}

        def PSF():
            rr["psf"] = (rr["psf"] + 1) % 6
            return psf[rr["psf"]]

        def PSB():
            rr["psb"] = (rr["psb"] + 1) % 2
            return psb[rr["psb"]]

        def phase_tiles(es, prefix):
            cnt = [0]

            def sb(shape, dt=F32, name=None, nsub=1):
                cnt[0] += 1
                nm = "%s_%s%d" % (prefix, name or "t", cnt[0])
                return T(es.enter_context(nc.sbuf_tensor(nm, list(shape), dt)), nm, nsub)
            return sb

        def load_w(sb, w_ap, ncols, name):
            t = sb([128, 8, ncols], BF16, name)
            src = w_ap.rearrange("(c p) n -> p c n", p=128)
            for c in range(8):
                for n0 in range(0, ncols, 1024):
                    n1 = min(ncols, n0 + 1024)
                    S.op("pool", lambda e, c=c, n0=n0, n1=n1: e.dma_start(out=t[:, c, n0:n1], in_=src[:, c, n0:n1]),
                         writes=[t.b], chan=t.b.name)
            return t

        def load_cols(sb, v_ap, nchunk, name):
            t = sb([128, nchunk], F32, name)
            with nc.allow_non_contiguous_dma("tiny param column load"):
                pass
            S.op("sp", lambda e: e.dma_start(out=t[:], in_=v_ap.rearrange("o (c p) -> p (o c)", p=128),
                                             allow_slow_non_contiguous=True), writes=[t.b], chan=t.b.name)
            return t

        def load_rep(sb, v_ap, n, name):
            t = sb([128, n], F32, name)
            S.op("sp", lambda e: e.dma_start(out=t[:], in_=v_ap.partition_broadcast(128)), writes=[t.b], chan=t.b.name)
            return t

        def make_rms(sb):
            st = dict(junk=sb([128, D], F32, "rjunk"), ss=sb([128, 2], F32, "rss"), xh=sb([128, D], BF16, "rxh"))

            def rms_T(x, Tn, gcol, hT, h32=None, grep=None):
                junk, ss, xh = st["junk"], st["ss"], st["xh"]
                S.op("dve", lambda e: e.memset(ss[:Tn, 0:1], 0.0), writes=[ss.b])
                S.op("act", lambda e: e.activation(out=junk[:Tn], in_=x[:Tn], func=AF.Square, accum_out=ss[:Tn, 0:1]),
                     reads=[x.b, ss.b], writes=[junk.b, ss.b])
                S.op("act", lambda e: e.activation(out=ss[:Tn, 1:2], in_=ss[:Tn, 0:1], func=AF.Sqrt, scale=1.0 / D, bias=1e-6),
                     reads=[ss.b], writes=[ss.b])
                S.op("dve", lambda e: e.reciprocal(out=ss[:Tn, 1:2], in_=ss[:Tn, 1:2]), reads=[ss.b], writes=[ss.b])
                S.op("dve", lambda e: e.tensor_scalar(out=xh[:Tn], in0=x[:Tn], scalar1=ss[:Tn, 1:2], scalar2=None, op0=ALU.mult),
                     reads=[x.b, ss.b], writes=[xh.b])
                if h32 is not None:
                    S.op("dve", lambda e: e.scalar_tensor_tensor(out=h32[:Tn], in0=x[:Tn], scalar=ss[:Tn, 1:2], in1=grep[:Tn],
                                                                  op0=ALU.mult, op1=ALU.mult),
                         reads=[x.b, ss.b, grep.b], writes=[h32.b])
                pb = PSB()
                for c in range(8):
                    S.op("pe", lambda e, c=c: e.transpose(out=pb[:, c * Tn:(c + 1) * Tn], in_=xh[:Tn, c * 128:(c + 1) * 128],
                                                          identity=identb[:Tn, :Tn]),
                         reads=[xh.b, identb.b], writes=[pb.b])
                pv = pb[:, 0:8 * Tn].rearrange("p (c t) -> p c t", c=8)
                S.op("dve", lambda e: e.tensor_tensor(out=hT[:, :, :Tn], in0=pv, in1=gcol[:, :].unsqueeze(2).to_broadcast([128, 8, Tn]),
                                                      op=ALU.mult),
                     reads=[pb.b, gcol.b], writes=[hT.b])
            rms_T.junk = st["junk"]
            return rms_T

        def mm_feat(ps, col0, nchunk, w, hT, Tn, wb=None):
            for j in range(nchunk):
                for c in range(8):
                    S.op("pe", lambda e, j=j, c=c: e.matmul(ps[:, j * Tn:(j + 1) * Tn], lhsT=w[:, c, col0 + j * 128:col0 + (j + 1) * 128],
                                                            rhs=hT[:, c, :Tn], start=(c == 0), stop=(c == 7)),
                         reads=[w.b, hT.b], writes=[ps.b])

        def mm_tok(ps, col0, ncol, w, hT, Tn, nk=8):
            for c in range(nk):
                S.op("pe", lambda e, c=c: e.matmul(ps[:Tn, 0:ncol], lhsT=hT[:, c, :Tn], rhs=w[:, c, col0:col0 + ncol],
                                                   start=(c == 0), stop=(c == nk - 1)),
                     reads=[w.b, hT.b], writes=[ps.b])

        def gelu_tanh(e_act, e_dve, out, x, shape_ap, tmp, rb, wb_, Tn=None):
            pass

        def QTs_view(scr, ti, Tn):
            return scr[ti].rearrange("p (c t) -> p c t", c=4)[:, :, 0:Tn]

        with ExitStack() as es:
            sb = phase_tiles(es, "p0")
            mem32 = sb([128, 2, D], F32, "mem32")
            memb = sb([128, 2, D], BF16, "memb")
            memT = sb([128, 8, 256], BF16, "memT")
            S.op("sp", lambda e: e.dma_start(out=mem32[:], in_=memp.rearrange("(m p) d -> p m d", p=128)), writes=[mem32.b], chan="mem32")
            S.op("dve", lambda e: e.tensor_copy(out=memb[:], in_=mem32[:]), reads=[mem32.b], writes=[memb.b])
            for mc in range(2):
                pb = PSB()
                for c in range(8):
                    S.op("pe", lambda e, c=c, mc=mc, pb=pb: e.transpose(out=pb[:, c * 128:(c + 1) * 128], in_=memb[:, mc, c * 128:(c + 1) * 128],
                                                                        identity=identb[:]),
                         reads=[memb.b, identb.b], writes=[pb.b])
                S.op("act", lambda e, mc=mc, pb=pb: e.copy(out=memT[:, :, mc * 128:(mc + 1) * 128],
                                                           in_=pb[:, :].rearrange("p (c t) -> p c t", c=8)),
                     reads=[pb.b], writes=[memT.b])
            stg = [sb([128, D], F32, "stg") for _ in range(2)]
            k = 0
            for l in range(2):
                for (wsrc, odst) in ((w_xk, o_mk), (w_xv, o_mv)):
                    w = load_w(sb, wsrc[l], D, "wkv")
                    for mc in range(2):
                        st = stg[k % 2]
                        k += 1
                        for n in range(2):
                            ps = PSF()
                            for c in range(8):
                                S.op("pe", lambda e, c=c, n=n, mc=mc, ps=ps, w=w: e.matmul(
                                    ps[:, :], lhsT=memT[:, c, mc * 128:(mc + 1) * 128], rhs=w[:, c, n * 512:(n + 1) * 512],
                                    start=(c == 0), stop=(c == 7)), reads=[memT.b, w.b], writes=[ps.b])
                            S.op("act", lambda e, n=n, ps=ps, st=st: e.copy(out=st[:, n * 512:(n + 1) * 512], in_=ps[:, :]),
                                 reads=[ps.b], writes=[st.b])
                        S.op("sp", lambda e, l=l, mc=mc, st=st, odst=odst: e.dma_start(out=odst[l, mc * 128:(mc + 1) * 128, :], in_=st[:]),
                             reads=[st.b], writes=[S.db("omem")], chan="st_" + st.b.name)
            S.emit()

        with ExitStack() as es:
            sb = phase_tiles(es, "p1")
            w = load_w(sb, w_in_ab, 2560, "win")
            gcol = load_cols(sb, norm_mix_g[0:1, :], 8, "gcol")
            cw = sb([128, 4, 31], F32, "cw")
            for c in range(4):
                S.op("sp", lambda e, c=c: e.dma_start(out=cw[:, c, :], in_=conv_a_w[:, c * 128:(c + 1) * 128].rearrange("k p -> p k"),
                                                      allow_slow_non_contiguous=True), writes=[cw.b], chan="cw")
            cb = load_cols(sb, conv_a_b, 4, "cb")
            lg = load_cols(sb, ln_a_g, 4, "lg")
            lb = load_cols(sb, ln_a_b, 4, "lb")
            rms_T = make_rms(sb)
            xt = [sb([128, D], F32, "x") for _ in range(2)]
            hT = sb([128, 8, 128], BF16, "hT")
            sg = sb([128, 4, 128], F32, "sg")
            cbuf = sb([128, 4, 158], F32, "cbuf")
            acc = [sb([128, 128], F32, "acc%d" % c) for c in range(4)]
            sq = sb([128, 4, 128], F32, "sq")
            mv_ = sb([128, 3, 128], F32, "mv")
            xn = sb([128, 4, 128], F32, "xn")
            caT = [sb([128, 4, 128], BF16, "caT") for _ in range(2)]
            qT = [sb([128, 4, 128], BF16, "qT") for _ in range(2)]
            kT = [sb([128, 4, 128], BF16, "kT") for _ in range(2)]
            kvt = [sb([128, 2, 512], F32, "kvt") for _ in range(2)]
            vx = [sb([128, 4, 129], BF16, "vx") for _ in range(2)]
            st30 = sb([32, 512], F32, "st30")
            cao = sb([32, 512], F32, "cao")
            onesm = sb([128, 128], F32, "onesm")
            S.op("dve", lambda e: e.memset(onesm[:], 1.0 / 512), writes=[onesm.b])
            for v in vx:
                S.op("dve", lambda e, v=v: e.memset(v[:], 1.0), writes=[v.b])
            for ti, t in enumerate(tiles):
                Tn = t["T"]
                x = xt[ti % 2]
                S.op("sp", lambda e, x=x, t=t, Tn=Tn: e.dma_start(out=x[:Tn], in_=xin_rows(t)), writes=[x.b], chan=x.b.name)
                rms_T(x, Tn, gcol, hT)
                if t["first"]:
                    if t["prompt"]:
                        S.op("pool", lambda e: e.memset(cbuf[:, :, 0:30], 0.0), writes=[cbuf.b])
                    else:
                        S.op("sp", lambda e, t=t: e.dma_start(out=st30[0:30, :], in_=sca[t["s"]]), writes=[st30.b], chan="st30")
                        ps = PSF()
                        for c in range(4):
                            S.op("pe", lambda e, c=c, ps=ps: e.transpose(out=ps[:, c * 30:(c + 1) * 30], in_=st30[0:30, c * 128:(c + 1) * 128],
                                                                         identity=identf[0:30, 0:30]),
                                 reads=[st30.b, identf.b], writes=[ps.b])
                        S.op("act", lambda e, ps=ps: e.copy(out=cbuf[:, :, 0:30], in_=ps[:, 0:120].rearrange("p (c t) -> p c t", c=4)),
                             reads=[ps.b], writes=[cbuf.b])
                else:
                    S.op("pool", lambda e: e.tensor_copy(out=cbuf[:, :, 0:30], in_=cbuf[:, :, 128:158]), reads=[cbuf.b], writes=[cbuf.b])
                pa, pg = PSF(), PSF()
                mm_feat(pa, 0, 4, w, hT, Tn)
                mm_feat(pg, 512, 4, w, hT, Tn)
                S.op("act", lambda e, pg=pg, Tn=Tn: e.activation(out=sg[:, :, :Tn], in_=pg[:, 0:4 * Tn].rearrange("p (c t) -> p c t", c=4),
                                                                 func=AF.Sigmoid), reads=[pg.b], writes=[sg.b])
                S.op("dve", lambda e, pa=pa, Tn=Tn: e.tensor_tensor(out=cbuf[:, :, 30:30 + Tn], in0=pa[:, 0:4 * Tn].rearrange("p (c t) -> p c t", c=4),
                                                                    in1=sg[:, :, :Tn], op=ALU.mult), reads=[pa.b, sg.b], writes=[cbuf.b])
                if t["last"]:
                    ps = PSF()
                    for c in range(4):
                        S.op("pe", lambda e, c=c, ps=ps, Tn=Tn: e.transpose(out=ps[0:30, c * 128:(c + 1) * 128], in_=cbuf[:, c, Tn:Tn + 30],
                                                                            identity=identf[:, :]),
                             reads=[cbuf.b, identf.b], writes=[ps.b])
                    S.op("act", lambda e, ps=ps: e.copy(out=cao[0:30, :], in_=ps[0:30, :]), reads=[ps.b], writes=[cao.b])
                    dst = o_cap if t["prompt"] else o_cas[t["s"]]
                    S.op("sp", lambda e, dst=dst: e.dma_start(out=dst, in_=cao[0:30, :]), reads=[cao.b], writes=[S.db("ocap")], chan="st_cao")
                for k in range(31):
                    for c in range(4):
                        eng = "dve"
                        if k == 0:
                            S.op(eng, lambda e, c=c, Tn=Tn: e.tensor_scalar(out=acc[c][:, :Tn], in0=cbuf[:, c, 0:Tn], scalar1=cw[:, c, 0:1],
                                                                            scalar2=cb[:, c:c + 1], op0=ALU.mult, op1=ALU.add),
                                 reads=[cbuf.b, cw.b, cb.b], writes=[acc[c].b])
                        else:
                            S.op(eng, lambda e, c=c, k=k, Tn=Tn: e.scalar_tensor_tensor(out=acc[c][:, :Tn], in0=cbuf[:, c, k:k + Tn],
                                                                                        scalar=cw[:, c, k:k + 1], in1=acc[c][:, :Tn],
                                                                                        op0=ALU.mult, op1=ALU.add),
                                 reads=[cbuf.b, cw.b, acc[c].b], writes=[acc[c].b])
                pm, pq = PSF(), PSF()
                for c in range(4):
                    S.op("act", lambda e, c=c, Tn=Tn: e.activation(out=sq[:, c, :Tn], in_=acc[c][:, :Tn], func=AF.Square),
                         reads=[acc[c].b], writes=[sq.b])
                for c in range(4):
                    S.op("pe", lambda e, c=c, pm=pm, Tn=Tn: e.matmul(pm[:, :Tn], lhsT=onesm[:, :], rhs=acc[c][:, :Tn], start=(c == 0), stop=(c == 3)),
                         reads=[onesm.b, acc[c].b], writes=[pm.b])
                for c in range(4):
                    S.op("pe", lambda e, c=c, pq=pq, Tn=Tn: e.matmul(pq[:, :Tn], lhsT=onesm[:, :], rhs=sq[:, c, :Tn], start=(c == 0), stop=(c == 3)),
                         reads=[onesm.b, sq.b], writes=[pq.b])
                S.op("act", lambda e, pm=pm, Tn=Tn: e.copy(out=mv_[:, 0, :Tn], in_=pm[:, :Tn]), reads=[pm.b], writes=[mv_.b])
                S.op("dve", lambda e, Tn=Tn: e.tensor_tensor(out=mv_[:, 1, :Tn], in0=mv_[:, 0, :Tn], in1=mv_[:, 0, :Tn], op=ALU.mult),
                     reads=[mv_.b], writes=[mv_.b])
                S.op("dve", lambda e, pq=pq, Tn=Tn: e.tensor_tensor(out=mv_[:, 1, :Tn], in0=pq[:, :Tn], in1=mv_[:, 1, :Tn], op=ALU.subtract),
                     reads=[pq.b, mv_.b], writes=[mv_.b])
                S.op("act", lambda e, Tn=Tn: e.activation(out=mv_[:, 2, :Tn], in_=mv_[:, 1, :Tn], func=AF.Sqrt, bias=1e-5, scale=1.0),
                     reads=[mv_.b], writes=[mv_.b])
                S.op("dve", lambda e, Tn=Tn: e.reciprocal(out=mv_[:, 2, :Tn], in_=mv_[:, 2, :Tn]), reads=[mv_.b], writes=[mv_.b])
                for c in range(4):
                    S.op("dve", lambda e, c=c, Tn=Tn: e.tensor_tensor(out=xn[:, c, :Tn], in0=acc[c][:, :Tn], in1=mv_[:, 0, :Tn], op=ALU.subtract),
                         reads=[acc[c].b, mv_.b], writes=[xn.b])
                S.op("dve", lambda e, Tn=Tn: e.tensor_tensor(out=xn[:, :, :Tn], in0=xn[:, :, :Tn],
                                                             in1=mv_[:, 2, :Tn].unsqueeze(1).to_broadcast([128, 4, Tn]), op=ALU.mult),
                     reads=[xn.b, mv_.b], writes=[xn.b])
                ca = caT[ti % 2]
                for c in range(4):
                    S.op("act", lambda e, c=c, ca=ca, Tn=Tn: e.activation(out=ca[:, c, :Tn], in_=xn[:, c, :Tn], func=AF.Silu,
                                                                          scale=lg[:, c:c + 1], bias=lb[:, c:c + 1]),
                         reads=[xn.b, lg.b, lb.b], writes=[ca.b])
                S.op("sp", lambda e, ca=ca, ti=ti, Tn=Tn: e.dma_start(out=QTs_view(CATs, ti, Tn), in_=ca[:, :, :Tn]),
                     reads=[ca.b], writes=[S.db("CATs%d" % ti)], chan="st_" + ca.b.name)
                pq2, pk2 = PSF(), PSF()
                mm_feat(pq2, 1024, 4, w, hT, Tn)
                mm_feat(pk2, 1536, 4, w, hT, Tn)
                q_, k_ = qT[ti % 2], kT[ti % 2]
                S.op("act", lambda e, pq2=pq2, q_=q_, Tn=Tn: e.copy(out=q_[:, :, :Tn], in_=pq2[:, 0:4 * Tn].rearrange("p (c t) -> p c t", c=4)),
                     reads=[pq2.b], writes=[q_.b])
                S.op("dve", lambda e, pk2=pk2, k_=k_, Tn=Tn: e.tensor_copy(out=k_[:, :, :Tn], in_=pk2[:, 0:4 * Tn].rearrange("p (c t) -> p c t", c=4)),
                     reads=[pk2.b], writes=[k_.b])
                S.op("sp", lambda e, q_=q_, ti=ti, Tn=Tn: e.dma_start(out=QTs_view(QTs, ti, Tn), in_=q_[:, :, :Tn]),
                     reads=[q_.b], writes=[S.db("QTs%d" % ti)], chan="st_" + q_.b.name)
                S.op("sp", lambda e, k_=k_, t=t, Tn=Tn: e.dma_start(out=KTs[:, :, t["tok0"]:t["tok0"] + Tn], in_=k_[:, :, :Tn]),
                     reads=[k_.b], writes=[S.db("KTs%d" % ti)], chan="st_" + k_.b.name)
                kv = kvt[ti % 2]
                for n in range(2):
                    ps = PSF()
                    mm_tok(ps, 1536 + n * 512, 512, w, hT, Tn)
                    S.op("act" if n == 0 else "dve",
                         (lambda e, ps=ps, kv=kv, n=n, Tn=Tn: e.copy(out=kv[:Tn, n, :], in_=ps[:Tn, :])) if n == 0 else
                         (lambda e, ps=ps, kv=kv, n=n, Tn=Tn: e.tensor_copy(out=kv[:Tn, n, :], in_=ps[:Tn, :])),
                         reads=[ps.b], writes=[kv.b])
                okd, ovd = (o_kp, o_vp) if t["prompt"] else (o_ks, o_vs)
                r0 = t["tok0"] if t["prompt"] else t["s"] * 64
                S.op("sp", lambda e, kv=kv, okd=okd, r0=r0, Tn=Tn: e.dma_start(out=okd[r0:r0 + Tn, :], in_=kv[:Tn, 0, :]),
                     reads=[kv.b], writes=[S.db("okv")], chan="st_" + kv.b.name)
                S.op("sp", lambda e, kv=kv, ovd=ovd, r0=r0, Tn=Tn: e.dma_start(out=ovd[r0:r0 + Tn, :], in_=kv[:Tn, 1, :]),
                     reads=[kv.b], writes=[S.db("okv")], chan="st_" + kv.b.name)
                v_ = vx[ti % 2]
                S.op("pool", lambda e, kv=kv, v_=v_, Tn=Tn: e.tensor_copy(out=v_[:Tn, :, 0:128], in_=kv[:Tn, 1, :].rearrange("p (h e) -> p h e", h=4)),
                     reads=[kv.b], writes=[v_.b])
                S.op("sp", lambda e, v_=v_, t=t, Tn=Tn: e.dma_start(out=VXs[t["tok0"]:t["tok0"] + Tn, :], in_=v_[:Tn].rearrange("p h e -> p (h e)")),
                     reads=[v_.b], writes=[S.db("VXs%d" % ti)], chan="st_" + v_.b.name)
            S.emit()

        with ExitStack() as es:
            sb = phase_tiles(es, "p2")
            NKT = max(NPT, NPK + 1)
            wo = load_w(sb, w_out_ab, D, "wo")
            KT = sb([128, 4, NKT * 128], BF16, "KT")
            VS = sb([128, NKT, 516], BF16, "VS")
            kcs = [sb([128, 4, 512], F32, "kcs") for _ in range(2)]
            xt = [sb([128, D], F32, "x") for _ in range(2)]
            qT = [sb([128, 4, 128], BF16, "qT") for _ in range(2)]
            caT = [sb([128, 4, 128], BF16, "caT") for _ in range(2)]
            PT = [sb([128, 4, 128], BF16, "PT") for _ in range(3)]
            ah = sb([128, 128], F32, "ah")
            rr_ = sb([128, 8], F32, "rr")
            junk = sb([128, 128], F32, "junk")
            otok = sb([128, 512], BF16, "otok")
            oT = sb([128, 4, 128], BF16, "oT")
            lam = sb([128, 8], F32, "lam")
            lv = [load_rep(sb, a, 64, "lv") for a in (lam_q1, lam_k1, lam_q2, lam_k2)]
            gs = load_rep(sb, subln_g, 128, "gs")
            lj = sb([128, 64], F32, "lj")
            S.op("dve", lambda e: e.memset(lam[:], 0.0), writes=[lam.b])
            S.op("dve", lambda e: e.scalar_tensor_tensor(out=lj[:], in0=lv[0][:], scalar=1.0, in1=lv[1][:], op0=ALU.mult, op1=ALU.mult,
                                                         accum_out=lam[:, 0:1]), reads=[lv[0].b, lv[1].b, lam.b], writes=[lj.b, lam.b])
            S.op("dve", lambda e: e.scalar_tensor_tensor(out=lj[:], in0=lv[2][:], scalar=1.0, in1=lv[3][:], op0=ALU.mult, op1=ALU.mult,
                                                         accum_out=lam[:, 1:2]), reads=[lv[2].b, lv[3].b, lam.b, lj.b], writes=[lj.b, lam.b])
            S.op("act", lambda e: e.activation(out=lam[:, 2:4], in_=lam[:, 0:2], func=AF.Exp), reads=[lam.b], writes=[lam.b])
            S.op("dve", lambda e: e.tensor_tensor(out=lam[:, 4:5], in0=lam[:, 3:4], in1=lam[:, 2:3], op=ALU.subtract), reads=[lam.b], writes=[lam.b])
            S.op("dve", lambda e: e.tensor_scalar(out=lam[:, 5:6], in0=lam[:, 4:5], scalar1=-LAM0, scalar2=None, op0=ALU.add), reads=[lam.b], writes=[lam.b])
            S.op("dve", lambda e: e.tensor_scalar(out=gs[:], in0=gs[:], scalar1=(1.0 - LAM0), scalar2=None, op0=ALU.mult), reads=[gs.b], writes=[gs.b])
            psO = [psf[0], psf[1]]
            psS = [psf[2], psf[3]]
            psX = [psf[4], psf[5]]
            cnt = {"s": 0, "p": 0}
            seqs = [(0, True)] + [(1 + s, False) for s in range(NSS)]
            for (sq_, isp) in seqs:
                stiles = [(ti, t) for ti, t in enumerate(tiles) if t["seq"] == sq_]
                if isp:
                    nkt_total = NPT
                    for h in range(4):
                        S.op("sp", lambda e, h=h: e.dma_start(out=KT[:, h, 0:NPT * 128], in_=KTs[:, h, 0:NPT * 128]),
                             reads=[S.db("KTs%d" % i) for i in range(NPT)], writes=[KT.b], chan="KT")
                    for j0 in range(0, NPT, 8):
                        j1 = min(NPT, j0 + 8)
                        S.op("sp", lambda e, j0=j0, j1=j1: e.dma_start(out=VS[:, j0:j1, :], in_=VXs[j0 * 128:j1 * 128, :].rearrange("(j p) f -> p j f", p=128)),
                             reads=[S.db("VXs%d" % i) for i in range(j0, j1)], writes=[VS.b], chan="VS")
                    klens = [128] * NPT
                else:
                    s = sq_ - 1
                    ti0 = NPT + s
                    tok0 = NPT * 128 + s * 64
                    S.op("dve", lambda e: e.memset(VS[:, 0:NPK, :], 1.0), writes=[VS.b])
                    for j in range(NPK):
                        S.op("pool", lambda e, s=s, j=j: e.dma_start(out=VS[:, j, :].rearrange("p (h e) -> p h e", h=4)[:, :, 0:128],
                                                                     in_=cv[s, j * 128:(j + 1) * 128, :].rearrange("p (h e) -> p h e", h=4)),
                             writes=[VS.b], chan="VS")
                    S.op("sp", lambda e, tok0=tok0: e.dma_start(out=VS[0:64, NPK, :], in_=VXs[tok0:tok0 + 64, :]),
                         reads=[S.db("VXs%d" % ti0)], writes=[VS.b], chan="VS")
                    S.op("sp", lambda e, tok0=tok0: e.dma_start(out=KT[:, :, PL:PL + 64], in_=KTs[:, :, tok0:tok0 + 64]),
                         reads=[S.db("KTs%d" % ti0)], writes=[KT.b], chan="KT")
                    for j in range(NPK):
                        kc = kcs[j % 2]
                        S.op("sp", lambda e, kc=kc, j=j, s=s: e.dma_start(out=kc[:, 0, :], in_=ck[s, j * 128:(j + 1) * 128, :]),
                             writes=[kc.b], chan=kc.b.name)
                        ps = psX[j % 2]
                        for h in range(4):
                            S.op("pe", lambda e, h=h, kc=kc, ps=ps: e.transpose(out=ps[:, h * 128:(h + 1) * 128], in_=kc[:, 0, h * 128:(h + 1) * 128],
                                                                                identity=identf[:, :]),
                                 reads=[kc.b, identf.b], writes=[ps.b])
                        S.op("act", lambda e, ps=ps, j=j: e.copy(out=KT[:, :, j * 128:(j + 1) * 128], in_=ps[:, :].rearrange("p (h k) -> p h k", h=4)),
                             reads=[ps.b], writes=[KT.b])
                    klens = [128] * NPK + [64]
                for (ti, t) in stiles:
                    Tn = t["T"]
                    x, q_, ca = xt[ti % 2], qT[ti % 2], caT[ti % 2]
                    S.op("sp", lambda e, x=x, t=t, Tn=Tn: e.dma_start(out=x[:Tn], in_=xin_rows(t)), writes=[x.b], chan=x.b.name)
                    S.op("sp", lambda e, q_=q_, ti=ti, Tn=Tn: e.dma_start(out=q_[:, :, :Tn], in_=QTs_view(QTs, ti, Tn)),
                         reads=[S.db("QTs%d" % ti)], writes=[q_.b], chan=q_.b.name)
                    S.op("sp", lambda e, ca=ca, ti=ti, Tn=Tn: e.dma_start(out=ca[:, :, :Tn], in_=QTs_view(CATs, ti, Tn)),
                         reads=[S.db("CATs%d" % ti)], writes=[ca.b], chan=ca.b.name)
                    nk = (t["i"] + 1) if isp else (NPK + 1)
                    for h in range(4):
                        for jb in range(0, nk, 4):
                            js = list(range(jb, min(nk, jb + 4)))
                            for tt in range(2):
                                p0 = 64 * tt
                                pS = psS[cnt["s"] % 2]
                                cnt["s"] += 1
                                P = PT[cnt["p"] % 3]
                                cnt["p"] += 1
                                for jj, j in enumerate(js):
                                    kl = klens[j]
                                    S.op("pe", lambda e, jj=jj, j=j, kl=kl, pS=pS, h=h, p0=p0, q_=q_, Tn=Tn: e.matmul(
                                        pS[:kl, jj * 128:jj * 128 + Tn], lhsT=KT[p0:p0 + 64, h, j * 128:j * 128 + kl], rhs=q_[p0:p0 + 64, h, :Tn],
                                        start=True, stop=True), reads=[KT.b, q_.b], writes=[pS.b])
                                nj = len(js)
                                S.op("act", lambda e, pS=pS, P=P, nj=nj, Tn=Tn: e.activation(
                                    out=P[:, 0:nj, :Tn], in_=pS[:, 0:nj * 128].rearrange("p (j t) -> p j t", j=nj)[:, :, :Tn], func=AF.Exp, scale=0.125),
                                    reads=[pS.b], writes=[P.b])
                                if isp and js[-1] == t["i"]:
                                    jj = len(js) - 1
                                    S.op("dve", lambda e, P=P, jj=jj: e.memset(P[64:128, jj, 0:64], 0.0), reads=[P.b], writes=[P.b])
                                for jj, j in enumerate(js):
                                    kl = klens[j]
                                    S.op("pe", lambda e, jj=jj, j=j, kl=kl, P=P, h=h, tt=tt, Tn=Tn, nk=nk: e.matmul(
                                        psO[tt][:Tn, 0:129], lhsT=P[:kl, jj, :Tn], rhs=VS[:kl, j, h * 129:(h + 1) * 129],
                                        start=(j == 0), stop=(j == nk - 1)), reads=[P.b, VS.b], writes=[psO[tt].b])
                        S.op("dve", lambda e, Tn=Tn: e.reciprocal(out=rr_[:Tn, 0:1], in_=psO[0][:Tn, 128:129]), reads=[psO[0].b], writes=[rr_.b])
                        S.op("dve", lambda e, Tn=Tn: e.reciprocal(out=rr_[:Tn, 1:2], in_=psO[1][:Tn, 128:129]), reads=[psO[1].b, rr_.b], writes=[rr_.b])
                        S.op("dve", lambda e, Tn=Tn: e.tensor_tensor(out=rr_[:Tn, 2:3], in0=rr_[:Tn, 1:2], in1=lam[:Tn, 5:6], op=ALU.mult),
                             reads=[rr_.b, lam.b], writes=[rr_.b])
                        S.op("dve", lambda e, Tn=Tn: e.tensor_scalar(out=ah[:Tn], in0=psO[0][:Tn, 0:128], scalar1=rr_[:Tn, 0:1], scalar2=None, op0=ALU.mult),
                             reads=[psO[0].b, rr_.b], writes=[ah.b])
                        S.op("dve", lambda e, Tn=Tn: e.scalar_tensor_tensor(out=ah[:Tn], in0=psO[1][:Tn, 0:128], scalar=rr_[:Tn, 2:3], in1=ah[:Tn],
                                                                            op0=ALU.mult, op1=ALU.add), reads=[psO[1].b, rr_.b, ah.b], writes=[ah.b])
                        S.op("dve", lambda e, Tn=Tn: e.memset(rr_[:Tn, 3:4], 0.0), reads=[rr_.b], writes=[rr_.b])
                        S.op("act", lambda e, Tn=Tn: e.activation(out=junk[:Tn], in_=ah[:Tn], func=AF.Square, accum_out=rr_[:Tn, 3:4]),
                             reads=[ah.b, rr_.b], writes=[junk.b, rr_.b])
                        S.op("act", lambda e, Tn=Tn: e.activation(out=rr_[:Tn, 4:5], in_=rr_[:Tn, 3:4], func=AF.Sqrt, scale=1.0 / 128, bias=1e-6),
                             reads=[rr_.b], writes=[rr_.b])
                        S.op("dve", lambda e, Tn=Tn: e.reciprocal(out=rr_[:Tn, 4:5], in_=rr_[:Tn, 4:5]), reads=[rr_.b], writes=[rr_.b])
                        S.op("dve", lambda e, Tn=Tn, h=h: e.scalar_tensor_tensor(out=otok[:Tn, h * 128:(h + 1) * 128], in0=ah[:Tn], scalar=rr_[:Tn, 4:5],
                                                                                 in1=gs[:Tn], op0=ALU.mult, op1=ALU.mult),
                             reads=[ah.b, rr_.b, gs.b], writes=[otok.b])
                    pb = PSB()
                    for h in range(4):
                        S.op("pe", lambda e, h=h, pb=pb, Tn=Tn: e.transpose(out=pb[:, h * Tn:(h + 1) * Tn], in_=otok[:Tn, h * 128:(h + 1) * 128],
                                                                            identity=identb[:Tn, :Tn]), reads=[otok.b, identb.b], writes=[pb.b])
                    S.op("act", lambda e, pb=pb, Tn=Tn: e.copy(out=oT[:, :, :Tn], in_=pb[:, 0:4 * Tn].rearrange("p (h t) -> p h t", h=4)),
                         reads=[pb.b], writes=[oT.b])
                    for n in range(2):
                        ps = psX[n]
                        for c in range(8):
                            src = ca if c < 4 else oT
                            S.op("pe", lambda e, c=c, n=n, ps=ps, src=src, Tn=Tn: e.matmul(ps[:Tn, :], lhsT=src[:, c % 4, :Tn],
                                                                                          rhs=wo[:, c, n * 512:(n + 1) * 512],
                                                                                          start=(c == 0), stop=(c == 7)),
                                 reads=[src.b, wo.b], writes=[ps.b])
                        S.op("dve", lambda e, n=n, ps=ps, x=x, Tn=Tn: e.tensor_tensor(out=x[:Tn, n * 512:(n + 1) * 512], in0=ps[:Tn, :],
                                                                                     in1=x[:Tn, n * 512:(n + 1) * 512], op=ALU.add),
                             reads=[ps.b, x.b], writes=[x.b])
                    S.op("sp", lambda e, x=x, t=t, Tn=Tn: e.dma_start(out=XS[t["tok0"]:t["tok0"] + Tn, :], in_=x[:Tn]),
                         reads=[x.b], writes=[S.db("XS%d" % ti)], chan="st_" + x.b.name)
            S.emit()

        def cross_peer(l, final, peer=True):
            with ExitStack() as es:
                sb = phase_tiles(es, "p3%d" % l)
                wq = load_w(sb, w_xq[l], D, "wq")
                wo_ = load_w(sb, w_xo[l], D, "wo")
                wp = load_w(sb, w_pq[l], 2048, "wp")
                gcx = load_cols(sb, norm_cross_g[l:l + 1, :], 8, "gcx")
                gcf = load_cols(sb, norm_ffn_g[l:l + 1, :], 8, "gcf")
                grf = load_rep(sb, norm_ffn_g[l:l + 1, :], D, "grf")
                if final:
                    grz = load_rep(sb, norm_final_g, D, "grz")
                skT = sb([128, 2, 128], F32, "skT")
                sk32 = sb([128, 2, 128], F32, "sk32")
                S.op("sp", lambda e: e.dma_start(out=sk32[:], in_=sub_keys[l].rearrange("c k d -> k c d")), writes=[sk32.b], chan="sk32")
                for c in range(2):
                    ps = PSF()
                    S.op("pe", lambda e, c=c, ps=ps: e.transpose(out=ps[:, 0:128], in_=sk32[:, c, :], identity=identf[:, :]),
                         reads=[sk32.b, identf.b], writes=[ps.b])
                    S.op("act", lambda e, c=c, ps=ps: e.copy(out=skT[:, c, :], in_=ps[:, 0:128]), reads=[ps.b], writes=[skT.b])
                rms_T = make_rms(sb)
                iot = sb([128, 16], F32, "iot")
                S.op("pool", lambda e: e.iota(iot[:], pattern=[[1, 16]], base=0, channel_multiplier=0, allow_small_or_imprecise_dtypes=True),
                     writes=[iot.b])
                mb = sb([128, 2, D], BF16, "mb")
                mkT = sb([128, 8, 256], BF16, "mkT")
                mvb = sb([128, 2, D], BF16, "mvb")
                xt = [sb([128, D], F32, "x") for _ in range(2)]
                hT = sb([128, 8, 128], BF16, "hT")
                h32 = sb([128, D], F32, "h32")
                qTx = sb([128, 8, 128], BF16, "qTx")
                PTx = sb([128, 8, 128], BF16, "PTx")
                rinv = sb([128, 4, 128], F32, "rinv")
                oTx = sb([128, 8, 128], BF16, "oTx")
                pqT = sb([128, 16, 128], F32, "pqT")
                sc = sb([128, 16, 128], F32, "sc")
                sc2 = sb([128, 16, 128], F32, "sc2")
                ts = sb([128, 16, 16], F32, "ts")
                tiu = sb([128, 16, 16], U32, "tiu")
                tif = sb([128, 16, 16], F32, "tif")
                cand = sb([128, 8, 256], F32, "cand")
                cand2 = T(sc2.ap.rearrange("p (h a) k -> p h (a k)", h=8), "cand2v")
                cand2.b = sc2.b
                bs = sb([128, 8, 16], F32, "bs")
                bpu = sb([128, 8, 16], U32, "bpu")
                bpf = sb([128, 8, 16], F32, "bpf")
                fa = sb([128, 8, 16], F32, "fa")
                fb_ = sb([128, 8, 16], F32, "fb")
                ia = sb([128, 8, 16], I32, "ia")
                oh = sb([128, 8, 16, 16], F32, "oh")
                i0f = sb([128, 8, 16], F32, "i0f")
                i1f = sb([128, 8, 16], F32, "i1f")
                idx = sb([128, 128], I32, "idx")
                gate = sb([128, 8, 16], F32, "gate")
                gsum = sb([128, 8], F32, "gsum")
                act_ = sb([128, 128], F32, "act")
                g1 = sb([128, 128], F32, "g1")
                g2 = sb([128, 128], F32, "g2")
                wgt = sb([128, 128], F32, "wgt")
                NG = 8
                gb = [sb([128, D], F32, "gb") for _ in range(NG)]
                pj = rms_T.junk
                accs = [sb([128, D], F32, "pacc") for _ in range(2)]
                S.op("dve", lambda e: e.memset(idx[:], 0), writes=[idx.b])
                gcnt = [0]
                cur_seq = [None]
                for ti, t in enumerate(tiles):
                    Tn = t["T"]
                    if cur_seq[0] != t["seq"]:
                        cur_seq[0] = t["seq"]
                        ksrc = o_mk[l] if t["prompt"] else cmk[l, t["s"]]
                        vsrc = o_mv[l] if t["prompt"] else cmv[l, t["s"]]
                        S.op("pool", lambda e, ksrc=ksrc: e.dma_start(out=mb[:], in_=ksrc.rearrange("(m p) d -> p m d", p=128)),
                             reads=[S.db("omem")], writes=[mb.b], chan="mb")
                        for mc in range(2):
                            pb = PSB()
                            for c in range(8):
                                S.op("pe", lambda e, c=c, mc=mc, pb=pb: e.transpose(out=pb[:, c * 128:(c + 1) * 128], in_=mb[:, mc, c * 128:(c + 1) * 128],
                                                                                    identity=identb[:]), reads=[mb.b, identb.b], writes=[pb.b])
                            S.op("act", lambda e, mc=mc, pb=pb: e.copy(out=mkT[:, :, mc * 128:(mc + 1) * 128],
                                                                       in_=pb[:, :].rearrange("p (c t) -> p c t", c=8)), reads=[pb.b], writes=[mkT.b])
                        S.op("pool", lambda e, vsrc=vsrc: e.dma_start(out=mvb[:], in_=vsrc.rearrange("(m p) d -> p m d", p=128)),
                             reads=[S.db("omem")], writes=[mvb.b], chan="mvb")
                    x = xt[ti % 2]
                    S.op("sp", lambda e, x=x, t=t, Tn=Tn: e.dma_start(out=x[:Tn], in_=XS[t["tok0"]:t["tok0"] + Tn, :]),
                         reads=[S.db("XS%d" % ti)], writes=[x.b], chan=x.b.name)
                    rms_T(x, Tn, gcx, hT)
                    for half in range(2):
                        ps = PSF()
                        mm_feat(ps, half * 512, 4, wq, hT, Tn)
                        S.op("act", lambda e, ps=ps, half=half, Tn=Tn: e.copy(out=qTx[:, half * 4:half * 4 + 4, :Tn],
                                                                             in_=ps[:, 0:4 * Tn].rearrange("p (c t) -> p c t", c=4)),
                             reads=[ps.b], writes=[qTx.b])
                    pss = [PSF(), PSF()]
                    for h in range(4):
                        for mc in range(2):
                            ps = pss[h // 2]
                            o0 = ((h % 2) * 2 + mc) * Tn
                            for dc in range(2):
                                S.op("pe", lambda e, h=h, mc=mc, dc=dc, ps=ps, o0=o0, Tn=Tn: e.matmul(
                                    ps[:, o0:o0 + Tn], lhsT=mkT[:, 2 * h + dc, mc * 128:(mc + 1) * 128], rhs=qTx[:, 2 * h + dc, :Tn],
                                    start=(dc == 0), stop=(dc == 1)), reads=[mkT.b, qTx.b], writes=[ps.b])
                    for hh in range(2):
                        S.op("act", lambda e, hh=hh, Tn=Tn, pss=pss: e.activation(out=PTx[:, hh * 4:hh * 4 + 4, :Tn],
                                                                         in_=pss[hh][:, 0:4 * Tn].rearrange("p (c t) -> p c t", c=4),
                                                                         func=AF.Exp, scale=1.0 / 16), reads=[pss[hh].b], writes=[PTx.b])
                    psr = PSF()
                    for h in range(4):
                        for mc in range(2):
                            S.op("pe", lambda e, h=h, mc=mc, Tn=Tn, psr=psr: e.matmul(psr[:, h * Tn:(h + 1) * Tn], lhsT=onesb[:, :], rhs=PTx[:, h * 2 + mc, :Tn],
                                                                            start=(mc == 0), stop=(mc == 1)), reads=[onesb.b, PTx.b], writes=[psr.b])
                    S.op("dve", lambda e, Tn=Tn, psr=psr: e.reciprocal(out=rinv[:, :, :Tn], in_=psr[:, 0:4 * Tn].rearrange("p (h t) -> p h t", h=4)),
                         reads=[psr.b], writes=[rinv.b])
                    for half in range(2):
                        ps = PSF()
                        for jj in range(4):
                            j = half * 4 + jj
                            h = j // 2
                            for mc in range(2):
                                S.op("pe", lambda e, j=j, jj=jj, h=h, mc=mc, ps=ps, Tn=Tn: e.matmul(
                                    ps[:, jj * Tn:(jj + 1) * Tn], lhsT=mvb[:, mc, j * 128:(j + 1) * 128], rhs=PTx[:, h * 2 + mc, :Tn],
                                    start=(mc == 0), stop=(mc == 1)), reads=[mvb.b, PTx.b], writes=[ps.b])
                        for hh in range(2):
                            h = half * 2 + hh
                            S.op("dve", lambda e, ps=ps, hh=hh, h=h, Tn=Tn: e.tensor_tensor(
                                out=oTx[:, 2 * h:2 * h + 2, :Tn], in0=ps[:, hh * 2 * Tn:(hh * 2 + 2) * Tn].rearrange("p (c t) -> p c t", c=2),
                                in1=rinv[:, h, :Tn].unsqueeze(1).to_broadcast([128, 2, Tn]), op=ALU.mult),
                                reads=[ps.b, rinv.b], writes=[oTx.b])
                    for n in range(2):
                        ps = PSF()
                        mm_tok(ps, n * 512, 512, wo_, oTx, Tn)
                        S.op("dve", lambda e, n=n, ps=ps, x=x, Tn=Tn: e.tensor_tensor(out=x[:Tn, n * 512:(n + 1) * 512], in0=ps[:Tn, :],
                                                                                     in1=x[:Tn, n * 512:(n + 1) * 512], op=ALU.add),
                             reads=[ps.b, x.b], writes=[x.b])
                    if not peer:
                        S.op("sp", lambda e, x=x, t=t, Tn=Tn: e.dma_start(out=XS[t["tok0"]:t["tok0"] + Tn, :], in_=x[:Tn]),
                             reads=[x.b], writes=[S.db("XS%d" % ti)], chan="st_" + x.b.name)
                        continue
                    rms_T(x, Tn, gcf, hT, h32=h32, grep=grf)
                    for q4 in range(4):
                        ps = PSF()
                        mm_feat(ps, q4 * 512, 4, wp, hT, Tn)
                        S.op("act", lambda e, ps=ps, q4=q4, Tn=Tn: e.copy(out=pqT[:, q4 * 4:q4 * 4 + 4, :Tn],
                                                                         in_=ps[:, 0:4 * Tn].rearrange("p (c t) -> p c t", c=4)),
                             reads=[ps.b], writes=[pqT.b])
                    for q4 in range(4):
                        ps = PSF()
                        for jj in range(4):
                            j = q4 * 4 + jj
                            S.op("pe", lambda e, j=j, jj=jj, ps=ps, Tn=Tn: e.matmul(ps[:Tn, jj * 128:(jj + 1) * 128], lhsT=pqT[:, j, :Tn],
                                                                                   rhs=skT[:, j % 2, :], start=True, stop=True),
                                 reads=[pqT.b, skT.b], writes=[ps.b])
                        S.op("act", lambda e, ps=ps, q4=q4, Tn=Tn: e.copy(out=sc[:Tn, q4 * 4:q4 * 4 + 4, :],
                                                                         in_=ps[:Tn, :].rearrange("p (c k) -> p c k", c=4)),
                             reads=[ps.b], writes=[sc.b])
                    for j in range(16):
                        S.op("dve", lambda e, j=j, Tn=Tn: e.max(out=ts[:Tn, j, 0:8], in_=sc[:Tn, j, :]), reads=[sc.b], writes=[ts.b])
                    for j in range(16):
                        S.op("dve", lambda e, j=j, Tn=Tn: e.max_index(out=tiu[:Tn, j, 0:8], in_max=ts[:Tn, j, 0:8], in_values=sc[:Tn, j, :]),
                             reads=[sc.b, ts.b], writes=[tiu.b])
                    for j in range(16):
                        S.op("dve", lambda e, j=j, Tn=Tn: e.match_replace(out=sc2[:Tn, j, :], in_to_replace=ts[:Tn, j, 0:8], in_values=sc[:Tn, j, :],
                                                                         imm_value=NEG), reads=[sc.b, ts.b], writes=[sc2.b])
                    for j in range(16):
                        S.op("dve", lambda e, j=j, Tn=Tn: e.max(out=ts[:Tn, j, 8:16], in_=sc2[:Tn, j, :]), reads=[sc2.b], writes=[ts.b])
                    for j in range(16):
                        S.op("dve", lambda e, j=j, Tn=Tn: e.max_index(out=tiu[:Tn, j, 8:16], in_max=ts[:Tn, j, 8:16], in_values=sc2[:Tn, j, :]),
                             reads=[sc2.b, ts.b], writes=[tiu.b])
                    S.op("dve", lambda e, Tn=Tn: e.tensor_copy(out=tif[:Tn], in_=tiu[:Tn]), reads=[tiu.b], writes=[tif.b])
                    tsv = ts[:Tn].rearrange("p (h c) k -> p h c k", c=2)
                    tfv = tif[:Tn].rearrange("p (h c) k -> p h c k", c=2)
                    S.op("dve", lambda e, Tn=Tn, tsv=tsv: e.tensor_tensor(
                        out=cand[:Tn].rearrange("p h (a b) -> p h a b", a=16),
                        in0=tsv[:, :, 0, :].unsqueeze(3).to_broadcast([Tn, 8, 16, 16]),
                        in1=tsv[:, :, 1, :].unsqueeze(2).to_broadcast([Tn, 8, 16, 16]), op=ALU.add), reads=[ts.b], writes=[cand.b])
                    for h in range(8):
                        S.op("dve", lambda e, h=h, Tn=Tn: e.max(out=bs[:Tn, h, 0:8], in_=cand[:Tn, h, :]), reads=[cand.b], writes=[bs.b])
                    for h in range(8):
                        S.op("dve", lambda e, h=h, Tn=Tn: e.max_index(out=bpu[:Tn, h, 0:8], in_max=bs[:Tn, h, 0:8], in_values=cand[:Tn, h, :]),
                             reads=[cand.b, bs.b], writes=[bpu.b])
                    for h in range(8):
                        S.op("dve", lambda e, h=h, Tn=Tn: e.match_replace(out=cand2[:Tn, h, :], in_to_replace=bs[:Tn, h, 0:8], in_values=cand[:Tn, h, :],
                                                                         imm_value=NEG), reads=[cand.b, bs.b], writes=[cand2.b])
                    for h in range(8):
                        S.op("dve", lambda e, h=h, Tn=Tn: e.max(out=bs[:Tn, h, 8:16], in_=cand2[:Tn, h, :]), reads=[cand2.b], writes=[bs.b])
                    for h in range(8):
                        S.op("dve", lambda e, h=h, Tn=Tn: e.max_index(out=bpu[:Tn, h, 8:16], in_max=bs[:Tn, h, 8:16], in_values=cand2[:Tn, h, :]),
                             reads=[cand2.b, bs.b], writes=[bpu.b])
                    S.op("dve", lambda e, Tn=Tn: e.tensor_tensor(out=gate[:Tn], in0=bs[:Tn], in1=bs[:Tn, :, 0:1].to_broadcast([Tn, 8, 16]), op=ALU.subtract),
                         reads=[bs.b], writes=[gate.b])
                    S.op("act", lambda e, Tn=Tn: e.activation(out=gate[:Tn], in_=gate[:Tn], func=AF.Exp), reads=[gate.b], writes=[gate.b])
                    S.op("dve", lambda e, Tn=Tn: e.tensor_reduce(out=gsum[:Tn], in_=gate[:Tn], axis=AX.X, op=ALU.add), reads=[gate.b], writes=[gsum.b])
                    S.op("dve", lambda e, Tn=Tn: e.reciprocal(out=gsum[:Tn], in_=gsum[:Tn]), reads=[gsum.b], writes=[gsum.b])
                    S.op("dve", lambda e, Tn=Tn: e.tensor_tensor(out=gate[:Tn], in0=gate[:Tn], in1=gsum[:Tn].unsqueeze(2).to_broadcast([Tn, 8, 16]), op=ALU.mult),
                         reads=[gate.b, gsum.b], writes=[gate.b])
                    S.op("dve", lambda e, Tn=Tn: e.tensor_copy(out=bpf[:Tn], in_=bpu[:Tn]), reads=[bpu.b], writes=[bpf.b])
                    S.op("dve", lambda e, Tn=Tn: e.tensor_scalar(out=fb_[:Tn], in0=bpf[:Tn], scalar1=0.0625, scalar2=-0.46875, op0=ALU.mult, op1=ALU.add),
                         reads=[bpf.b], writes=[fb_.b])
                    S.op("dve", lambda e, Tn=Tn: e.tensor_copy(out=ia[:Tn], in_=fb_[:Tn]), reads=[fb_.b], writes=[ia.b])
                    S.op("dve", lambda e, Tn=Tn: e.tensor_copy(out=fa[:Tn], in_=ia[:Tn]), reads=[ia.b], writes=[fa.b])
                    S.op("dve", lambda e, Tn=Tn: e.scalar_tensor_tensor(out=fb_[:Tn], in0=fa[:Tn], scalar=-16.0, in1=bpf[:Tn], op0=ALU.mult, op1=ALU.add),
                         reads=[fa.b, bpf.b, fb_.b], writes=[fb_.b])
                    for (pos, cc, dst) in ((fa, 0, i0f), (fb_, 1, i1f)):
                        S.op("dve", lambda e, pos=pos, Tn=Tn: e.tensor_tensor(
                            out=oh[:Tn], in0=iot[:Tn, :].unsqueeze(1).unsqueeze(1).to_broadcast([Tn, 8, 16, 16]),
                            in1=pos[:Tn].unsqueeze(3).to_broadcast([Tn, 8, 16, 16]), op=ALU.is_equal), reads=[iot.b, pos.b], writes=[oh.b])
                        S.op("dve", lambda e, cc=cc, Tn=Tn, tfv=tfv: e.tensor_tensor(
                            out=oh[:Tn], in0=oh[:Tn], in1=tfv[:, :, cc, :].unsqueeze(2).to_broadcast([Tn, 8, 16, 16]), op=ALU.mult),
                            reads=[oh.b, tif.b], writes=[oh.b])
                        S.op("dve", lambda e, dst=dst, Tn=Tn: e.tensor_reduce(out=dst[:Tn], in_=oh[:Tn], axis=AX.X, op=ALU.add), reads=[oh.b], writes=[dst.b])
                    S.op("dve", lambda e, Tn=Tn: e.scalar_tensor_tensor(out=i0f[:Tn], in0=i0f[:Tn], scalar=128.0, in1=i1f[:Tn], op0=ALU.mult, op1=ALU.add),
                         reads=[i0f.b, i1f.b], writes=[i0f.b])
                    S.op("dve", lambda e, Tn=Tn: e.tensor_copy(out=idx[:Tn, :], in_=i0f[:Tn].rearrange("p h k -> p (h k)")), reads=[i0f.b], writes=[idx.b])
                    S.op("dve", lambda e: e.memset(act_[:], 0.0), writes=[act_.b])
                    for s_ in range(128):
                        g = gb[gcnt[0] % NG]
                        gcnt[0] += 1
                        S.op("pool", lambda e, g=g, s_=s_: e.indirect_dma_start(out=g[:, :], out_offset=None, in_=expert_u.rearrange("l e d -> (l e) d"),
                                                                                in_offset=bass.IndirectOffsetOnAxis(ap=idx[:, s_:s_ + 1], axis=0),
                                                                                element_offset=l * 16384 * D),
                             reads=[idx.b], writes=[g.b], chan=g.b.name)
                        S.op("dve", lambda e, g=g, s_=s_, Tn=Tn: e.scalar_tensor_tensor(out=pj[:Tn], in0=g[:Tn], scalar=1.0, in1=h32[:Tn],
                                                                                        op0=ALU.mult, op1=ALU.mult, accum_out=act_[:Tn, s_:s_ + 1]),
                             reads=[g.b, h32.b], writes=[pj.b, act_.b])
                    S.op("dve", lambda e, Tn=Tn: e.tensor_tensor(out=g1[:Tn], in0=act_[:Tn], in1=act_[:Tn], op=ALU.mult), reads=[act_.b], writes=[g1.b])
                    S.op("dve", lambda e, Tn=Tn: e.tensor_scalar(out=g1[:Tn], in0=g1[:Tn], scalar1=0.044715, scalar2=1.0, op0=ALU.mult, op1=ALU.add),
                         reads=[g1.b], writes=[g1.b])
                    S.op("dve", lambda e, Tn=Tn: e.tensor_tensor(out=g1[:Tn], in0=g1[:Tn], in1=act_[:Tn], op=ALU.mult), reads=[g1.b, act_.b], writes=[g1.b])
                    S.op("act", lambda e, Tn=Tn: e.activation(out=g2[:Tn], in_=g1[:Tn], func=AF.Sigmoid, scale=1.5957691216057308),
                         reads=[g1.b], writes=[g2.b])
                    S.op("dve", lambda e, Tn=Tn: e.tensor_tensor(out=g2[:Tn], in0=g2[:Tn], in1=act_[:Tn], op=ALU.mult), reads=[g2.b, act_.b], writes=[g2.b])
                    S.op("dve", lambda e, Tn=Tn: e.tensor_tensor(out=wgt[:Tn], in0=g2[:Tn], in1=gate[:Tn].rearrange("p h k -> p (h k)"), op=ALU.mult),
                         reads=[g2.b, gate.b], writes=[wgt.b])
                    for s_ in range(128):
                        g = gb[gcnt[0] % NG]
                        gcnt[0] += 1
                        S.op("pool", lambda e, g=g, s_=s_: e.indirect_dma_start(out=g[:, :], out_offset=None, in_=expert_v.rearrange("l e d -> (l e) d"),
                                                                                in_offset=bass.IndirectOffsetOnAxis(ap=idx[:, s_:s_ + 1], axis=0),
                                                                                element_offset=l * 16384 * D),
                             reads=[idx.b], writes=[g.b], chan=g.b.name)
                        a_ = accs[s_ % 2]
                        if s_ < 2:
                            S.op("dve", lambda e, g=g, s_=s_, a_=a_, Tn=Tn: e.tensor_scalar(out=a_[:Tn], in0=g[:Tn], scalar1=wgt[:Tn, s_:s_ + 1], scalar2=None,
                                                                                          op0=ALU.mult), reads=[g.b, wgt.b], writes=[a_.b])
                        else:
                            S.op("dve", lambda e, g=g, s_=s_, a_=a_, Tn=Tn: e.scalar_tensor_tensor(out=a_[:Tn], in0=g[:Tn], scalar=wgt[:Tn, s_:s_ + 1], in1=a_[:Tn],
                                                                                                  op0=ALU.mult, op1=ALU.add),
                                 reads=[g.b, wgt.b, a_.b], writes=[a_.b])
                    S.op("dve", lambda e, x=x, Tn=Tn: e.tensor_tensor(out=x[:Tn], in0=x[:Tn], in1=accs[0][:Tn], op=ALU.add), reads=[x.b, accs[0].b], writes=[x.b])
                    S.op("dve", lambda e, x=x, Tn=Tn: e.tensor_tensor(out=x[:Tn], in0=x[:Tn], in1=accs[1][:Tn], op=ALU.add), reads=[x.b, accs[1].b], writes=[x.b])
                    if final:
                        ss = sb([128, 2], F32, "fss") if ti == 0 else ss
                        S.op("dve", lambda e, ss=ss, Tn=Tn: e.memset(ss[:Tn, 0:1], 0.0), writes=[ss.b])
                        S.op("act", lambda e, ss=ss, x=x, Tn=Tn: e.activation(out=pj[:Tn], in_=x[:Tn], func=AF.Square, accum_out=ss[:Tn, 0:1]),
                             reads=[x.b, ss.b], writes=[pj.b, ss.b])
                        S.op("act", lambda e, ss=ss, Tn=Tn: e.activation(out=ss[:Tn, 1:2], in_=ss[:Tn, 0:1], func=AF.Sqrt, scale=1.0 / D, bias=1e-6),
                             reads=[ss.b], writes=[ss.b])
                        S.op("dve", lambda e, ss=ss, Tn=Tn: e.reciprocal(out=ss[:Tn, 1:2], in_=ss[:Tn, 1:2]), reads=[ss.b], writes=[ss.b])
                        S.op("dve", lambda e, ss=ss, x=x, Tn=Tn: e.scalar_tensor_tensor(out=x[:Tn], in0=x[:Tn], scalar=ss[:Tn, 1:2], in1=grz[:Tn],
                                                                                       op0=ALU.mult, op1=ALU.mult), reads=[x.b, ss.b, grz.b], writes=[x.b])
                        dst = y_p[t["tok0"]:t["tok0"] + Tn, :] if t["prompt"] else y_s[t["s"] * 64:(t["s"] + 1) * 64, :]
                        S.op("sp", lambda e, x=x, dst=dst, Tn=Tn: e.dma_start(out=dst, in_=x[:Tn]), reads=[x.b], writes=[S.db("yout")], chan="st_" + x.b.name)
                    else:
                        S.op("sp", lambda e, x=x, t=t, Tn=Tn: e.dma_start(out=XS[t["tok0"]:t["tok0"] + Tn, :], in_=x[:Tn]),
                             reads=[x.b], writes=[S.db("XS%d" % ti)], chan="st_" + x.b.name)
                S.emit()

        def dump_xs():
            S.op("sp", lambda e: e.dma_start(out=y_p[:, :], in_=XS[0:NPT * 128, :]), writes=[S.db("yout")], chan="dbg")
            S.op("sp", lambda e: e.dma_start(out=y_s[:, :], in_=XS[NPT * 128:NTOK, :]), writes=[S.db("yout")], chan="dbg")
            S.emit()
        if stop_after == 2:
            dump_xs()
            return nc
        if stop_after == 25:
            cross_peer(0, False, peer=False)
            dump_xs()
            return nc
        cross_peer(0, False)
        if stop_after == 3:
            dump_xs()
            return nc
        P4(nc, S, locals())
        if stop_after == 4:
            dump_xs()
            return nc
        cross_peer(1, True)
    return nc


def P4(nc, S, G):
    tiles = G["tiles"]; PSF = G["PSF"]; PSB = G["PSB"]; phase_tiles = G["phase_tiles"]; load_w = G["load_w"]
    load_cols = G["load_cols"]; load_rep = G["load_rep"]; make_rms = G["make_rms"]; mm_feat = G["mm_feat"]; mm_tok = G["mm_tok"]
    identf = G["identf"]; identb = G["identb"]; onesf = G["onesf"]; XS = G["XS"]
    NPT = G["NPT"]; NSS = G["NSS"]
    with ExitStack() as es:
        sb = phase_tiles(es, "p4")
        w = load_w(sb, G["w_in_cd"], 2568, "win")
        wo = load_w(sb, G["w_out_cd"], D, "wo")
        gcol = load_cols(sb, G["norm_mix_g"][1:2, :], 8, "gcol")
        lcg = load_rep(sb, G["ln_c_g"], 512, "lcg")
        lcb = load_rep(sb, G["ln_c_b"], 512, "lcb")
        ndg = load_rep(sb, G["norm_d_g"], 512, "ndg")
        dtb = load_rep(sb, G["dt_bias"], 8, "dtb")
        alog = load_rep(sb, G["a_log"], 8, "alog")
        dsk = load_rep(sb, G["d_skip"], 8, "dsk")
        cdw = sb([128, 8, 4], F32, "cdw")
        for c in range(8):
            S.op("sp", lambda e, c=c: e.dma_start(out=cdw[:, c, :], in_=G["conv_d_w"][:, c * 128:(c + 1) * 128].rearrange("k p -> p k"),
                                                  allow_slow_non_contiguous=True), writes=[cdw.b], chan="cdw")
        cdb = load_cols(sb, G["conv_d_b"], 8, "cdb")
        ws32 = sb([128, 4, 128], F32, "ws32")
        wsT = sb([128, 4, 128], BF16, "wsT")
        bsc = sb([128, 4], F32, "bsc")
        S.op("sp", lambda e: e.dma_start(out=ws32[:], in_=G["gm_w_s"].rearrange("g i j -> i g j")), writes=[ws32.b], chan="ws32")
        S.op("sp", lambda e: e.dma_start(out=bsc[:], in_=G["gm_b_s"].rearrange("g i -> i g"), allow_slow_non_contiguous=True), writes=[bsc.b], chan="bsc")
        for g in range(4):
            S.op("pool", lambda e, g=g: e.affine_select(out=ws32[:, g, :], in_=ws32[:, g, :], pattern=[[-1, 128]], compare_op=ALU.is_ge, fill=0.0,
                                                        base=0, channel_multiplier=1), reads=[ws32.b], writes=[ws32.b])
        ps = PSF()
        for g in range(4):
            S.op("pe", lambda e, g=g, ps=ps: e.transpose(out=ps[:, g * 128:(g + 1) * 128], in_=ws32[:, g, :], identity=identf[:, :]),
                 reads=[ws32.b, identf.b], writes=[ps.b])
        S.op("act", lambda e, ps=ps: e.copy(out=wsT[:], in_=ps[:, :].rearrange("p (g i) -> p g i", g=4)), reads=[ps.b], writes=[wsT.b])
        tri = sb([64, 64], F32, "tri")
        negm = sb([64, 64], F32, "negm")
        ones64 = sb([64, 128], F32, "ones64")
        S.op("pool", lambda e: e.memset(tri[:], 1.0), writes=[tri.b])
        S.op("pool", lambda e: e.affine_select(out=tri[:], in_=tri[:], pattern=[[1, 64]], compare_op=ALU.is_ge, fill=0.0, base=0, channel_multiplier=-1),
             reads=[tri.b], writes=[tri.b])
        S.op("pool", lambda e: e.memset(negm[:], 0.0), writes=[negm.b])
        S.op("pool", lambda e: e.affine_select(out=negm[:], in_=negm[:], pattern=[[1, 64]], compare_op=ALU.is_ge, fill=NEG, base=0, channel_multiplier=-1),
             reads=[negm.b], writes=[negm.b])
        S.op("pool", lambda e: e.memset(ones64[:], 1.0), writes=[ones64.b])
        Arep = sb([128, 8], F32, "Arep")
        S.op("act", lambda e: e.activation(out=Arep[:], in_=alog[:], func=AF.Exp), reads=[alog.b], writes=[Arep.b])
        S.op("dve", lambda e: e.tensor_scalar(out=Arep[:], in0=Arep[:], scalar1=-1.0, scalar2=None, op0=ALU.mult), reads=[Arep.b], writes=[Arep.b])
        rms_T = make_rms(sb)
        xt = [sb([128, D], F32, "x") for _ in range(2)]
        hT = sb([128, 8, 128], BF16, "hT")
        cg = sb([128, 1024], F32, "cg")
        t1 = sb([128, 1024], F32, "t1")
        t2 = sb([128, 1024], F32, "t2")
        st_ = sb([128, 8], F32, "st")
        vvn = sb([128, 512], F32, "vvn")
        vvb = sb([128, 512], BF16, "vvb")
        cout = sb([128, 512], BF16, "cout")
        cyT = sb([128, 8, 128], BF16, "cyT")
        cbuf = sb([128, 8, 131], F32, "cbuf")
        xc = sb([128, 8, 128], F32, "xc")
        cacc = [sb([128, 128], F32, "cacc%d" % c) for c in range(8)]
        xstok = sb([128, 512], F32, "xstok")
        btok = sb([64, 2, 256], F32, "btok")
        gate = sb([128, 512], F32, "gate")
        dtt = sb([128, 8], F32, "dtt")
        dta = sb([128, 8], F32, "dta")
        ST = sb([128, 8, 64], F32, "ST")
        stin = sb([64, 8, 128], F32, "stin")
        c3 = sb([8, 1024], F32, "c3")
        yt = sb([128, 512], F32, "yt")
        acs = sb([64, 8], F32, "acs")
        nacs = sb([64, 8], F32, "nacs")
        eacs = sb([64, 8], F32, "eacs")
        dec = sb([64, 8], F32, "dec")
        cdec = sb([128, 8], F32, "cdec")
        LT = sb([64, 8, 64], F32, "LT")
        MT = sb([64, 8, 64], F32, "MT")
        xdt = sb([64, 8, 64], F32, "xdt")
        xdd = sb([64, 8, 64], F32, "xdd")
        ydg = sb([64, 512], F32, "ydg")
        dtab = sb([64, 8, 64], F32, "dtab")
        for ti, t in enumerate(tiles):
            Tn = t["T"]
            x = xt[ti % 2]
            S.op("sp", lambda e, x=x, t=t, Tn=Tn: e.dma_start(out=x[:Tn], in_=XS[t["tok0"]:t["tok0"] + Tn, :]),
                 reads=[S.db("XS%d" % ti)], writes=[x.b], chan=x.b.name)
            rms_T(x, Tn, gcol, hT)
            for n in range(2):
                ps = PSF()
                mm_tok(ps, n * 512, 512, w, hT, Tn)
                sl = slice(n * 512, (n + 1) * 512)
                S.op("act", lambda e, ps=ps, sl=sl, Tn=Tn: e.activation(out=t1[:Tn, sl], in_=ps[:Tn, :], func=AF.Square), reads=[ps.b], writes=[t1.b])
                S.op("dve", lambda e, sl=sl, Tn=Tn: e.tensor_scalar(out=t1[:Tn, sl], in0=t1[:Tn, sl], scalar1=0.044715, scalar2=1.0, op0=ALU.mult, op1=ALU.add),
                     reads=[t1.b], writes=[t1.b])
                S.op("dve", lambda e, ps=ps, sl=sl, Tn=Tn: e.tensor_tensor(out=t1[:Tn, sl], in0=ps[:Tn, :], in1=t1[:Tn, sl], op=ALU.mult),
                     reads=[ps.b, t1.b], writes=[t1.b])
                S.op("act", lambda e, sl=sl, Tn=Tn: e.activation(out=t2[:Tn, sl], in_=t1[:Tn, sl], func=AF.Sigmoid, scale=1.5957691216057308),
                     reads=[t1.b], writes=[t2.b])
                S.op("dve", lambda e, ps=ps, sl=sl, Tn=Tn: e.tensor_tensor(out=cg[:Tn, sl], in0=ps[:Tn, :], in1=t2[:Tn, sl], op=ALU.mult),
                     reads=[ps.b, t2.b], writes=[cg.b])
            S.op("dve", lambda e, Tn=Tn: e.memset(st_[:Tn, 0:2], 0.0), writes=[st_.b])
            S.op("act", lambda e, Tn=Tn: e.activation(out=t1[:Tn, 0:512], in_=cg[:Tn, 512:1024], func=AF.Copy, accum_out=st_[:Tn, 0:1]),
                 reads=[cg.b, st_.b], writes=[t1.b, st_.b])
            S.op("act", lambda e, Tn=Tn: e.activation(out=t1[:Tn, 512:1024], in_=cg[:Tn, 512:1024], func=AF.Square, accum_out=st_[:Tn, 1:2]),
                 reads=[cg.b, st_.b], writes=[t1.b, st_.b])
            S.op("dve", lambda e, Tn=Tn: e.tensor_scalar(out=st_[:Tn, 2:4], in0=st_[:Tn, 0:2], scalar1=1.0 / 512, scalar2=None, op0=ALU.mult),
                 reads=[st_.b], writes=[st_.b])
            S.op("dve", lambda e, Tn=Tn: e.tensor_tensor(out=st_[:Tn, 4:5], in0=st_[:Tn, 2:3], in1=st_[:Tn, 2:3], op=ALU.mult), reads=[st_.b], writes=[st_.b])
            S.op("dve", lambda e, Tn=Tn: e.tensor_tensor(out=st_[:Tn, 4:5], in0=st_[:Tn, 3:4], in1=st_[:Tn, 4:5], op=ALU.subtract), reads=[st_.b], writes=[st_.b])
            S.op("act", lambda e, Tn=Tn: e.activation(out=st_[:Tn, 5:6], in_=st_[:Tn, 4:5], func=AF.Sqrt, bias=1e-5, scale=1.0), reads=[st_.b], writes=[st_.b])
            S.op("dve", lambda e, Tn=Tn: e.reciprocal(out=st_[:Tn, 5:6], in_=st_[:Tn, 5:6]), reads=[st_.b], writes=[st_.b])
            S.op("dve", lambda e, Tn=Tn: e.tensor_scalar(out=vvn[:Tn], in0=cg[:Tn, 512:1024], scalar1=st_[:Tn, 2:3], scalar2=st_[:Tn, 5:6],
                                                         op0=ALU.subtract, op1=ALU.mult), reads=[cg.b, st_.b], writes=[vvn.b])
            S.op("dve", lambda e, Tn=Tn: e.tensor_tensor(out=vvn[:Tn], in0=vvn[:Tn], in1=lcg[:Tn], op=ALU.mult), reads=[vvn.b, lcg.b], writes=[vvn.b])
            S.op("dve", lambda e, Tn=Tn: e.tensor_tensor(out=vvn[:Tn], in0=vvn[:Tn], in1=lcb[:Tn], op=ALU.add), reads=[vvn.b, lcb.b], writes=[vvn.b])
            if not t["prompt"]:
                S.op("sp", lambda e, t=t: e.dma_start(out=G["o_gvs"][t["s"] * 64:(t["s"] + 1) * 64, :], in_=vvn[:64]), reads=[vvn.b],
                     writes=[S.db("ogvs")], chan="st_vvn")
            S.op("act", lambda e, Tn=Tn: e.copy(out=vvb[:Tn], in_=vvn[:Tn]), reads=[vvn.b], writes=[vvb.b])
            ps = PSF()
            for g in range(4):
                S.op("pe", lambda e, g=g, ps=ps, Tn=Tn: e.matmul(ps[:Tn, g * 128:(g + 1) * 128], lhsT=wsT[:Tn, g, :Tn], rhs=vvb[:Tn, g * 128:(g + 1) * 128],
                                                                start=True, stop=True), reads=[wsT.b, vvb.b], writes=[ps.b])
            S.op("dve", lambda e, ps=ps, Tn=Tn: e.tensor_tensor(out=t1[:Tn, 0:512].rearrange("p (g d) -> p g d", g=4),
                                                               in0=ps[:Tn, :].rearrange("p (g d) -> p g d", g=4),
                                                               in1=bsc[:Tn, :].unsqueeze(2).to_broadcast([Tn, 4, 128]), op=ALU.add),
                 reads=[ps.b, bsc.b], writes=[t1.b])
            S.op("dve", lambda e, Tn=Tn: e.tensor_tensor(out=cout[:Tn], in0=t1[:Tn, 0:512], in1=cg[:Tn, 0:512], op=ALU.mult), reads=[t1.b, cg.b], writes=[cout.b])
            pb = PSB()
            for c in range(4):
                S.op("pe", lambda e, c=c, pb=pb, Tn=Tn: e.transpose(out=pb[:, c * Tn:(c + 1) * Tn], in_=cout[:Tn, c * 128:(c + 1) * 128], identity=identb[:Tn, :Tn]),
                     reads=[cout.b, identb.b], writes=[pb.b])
            S.op("act", lambda e, pb=pb, Tn=Tn: e.copy(out=cyT[:, 0:4, :Tn], in_=pb[:, 0:4 * Tn].rearrange("p (c t) -> p c t", c=4)), reads=[pb.b], writes=[cyT.b])
            if t["first"]:
                if t["prompt"]:
                    S.op("pool", lambda e: e.memset(cbuf[:, :, 0:3], 0.0), writes=[cbuf.b])
                    S.op("pool", lambda e: e.memset(ST[:], 0.0), writes=[ST.b])
                else:
                    S.op("sp", lambda e, t=t: e.dma_start(out=c3[0:3, :], in_=G["scs"][t["s"]]), writes=[c3.b], chan="c3")
                    ps = PSF()
                    for c in range(8):
                        S.op("pe", lambda e, c=c, ps=ps: e.transpose(out=ps[:, c * 3:(c + 1) * 3], in_=c3[0:3, c * 128:(c + 1) * 128], identity=identf[0:3, 0:3]),
                             reads=[c3.b, identf.b], writes=[ps.b])
                    S.op("act", lambda e, ps=ps: e.copy(out=cbuf[:, :, 0:3], in_=ps[:, 0:24].rearrange("p (c t) -> p c t", c=8)), reads=[ps.b], writes=[cbuf.b])
                    S.op("sp", lambda e, t=t: e.dma_start(out=stin[:], in_=G["sssd"][t["s"]].rearrange("h p n -> p h n")), writes=[stin.b], chan="stin")
                    ps = PSF()
                    for h in range(8):
                        S.op("pe", lambda e, h=h, ps=ps: e.transpose(out=ps[:, h * 64:(h + 1) * 64], in_=stin[:, h, :], identity=identf[0:64, 0:64]),
                             reads=[stin.b, identf.b], writes=[ps.b])
                    S.op("act", lambda e, ps=ps: e.copy(out=ST[:], in_=ps[:, :].rearrange("p (h q) -> p h q", h=8)), reads=[ps.b], writes=[ST.b])
            else:
                S.op("pool", lambda e: e.tensor_copy(out=cbuf[:, :, 0:3], in_=cbuf[:, :, 128:131]), reads=[cbuf.b], writes=[cbuf.b])
            for half in range(2):
                ps = PSF()
                mm_feat(ps, 1536 + half * 512, 4, w, hT, Tn)
                S.op("act", lambda e, ps=ps, half=half, Tn=Tn: e.copy(out=cbuf[:, half * 4:half * 4 + 4, 3:3 + Tn],
                                                                     in_=ps[:, 0:4 * Tn].rearrange("p (c t) -> p c t", c=4)), reads=[ps.b], writes=[cbuf.b])
            if t["last"]:
                for half in range(2):
                    ps = PSF()
                    for cc in range(4):
                        c = half * 4 + cc
                        S.op("pe", lambda e, c=c, cc=cc, ps=ps, Tn=Tn: e.transpose(out=ps[0:3, cc * 128:(cc + 1) * 128], in_=cbuf[:, c, Tn:Tn + 3], identity=identf[:, :]),
                             reads=[cbuf.b, identf.b], writes=[ps.b])
                    S.op("act", lambda e, ps=ps, half=half: e.copy(out=c3[0:3, half * 512:(half + 1) * 512], in_=ps[0:3, :]), reads=[ps.b], writes=[c3.b])
                dst = G["o_csp"] if t["prompt"] else G["o_css"][t["s"]]
                S.op("sp", lambda e, dst=dst: e.dma_start(out=dst, in_=c3[0:3, :]), reads=[c3.b], writes=[S.db("ocs")], chan="st_c3")
            for k in range(4):
                for c in range(8):
                    eng = "dve"
                    if k == 0:
                        S.op(eng, lambda e, c=c, Tn=Tn: e.tensor_scalar(out=cacc[c][:, :Tn], in0=cbuf[:, c, 0:Tn], scalar1=cdw[:, c, 0:1], scalar2=None, op0=ALU.mult),
                             reads=[cbuf.b, cdw.b], writes=[cacc[c].b])
                    else:
                        S.op(eng, lambda e, c=c, k=k, Tn=Tn: e.scalar_tensor_tensor(out=cacc[c][:, :Tn], in0=cbuf[:, c, k:k + Tn], scalar=cdw[:, c, k:k + 1],
                                                                                    in1=cacc[c][:, :Tn], op0=ALU.mult, op1=ALU.add),
                             reads=[cbuf.b, cdw.b, cacc[c].b], writes=[cacc[c].b])
            for c in range(8):
                S.op("act", lambda e, c=c, Tn=Tn: e.activation(out=xc[:, c, :Tn], in_=cacc[c][:, :Tn], func=AF.Silu, bias=cdb[:, c:c + 1], scale=1.0),
                     reads=[cacc[c].b, cdb.b], writes=[xc.b])
            for ch in range(Tn // 64):
                c0 = ch * 64
                cs = slice(c0, c0 + 64)
                ps = PSF()
                for c in range(4):
                    S.op("pe", lambda e, c=c, ps=ps, cs=cs: e.transpose(out=ps[0:64, c * 128:(c + 1) * 128], in_=xc[:, c, cs], identity=identf[:, :]),
                         reads=[xc.b, identf.b], writes=[ps.b])
                S.op("act", lambda e, ps=ps: e.copy(out=xstok[0:64, :], in_=ps[0:64, :]), reads=[ps.b], writes=[xstok.b])
                ps = PSF()
                for g in range(2):
                    S.op("pe", lambda e, g=g, ps=ps, cs=cs: e.transpose(out=ps[0:64, g * 128:(g + 1) * 128], in_=xc[:, 4 + g, cs], identity=identf[:, :]),
                         reads=[xc.b, identf.b], writes=[ps.b])
                S.op("act", lambda e, ps=ps: e.copy(out=btok[:, 0, :], in_=ps[0:64, 0:256]), reads=[ps.b], writes=[btok.b])
                ps = PSF()
                for c in range(8):
                    S.op("pe", lambda e, c=c, ps=ps, cs=cs: e.matmul(ps[0:64, :], lhsT=hT[:, c, cs], rhs=w[:, c, 1024:1536], start=(c == 0), stop=(c == 7)),
                         reads=[hT.b, w.b], writes=[ps.b])
                S.op("act", lambda e, ps=ps: e.activation(out=gate[0:64, :], in_=ps[0:64, :], func=AF.Silu), reads=[ps.b], writes=[gate.b])
                ps = PSF()
                for c in range(8):
                    S.op("pe", lambda e, c=c, ps=ps, cs=cs: e.matmul(ps[0:64, 0:8], lhsT=hT[:, c, cs], rhs=w[:, c, 2560:2568], start=(c == 0), stop=(c == 7)),
                         reads=[hT.b, w.b], writes=[ps.b])
                S.op("dve", lambda e, ps=ps: e.tensor_tensor(out=dtt[0:64, :], in0=ps[0:64, 0:8], in1=dtb[0:64, :], op=ALU.add), reads=[ps.b, dtb.b], writes=[dtt.b])
                S.op("act", lambda e: e.activation(out=dtt[0:64, :], in_=dtt[0:64, :], func=AF.Exp), reads=[dtt.b], writes=[dtt.b])
                S.op("act", lambda e: e.activation(out=dtt[0:64, :], in_=dtt[0:64, :], func=AF.Ln, bias=1.0, scale=1.0), reads=[dtt.b], writes=[dtt.b])
                S.op("dve", lambda e: e.tensor_tensor(out=dta[0:64, :], in0=dtt[0:64, :], in1=Arep[0:64, :], op=ALU.mult), reads=[dtt.b, Arep.b], writes=[dta.b])
                S.op("dve", lambda e: e.tensor_copy(out=dtab[:], in_=dta[0:64, :].unsqueeze(2).to_broadcast([64, 8, 64])), reads=[dta.b], writes=[dtab.b])
                psa = PSF()
                S.op("pe", lambda e, psa=psa: e.matmul(psa[0:64, 0:8], lhsT=tri[:, :], rhs=dta[0:64, :], start=True, stop=True), reads=[tri.b, dta.b], writes=[psa.b])
                S.op("act", lambda e, psa=psa: e.copy(out=acs[:], in_=psa[0:64, 0:8]), reads=[psa.b], writes=[acs.b])
                S.op("dve", lambda e, psa=psa: e.tensor_scalar(out=nacs[:], in0=psa[0:64, 0:8], scalar1=-1.0, scalar2=None, op0=ALU.mult), reads=[psa.b], writes=[nacs.b])
                S.op("act", lambda e: e.activation(out=eacs[:], in_=acs[:], func=AF.Exp), reads=[acs.b], writes=[eacs.b])
                psl = PSF()
                S.op("pe", lambda e, psl=psl: e.matmul(psl[:, 0:8], lhsT=ones64[:, :], rhs=dta[0:64, :], start=True, stop=True), reads=[ones64.b, dta.b], writes=[psl.b])
                S.op("act", lambda e, psl=psl: e.activation(out=cdec[:], in_=psl[:, 0:8], func=AF.Exp), reads=[psl.b], writes=[cdec.b])
                S.op("dve", lambda e, psl=psl: e.tensor_tensor(out=dec[:], in0=psl[0:64, 0:8], in1=acs[:], op=ALU.subtract), reads=[psl.b, acs.b], writes=[dec.b])
                S.op("act", lambda e: e.activation(out=dec[:], in_=dec[:], func=AF.Exp), reads=[dec.b], writes=[dec.b])
                psb_ = PSF()
                for h in range(8):
                    S.op("pe", lambda e, h=h, psb_=psb_: e.matmul(psb_[0:64, h * 64:(h + 1) * 64], lhsT=dtab[:, h, :], rhs=tri[:, :], start=True, stop=True),
                         reads=[dtab.b, tri.b], writes=[psb_.b])
                S.op("dve", lambda e, psb_=psb_: e.tensor_tensor(out=LT[:], in0=psb_[0:64, :].rearrange("p (h i) -> p h i", h=8),
                                                                 in1=negm[:, :].unsqueeze(1).to_broadcast([64, 8, 64]), op=ALU.add),
                     reads=[psb_.b, negm.b], writes=[LT.b])
                for h in range(8):
                    S.op("act", lambda e, h=h: e.activation(out=LT[:, h, :], in_=LT[:, h, :], func=AF.Exp, bias=nacs[:, h:h + 1], scale=1.0),
                         reads=[LT.b, nacs.b], writes=[LT.b])
                psc = PSF()
                for g in range(2):
                    S.op("pe", lambda e, g=g, psc=psc, cs=cs: e.matmul(psc[0:64, g * 64:(g + 1) * 64], lhsT=xc[:, 4 + g, cs], rhs=xc[:, 6 + g, cs], start=True, stop=True),
                         reads=[xc.b], writes=[psc.b])
                S.op("dve", lambda e, psc=psc: e.tensor_tensor(out=MT[:].rearrange("p (g r) i -> p g r i", g=2), in0=LT[:].rearrange("p (g r) i -> p g r i", g=2),
                                                               in1=psc[0:64, 0:128].rearrange("p (g i) -> p g i", g=2).unsqueeze(2).to_broadcast([64, 2, 4, 64]),
                                                               op=ALU.mult), reads=[psc.b, LT.b], writes=[MT.b])
                S.op("dve", lambda e: e.tensor_tensor(out=xdt[:], in0=xstok[0:64, :].rearrange("p (h q) -> p h q", h=8),
                                                      in1=dtt[0:64, :].unsqueeze(2).to_broadcast([64, 8, 64]), op=ALU.mult), reads=[xstok.b, dtt.b], writes=[xdt.b])
                S.op("dve", lambda e: e.tensor_tensor(out=xdd[:], in0=xdt[:], in1=dec[:, :].unsqueeze(2).to_broadcast([64, 8, 64]), op=ALU.mult),
                     reads=[xdt.b, dec.b], writes=[xdd.b])
                psy, pso, psn = PSF(), PSF(), PSF()
                for h in range(8):
                    S.op("pe", lambda e, h=h, psy=psy: e.matmul(psy[0:64, h * 64:(h + 1) * 64], lhsT=MT[:, h, :], rhs=xdt[:, h, :], start=True, stop=True),
                         reads=[MT.b, xdt.b], writes=[psy.b])
                for h in range(8):
                    S.op("pe", lambda e, h=h, pso=pso, cs=cs: e.matmul(pso[0:64, h * 64:(h + 1) * 64], lhsT=xc[:, 6 + h // 4, cs], rhs=ST[:, h, :], start=True, stop=True),
                         reads=[xc.b, ST.b], writes=[pso.b])
                for h in range(8):
                    S.op("pe", lambda e, h=h, psn=psn: e.matmul(psn[:, h * 64:(h + 1) * 64], lhsT=btok[:, 0, (h // 4) * 128:(h // 4 + 1) * 128], rhs=xdd[:, h, :],
                                                                start=True, stop=True), reads=[btok.b, xdd.b], writes=[psn.b])
                S.op("act", lambda e, psy=psy: e.copy(out=ydg[:], in_=psy[0:64, :]), reads=[psy.b], writes=[ydg.b])
                S.op("dve", lambda e, pso=pso: e.tensor_tensor(out=yt[0:64, :].rearrange("p (h q) -> p h q", h=8), in0=pso[0:64, :].rearrange("p (h q) -> p h q", h=8),
                                                               in1=eacs[:, :].unsqueeze(2).to_broadcast([64, 8, 64]), op=ALU.mult), reads=[pso.b, eacs.b], writes=[yt.b])
                S.op("dve", lambda e: e.tensor_tensor(out=yt[0:64, :], in0=yt[0:64, :], in1=ydg[:], op=ALU.add), reads=[yt.b, ydg.b], writes=[yt.b])
                S.op("pool", lambda e: e.tensor_tensor(out=ST[:], in0=ST[:], in1=cdec[:, :].unsqueeze(2).to_broadcast([128, 8, 64]), op=ALU.mult),
                     reads=[ST.b, cdec.b], writes=[ST.b])
                S.op("dve", lambda e, psn=psn: e.tensor_tensor(out=ST[:], in0=psn[:, :].rearrange("p (h q) -> p h q", h=8), in1=ST[:], op=ALU.add),
                     reads=[psn.b, ST.b], writes=[ST.b])
                S.op("dve", lambda e: e.tensor_tensor(out=xdt[:], in0=xstok[0:64, :].rearrange("p (h q) -> p h q", h=8),
                                                      in1=dsk[0:64, :].unsqueeze(2).to_broadcast([64, 8, 64]), op=ALU.mult), reads=[xstok.b, dsk.b, xdt.b], writes=[xdt.b])
                S.op("dve", lambda e: e.tensor_tensor(out=yt[0:64, :], in0=yt[0:64, :], in1=xdt[:].rearrange("p h q -> p (h q)"), op=ALU.add),
                     reads=[yt.b, xdt.b], writes=[yt.b])
                S.op("dve", lambda e: e.tensor_tensor(out=yt[0:64, :], in0=yt[0:64, :], in1=gate[0:64, :], op=ALU.mult), reads=[yt.b, gate.b], writes=[yt.b])
                S.op("dve", lambda e: e.memset(st_[0:64, 0:2], 0.0), writes=[st_.b])
                for g in range(2):
                    S.op("act", lambda e, g=g: e.activation(out=ydg[:, g * 256:(g + 1) * 256], in_=yt[0:64, g * 256:(g + 1) * 256], func=AF.Square,
                                                            accum_out=st_[0:64, g:g + 1]), reads=[yt.b, st_.b], writes=[ydg.b, st_.b])
                S.op("act", lambda e: e.activation(out=st_[0:64, 2:4], in_=st_[0:64, 0:2], func=AF.Sqrt, scale=1.0 / 256, bias=1e-6), reads=[st_.b], writes=[st_.b])
                S.op("dve", lambda e: e.reciprocal(out=st_[0:64, 2:4], in_=st_[0:64, 2:4]), reads=[st_.b], writes=[st_.b])
                S.op("dve", lambda e: e.tensor_tensor(out=yt[0:64, :].rearrange("p (g q) -> p g q", g=2), in0=yt[0:64, :].rearrange("p (g q) -> p g q", g=2),
                                                      in1=st_[0:64, 2:4].unsqueeze(2).to_broadcast([64, 2, 256]), op=ALU.mult), reads=[yt.b, st_.b], writes=[yt.b])
                S.op("dve", lambda e: e.tensor_tensor(out=cout[0:64, :], in0=yt[0:64, :], in1=ndg[0:64, :], op=ALU.mult), reads=[yt.b, ndg.b], writes=[cout.b])
                pb = PSB()
                for c in range(4):
                    S.op("pe", lambda e, c=c, pb=pb: e.transpose(out=pb[:, c * 64:(c + 1) * 64], in_=cout[0:64, c * 128:(c + 1) * 128], identity=identb[0:64, 0:64]),
                         reads=[cout.b, identb.b], writes=[pb.b])
                S.op("act", lambda e, pb=pb, cs=cs: e.copy(out=cyT[:, 4:8, cs], in_=pb[:, 0:256].rearrange("p (c t) -> p c t", c=4)), reads=[pb.b], writes=[cyT.b])
            if t["last"]:
                for half in range(2):
                    ps = PSF()
                    for hh in range(4):
                        h = half * 4 + hh
                        S.op("pe", lambda e, h=h, hh=hh, ps=ps: e.transpose(out=ps[0:64, hh * 128:(hh + 1) * 128], in_=ST[:, h, :], identity=identf[:, :]),
                             reads=[ST.b, identf.b], writes=[ps.b])
                    S.op("act", lambda e, ps=ps, half=half: e.copy(out=stin[:, half * 4:half * 4 + 4, :], in_=ps[0:64, :].rearrange("p (h n) -> p h n", h=4)),
                         reads=[ps.b], writes=[stin.b])
                dst = G["o_ssdp"] if t["prompt"] else G["o_ssds"][t["s"]]
                S.op("sp", lambda e, dst=dst: e.dma_start(out=dst.rearrange("h p n -> p h n"), in_=stin[:]), reads=[stin.b], writes=[S.db("ossd")], chan="st_stin")
            for n in range(2):
                ps = PSF()
                mm_tok(ps, n * 512, 512, wo, cyT, Tn)
                S.op("dve", lambda e, n=n, ps=ps, x=x, Tn=Tn: e.tensor_tensor(out=x[:Tn, n * 512:(n + 1) * 512], in0=ps[:Tn, :], in1=x[:Tn, n * 512:(n + 1) * 512], op=ALU.add),
                     reads=[ps.b, x.b], writes=[x.b])
            S.op("sp", lambda e, x=x, t=t, Tn=Tn: e.dma_start(out=XS[t["tok0"]:t["tok0"] + Tn, :], in_=x[:Tn]),
                 reads=[x.b], writes=[S.db("XS%d" % ti)], chan="st_" + x.b.name)
        S.emit()


_W_NAMES = ["norm_mix_g", "norm_cross_g", "norm_ffn_g", "norm_final_g", "w_in_ab", "conv_a_w", "conv_a_b", "ln_a_g", "ln_a_b",
            "lam_q1", "lam_k1", "lam_q2", "lam_k2", "subln_g", "w_out_ab", "w_in_cd", "ln_c_g", "ln_c_b", "gm_w_s", "gm_b_s",
            "conv_d_w", "conv_d_b", "dt_bias", "a_log", "d_skip", "norm_d_g", "w_out_cd", "w_xq", "w_xk", "w_xv", "w_xo", "w_pq",
            "sub_keys", "expert_u", "expert_v"]


def _layout_weights(inp):
    f = lambda a: np.ascontiguousarray(np.asarray(a, dtype=np.float32))
    W = {}
    W["norm_mix_g"] = f(inp["norm_mix_g"]); W["norm_cross_g"] = f(inp["norm_cross_g"]); W["norm_ffn_g"] = f(inp["norm_ffn_g"])
    W["norm_final_g"] = f(inp["norm_final_g"]).reshape(1, D)
    W["w_in_ab"] = f(inp["w_in_ab"][0]); W["conv_a_w"] = f(inp["conv_a_w"][0])
    for k in ("conv_a_b", "ln_a_g", "ln_a_b", "lam_q1", "lam_k1", "lam_q2", "lam_k2", "subln_g", "ln_c_g", "ln_c_b", "conv_d_b",
              "dt_bias", "a_log", "d_skip", "norm_d_g"):
        W[k] = f(inp[k][0]).reshape(1, -1)
    W["w_out_ab"] = f(inp["w_out_ab"][0]); W["w_in_cd"] = f(inp["w_in_cd"][0]); W["gm_w_s"] = f(inp["gm_w_s"][0]); W["gm_b_s"] = f(inp["gm_b_s"][0])
    W["conv_d_w"] = f(inp["conv_d_w"][0]); W["w_out_cd"] = f(inp["w_out_cd"][0])
    for k in ("w_xq", "w_xk", "w_xv", "w_xo", "w_pq", "sub_keys", "expert_u", "expert_v"):
        W[k] = f(inp[k])
    return W


def run_cores(inp, n_cores, NPT, NSS, PL, prompt_of_core, trace=False, stop_after=None):
    f = lambda a: np.ascontiguousarray(np.asarray(a, dtype=np.float32))
    W = _layout_weights(inp)
    in_maps = []
    for c in range(n_cores):
        b = prompt_of_core(c)
        sl = slice(c * NSS, (c + 1) * NSS)
        m = dict(W)
        m["x_prompt"] = f(inp["x_prompt"][b])
        m["x_sample"] = f(inp["x_sample"][sl]).reshape(NSS * 64, D)
        m["cache_attn_k"] = f(inp["cache_attn_k"][0, sl]).reshape(NSS, PL, 512)
        m["cache_attn_v"] = f(inp["cache_attn_v"][0, sl]).reshape(NSS, PL, 512)
        m["state_conv_a"] = f(inp["state_conv_a"][0, sl])
        m["state_ssd"] = f(inp["state_ssd"][0, sl])
        m["state_conv_ssm"] = f(inp["state_conv_ssm"][0, sl])
        m["cache_mem_k"] = f(inp["cache_mem_k"][:, sl]).reshape(2, NSS, 256, D)
        m["cache_mem_v"] = f(inp["cache_mem_v"][:, sl]).reshape(2, NSS, 256, D)
        m["mem_prompt"] = f(inp["mem_prompt"][b])
        in_maps.append(m)
    nc = build(NPT, NSS, PL, stop_after=stop_after)
    res = run_bass_kernel_spmd(nc, in_maps, core_ids=list(range(n_cores)), **({"trace": True} if trace else {}))
    return res


def assemble(R, nb, n_cores, NPT, NSS):
    L = NPT * 128
    pc = list(range(nb))
    cat = lambda key, cores: np.stack([np.asarray(R[c][key]) for c in cores])
    y_prompt = cat("y_p", pc).reshape(nb, L, D)
    y_sample = cat("y_s", range(n_cores)).reshape(n_cores * NSS, 64, D)
    kp = cat("o_kp", pc).reshape(1, nb, L, 4, 128)
    vp = cat("o_vp", pc).reshape(1, nb, L, 4, 128)
    cap = cat("o_cap", pc).reshape(1, nb, 30, 512)
    ssdp = cat("o_ssdp", pc).reshape(1, nb, 8, 64, 128)
    csp = cat("o_csp", pc).reshape(1, nb, 3, 1024)
    mk = np.stack([np.asarray(R[c]["o_mk"]) for c in pc], axis=1).reshape(2, nb, 256, 4, 256)
    mv = np.stack([np.asarray(R[c]["o_mv"]) for c in pc], axis=1).reshape(2, nb, 256, 4, 256)
    ns = n_cores * NSS
    ks = cat("o_ks", range(n_cores)).reshape(1, ns, 64, 4, 128)
    vs = cat("o_vs", range(n_cores)).reshape(1, ns, 64, 4, 128)
    cas = cat("o_cas", range(n_cores)).reshape(1, ns, 30, 512)
    gvs = cat("o_gvs", range(n_cores)).reshape(1, ns, 64, 4, 128)
    ssds = cat("o_ssds", range(n_cores)).reshape(1, ns, 8, 64, 128)
    css = cat("o_css", range(n_cores)).reshape(1, ns, 3, 1024)
    outs = (y_prompt, y_sample, kp, vp, cap, ssdp, csp, mk, mv, ks, vs, cas, gvs, ssds, css)
    return tuple(np.ascontiguousarray(o, dtype=np.float32) for o in outs)


def kernel(x_prompt, x_sample, cache_attn_k, cache_attn_v, state_conv_a, state_ssd, state_conv_ssm,
           cache_mem_k, cache_mem_v, mem_prompt,
           norm_mix_g, norm_cross_g, norm_ffn_g, norm_final_g,
           w_in_ab, conv_a_w, conv_a_b, ln_a_g, ln_a_b, lam_q1, lam_k1, lam_q2, lam_k2, subln_g, w_out_ab,
           w_in_cd, ln_c_g, ln_c_b, gm_w_s, gm_b_s, conv_d_w, conv_d_b, dt_bias, a_log, d_skip, norm_d_g, w_out_cd,
           w_xq, w_xk, w_xv, w_xo, w_pq, sub_keys, expert_u, expert_v):
    inp = dict(x_prompt=x_prompt, x_sample=x_sample, cache_attn_k=cache_attn_k, cache_attn_v=cache_attn_v, state_conv_a=state_conv_a,
               state_ssd=state_ssd, state_conv_ssm=state_conv_ssm, cache_mem_k=cache_mem_k, cache_mem_v=cache_mem_v, mem_prompt=mem_prompt,
               norm_mix_g=norm_mix_g, norm_cross_g=norm_cross_g, norm_ffn_g=norm_ffn_g, norm_final_g=norm_final_g,
               w_in_ab=w_in_ab, conv_a_w=conv_a_w, conv_a_b=conv_a_b, ln_a_g=ln_a_g, ln_a_b=ln_a_b, lam_q1=lam_q1, lam_k1=lam_k1,
               lam_q2=lam_q2, lam_k2=lam_k2, subln_g=subln_g, w_out_ab=w_out_ab, w_in_cd=w_in_cd, ln_c_g=ln_c_g, ln_c_b=ln_c_b,
               gm_w_s=gm_w_s, gm_b_s=gm_b_s, conv_d_w=conv_d_w, conv_d_b=conv_d_b, dt_bias=dt_bias, a_log=a_log, d_skip=d_skip,
               norm_d_g=norm_d_g, w_out_cd=w_out_cd, w_xq=w_xq, w_xk=w_xk, w_xv=w_xv, w_xo=w_xo, w_pq=w_pq, sub_keys=sub_keys,
               expert_u=expert_u, expert_v=expert_v)
    nb = x_prompt.shape[0]
    L = x_prompt.shape[1]
    n_cores = 8
    NSS = x_sample.shape[0] // n_cores
    PL = cache_attn_k.shape[2]
    res = run_cores(inp, n_cores, L // 128, NSS, PL, lambda c: c % nb)
    return assemble(res.results, nb, n_cores, L // 128, NSS)
```

```python
import numpy as np
from contextlib import ExitStack
import concourse.bass as bass
import concourse.mybir as mybir
from concourse.bass_utils import run_bass_kernel_spmd

F32 = mybir.dt.float32
BF16 = mybir.dt.bfloat16
I32 = mybir.dt.int32
U32 = mybir.dt.uint32
AF = mybir.ActivationFunctionType
ALU = mybir.AluOpType
AX = mybir.AxisListType
D = 1024
NEG = -1.0e30


class Buf:
    __slots__ = ("name", "w", "r")

    def __init__(self, name):
        self.name = name
        self.w = None
        self.r = {}


class Sched:
    ENGS = ("pe", "act", "dve", "pool", "sp")

    def __init__(self, nc, es):
        self.nc = nc
        self.sems = {}
        self.esem = {}
        for e in ("pe", "act", "dve", "pool"):
            s = es.enter_context(nc.semaphore("es_" + e))
            self.sems["e_" + e] = s
            self.esem[e] = "e_" + e
        self.ecnt = {e: 0 for e in self.esem}
        self.nchan = 28
        self.ccnt = {}
        for pre in ("c", "g"):
            for i in range(self.nchan):
                self.sems["%s%d" % (pre, i)] = es.enter_context(nc.semaphore("%ss%d" % (pre, i)))
                self.ccnt["%s%d" % (pre, i)] = 0
        self.known = {e: {} for e in self.ENGS}
        self.reset_phase()

    def reset_phase(self):
        self.ops = {e: [] for e in self.ENGS}
        self.chanmap = {}
        self.dbufs = {}

    def chan(self, eng, name):
        pre = "g" if eng == "pool" else "c"
        key = (pre, name)
        if key not in self.chanmap:
            n = sum(1 for (p, _) in self.chanmap if p == pre)
            assert n < self.nchan, "too many dma channels"
            self.chanmap[key] = "%s%d" % (pre, n)
        return self.chanmap[key]

    def db(self, name):
        if name not in self.dbufs:
            self.dbufs[name] = Buf(name)
        return self.dbufs[name]

    def op(self, eng, fn, reads=(), writes=(), chan=None, amt=16):
        need = {}
        own = self.esem.get(eng) if chan is None else None

        def add(sig, raw):
            if sig is None:
                return
            k, v = sig
            if k == own and not raw:
                return
            if need.get(k, 0) < v:
                need[k] = v

        for b in reads:
            add(b.w, True)
        for b in writes:
            add(b.w, False)
            for k, v in b.r.items():
                add((k, v), False)
        kn = self.known[eng]
        waits = []
        for k, v in need.items():
            if kn.get(k, 0) < v:
                waits.append((k, v))
                kn[k] = v
        if chan is not None:
            k = self.chan(eng, chan)
            self.ccnt[k] += amt
            sig = (k, self.ccnt[k])
        else:
            k = self.esem[eng]
            self.ecnt[eng] += 1
            sig = (k, self.ecnt[eng])
            amt = 1
        self.ops[eng].append((waits, fn, k, amt))
        for b in reads:
            if b.r.get(sig[0], 0) < sig[1]:
                b.r[sig[0]] = sig[1]
        for b in writes:
            b.w = sig
            b.r = {}

    def drain(self):
        waits = []
        kn = self.known["sp"]
        for k, v in list(self.ccnt.items()) + [(self.esem[e], self.ecnt[e]) for e in self.esem]:
            if v > 0 and kn.get(k, 0) < v:
                waits.append((k, v))
                kn[k] = v
        self.ops["sp"].append((waits, None, None, 0))

    def emit(self):
        self.drain()
        nc = self.nc
        with nc.Block() as blk:
            def run(name):
                def f(e):
                    for waits, fn, k, amt in self.ops[name]:
                        for wk, wv in waits:
                            e.wait_ge(self.sems[wk], wv)
                        if fn is not None:
                            fn(e).then_inc(self.sems[k], amt)
                return f
            blk.tensor(run("pe"))
            blk.scalar(run("act"))
            blk.vector(run("dve"))
            blk.gpsimd(run("pool"))
            blk.sync(run("sp"))
        for e in self.ENGS:
            for k, v in list(self.ccnt.items()) + [(self.esem[x], self.ecnt[x]) for x in self.esem]:
                self.known[e][k] = v
        self.reset_phase()


class T:
    def __init__(self, ap, name, nsub=1):
        self.ap = ap
        self.b = Buf(name)
        self.sub = [Buf(name + str(i)) for i in range(nsub)] if nsub > 1 else None

    def __getitem__(self, k):
        return self.ap[k]


def build(NPT, NSS, PL, lam_inits=(0.8 - 0.6, 0.0), stop_after=None, split=False):
    nc = bass.Bass("TRN2", target_bir_lowering=False)
    NTILE = NPT + NSS
    NTOK = NPT * 128 + NSS * 64
    NPK = PL // 128
    LAM0 = lam_inits[0]

    def din(name, shape, dt=F32):
        return nc.dram_tensor(name, list(shape), dt, kind="ExternalInput").ap()

    def dout(name, shape, dt=F32):
        return nc.dram_tensor(name, list(shape), dt, kind="ExternalOutput").ap()

    def dscr(name, shape, dt=F32):
        return nc.dram_tensor(name, list(shape), dt, kind="Internal").ap()

    xp = din("x_prompt", [NPT * 128, D])
    xs = din("x_sample", [NSS * 64, D])
    ck = din("cache_attn_k", [NSS, PL, 512])
    cv = din("cache_attn_v", [NSS, PL, 512])
    sca = din("state_conv_a", [NSS, 30, 512])
    sssd = din("state_ssd", [NSS, 8, 64, 128])
    scs = din("state_conv_ssm", [NSS, 3, 1024])
    cmk = din("cache_mem_k", [2, NSS, 256, D])
    cmv = din("cache_mem_v", [2, NSS, 256, D])
    memp = din("mem_prompt", [256, D])
    norm_mix_g = din("norm_mix_g", [2, D])
    norm_cross_g = din("norm_cross_g", [2, D])
    norm_ffn_g = din("norm_ffn_g", [2, D])
    norm_final_g = din("norm_final_g", [1, D])
    w_in_ab = din("w_in_ab", [D, 2560])
    conv_a_w = din("conv_a_w", [31, 512])
    conv_a_b = din("conv_a_b", [1, 512])
    ln_a_g = din("ln_a_g", [1, 512])
    ln_a_b = din("ln_a_b", [1, 512])
    lam_q1 = din("lam_q1", [1, 64])
    lam_k1 = din("lam_k1", [1, 64])
    lam_q2 = din("lam_q2", [1, 64])
    lam_k2 = din("lam_k2", [1, 64])
    subln_g = din("subln_g", [1, 128])
    w_out_ab = din("w_out_ab", [D, D])
    w_in_cd = din("w_in_cd", [D, 2568])
    ln_c_g = din("ln_c_g", [1, 512])
    ln_c_b = din("ln_c_b", [1, 512])
    gm_w_s = din("gm_w_s", [4, 128, 128])
    gm_b_s = din("gm_b_s", [4, 128])
    conv_d_w = din("conv_d_w", [4, 1024])
    conv_d_b = din("conv_d_b", [1, 1024])
    dt_bias = din("dt_bias", [1, 8])
    a_log = din("a_log", [1, 8])
    d_skip = din("d_skip", [1, 8])
    norm_d_g = din("norm_d_g", [1, 512])
    w_out_cd = din("w_out_cd", [D, D])
    w_xq = din("w_xq", [2, D, D])
    w_xk = din("w_xk", [2, D, D])
    w_xv = din("w_xv", [2, D, D])
    w_xo = din("w_xo", [2, D, D])
    w_pq = din("w_pq", [2, D, 2048])
    sub_keys = din("sub_keys", [2, 2, 128, 128])
    expert_u = din("expert_u", [2, 16384, D])
    expert_v = din("expert_v", [2, 16384, D])

    NOWN = NPT // 2 if split else NPT
    CHT = 4
    y_p = dout("y_p", [NOWN * 128, D])
    if split:
        rowidx = din("rowidx", [128, NOWN], I32)
        SND = dscr("SND", [NOWN * 128, D])
        RCV = [dscr("RCV%d" % j, [2 * CHT * 128, D]) for j in range(NOWN // CHT)]
    y_s = dout("y_s", [NSS * 64, D])
    o_kp = dout("o_kp", [NPT * 128, 512])
    o_vp = dout("o_vp", [NPT * 128, 512])
    o_cap = dout("o_cap", [30, 512])
    o_ssdp = dout("o_ssdp", [8, 64, 128])
    o_csp = dout("o_csp", [3, 1024])
    o_mk = dout("o_mk", [2, 256, D])
    o_mv = dout("o_mv", [2, 256, D])
    o_ks = dout("o_ks", [NSS * 64, 512])
    o_vs = dout("o_vs", [NSS * 64, 512])
    o_cas = dout("o_cas", [NSS, 30, 512])
    o_gvs = dout("o_gvs", [NSS * 64, 512])
    o_ssds = dout("o_ssds", [NSS, 8, 64, 128])
    o_css = dout("o_css", [NSS, 3, 1024])

    XS = dscr("XS", [NTOK, D])
    QTs = dscr("QTs", [NTILE, 128, 512], BF16)
    CATs = dscr("CATs", [NTILE, 128, 512], BF16)
    KTs = dscr("KTs", [128, 4, NTOK], BF16)
    VXs = dscr("VXs", [NTOK, 516], BF16)

    tiles = []
    for i in range(NPT):
        tiles.append(dict(seq=0, prompt=True, T=128, tok0=i * 128, i=i, first=(i == 0), last=(i == NPT - 1), s=-1))
    for s in range(NSS):
        tiles.append(dict(seq=1 + s, prompt=False, T=64, tok0=NPT * 128 + s * 64, i=0, first=True, last=True, s=s))

    def xin_rows(t):
        if t["prompt"]:
            return xp[t["tok0"]:t["tok0"] + 128, :]
        return xs[t["s"] * 64:(t["s"] + 1) * 64, :]

    with ExitStack() as ges:
        S = Sched(nc, ges)
        psf = [T(ges.enter_context(nc.psum_tensor("psf%d" % i, [128, 512], F32)), "psf%d" % i) for i in range(6)]
        psb = [T(ges.enter_context(nc.psum_tensor("psb%d" % i, [128, 1024], BF16)), "psb%d" % i) for i in range(2)]
        identf = T(ges.enter_context(nc.sbuf_tensor("identf", [128, 128], F32)), "identf")
        identb = T(ges.enter_context(nc.sbuf_tensor("identb", [128, 128], BF16)), "identb")
        onesb = T(ges.enter_context(nc.sbuf_tensor("onesb", [128, 128], BF16)), "onesb")
        onesf = T(ges.enter_context(nc.sbuf_tensor("onesf", [128, 128], F32)), "onesf")

        S.op("pool", lambda e: e.memset(identf[:], 1.0), writes=[identf.b])
        S.op("pool", lambda e: e.affine_select(out=identf[:], in_=identf[:], pattern=[[-1, 128]], compare_op=ALU.is_equal,
                                               fill=0.0, base=0, channel_multiplier=1), reads=[identf.b], writes=[identf.b])
        S.op("dve", lambda e: e.tensor_copy(out=identb[:], in_=identf[:]), reads=[identf.b], writes=[identb.b])
        S.op("dve", lambda e: e.memset(onesb[:], 1.0), writes=[onesb.b])
        S.op("dve", lambda e: e.memset(onesf[:], 1.0), writes=[onesf.b])

        rr = {"psf": 0, "psb": 0}

        def PSF():
            rr["psf"] = (rr["psf"] + 1) % 6
            return psf[rr["psf"]]

        def PSB():
            rr["psb"] = (rr["psb"] + 1) % 2
            return psb[rr["psb"]]

        def phase_tiles(es, prefix):
            cnt = [0]

            def sb(shape, dt=F32, name=None, nsub=1):
                cnt[0] += 1
                nm = "%s_%s%d" % (prefix, name or "t", cnt[0])
                return T(es.enter_context(nc.sbuf_tensor(nm, list(shape), dt)), nm, nsub)
            return sb

        def load_w(sb, w_ap, ncols, name):
            t = sb([128, 8, ncols], BF16, name)
            src = w_ap.rearrange("(c p) n -> p c n", p=128)
            for c in range(8):
                for n0 in range(0, ncols, 1024):
                    n1 = min(ncols, n0 + 1024)
                    S.op("pool", lambda e, c=c, n0=n0, n1=n1: e.dma_start(out=t[:, c, n0:n1], in_=src[:, c, n0:n1]),
                         writes=[t.b], chan=t.b.name)
            return t

        def load_cols(sb, v_ap, nchunk, name):
            t = sb([128, nchunk], F32, name)
            with nc.allow_non_contiguous_dma("tiny param column load"):
                pass
            S.op("sp", lambda e: e.dma_start(out=t[:], in_=v_ap.rearrange("o (c p) -> p (o c)", p=128),
                                             allow_slow_non_contiguous=True), writes=[t.b], chan=t.b.name)
            return t

        def load_rep(sb, v_ap, n, name):
            t = sb([128, n], F32, name)
            S.op("sp", lambda e: e.dma_start(out=t[:], in_=v_ap.partition_broadcast(128)), writes=[t.b], chan=t.b.name)
            return t

        def make_rms(sb):
            st = dict(junk=sb([128, D], F32, "rjunk"), ss=sb([128, 2], F32, "rss"), xh=sb([128, D], BF16, "rxh"))

            def rms_T(x, Tn, gcol, hT, h32=None, grep=None):
                junk, ss, xh = st["junk"], st["ss"], st["xh"]
                S.op("dve", lambda e: e.memset(ss[:Tn, 0:1], 0.0), writes=[ss.b])
                S.op("act", lambda e: e.activation(out=junk[:Tn], in_=x[:Tn], func=AF.Square, accum_out=ss[:Tn, 0:1]),
                     reads=[x.b, ss.b], writes=[junk.b, ss.b])
                S.op("act", lambda e: e.activation(out=ss[:Tn, 1:2], in_=ss[:Tn, 0:1], func=AF.Sqrt, scale=1.0 / D, bias=1e-6),
                     reads=[ss.b], writes=[ss.b])
                S.op("dve", lambda e: e.reciprocal(out=ss[:Tn, 1:2], in_=ss[:Tn, 1:2]), reads=[ss.b], writes=[ss.b])
                S.op("dve", lambda e: e.tensor_scalar(out=xh[:Tn], in0=x[:Tn], scalar1=ss[:Tn, 1:2], scalar2=None, op0=ALU.mult),
                     reads=[x.b, ss.b], writes=[xh.b])
                if h32 is not None:
                    S.op("dve", lambda e: e.scalar_tensor_tensor(out=h32[:Tn], in0=x[:Tn], scalar=ss[:Tn, 1:2], in1=grep[:Tn],
                                                                  op0=ALU.mult, op1=ALU.mult),
                         reads=[x.b, ss.b, grep.b], writes=[h32.b])
                pb = PSB()
                for c in range(8):
                    S.op("pe", lambda e, c=c: e.transpose(out=pb[:, c * Tn:(c + 1) * Tn], in_=xh[:Tn, c * 128:(c + 1) * 128],
                                                          identity=identb[:Tn, :Tn]),
                         reads=[xh.b, identb.b], writes=[pb.b])
                pv = pb[:, 0:8 * Tn].rearrange("p (c t) -> p c t", c=8)
                S.op("dve", lambda e: e.tensor_tensor(out=hT[:, :, :Tn], in0=pv, in1=gcol[:, :].unsqueeze(2).to_broadcast([128, 8, Tn]),
                                                      op=ALU.mult),
                     reads=[pb.b, gcol.b], writes=[hT.b])
            rms_T.junk = st["junk"]
            return rms_T

        def mm_feat(ps, col0, nchunk, w, hT, Tn, wb=None):
            for j in range(nchunk):
                for c in range(8):
                    S.op("pe", lambda e, j=j, c=c: e.matmul(ps[:, j * Tn:(j + 1) * Tn], lhsT=w[:, c, col0 + j * 128:col0 + (j + 1) * 128],
                                                            rhs=hT[:, c, :Tn], start=(c == 0), stop=(c == 7)),
                         reads=[w.b, hT.b], writes=[ps.b])

        def mm_tok(ps, col0, ncol, w, hT, Tn, nk=8):
            for c in range(nk):
                S.op("pe", lambda e, c=c: e.matmul(ps[:Tn, 0:ncol], lhsT=hT[:, c, :Tn], rhs=w[:, c, col0:col0 + ncol],
                                                   start=(c == 0), stop=(c == nk - 1)),
                     reads=[w.b, hT.b], writes=[ps.b])

        def gelu_tanh(e_act, e_dve, out, x, shape_ap, tmp, rb, wb_, Tn=None):
            pass

        def QTs_view(scr, ti, Tn):
            return scr[ti].rearrange("p (c t) -> p c t", c=4)[:, :, 0:Tn]

        with ExitStack() as es:
            sb = phase_tiles(es, "p0")
            mem32 = sb([128, 2, D], F32, "mem32")
            memb = sb([128, 2, D], BF16, "memb")
            memT = sb([128, 8, 256], BF16, "memT")
            S.op("sp", lambda e: e.dma_start(out=mem32[:], in_=memp.rearrange("(m p) d -> p m d", p=128)), writes=[mem32.b], chan="mem32")
            S.op("dve", lambda e: e.tensor_copy(out=memb[:], in_=mem32[:]), reads=[mem32.b], writes=[memb.b])
            for mc in range(2):
                pb = PSB()
                for c in range(8):
                    S.op("pe", lambda e, c=c, mc=mc, pb=pb: e.transpose(out=pb[:, c * 128:(c + 1) * 128], in_=memb[:, mc, c * 128:(c + 1) * 128],
                                                                        identity=identb[:]),
                         reads=[memb.b, identb.b], writes=[pb.b])
                S.op("act", lambda e, mc=mc, pb=pb: e.copy(out=memT[:, :, mc * 128:(mc + 1) * 128],
                                                           in_=pb[:, :].rearrange("p (c t) -> p c t", c=8)),
                     reads=[pb.b], writes=[memT.b])
            stg = [sb([128, D], F32, "stg") for _ in range(2)]
            k = 0
            for l in range(2):
                for (wsrc, odst) in ((w_xk, o_mk), (w_xv, o_mv)):
                    w = load_w(sb, wsrc[l], D, "wkv")
                    for mc in range(2):
                        st = stg[k % 2]
                        k += 1
                        for n in range(2):
                            ps = PSF()
                            for c in range(8):
                                S.op("pe", lambda e, c=c, n=n, mc=mc, ps=ps, w=w: e.matmul(
                                    ps[:, :], lhsT=memT[:, c, mc * 128:(mc + 1) * 128], rhs=w[:, c, n * 512:(n + 1) * 512],
                                    start=(c == 0), stop=(c == 7)), reads=[memT.b, w.b], writes=[ps.b])
                            S.op("act", lambda e, n=n, ps=ps, st=st: e.copy(out=st[:, n * 512:(n + 1) * 512], in_=ps[:, :]),
                                 reads=[ps.b], writes=[st.b])
                        S.op("sp", lambda e, l=l, mc=mc, st=st, odst=odst: e.dma_start(out=odst[l, mc * 128:(mc + 1) * 128, :], in_=st[:]),
                             reads=[st.b], writes=[S.db("omem")], chan="st_" + st.b.name)
            S.emit()

        with ExitStack() as es:
            sb = phase_tiles(es, "p1")
            w = load_w(sb, w_in_ab, 2560, "win")
            gcol = load_cols(sb, norm_mix_g[0:1, :], 8, "gcol")
            cw = sb([128, 4, 31], F32, "cw")
            for c in range(4):
                S.op("sp", lambda e, c=c: e.dma_start(out=cw[:, c, :], in_=conv_a_w[:, c * 128:(c + 1) * 128].rearrange("k p -> p k"),
                                                      allow_slow_non_contiguous=True), writes=[cw.b], chan="cw")
            cb = load_cols(sb, conv_a_b, 4, "cb")
            lg = load_cols(sb, ln_a_g, 4, "lg")
            lb = load_cols(sb, ln_a_b, 4, "lb")
            rms_T = make_rms(sb)
            xt = [sb([128, D], F32, "x") for _ in range(2)]
            hT = sb([128, 8, 128], BF16, "hT")
            sg = sb([128, 4, 128], F32, "sg")
            cbuf = sb([128, 4, 158], F32, "cbuf")
            acc = [sb([128, 128], F32, "acc%d" % c) for c in range(4)]
            sq = sb([128, 4, 128], F32, "sq")
            mv_ = sb([128, 3, 128], F32, "mv")
            xn = sb([128, 4, 128], F32, "xn")
            caT = [sb([128, 4, 128], BF16, "caT") for _ in range(2)]
            qT = [sb([128, 4, 128], BF16, "qT") for _ in range(2)]
            kT = [sb([128, 4, 128], BF16, "kT") for _ in range(2)]
            kvt = [sb([128, 2, 512], F32, "kvt") for _ in range(2)]
            vx = [sb([128, 4, 129], BF16, "vx") for _ in range(2)]
            st30 = sb([32, 512], F32, "st30")
            cao = sb([32, 512], F32, "cao")
            onesm = sb([128, 128], F32, "onesm")
            S.op("dve", lambda e: e.memset(onesm[:], 1.0 / 512), writes=[onesm.b])
            for v in vx:
                S.op("dve", lambda e, v=v: e.memset(v[:], 1.0), writes=[v.b])
            for ti, t in enumerate(tiles):
                Tn = t["T"]
                x = xt[ti % 2]
                S.op("sp", lambda e, x=x, t=t, Tn=Tn: e.dma_start(out=x[:Tn], in_=xin_rows(t)), writes=[x.b], chan=x.b.name)
                rms_T(x, Tn, gcol, hT)
                if t["first"]:
                    if t["prompt"]:
                        S.op("pool", lambda e: e.memset(cbuf[:, :, 0:30], 0.0), writes=[cbuf.b])
                    else:
                        S.op("sp", lambda e, t=t: e.dma_start(out=st30[0:30, :], in_=sca[t["s"]]), writes=[st30.b], chan="st30")
                        ps = PSF()
                        for c in range(4):
                            S.op("pe", lambda e, c=c, ps=ps: e.transpose(out=ps[:, c * 30:(c + 1) * 30], in_=st30[0:30, c * 128:(c + 1) * 128],
                                                                         identity=identf[0:30, 0:30]),
                                 reads=[st30.b, identf.b], writes=[ps.b])
                        S.op("act", lambda e, ps=ps: e.copy(out=cbuf[:, :, 0:30], in_=ps[:, 0:120].rearrange("p (c t) -> p c t", c=4)),
                             reads=[ps.b], writes=[cbuf.b])
                else:
                    S.op("pool", lambda e: e.tensor_copy(out=cbuf[:, :, 0:30], in_=cbuf[:, :, 128:158]), reads=[cbuf.b], writes=[cbuf.b])
                pa, pg = PSF(), PSF()
                mm_feat(pa, 0, 4, w, hT, Tn)
                mm_feat(pg, 512, 4, w, hT, Tn)
                S.op("act", lambda e, pg=pg, Tn=Tn: e.activation(out=sg[:, :, :Tn], in_=pg[:, 0:4 * Tn].rearrange("p (c t) -> p c t", c=4),
                                                                 func=AF.Sigmoid), reads=[pg.b], writes=[sg.b])
                S.op("dve", lambda e, pa=pa, Tn=Tn: e.tensor_tensor(out=cbuf[:, :, 30:30 + Tn], in0=pa[:, 0:4 * Tn].rearrange("p (c t) -> p c t", c=4),
                                                                    in1=sg[:, :, :Tn], op=ALU.mult), reads=[pa.b, sg.b], writes=[cbuf.b])
                if t["last"]:
                    ps = PSF()
                    for c in range(4):
                        S.op("pe", lambda e, c=c, ps=ps, Tn=Tn: e.transpose(out=ps[0:30, c * 128:(c + 1) * 128], in_=cbuf[:, c, Tn:Tn + 30],
                                                                            identity=identf[:, :]),
                             reads=[cbuf.b, identf.b], writes=[ps.b])
                    S.op("act", lambda e, ps=ps: e.copy(out=cao[0:30, :], in_=ps[0:30, :]), reads=[ps.b], writes=[cao.b])
                    dst = o_cap if t["prompt"] else o_cas[t["s"]]
                    S.op("sp", lambda e, dst=dst: e.dma_start(out=dst, in_=cao[0:30, :]), reads=[cao.b], writes=[S.db("ocap")], chan="st_cao")
                for k in range(31):
                    for c in range(4):
                        eng = "dve"
                        if k == 0:
                            S.op(eng, lambda e, c=c, Tn=Tn: e.tensor_scalar(out=acc[c][:, :Tn], in0=cbuf[:, c, 0:Tn], scalar1=cw[:, c, 0:1],
                                                                            scalar2=cb[:, c:c + 1], op0=ALU.mult, op1=ALU.add),
                                 reads=[cbuf.b, cw.b, cb.b], writes=[acc[c].b])
                        else:
                            S.op(eng, lambda e, c=c, k=k, Tn=Tn: e.scalar_tensor_tensor(out=acc[c][:, :Tn], in0=cbuf[:, c, k:k + Tn],
                                                                                        scalar=cw[:, c, k:k + 1], in1=acc[c][:, :Tn],
                                                                                        op0=ALU.mult, op1=ALU.add),
                                 reads=[cbuf.b, cw.b, acc[c].b], writes=[acc[c].b])
                pm, pq = PSF(), PSF()
                for c in range(4):
                    S.op("act", lambda e, c=c, Tn=Tn: e.activation(out=sq[:, c, :Tn], in_=acc[c][:, :Tn], func=AF.Square),
                         reads=[acc[c].b], writes=[sq.b])
                for c in range(4):
                    S.op("pe", lambda e, c=c, pm=pm, Tn=Tn: e.matmul(pm[:, :Tn], lhsT=onesm[:, :], rhs=acc[c][:, :Tn], start=(c == 0), stop=(c == 3)),
                         reads=[onesm.b, acc[c].b], writes=[pm.b])
                for c in range(4):
                    S.op("pe", lambda e, c=c, pq=pq, Tn=Tn: e.matmul(pq[:, :Tn], lhsT=onesm[:, :], rhs=sq[:, c, :Tn], start=(c == 0), stop=(c == 3)),
                         reads=[onesm.b, sq.b], writes=[pq.b])
                S.op("act", lambda e, pm=pm, Tn=Tn: e.copy(out=mv_[:, 0, :Tn], in_=pm[:, :Tn]), reads=[pm.b], writes=[mv_.b])
                S.op("dve", lambda e, Tn=Tn: e.tensor_tensor(out=mv_[:, 1, :Tn], in0=mv_[:, 0, :Tn], in1=mv_[:, 0, :Tn], op=ALU.mult),
                     reads=[mv_.b], writes=[mv_.b])
                S.op("dve", lambda e, pq=pq, Tn=Tn: e.tensor_tensor(out=mv_[:, 1, :Tn], in0=pq[:, :Tn], in1=mv_[:, 1, :Tn], op=ALU.subtract),
                     reads=[pq.b, mv_.b], writes=[mv_.b])
                S.op("act", lambda e, Tn=Tn: e.activation(out=mv_[:, 2, :Tn], in_=mv_[:, 1, :Tn], func=AF.Sqrt, bias=1e-5, scale=1.0),
                     reads=[mv_.b], writes=[mv_.b])
                S.op("dve", lambda e, Tn=Tn: e.reciprocal(out=mv_[:, 2, :Tn], in_=mv_[:, 2, :Tn]), reads=[mv_.b], writes=[mv_.b])
                for c in range(4):
                    S.op("dve", lambda e, c=c, Tn=Tn: e.tensor_tensor(out=xn[:, c, :Tn], in0=acc[c][:, :Tn], in1=mv_[:, 0, :Tn], op=ALU.subtract),
                         reads=[acc[c].b, mv_.b], writes=[xn.b])
                S.op("dve", lambda e, Tn=Tn: e.tensor_tensor(out=xn[:, :, :Tn], in0=xn[:, :, :Tn],
                                                             in1=mv_[:, 2, :Tn].unsqueeze(1).to_broadcast([128, 4, Tn]), op=ALU.mult),
                     reads=[xn.b, mv_.b], writes=[xn.b])
                ca = caT[ti % 2]
                for c in range(4):
                    S.op("act", lambda e, c=c, ca=ca, Tn=Tn: e.activation(out=ca[:, c, :Tn], in_=xn[:, c, :Tn], func=AF.Silu,
                                                                          scale=lg[:, c:c + 1], bias=lb[:, c:c + 1]),
                         reads=[xn.b, lg.b, lb.b], writes=[ca.b])
                S.op("sp", lambda e, ca=ca, ti=ti, Tn=Tn: e.dma_start(out=QTs_view(CATs, ti, Tn), in_=ca[:, :, :Tn]),
                     reads=[ca.b], writes=[S.db("CATs%d" % ti)], chan="st_" + ca.b.name)
                pq2, pk2 = PSF(), PSF()
                mm_feat(pq2, 1024, 4, w, hT, Tn)
                mm_feat(pk2, 1536, 4, w, hT, Tn)
                q_, k_ = qT[ti % 2], kT[ti % 2]
                S.op("act", lambda e, pq2=pq2, q_=q_, Tn=Tn: e.copy(out=q_[:, :, :Tn], in_=pq2[:, 0:4 * Tn].rearrange("p (c t) -> p c t", c=4)),
                     reads=[pq2.b], writes=[q_.b])
                S.op("dve", lambda e, pk2=pk2, k_=k_, Tn=Tn: e.tensor_copy(out=k_[:, :, :Tn], in_=pk2[:, 0:4 * Tn].rearrange("p (c t) -> p c t", c=4)),
                     reads=[pk2.b], writes=[k_.b])
                S.op("sp", lambda e, q_=q_, ti=ti, Tn=Tn: e.dma_start(out=QTs_view(QTs, ti, Tn), in_=q_[:, :, :Tn]),
                     reads=[q_.b], writes=[S.db("QTs%d" % ti)], chan="st_" + q_.b.name)
                S.op("sp", lambda e, k_=k_, t=t, Tn=Tn: e.dma_start(out=KTs[:, :, t["tok0"]:t["tok0"] + Tn], in_=k_[:, :, :Tn]),
                     reads=[k_.b], writes=[S.db("KTs%d" % ti)], chan="st_" + k_.b.name)
                kv = kvt[ti % 2]
                for n in range(2):
                    ps = PSF()
                    mm_tok(ps, 1536 + n * 512, 512, w, hT, Tn)
                    S.op("act" if n == 0 else "dve",
                         (lambda e, ps=ps, kv=kv, n=n, Tn=Tn: e.copy(out=kv[:Tn, n, :], in_=ps[:Tn, :])) if n == 0 else
                         (lambda e, ps=ps, kv=kv, n=n, Tn=Tn: e.tensor_copy(out=kv[:Tn, n, :], in_=ps[:Tn, :])),
                         reads=[ps.b], writes=[kv.b])
                okd, ovd = (o_kp, o_vp) if t["prompt"] else (o_ks, o_vs)
                r0 = t["tok0"] if t["prompt"] else t["s"] * 64
                S.op("sp", lambda e, kv=kv, okd=okd, r0=r0, Tn=Tn: e.dma_start(out=okd[r0:r0 + Tn, :], in_=kv[:Tn, 0, :]),
                     reads=[kv.b], writes=[S.db("okv")], chan="st_" + kv.b.name)
                S.op("sp", lambda e, kv=kv, ovd=ovd, r0=r0, Tn=Tn: e.dma_start(out=ovd[r0:r0 + Tn, :], in_=kv[:Tn, 1, :]),
                     reads=[kv.b], writes=[S.db("okv")], chan="st_" + kv.b.name)
                v_ = vx[ti % 2]
                S.op("pool", lambda e, kv=kv, v_=v_, Tn=Tn: e.tensor_copy(out=v_[:Tn, :, 0:128], in_=kv[:Tn, 1, :].rearrange("p (h e) -> p h e", h=4)),
                     reads=[kv.b], writes=[v_.b])
                S.op("sp", lambda e, v_=v_, t=t, Tn=Tn: e.dma_start(out=VXs[t["tok0"]:t["tok0"] + Tn, :], in_=v_[:Tn].rearrange("p h e -> p (h e)")),
                     reads=[v_.b], writes=[S.db("VXs%d" % ti)], chan="st_" + v_.b.name)
            S.emit()

        with ExitStack() as es:
            sb = phase_tiles(es, "p2")
            NKT = max(NPT, NPK + 1)
            wo = load_w(sb, w_out_ab, D, "wo")
            KT = sb([128, 4, NKT * 128], BF16, "KT")
            VS = sb([128, NKT, 516], BF16, "VS")
            kcs = [sb([128, 4, 512], F32, "kcs") for _ in range(2)]
            xt = [sb([128, D], F32, "x") for _ in range(2)]
            qT = [sb([128, 4, 128], BF16, "qT") for _ in range(2)]
            caT = [sb([128, 4, 128], BF16, "caT") for _ in range(2)]
            PT = [sb([128, 4, 128], BF16, "PT") for _ in range(3)]
            ah = sb([128, 128], F32, "ah")
            rr_ = sb([128, 8], F32, "rr")
            junk = sb([128, 128], F32, "junk")
            otok = sb([128, 512], BF16, "otok")
            oT = sb([128, 4, 128], BF16, "oT")
            lam = sb([128, 8], F32, "lam")
            lv = [load_rep(sb, a, 64, "lv") for a in (lam_q1, lam_k1, lam_q2, lam_k2)]
            gs = load_rep(sb, subln_g, 128, "gs")
            lj = sb([128, 64], F32, "lj")
            S.op("dve", lambda e: e.memset(lam[:], 0.0), writes=[lam.b])
            S.op("dve", lambda e: e.scalar_tensor_tensor(out=lj[:], in0=lv[0][:], scalar=1.0, in1=lv[1][:], op0=ALU.mult, op1=ALU.mult,
                                                         accum_out=lam[:, 0:1]), reads=[lv[0].b, lv[1].b, lam.b], writes=[lj.b, lam.b])
            S.op("dve", lambda e: e.scalar_tensor_tensor(out=lj[:], in0=lv[2][:], scalar=1.0, in1=lv[3][:], op0=ALU.mult, op1=ALU.mult,
                                                         accum_out=lam[:, 1:2]), reads=[lv[2].b, lv[3].b, lam.b, lj.b], writes=[lj.b, lam.b])
            S.op("act", lambda e: e.activation(out=lam[:, 2:4], in_=lam[:, 0:2], func=AF.Exp), reads=[lam.b], writes=[lam.b])
            S.op("dve", lambda e: e.tensor_tensor(out=lam[:, 4:5], in0=lam[:, 3:4], in1=lam[:, 2:3], op=ALU.subtract), reads=[lam.b], writes=[lam.b])
            S.op("dve", lambda e: e.tensor_scalar(out=lam[:, 5:6], in0=lam[:, 4:5], scalar1=-LAM0, scalar2=None, op0=ALU.add), reads=[lam.b], writes=[lam.b])
            S.op("dve", lambda e: e.tensor_scalar(out=gs[:], in0=gs[:], scalar1=(1.0 - LAM0), scalar2=None, op0=ALU.mult), reads=[gs.b], writes=[gs.b])
            psO = [psf[0], psf[1]]
            psS = [psf[2], psf[3]]
            psX = [psf[4], psf[5]]
            cnt = {"s": 0, "p": 0}
            seqs = [(0, True)] + [(1 + s, False) for s in range(NSS)]
            for (sq_, isp) in seqs:
                stiles = [(ti, t) for ti, t in enumerate(tiles) if t["seq"] == sq_]
                if isp:
                    nkt_total = NPT
                    for h in range(4):
                        S.op("sp", lambda e, h=h: e.dma_start(out=KT[:, h, 0:NPT * 128], in_=KTs[:, h, 0:NPT * 128]),
                             reads=[S.db("KTs%d" % i) for i in range(NPT)], writes=[KT.b], chan="KT")
                    for j0 in range(0, NPT, 8):
                        j1 = min(NPT, j0 + 8)
                        S.op("sp", lambda e, j0=j0, j1=j1: e.dma_start(out=VS[:, j0:j1, :], in_=VXs[j0 * 128:j1 * 128, :].rearrange("(j p) f -> p j f", p=128)),
                             reads=[S.db("VXs%d" % i) for i in range(j0, j1)], writes=[VS.b], chan="VS")
                    klens = [128] * NPT
                else:
                    s = sq_ - 1
                    ti0 = NPT + s
                    tok0 = NPT * 128 + s * 64
                    S.op("dve", lambda e: e.memset(VS[:, 0:NPK, :], 1.0), writes=[VS.b])
                    for j in range(NPK):
                        S.op("pool", lambda e, s=s, j=j: e.dma_start(out=VS[:, j, :].rearrange("p (h e) -> p h e", h=4)[:, :, 0:128],
                                                                     in_=cv[s, j * 128:(j + 1) * 128, :].rearrange("p (h e) -> p h e", h=4)),
                             writes=[VS.b], chan="VS")
                    S.op("sp", lambda e, tok0=tok0: e.dma_start(out=VS[0:64, NPK, :], in_=VXs[tok0:tok0 + 64, :]),
                         reads=[S.db("VXs%d" % ti0)], writes=[VS.b], chan="VS")
                    S.op("sp", lambda e, tok0=tok0: e.dma_start(out=KT[:, :, PL:PL + 64], in_=KTs[:, :, tok0:tok0 + 64]),
                         reads=[S.db("KTs%d" % ti0)], writes=[KT.b], chan="KT")
                    for j in range(NPK):
                        kc = kcs[j % 2]
                        S.op("sp", lambda e, kc=kc, j=j, s=s: e.dma_start(out=kc[:, 0, :], in_=ck[s, j * 128:(j + 1) * 128, :]),
                             writes=[kc.b], chan=kc.b.name)
                        ps = psX[j % 2]
                        for h in range(4):
                            S.op("pe", lambda e, h=h, kc=kc, ps=ps: e.transpose(out=ps[:, h * 128:(h + 1) * 128], in_=kc[:, 0, h * 128:(h + 1) * 128],
                                                                                identity=identf[:, :]),
                                 reads=[kc.b, identf.b], writes=[ps.b])
                        S.op("act", lambda e, ps=ps, j=j: e.copy(out=KT[:, :, j * 128:(j + 1) * 128], in_=ps[:, :].rearrange("p (h k) -> p h k", h=4)),
                             reads=[ps.b], writes=[KT.b])
                    klens = [128] * NPK + [64]
                for (ti, t) in stiles:
                    Tn = t["T"]
                    x, q_, ca = xt[ti % 2], qT[ti % 2], caT[ti % 2]
                    S.op("sp", lambda e, x=x, t=t, Tn=Tn: e.dma_start(out=x[:Tn], in_=xin_rows(t)), writes=[x.b], chan=x.b.name)
                    S.op("sp", lambda e, q_=q_, ti=ti, Tn=Tn: e.dma_start(out=q_[:, :, :Tn], in_=QTs_view(QTs, ti, Tn)),
                         reads=[S.db("QTs%d" % ti)], writes=[q_.b], chan=q_.b.name)
                    S.op("sp", lambda e, ca=ca, ti=ti, Tn=Tn: e.dma_start(out=ca[:, :, :Tn], in_=QTs_view(CATs, ti, Tn)),
                         reads=[S.db("CATs%d" % ti)], writes=[ca.b], chan=ca.b.name)
                    nk = (t["i"] + 1) if isp else (NPK + 1)
                    for h in range(4):
                        for jb in range(0, nk, 4):
                            js = list(range(jb, min(nk, jb + 4)))
                            for tt in range(2):
                                p0 = 64 * tt
                                pS = psS[cnt["s"] % 2]
                                cnt["s"] += 1
                                P = PT[cnt["p"] % 3]
                                cnt["p"] += 1
                                for jj, j in enumerate(js):
                                    kl = klens[j]
                                    S.op("pe", lambda e, jj=jj, j=j, kl=kl, pS=pS, h=h, p0=p0, q_=q_, Tn=Tn: e.matmul(
                                        pS[:kl, jj * 128:jj * 128 + Tn], lhsT=KT[p0:p0 + 64, h, j * 128:j * 128 + kl], rhs=q_[p0:p0 + 64, h, :Tn],
                                        start=True, stop=True), reads=[KT.b, q_.b], writes=[pS.b])
                                nj = len(js)
                                S.op("act", lambda e, pS=pS, P=P, nj=nj, Tn=Tn: e.activation(
                                    out=P[:, 0:nj, :Tn], in_=pS[:, 0:nj * 128].rearrange("p (j t) -> p j t", j=nj)[:, :, :Tn], func=AF.Exp, scale=0.125),
                                    reads=[pS.b], writes=[P.b])
                                if isp and js[-1] == t["i"]:
                                    jj = len(js) - 1
                                    S.op("dve", lambda e, P=P, jj=jj: e.memset(P[64:128, jj, 0:64], 0.0), reads=[P.b], writes=[P.b])
                                for jj, j in enumerate(js):
                                    kl = klens[j]
                                    S.op("pe", lambda e, jj=jj, j=j, kl=kl, P=P, h=h, tt=tt, Tn=Tn, nk=nk: e.matmul(
                                        psO[tt][:Tn, 0:129], lhsT=P[:kl, jj, :Tn], rhs=VS[:kl, j, h * 129:(h + 1) * 129],
                                        start=(j == 0), stop=(j == nk - 1)), reads=[P.b, VS.b], writes=[psO[tt].b])
                        S.op("dve", lambda e, Tn=Tn: e.reciprocal(out=rr_[:Tn, 0:1], in_=psO[0][:Tn, 128:129]), reads=[psO[0].b], writes=[rr_.b])
                        S.op("dve", lambda e, Tn=Tn: e.reciprocal(out=rr_[:Tn, 1:2], in_=psO[1][:Tn, 128:129]), reads=[psO[1].b, rr_.b], writes=[rr_.b])
                        S.op("dve", lambda e, Tn=Tn: e.tensor_tensor(out=rr_[:Tn, 2:3], in0=rr_[:Tn, 1:2], in1=lam[:Tn, 5:6], op=ALU.mult),
                             reads=[rr_.b, lam.b], writes=[rr_.b])
                        S.op("dve", lambda e, Tn=Tn: e.tensor_scalar(out=ah[:Tn], in0=psO[0][:Tn, 0:128], scalar1=rr_[:Tn, 0:1], scalar2=None, op0=ALU.mult),
                             reads=[psO[0].b, rr_.b], writes=[ah.b])
                        S.op("dve", lambda e, Tn=Tn: e.scalar_tensor_tensor(out=ah[:Tn], in0=psO[1][:Tn, 0:128], scalar=rr_[:Tn, 2:3], in1=ah[:Tn],
                                                                            op0=ALU.mult, op1=ALU.add), reads=[psO[1].b, rr_.b, ah.b], writes=[ah.b])
                        S.op("dve", lambda e, Tn=Tn: e.memset(rr_[:Tn, 3:4], 0.0), reads=[rr_.b], writes=[rr_.b])
                        S.op("act", lambda e, Tn=Tn: e.activation(out=junk[:Tn], in_=ah[:Tn], func=AF.Square, accum_out=rr_[:Tn, 3:4]),
                             reads=[ah.b, rr_.b], writes=[junk.b, rr_.b])
                        S.op("act", lambda e, Tn=Tn: e.activation(out=rr_[:Tn, 4:5], in_=rr_[:Tn, 3:4], func=AF.Sqrt, scale=1.0 / 128, bias=1e-6),
                             reads=[rr_.b], writes=[rr_.b])
                        S.op("dve", lambda e, Tn=Tn: e.reciprocal(out=rr_[:Tn, 4:5], in_=rr_[:Tn, 4:5]), reads=[rr_.b], writes=[rr_.b])
                        S.op("dve", lambda e, Tn=Tn, h=h: e.scalar_tensor_tensor(out=otok[:Tn, h * 128:(h + 1) * 128], in0=ah[:Tn], scalar=rr_[:Tn, 4:5],
                                                                                 in1=gs[:Tn], op0=ALU.mult, op1=ALU.mult),
                             reads=[ah.b, rr_.b, gs.b], writes=[otok.b])
                    pb = PSB()
                    for h in range(4):
                        S.op("pe", lambda e, h=h, pb=pb, Tn=Tn: e.transpose(out=pb[:, h * Tn:(h + 1) * Tn], in_=otok[:Tn, h * 128:(h + 1) * 128],
                                                                            identity=identb[:Tn, :Tn]), reads=[otok.b, identb.b], writes=[pb.b])
                    S.op("act", lambda e, pb=pb, Tn=Tn: e.copy(out=oT[:, :, :Tn], in_=pb[:, 0:4 * Tn].rearrange("p (h t) -> p h t", h=4)),
                         reads=[pb.b], writes=[oT.b])
                    for n in range(2):
                        ps = psX[n]
                        for c in range(8):
                            src = ca if c < 4 else oT
                            S.op("pe", lambda e, c=c, n=n, ps=ps, src=src, Tn=Tn: e.matmul(ps[:Tn, :], lhsT=src[:, c % 4, :Tn],
                                                                                          rhs=wo[:, c, n * 512:(n + 1) * 512],
                                                                                          start=(c == 0), stop=(c == 7)),
                                 reads=[src.b, wo.b], writes=[ps.b])
                        S.op("dve", lambda e, n=n, ps=ps, x=x, Tn=Tn: e.tensor_tensor(out=x[:Tn, n * 512:(n + 1) * 512], in0=ps[:Tn, :],
                                                                                     in1=x[:Tn, n * 512:(n + 1) * 512], op=ALU.add),
                             reads=[ps.b, x.b], writes=[x.b])
                    S.op("sp", lambda e, x=x, t=t, Tn=Tn: e.dma_start(out=XS[t["tok0"]:t["tok0"] + Tn, :], in_=x[:Tn]),
                         reads=[x.b], writes=[S.db("XS%d" % ti)], chan="st_" + x.b.name)
            S.emit()

        def cross_peer(l, final, peer=True):
            with ExitStack() as es:
                sb = phase_tiles(es, "p3%d" % l)
                wq = load_w(sb, w_xq[l], D, "wq")
                wo_ = load_w(sb, w_xo[l], D, "wo")
                wp = load_w(sb, w_pq[l], 2048, "wp")
                gcx = load_cols(sb, norm_cross_g[l:l + 1, :], 8, "gcx")
                gcf = load_cols(sb, norm_ffn_g[l:l + 1, :], 8, "gcf")
                grf = load_rep(sb, norm_ffn_g[l:l + 1, :], D, "grf")
                if final:
                    grz = load_rep(sb, norm_final_g, D, "grz")
                skT = sb([128, 2, 128], F32, "skT")
                sk32 = sb([128, 2, 128], F32, "sk32")
                S.op("sp", lambda e: e.dma_start(out=sk32[:], in_=sub_keys[l].rearrange("c k d -> k c d")), writes=[sk32.b], chan="sk32")
                for c in range(2):
                    ps = PSF()
                    S.op("pe", lambda e, c=c, ps=ps: e.transpose(out=ps[:, 0:128], in_=sk32[:, c, :], identity=identf[:, :]),
                         reads=[sk32.b, identf.b], writes=[ps.b])
                    S.op("act", lambda e, c=c, ps=ps: e.copy(out=skT[:, c, :], in_=ps[:, 0:128]), reads=[ps.b], writes=[skT.b])
                rms_T = make_rms(sb)
                iot = sb([128, 16], F32, "iot")
                S.op("pool", lambda e: e.iota(iot[:], pattern=[[1, 16]], base=0, channel_multiplier=0, allow_small_or_imprecise_dtypes=True),
                     writes=[iot.b])
                mb = sb([128, 2, D], BF16, "mb")
                mkT = sb([128, 8, 256], BF16, "mkT")
                mvb = sb([128, 2, D], BF16, "mvb")
                xt = [sb([128, D], F32, "x") for _ in range(2)]
                hT = sb([128, 8, 128], BF16, "hT")
                h32 = sb([128, D], F32, "h32")
                qTx = sb([128, 8, 128], BF16, "qTx")
                PTx = sb([128, 8, 128], BF16, "PTx")
                rinv = sb([128, 4, 128], F32, "rinv")
                oTx = sb([128, 8, 128], BF16, "oTx")
                pqT = sb([128, 16, 128], F32, "pqT")
                sc = sb([128, 16, 128], F32, "sc")
                sc2 = sb([128, 16, 128], F32, "sc2")
                ts = sb([128, 16, 16], F32, "ts")
                tiu = sb([128, 16, 16], U32, "tiu")
                tif = sb([128, 16, 16], F32, "tif")
                cand = sb([128, 8, 256], F32, "cand")
                cand2 = T(sc2.ap.rearrange("p (h a) k -> p h (a k)", h=8), "cand2v")
                cand2.b = sc2.b
                bs = sb([128, 8, 16], F32, "bs")
                bpu = sb([128, 8, 16], U32, "bpu")
                bpf = sb([128, 8, 16], F32, "bpf")
                fa = sb([128, 8, 16], F32, "fa")
                fb_ = sb([128, 8, 16], F32, "fb")
                ia = sb([128, 8, 16], I32, "ia")
                oh = sb([128, 8, 16, 16], F32, "oh")
                i0f = sb([128, 8, 16], F32, "i0f")
                i1f = sb([128, 8, 16], F32, "i1f")
                idx = sb([128, 128], I32, "idx")
                gate = sb([128, 8, 16], F32, "gate")
                gsum = sb([128, 8], F32, "gsum")
                act_ = sb([128, 128], F32, "act")
                g1 = sb([128, 128], F32, "g1")
                g2 = sb([128, 128], F32, "g2")
                wgt = sb([128, 128], F32, "wgt")
                NG = 8
                gb = [sb([128, D], F32, "gb") for _ in range(NG)]
                pj = rms_T.junk
                accs = [sb([128, D], F32, "pacc") for _ in range(2)]
                S.op("dve", lambda e: e.memset(idx[:], 0), writes=[idx.b])
                gcnt = [0]
                cur_seq = [None]
                if split:
                    ridx = sb([128, NOWN], I32, "ridx")
                    S.op("sp", lambda e: e.dma_start(out=ridx[:], in_=rowidx), writes=[ridx.b], chan="ridx")
                    own_tiles = [(k, dict(tiles[0], slot=k)) for k in range(NOWN)] + [(ti, t) for ti, t in enumerate(tiles) if not t["prompt"]]
                else:
                    own_tiles = [(ti, dict(t, slot=t["i"])) for ti, t in enumerate(tiles)]
                for ti, t in own_tiles:
                    Tn = t["T"]
                    if cur_seq[0] != t["seq"]:
                        cur_seq[0] = t["seq"]
                        ksrc = o_mk[l] if t["prompt"] else cmk[l, t["s"]]
                        vsrc = o_mv[l] if t["prompt"] else cmv[l, t["s"]]
                        S.op("pool", lambda e, ksrc=ksrc: e.dma_start(out=mb[:], in_=ksrc.rearrange("(m p) d -> p m d", p=128)),
                             reads=[S.db("omem")], writes=[mb.b], chan="mb")
                        for mc in range(2):
                            pb = PSB()
                            for c in range(8):
                                S.op("pe", lambda e, c=c, mc=mc, pb=pb: e.transpose(out=pb[:, c * 128:(c + 1) * 128], in_=mb[:, mc, c * 128:(c + 1) * 128],
                                                                                    identity=identb[:]), reads=[mb.b, identb.b], writes=[pb.b])
                            S.op("act", lambda e, mc=mc, pb=pb: e.copy(out=mkT[:, :, mc * 128:(mc + 1) * 128],
                                                                       in_=pb[:, :].rearrange("p (c t) -> p c t", c=8)), reads=[pb.b], writes=[mkT.b])
                        S.op("pool", lambda e, vsrc=vsrc: e.dma_start(out=mvb[:], in_=vsrc.rearrange("(m p) d -> p m d", p=128)),
                             reads=[S.db("omem")], writes=[mvb.b], chan="mvb")
                    x = xt[ti % 2]
                    if split and t["prompt"]:
                        S.op("pool", lambda e, x=x, t=t: e.indirect_dma_start(out=x[:, :], out_offset=None, in_=XS,
                                                                              in_offset=bass.IndirectOffsetOnAxis(ap=ridx[:, t["slot"]:t["slot"] + 1], axis=0)),
                             reads=[ridx.b], writes=[x.b], chan=x.b.name)
                    else:
                        S.op("sp", lambda e, x=x, t=t, Tn=Tn: e.dma_start(out=x[:Tn], in_=XS[t["tok0"]:t["tok0"] + Tn, :]),
                             reads=[S.db("XS%d" % ti)], writes=[x.b], chan=x.b.name)
                    rms_T(x, Tn, gcx, hT)
                    for half in range(2):
                        ps = PSF()
                        mm_feat(ps, half * 512, 4, wq, hT, Tn)
                        S.op("act", lambda e, ps=ps, half=half, Tn=Tn: e.copy(out=qTx[:, half * 4:half * 4 + 4, :Tn],
                                                                             in_=ps[:, 0:4 * Tn].rearrange("p (c t) -> p c t", c=4)),
                             reads=[ps.b], writes=[qTx.b])
                    pss = [PSF(), PSF()]
                    for h in range(4):
                        for mc in range(2):
                            ps = pss[h // 2]
                            o0 = ((h % 2) * 2 + mc) * Tn
                            for dc in range(2):
                                S.op("pe", lambda e, h=h, mc=mc, dc=dc, ps=ps, o0=o0, Tn=Tn: e.matmul(
                                    ps[:, o0:o0 + Tn], lhsT=mkT[:, 2 * h + dc, mc * 128:(mc + 1) * 128], rhs=qTx[:, 2 * h + dc, :Tn],
                                    start=(dc == 0), stop=(dc == 1)), reads=[mkT.b, qTx.b], writes=[ps.b])
                    for hh in range(2):
                        S.op("act", lambda e, hh=hh, Tn=Tn, pss=pss: e.activation(out=PTx[:, hh * 4:hh * 4 + 4, :Tn],
                                                                         in_=pss[hh][:, 0:4 * Tn].rearrange("p (c t) -> p c t", c=4),
                                                                         func=AF.Exp, scale=1.0 / 16), reads=[pss[hh].b], writes=[PTx.b])
                    psr = PSF()
                    for h in range(4):
                        for mc in range(2):
                            S.op("pe", lambda e, h=h, mc=mc, Tn=Tn, psr=psr: e.matmul(psr[:, h * Tn:(h + 1) * Tn], lhsT=onesb[:, :], rhs=PTx[:, h * 2 + mc, :Tn],
                                                                            start=(mc == 0), stop=(mc == 1)), reads=[onesb.b, PTx.b], writes=[psr.b])
                    S.op("dve", lambda e, Tn=Tn, psr=psr: e.reciprocal(out=rinv[:, :, :Tn], in_=psr[:, 0:4 * Tn].rearrange("p (h t) -> p h t", h=4)),
                         reads=[psr.b], writes=[rinv.b])
                    for half in range(2):
                        ps = PSF()
                        for jj in range(4):
                            j = half * 4 + jj
                            h = j // 2
                            for mc in range(2):
                                S.op("pe", lambda e, j=j, jj=jj, h=h, mc=mc, ps=ps, Tn=Tn: e.matmul(
                                    ps[:, jj * Tn:(jj + 1) * Tn], lhsT=mvb[:, mc, j * 128:(j + 1) * 128], rhs=PTx[:, h * 2 + mc, :Tn],
                                    start=(mc == 0), stop=(mc == 1)), reads=[mvb.b, PTx.b], writes=[ps.b])
                        for hh in range(2):
                            h = half * 2 + hh
                            S.op("dve", lambda e, ps=ps, hh=hh, h=h, Tn=Tn: e.tensor_tensor(
                                out=oTx[:, 2 * h:2 * h + 2, :Tn], in0=ps[:, hh * 2 * Tn:(hh * 2 + 2) * Tn].rearrange("p (c t) -> p c t", c=2),
                                in1=rinv[:, h, :Tn].unsqueeze(1).to_broadcast([128, 2, Tn]), op=ALU.mult),
                                reads=[ps.b, rinv.b], writes=[oTx.b])
                    for n in range(2):
                        ps = PSF()
                        mm_tok(ps, n * 512, 512, wo_, oTx, Tn)
                        S.op("dve", lambda e, n=n, ps=ps, x=x, Tn=Tn: e.tensor_tensor(out=x[:Tn, n * 512:(n + 1) * 512], in0=ps[:Tn, :],
                                                                                     in1=x[:Tn, n * 512:(n + 1) * 512], op=ALU.add),
                             reads=[ps.b, x.b], writes=[x.b])
                    if not peer:
                        S.op("sp", lambda e, x=x, t=t, Tn=Tn: e.dma_start(out=XS[t["tok0"]:t["tok0"] + Tn, :], in_=x[:Tn]),
                             reads=[x.b], writes=[S.db("XS%d" % ti)], chan="st_" + x.b.name)
                        continue
                    rms_T(x, Tn, gcf, hT, h32=h32, grep=grf)
                    for q4 in range(4):
                        ps = PSF()
                        mm_feat(ps, q4 * 512, 4, wp, hT, Tn)
                        S.op("act", lambda e, ps=ps, q4=q4, Tn=Tn: e.copy(out=pqT[:, q4 * 4:q4 * 4 + 4, :Tn],
                                                                         in_=ps[:, 0:4 * Tn].rearrange("p (c t) -> p c t", c=4)),
                             reads=[ps.b], writes=[pqT.b])
                    for q4 in range(4):
                        ps = PSF()
                        for jj in range(4):
                            j = q4 * 4 + jj
                            S.op("pe", lambda e, j=j, jj=jj, ps=ps, Tn=Tn: e.matmul(ps[:Tn, jj * 128:(jj + 1) * 128], lhsT=pqT[:, j, :Tn],
                                                                                   rhs=skT[:, j % 2, :], start=True, stop=True),
                                 reads=[pqT.b, skT.b], writes=[ps.b])
                        S.op("act", lambda e, ps=ps, q4=q4, Tn=Tn: e.copy(out=sc[:Tn, q4 * 4:q4 * 4 + 4, :],
                                                                         in_=ps[:Tn, :].rearrange("p (c k) -> p c k", c=4)),
                             reads=[ps.b], writes=[sc.b])
                    for j in range(16):
                        S.op("dve", lambda e, j=j, Tn=Tn: e.max(out=ts[:Tn, j, 0:8], in_=sc[:Tn, j, :]), reads=[sc.b], writes=[ts.b])
                    for j in range(16):
                        S.op("dve", lambda e, j=j, Tn=Tn: e.max_index(out=tiu[:Tn, j, 0:8], in_max=ts[:Tn, j, 0:8], in_values=sc[:Tn, j, :]),
                             reads=[sc.b, ts.b], writes=[tiu.b])
                    for j in range(16):
                        S.op("dve", lambda e, j=j, Tn=Tn: e.match_replace(out=sc2[:Tn, j, :], in_to_replace=ts[:Tn, j, 0:8], in_values=sc[:Tn, j, :],
                                                                         imm_value=NEG), reads=[sc.b, ts.b], writes=[sc2.b])
                    for j in range(16):
                        S.op("dve", lambda e, j=j, Tn=Tn: e.max(out=ts[:Tn, j, 8:16], in_=sc2[:Tn, j, :]), reads=[sc2.b], writes=[ts.b])
                    for j in range(16):
                        S.op("dve", lambda e, j=j, Tn=Tn: e.max_index(out=tiu[:Tn, j, 8:16], in_max=ts[:Tn, j, 8:16], in_values=sc2[:Tn, j, :]),
                             reads=[sc2.b, ts.b], writes=[tiu.b])
                    S.op("dve", lambda e, Tn=Tn: e.tensor_copy(out=tif[:Tn], in_=tiu[:Tn]), reads=[tiu.b], writes=[tif.b])
                    tsv = ts[:Tn].rearrange("p (h c) k -> p h c k", c=2)
                    tfv = tif[:Tn].rearrange("p (h c) k -> p h c k", c=2)
                    S.op("dve", lambda e, Tn=Tn, tsv=tsv: e.tensor_tensor(
                        out=cand[:Tn].rearrange("p h (a b) -> p h a b", a=16),
                        in0=tsv[:, :, 0, :].unsqueeze(3).to_broadcast([Tn, 8, 16, 16]),
                        in1=tsv[:, :, 1, :].unsqueeze(2).to_broadcast([Tn, 8, 16, 16]), op=ALU.add), reads=[ts.b], writes=[cand.b])
                    for h in range(8):
                        S.op("dve", lambda e, h=h, Tn=Tn: e.max(out=bs[:Tn, h, 0:8], in_=cand[:Tn, h, :]), reads=[cand.b], writes=[bs.b])
                    for h in range(8):
                        S.op("dve", lambda e, h=h, Tn=Tn: e.max_index(out=bpu[:Tn, h, 0:8], in_max=bs[:Tn, h, 0:8], in_values=cand[:Tn, h, :]),
                             reads=[cand.b, bs.b], writes=[bpu.b])
                    for h in range(8):
                        S.op("dve", lambda e, h=h, Tn=Tn: e.match_replace(out=cand2[:Tn, h, :], in_to_replace=bs[:Tn, h, 0:8], in_values=cand[:Tn, h, :],
                                                                         imm_value=NEG), reads=[cand.b, bs.b], writes=[cand2.b])
                    for h in range(8):
                        S.op("dve", lambda e, h=h, Tn=Tn: e.max(out=bs[:Tn, h, 8:16], in_=cand2[:Tn, h, :]), reads=[cand2.b], writes=[bs.b])
                    for h in range(8):
                        S.op("dve", lambda e, h=h, Tn=Tn: e.max_index(out=bpu[:Tn, h, 8:16], in_max=bs[:Tn, h, 8:16], in_values=cand2[:Tn, h, :]),
                             reads=[cand2.b, bs.b], writes=[bpu.b])
                    S.op("dve", lambda e, Tn=Tn: e.tensor_tensor(out=gate[:Tn], in0=bs[:Tn], in1=bs[:Tn, :, 0:1].to_broadcast([Tn, 8, 16]), op=ALU.subtract),
                         reads=[bs.b], writes=[gate.b])
                    S.op("act", lambda e, Tn=Tn: e.activation(out=gate[:Tn], in_=gate[:Tn], func=AF.Exp), reads=[gate.b], writes=[gate.b])
                    S.op("dve", lambda e, Tn=Tn: e.tensor_reduce(out=gsum[:Tn], in_=gate[:Tn], axis=AX.X, op=ALU.add), reads=[gate.b], writes=[gsum.b])
                    S.op("dve", lambda e, Tn=Tn: e.reciprocal(out=gsum[:Tn], in_=gsum[:Tn]), reads=[gsum.b], writes=[gsum.b])
                    S.op("dve", lambda e, Tn=Tn: e.tensor_tensor(out=gate[:Tn], in0=gate[:Tn], in1=gsum[:Tn].unsqueeze(2).to_broadcast([Tn, 8, 16]), op=ALU.mult),
                         reads=[gate.b, gsum.b], writes=[gate.b])
                    S.op("dve", lambda e, Tn=Tn: e.tensor_copy(out=bpf[:Tn], in_=bpu[:Tn]), reads=[bpu.b], writes=[bpf.b])
                    S.op("dve", lambda e, Tn=Tn: e.tensor_scalar(out=fb_[:Tn], in0=bpf[:Tn], scalar1=0.0625, scalar2=-0.46875, op0=ALU.mult, op1=ALU.add),
                         reads=[bpf.b], writes=[fb_.b])
                    S.op("dve", lambda e, Tn=Tn: e.tensor_copy(out=ia[:Tn], in_=fb_[:Tn]), reads=[fb_.b], writes=[ia.b])
                    S.op("dve", lambda e, Tn=Tn: e.tensor_copy(out=fa[:Tn], in_=ia[:Tn]), reads=[ia.b], writes=[fa.b])
                    S.op("dve", lambda e, Tn=Tn: e.scalar_tensor_tensor(out=fb_[:Tn], in0=fa[:Tn], scalar=-16.0, in1=bpf[:Tn], op0=ALU.mult, op1=ALU.add),
                         reads=[fa.b, bpf.b, fb_.b], writes=[fb_.b])
                    for (pos, cc, dst) in ((fa, 0, i0f), (fb_, 1, i1f)):
                        S.op("dve", lambda e, pos=pos, Tn=Tn: e.tensor_tensor(
                            out=oh[:Tn], in0=iot[:Tn, :].unsqueeze(1).unsqueeze(1).to_broadcast([Tn, 8, 16, 16]),
                            in1=pos[:Tn].unsqueeze(3).to_broadcast([Tn, 8, 16, 16]), op=ALU.is_equal), reads=[iot.b, pos.b], writes=[oh.b])
                        S.op("dve", lambda e, cc=cc, Tn=Tn, tfv=tfv: e.tensor_tensor(
                            out=oh[:Tn], in0=oh[:Tn], in1=tfv[:, :, cc, :].unsqueeze(2).to_broadcast([Tn, 8, 16, 16]), op=ALU.mult),
                            reads=[oh.b, tif.b], writes=[oh.b])
                        S.op("dve", lambda e, dst=dst, Tn=Tn: e.tensor_reduce(out=dst[:Tn], in_=oh[:Tn], axis=AX.X, op=ALU.add), reads=[oh.b], writes=[dst.b])
                    S.op("dve", lambda e, Tn=Tn: e.scalar_tensor_tensor(out=i0f[:Tn], in0=i0f[:Tn], scalar=128.0, in1=i1f[:Tn], op0=ALU.mult, op1=ALU.add),
                         reads=[i0f.b, i1f.b], writes=[i0f.b])
                    S.op("dve", lambda e, Tn=Tn: e.tensor_copy(out=idx[:Tn, :], in_=i0f[:Tn].rearrange("p h k -> p (h k)")), reads=[i0f.b], writes=[idx.b])
                    S.op("dve", lambda e: e.memset(act_[:], 0.0), writes=[act_.b])
                    for s_ in range(128):
                        g = gb[gcnt[0] % NG]
                        gcnt[0] += 1
                        S.op("pool", lambda e, g=g, s_=s_: e.indirect_dma_start(out=g[:, :], out_offset=None, in_=expert_u.rearrange("l e d -> (l e) d"),
                                                                                in_offset=bass.IndirectOffsetOnAxis(ap=idx[:, s_:s_ + 1], axis=0),
                                                                                element_offset=l * 16384 * D),
                             reads=[idx.b], writes=[g.b], chan=g.b.name)
                        S.op("dve", lambda e, g=g, s_=s_, Tn=Tn: e.scalar_tensor_tensor(out=pj[:Tn], in0=g[:Tn], scalar=1.0, in1=h32[:Tn],
                                                                                        op0=ALU.mult, op1=ALU.mult, accum_out=act_[:Tn, s_:s_ + 1]),
                             reads=[g.b, h32.b], writes=[pj.b, act_.b])
                    S.op("dve", lambda e, Tn=Tn: e.tensor_tensor(out=g1[:Tn], in0=act_[:Tn], in1=act_[:Tn], op=ALU.mult), reads=[act_.b], writes=[g1.b])
                    S.op("dve", lambda e, Tn=Tn: e.tensor_scalar(out=g1[:Tn], in0=g1[:Tn], scalar1=0.044715, scalar2=1.0, op0=ALU.mult, op1=ALU.add),
                         reads=[g1.b], writes=[g1.b])
                    S.op("dve", lambda e, Tn=Tn: e.tensor_tensor(out=g1[:Tn], in0=g1[:Tn], in1=act_[:Tn], op=ALU.mult), reads=[g1.b, act_.b], writes=[g1.b])
                    S.op("act", lambda e, Tn=Tn: e.activation(out=g2[:Tn], in_=g1[:Tn], func=AF.Sigmoid, scale=1.5957691216057308),
                         reads=[g1.b], writes=[g2.b])
                    S.op("dve", lambda e, Tn=Tn: e.tensor_tensor(out=g2[:Tn], in0=g2[:Tn], in1=act_[:Tn], op=ALU.mult), reads=[g2.b, act_.b], writes=[g2.b])
                    S.op("dve", lambda e, Tn=Tn: e.tensor_tensor(out=wgt[:Tn], in0=g2[:Tn], in1=gate[:Tn].rearrange("p h k -> p (h k)"), op=ALU.mult),
                         reads=[g2.b, gate.b], writes=[wgt.b])
                    for s_ in range(128):
                        g = gb[gcnt[0] % NG]
                        gcnt[0] += 1
                        S.op("pool", lambda e, g=g, s_=s_: e.indirect_dma_start(out=g[:, :], out_offset=None, in_=expert_v.rearrange("l e d -> (l e) d"),
                                                                                in_offset=bass.IndirectOffsetOnAxis(ap=idx[:, s_:s_ + 1], axis=0),
                                                                                element_offset=l * 16384 * D),
                             reads=[idx.b], writes=[g.b], chan=g.b.name)
                        a_ = accs[s_ % 2]
                        if s_ < 2:
                            S.op("dve", lambda e, g=g, s_=s_, a_=a_, Tn=Tn: e.tensor_scalar(out=a_[:Tn], in0=g[:Tn], scalar1=wgt[:Tn, s_:s_ + 1], scalar2=None,
                                                                                          op0=ALU.mult), reads=[g.b, wgt.b], writes=[a_.b])
                        else:
                            S.op("dve", lambda e, g=g, s_=s_, a_=a_, Tn=Tn: e.scalar_tensor_tensor(out=a_[:Tn], in0=g[:Tn], scalar=wgt[:Tn, s_:s_ + 1], in1=a_[:Tn],
                                                                                                  op0=ALU.mult, op1=ALU.add),
                                 reads=[g.b, wgt.b, a_.b], writes=[a_.b])
                    S.op("dve", lambda e, x=x, Tn=Tn: e.tensor_tensor(out=x[:Tn], in0=x[:Tn], in1=accs[0][:Tn], op=ALU.add), reads=[x.b, accs[0].b], writes=[x.b])
                    S.op("dve", lambda e, x=x, Tn=Tn: e.tensor_tensor(out=x[:Tn], in0=x[:Tn], in1=accs[1][:Tn], op=ALU.add), reads=[x.b, accs[1].b], writes=[x.b])
                    if final:
                        ss = sb([128, 2], F32, "fss") if ti == 0 else ss
                        S.op("dve", lambda e, ss=ss, Tn=Tn: e.memset(ss[:Tn, 0:1], 0.0), writes=[ss.b])
                        S.op("act", lambda e, ss=ss, x=x, Tn=Tn: e.activation(out=pj[:Tn], in_=x[:Tn], func=AF.Square, accum_out=ss[:Tn, 0:1]),
                             reads=[x.b, ss.b], writes=[pj.b, ss.b])
                        S.op("act", lambda e, ss=ss, Tn=Tn: e.activation(out=ss[:Tn, 1:2], in_=ss[:Tn, 0:1], func=AF.Sqrt, scale=1.0 / D, bias=1e-6),
                             reads=[ss.b], writes=[ss.b])
                        S.op("dve", lambda e, ss=ss, Tn=Tn: e.reciprocal(out=ss[:Tn, 1:2], in_=ss[:Tn, 1:2]), reads=[ss.b], writes=[ss.b])
                        S.op("dve", lambda e, ss=ss, x=x, Tn=Tn: e.scalar_tensor_tensor(out=x[:Tn], in0=x[:Tn], scalar=ss[:Tn, 1:2], in1=grz[:Tn],
                                                                                       op0=ALU.mult, op1=ALU.mult), reads=[x.b, ss.b, grz.b], writes=[x.b])
                        dst = y_p[t["slot"] * 128:(t["slot"] + 1) * 128, :] if t["prompt"] else y_s[t["s"] * 64:(t["s"] + 1) * 64, :]
                        S.op("sp", lambda e, x=x, dst=dst, Tn=Tn: e.dma_start(out=dst, in_=x[:Tn]), reads=[x.b], writes=[S.db("yout")], chan="st_" + x.b.name)
                    elif split and t["prompt"]:
                        S.op("sp", lambda e, x=x, t=t: e.dma_start(out=SND[t["slot"] * 128:(t["slot"] + 1) * 128, :], in_=x[:, :]),
                             reads=[x.b], writes=[S.db("SND")], chan="st_" + x.b.name)
                    else:
                        S.op("sp", lambda e, x=x, t=t, Tn=Tn: e.dma_start(out=XS[t["tok0"]:t["tok0"] + Tn, :], in_=x[:Tn]),
                             reads=[x.b], writes=[S.db("XS%d" % ti)], chan="st_" + x.b.name)
                S.emit()
                if split and not final:
                    for j in range(NOWN // CHT):
                        S.op("pool", lambda e, j=j: e.collective_compute("AllGather", ALU.bypass, replica_groups=[[0, 4], [1, 5], [2, 6], [3, 7]],
                                                                         ins=[SND[j * CHT * 128:(j + 1) * CHT * 128, :]], outs=[RCV[j]]),
                             writes=[S.db("RCV%d" % j)], chan="cc", amt=1)
                    S.emit()

        def dump_xs():
            S.op("sp", lambda e: e.dma_start(out=y_p[:, :], in_=XS[0:NPT * 128, :]), writes=[S.db("yout")], chan="dbg")
            S.op("sp", lambda e: e.dma_start(out=y_s[:, :], in_=XS[NPT * 128:NTOK, :]), writes=[S.db("yout")], chan="dbg")
            S.emit()
        if stop_after == 2:
            dump_xs()
            return nc
        if stop_after == 25:
            cross_peer(0, False, peer=False)
            dump_xs()
            return nc
        cross_peer(0, False)
        if stop_after == 3:
            dump_xs()
            return nc
        P4(nc, S, locals())
        if stop_after == 4:
            dump_xs()
            return nc
        cross_peer(1, True)
    return nc


def P4(nc, S, G):
    tiles = G["tiles"]; PSF = G["PSF"]; PSB = G["PSB"]; phase_tiles = G["phase_tiles"]; load_w = G["load_w"]
    load_cols = G["load_cols"]; load_rep = G["load_rep"]; make_rms = G["make_rms"]; mm_feat = G["mm_feat"]; mm_tok = G["mm_tok"]
    identf = G["identf"]; identb = G["identb"]; onesf = G["onesf"]; XS = G["XS"]
    NPT = G["NPT"]; NSS = G["NSS"]
    with ExitStack() as es:
        sb = phase_tiles(es, "p4")
        w = load_w(sb, G["w_in_cd"], 2568, "win")
        wo = load_w(sb, G["w_out_cd"], D, "wo")
        gcol = load_cols(sb, G["norm_mix_g"][1:2, :], 8, "gcol")
        lcg = load_rep(sb, G["ln_c_g"], 512, "lcg")
        lcb = load_rep(sb, G["ln_c_b"], 512, "lcb")
        ndg = load_rep(sb, G["norm_d_g"], 512, "ndg")
        dtb = load_rep(sb, G["dt_bias"], 8, "dtb")
        alog = load_rep(sb, G["a_log"], 8, "alog")
        dsk = load_rep(sb, G["d_skip"], 8, "dsk")
        cdw = sb([128, 8, 4], F32, "cdw")
        for c in range(8):
            S.op("sp", lambda e, c=c: e.dma_start(out=cdw[:, c, :], in_=G["conv_d_w"][:, c * 128:(c + 1) * 128].rearrange("k p -> p k"),
                                                  allow_slow_non_contiguous=True), writes=[cdw.b], chan="cdw")
        cdb = load_cols(sb, G["conv_d_b"], 8, "cdb")
        ws32 = sb([128, 4, 128], F32, "ws32")
        wsT = sb([128, 4, 128], BF16, "wsT")
        bsc = sb([128, 4], F32, "bsc")
        S.op("sp", lambda e: e.dma_start(out=ws32[:], in_=G["gm_w_s"].rearrange("g i j -> i g j")), writes=[ws32.b], chan="ws32")
        S.op("sp", lambda e: e.dma_start(out=bsc[:], in_=G["gm_b_s"].rearrange("g i -> i g"), allow_slow_non_contiguous=True), writes=[bsc.b], chan="bsc")
        for g in range(4):
            S.op("pool", lambda e, g=g: e.affine_select(out=ws32[:, g, :], in_=ws32[:, g, :], pattern=[[-1, 128]], compare_op=ALU.is_ge, fill=0.0,
                                                        base=0, channel_multiplier=1), reads=[ws32.b], writes=[ws32.b])
        ps = PSF()
        for g in range(4):
            S.op("pe", lambda e, g=g, ps=ps: e.transpose(out=ps[:, g * 128:(g + 1) * 128], in_=ws32[:, g, :], identity=identf[:, :]),
                 reads=[ws32.b, identf.b], writes=[ps.b])
        S.op("act", lambda e, ps=ps: e.copy(out=wsT[:], in_=ps[:, :].rearrange("p (g i) -> p g i", g=4)), reads=[ps.b], writes=[wsT.b])
        tri = sb([64, 64], F32, "tri")
        negm = sb([64, 64], F32, "negm")
        ones64 = sb([64, 128], F32, "ones64")
        S.op("pool", lambda e: e.memset(tri[:], 1.0), writes=[tri.b])
        S.op("pool", lambda e: e.affine_select(out=tri[:], in_=tri[:], pattern=[[1, 64]], compare_op=ALU.is_ge, fill=0.0, base=0, channel_multiplier=-1),
             reads=[tri.b], writes=[tri.b])
        S.op("pool", lambda e: e.memset(negm[:], 0.0), writes=[negm.b])
        S.op("pool", lambda e: e.affine_select(out=negm[:], in_=negm[:], pattern=[[1, 64]], compare_op=ALU.is_ge, fill=NEG, base=0, channel_multiplier=-1),
             reads=[negm.b], writes=[negm.b])
        S.op("pool", lambda e: e.memset(ones64[:], 1.0), writes=[ones64.b])
        Arep = sb([128, 8], F32, "Arep")
        S.op("act", lambda e: e.activation(out=Arep[:], in_=alog[:], func=AF.Exp), reads=[alog.b], writes=[Arep.b])
        S.op("dve", lambda e: e.tensor_scalar(out=Arep[:], in0=Arep[:], scalar1=-1.0, scalar2=None, op0=ALU.mult), reads=[Arep.b], writes=[Arep.b])
        rms_T = make_rms(sb)
        xt = [sb([128, D], F32, "x") for _ in range(2)]
        hT = sb([128, 8, 128], BF16, "hT")
        cg = sb([128, 1024], F32, "cg")
        t1 = sb([128, 1024], F32, "t1")
        t2 = sb([128, 1024], F32, "t2")
        st_ = sb([128, 8], F32, "st")
        vvn = sb([128, 512], F32, "vvn")
        vvb = sb([128, 512], BF16, "vvb")
        cout = sb([128, 512], BF16, "cout")
        cyT = sb([128, 8, 128], BF16, "cyT")
        cbuf = sb([128, 8, 131], F32, "cbuf")
        xc = sb([128, 8, 128], F32, "xc")
        cacc = [sb([128, 128], F32, "cacc%d" % c) for c in range(8)]
        xstok = sb([128, 512], F32, "xstok")
        btok = sb([64, 2, 256], F32, "btok")
        gate = sb([128, 512], F32, "gate")
        dtt = sb([128, 8], F32, "dtt")
        dta = sb([128, 8], F32, "dta")
        ST = sb([128, 8, 64], F32, "ST")
        stin = sb([64, 8, 128], F32, "stin")
        c3 = sb([8, 1024], F32, "c3")
        yt = sb([128, 512], F32, "yt")
        acs = sb([64, 8], F32, "acs")
        nacs = sb([64, 8], F32, "nacs")
        eacs = sb([64, 8], F32, "eacs")
        dec = sb([64, 8], F32, "dec")
        cdec = sb([128, 8], F32, "cdec")
        LT = sb([64, 8, 64], F32, "LT")
        MT = sb([64, 8, 64], F32, "MT")
        xdt = sb([64, 8, 64], F32, "xdt")
        xdd = sb([64, 8, 64], F32, "xdd")
        ydg = sb([64, 512], F32, "ydg")
        dtab = sb([64, 8, 64], F32, "dtab")
        for ti, t in enumerate(tiles):
            Tn = t["T"]
            x = xt[ti % 2]
            if G["split"] and t["prompt"]:
                NOWN, CHT = G["NOWN"], G["CHT"]
                rk, kk = t["i"] // NOWN, t["i"] % NOWN
                src = G["RCV"][kk // CHT][rk * CHT * 128 + (kk % CHT) * 128: rk * CHT * 128 + (kk % CHT + 1) * 128, :]
                S.op("sp", lambda e, x=x, src=src: e.dma_start(out=x[:, :], in_=src), writes=[x.b], chan=x.b.name)
            else:
                S.op("sp", lambda e, x=x, t=t, Tn=Tn: e.dma_start(out=x[:Tn], in_=XS[t["tok0"]:t["tok0"] + Tn, :]),
                     reads=[S.db("XS%d" % ti)], writes=[x.b], chan=x.b.name)
            rms_T(x, Tn, gcol, hT)
            for n in range(2):
                ps = PSF()
                mm_tok(ps, n * 512, 512, w, hT, Tn)
                sl = slice(n * 512, (n + 1) * 512)
                S.op("act", lambda e, ps=ps, sl=sl, Tn=Tn: e.activation(out=t1[:Tn, sl], in_=ps[:Tn, :], func=AF.Square), reads=[ps.b], writes=[t1.b])
                S.op("dve", lambda e, sl=sl, Tn=Tn: e.tensor_scalar(out=t1[:Tn, sl], in0=t1[:Tn, sl], scalar1=0.044715, scalar2=1.0, op0=ALU.mult, op1=ALU.add),
                     reads=[t1.b], writes=[t1.b])
                S.op("dve", lambda e, ps=ps, sl=sl, Tn=Tn: e.tensor_tensor(out=t1[:Tn, sl], in0=ps[:Tn, :], in1=t1[:Tn, sl], op=ALU.mult),
                     reads=[ps.b, t1.b], writes=[t1.b])
                S.op("act", lambda e, sl=sl, Tn=Tn: e.activation(out=t2[:Tn, sl], in_=t1[:Tn, sl], func=AF.Sigmoid, scale=1.5957691216057308),
                     reads=[t1.b], writes=[t2.b])
                S.op("dve", lambda e, ps=ps, sl=sl, Tn=Tn: e.tensor_tensor(out=cg[:Tn, sl], in0=ps[:Tn, :], in1=t2[:Tn, sl], op=ALU.mult),
                     reads=[ps.b, t2.b], writes=[cg.b])
            S.op("dve", lambda e, Tn=Tn: e.memset(st_[:Tn, 0:2], 0.0), writes=[st_.b])
            S.op("act", lambda e, Tn=Tn: e.activation(out=t1[:Tn, 0:512], in_=cg[:Tn, 512:1024], func=AF.Copy, accum_out=st_[:Tn, 0:1]),
                 reads=[cg.b, st_.b], writes=[t1.b, st_.b])
            S.op("act", lambda e, Tn=Tn: e.activation(out=t1[:Tn, 512:1024], in_=cg[:Tn, 512:1024], func=AF.Square, accum_out=st_[:Tn, 1:2]),
                 reads=[cg.b, st_.b], writes=[t1.b, st_.b])
            S.op("dve", lambda e, Tn=Tn: e.tensor_scalar(out=st_[:Tn, 2:4], in0=st_[:Tn, 0:2], scalar1=1.0 / 512, scalar2=None, op0=ALU.mult),
                 reads=[st_.b], writes=[st_.b])
            S.op("dve", lambda e, Tn=Tn: e.tensor_tensor(out=st_[:Tn, 4:5], in0=st_[:Tn, 2:3], in1=st_[:Tn, 2:3], op=ALU.mult), reads=[st_.b], writes=[st_.b])
            S.op("dve", lambda e, Tn=Tn: e.tensor_tensor(out=st_[:Tn, 4:5], in0=st_[:Tn, 3:4], in1=st_[:Tn, 4:5], op=ALU.subtract), reads=[st_.b], writes=[st_.b])
            S.op("act", lambda e, Tn=Tn: e.activation(out=st_[:Tn, 5:6], in_=st_[:Tn, 4:5], func=AF.Sqrt, bias=1e-5, scale=1.0), reads=[st_.b], writes=[st_.b])
            S.op("dve", lambda e, Tn=Tn: e.reciprocal(out=st_[:Tn, 5:6], in_=st_[:Tn, 5:6]), reads=[st_.b], writes=[st_.b])
            S.op("dve", lambda e, Tn=Tn: e.tensor_scalar(out=vvn[:Tn], in0=cg[:Tn, 512:1024], scalar1=st_[:Tn, 2:3], scalar2=st_[:Tn, 5:6],
                                                         op0=ALU.subtract, op1=ALU.mult), reads=[cg.b, st_.b], writes=[vvn.b])
            S.op("dve", lambda e, Tn=Tn: e.tensor_tensor(out=vvn[:Tn], in0=vvn[:Tn], in1=lcg[:Tn], op=ALU.mult), reads=[vvn.b, lcg.b], writes=[vvn.b])
            S.op("dve", lambda e, Tn=Tn: e.tensor_tensor(out=vvn[:Tn], in0=vvn[:Tn], in1=lcb[:Tn], op=ALU.add), reads=[vvn.b, lcb.b], writes=[vvn.b])
            if not t["prompt"]:
                S.op("sp", lambda e, t=t: e.dma_start(out=G["o_gvs"][t["s"] * 64:(t["s"] + 1) * 64, :], in_=vvn[:64]), reads=[vvn.b],
                     writes=[S.db("ogvs")], chan="st_vvn")
            S.op("act", lambda e, Tn=Tn: e.copy(out=vvb[:Tn], in_=vvn[:Tn]), reads=[vvn.b], writes=[vvb.b])
            ps = PSF()
            for g in range(4):
                S.op("pe", lambda e, g=g, ps=ps, Tn=Tn: e.matmul(ps[:Tn, g * 128:(g + 1) * 128], lhsT=wsT[:Tn, g, :Tn], rhs=vvb[:Tn, g * 128:(g + 1) * 128],
                                                                start=True, stop=True), reads=[wsT.b, vvb.b], writes=[ps.b])
            S.op("dve", lambda e, ps=ps, Tn=Tn: e.tensor_tensor(out=t1[:Tn, 0:512].rearrange("p (g d) -> p g d", g=4),
                                                               in0=ps[:Tn, :].rearrange("p (g d) -> p g d", g=4),
                                                               in1=bsc[:Tn, :].unsqueeze(2).to_broadcast([Tn, 4, 128]), op=ALU.add),
                 reads=[ps.b, bsc.b], writes=[t1.b])
            S.op("dve", lambda e, Tn=Tn: e.tensor_tensor(out=cout[:Tn], in0=t1[:Tn, 0:512], in1=cg[:Tn, 0:512], op=ALU.mult), reads=[t1.b, cg.b], writes=[cout.b])
            pb = PSB()
            for c in range(4):
                S.op("pe", lambda e, c=c, pb=pb, Tn=Tn: e.transpose(out=pb[:, c * Tn:(c + 1) * Tn], in_=cout[:Tn, c * 128:(c + 1) * 128], identity=identb[:Tn, :Tn]),
                     reads=[cout.b, identb.b], writes=[pb.b])
            S.op("act", lambda e, pb=pb, Tn=Tn: e.copy(out=cyT[:, 0:4, :Tn], in_=pb[:, 0:4 * Tn].rearrange("p (c t) -> p c t", c=4)), reads=[pb.b], writes=[cyT.b])
            if t["first"]:
                if t["prompt"]:
                    S.op("pool", lambda e: e.memset(cbuf[:, :, 0:3], 0.0), writes=[cbuf.b])
                    S.op("pool", lambda e: e.memset(ST[:], 0.0), writes=[ST.b])
                else:
                    S.op("sp", lambda e, t=t: e.dma_start(out=c3[0:3, :], in_=G["scs"][t["s"]]), writes=[c3.b], chan="c3")
                    ps = PSF()
                    for c in range(8):
                        S.op("pe", lambda e, c=c, ps=ps: e.transpose(out=ps[:, c * 3:(c + 1) * 3], in_=c3[0:3, c * 128:(c + 1) * 128], identity=identf[0:3, 0:3]),
                             reads=[c3.b, identf.b], writes=[ps.b])
                    S.op("act", lambda e, ps=ps: e.copy(out=cbuf[:, :, 0:3], in_=ps[:, 0:24].rearrange("p (c t) -> p c t", c=8)), reads=[ps.b], writes=[cbuf.b])
                    S.op("sp", lambda e, t=t: e.dma_start(out=stin[:], in_=G["sssd"][t["s"]].rearrange("h p n -> p h n")), writes=[stin.b], chan="stin")
                    ps = PSF()
                    for h in range(8):
                        S.op("pe", lambda e, h=h, ps=ps: e.transpose(out=ps[:, h * 64:(h + 1) * 64], in_=stin[:, h, :], identity=identf[0:64, 0:64]),
                             reads=[stin.b, identf.b], writes=[ps.b])
                    S.op("act", lambda e, ps=ps: e.copy(out=ST[:], in_=ps[:, :].rearrange("p (h q) -> p h q", h=8)), reads=[ps.b], writes=[ST.b])
            else:
                S.op("pool", lambda e: e.tensor_copy(out=cbuf[:, :, 0:3], in_=cbuf[:, :, 128:131]), reads=[cbuf.b], writes=[cbuf.b])
            for half in range(2):
                ps = PSF()
                mm_feat(ps, 1536 + half * 512, 4, w, hT, Tn)
                S.op("act", lambda e, ps=ps, half=half, Tn=Tn: e.copy(out=cbuf[:, half * 4:half * 4 + 4, 3:3 + Tn],
                                                                     in_=ps[:, 0:4 * Tn].rearrange("p (c t) -> p c t", c=4)), reads=[ps.b], writes=[cbuf.b])
            if t["last"]:
                for half in range(2):
                    ps = PSF()
                    for cc in range(4):
                        c = half * 4 + cc
                        S.op("pe", lambda e, c=c, cc=cc, ps=ps, Tn=Tn: e.transpose(out=ps[0:3, cc * 128:(cc + 1) * 128], in_=cbuf[:, c, Tn:Tn + 3], identity=identf[:, :]),
                             reads=[cbuf.b, identf.b], writes=[ps.b])
                    S.op("act", lambda e, ps=ps, half=half: e.copy(out=c3[0:3, half * 512:(half + 1) * 512], in_=ps[0:3, :]), reads=[ps.b], writes=[c3.b])
                dst = G["o_csp"] if t["prompt"] else G["o_css"][t["s"]]
                S.op("sp", lambda e, dst=dst: e.dma_start(out=dst, in_=c3[0:3, :]), reads=[c3.b], writes=[S.db("ocs")], chan="st_c3")
            for k in range(4):
                for c in range(8):
                    eng = "dve"
                    if k == 0:
                        S.op(eng, lambda e, c=c, Tn=Tn: e.tensor_scalar(out=cacc[c][:, :Tn], in0=cbuf[:, c, 0:Tn], scalar1=cdw[:, c, 0:1], scalar2=None, op0=ALU.mult),
                             reads=[cbuf.b, cdw.b], writes=[cacc[c].b])
                    else:
                        S.op(eng, lambda e, c=c, k=k, Tn=Tn: e.scalar_tensor_tensor(out=cacc[c][:, :Tn], in0=cbuf[:, c, k:k + Tn], scalar=cdw[:, c, k:k + 1],
                                                                                    in1=cacc[c][:, :Tn], op0=ALU.mult, op1=ALU.add),
                             reads=[cbuf.b, cdw.b, cacc[c].b], writes=[cacc[c].b])
            for c in range(8):
                S.op("act", lambda e, c=c, Tn=Tn: e.activation(out=xc[:, c, :Tn], in_=cacc[c][:, :Tn], func=AF.Silu, bias=cdb[:, c:c + 1], scale=1.0),
                     reads=[cacc[c].b, cdb.b], writes=[xc.b])
            for ch in range(Tn // 64):
                c0 = ch * 64
                cs = slice(c0, c0 + 64)
                ps = PSF()
                for c in range(4):
                    S.op("pe", lambda e, c=c, ps=ps, cs=cs: e.transpose(out=ps[0:64, c * 128:(c + 1) * 128], in_=xc[:, c, cs], identity=identf[:, :]),
                         reads=[xc.b, identf.b], writes=[ps.b])
                S.op("act", lambda e, ps=ps: e.copy(out=xstok[0:64, :], in_=ps[0:64, :]), reads=[ps.b], writes=[xstok.b])
                ps = PSF()
                for g in range(2):
                    S.op("pe", lambda e, g=g, ps=ps, cs=cs: e.transpose(out=ps[0:64, g * 128:(g + 1) * 128], in_=xc[:, 4 + g, cs], identity=identf[:, :]),
                         reads=[xc.b, identf.b], writes=[ps.b])
                S.op("act", lambda e, ps=ps: e.copy(out=btok[:, 0, :], in_=ps[0:64, 0:256]), reads=[ps.b], writes=[btok.b])
                ps = PSF()
                for c in range(8):
                    S.op("pe", lambda e, c=c, ps=ps, cs=cs: e.matmul(ps[0:64, :], lhsT=hT[:, c, cs], rhs=w[:, c, 1024:1536], start=(c == 0), stop=(c == 7)),
                         reads=[hT.b, w.b], writes=[ps.b])
                S.op("act", lambda e, ps=ps: e.activation(out=gate[0:64, :], in_=ps[0:64, :], func=AF.Silu), reads=[ps.b], writes=[gate.b])
                ps = PSF()
                for c in range(8):
                    S.op("pe", lambda e, c=c, ps=ps, cs=cs: e.matmul(ps[0:64, 0:8], lhsT=hT[:, c, cs], rhs=w[:, c, 2560:2568], start=(c == 0), stop=(c == 7)),
                         reads=[hT.b, w.b], writes=[ps.b])
                S.op("dve", lambda e, ps=ps: e.tensor_tensor(out=dtt[0:64, :], in0=ps[0:64, 0:8], in1=dtb[0:64, :], op=ALU.add), reads=[ps.b, dtb.b], writes=[dtt.b])
                S.op("act", lambda e: e.activation(out=dtt[0:64, :], in_=dtt[0:64, :], func=AF.Exp), reads=[dtt.b], writes=[dtt.b])
                S.op("act", lambda e: e.activation(out=dtt[0:64, :], in_=dtt[0:64, :], func=AF.Ln, bias=1.0, scale=1.0), reads=[dtt.b], writes=[dtt.b])
                S.op("dve", lambda e: e.tensor_tensor(out=dta[0:64, :], in0=dtt[0:64, :], in1=Arep[0:64, :], op=ALU.mult), reads=[dtt.b, Arep.b], writes=[dta.b])
                S.op("dve", lambda e: e.tensor_copy(out=dtab[:], in_=dta[0:64, :].unsqueeze(2).to_broadcast([64, 8, 64])), reads=[dta.b], writes=[dtab.b])
                psa = PSF()
                S.op("pe", lambda e, psa=psa: e.matmul(psa[0:64, 0:8], lhsT=tri[:, :], rhs=dta[0:64, :], start=True, stop=True), reads=[tri.b, dta.b], writes=[psa.b])
                S.op("act", lambda e, psa=psa: e.copy(out=acs[:], in_=psa[0:64, 0:8]), reads=[psa.b], writes=[acs.b])
                S.op("dve", lambda e, psa=psa: e.tensor_scalar(out=nacs[:], in0=psa[0:64, 0:8], scalar1=-1.0, scalar2=None, op0=ALU.mult), reads=[psa.b], writes=[nacs.b])
                S.op("act", lambda e: e.activation(out=eacs[:], in_=acs[:], func=AF.Exp), reads=[acs.b], writes=[eacs.b])
                psl = PSF()
                S.op("pe", lambda e, psl=psl: e.matmul(psl[:, 0:8], lhsT=ones64[:, :], rhs=dta[0:64, :], start=True, stop=True), reads=[ones64.b, dta.b], writes=[psl.b])
                S.op("act", lambda e, psl=psl: e.activation(out=cdec[:], in_=psl[:, 0:8], func=AF.Exp), reads=[psl.b], writes=[cdec.b])
                S.op("dve", lambda e, psl=psl: e.tensor_tensor(out=dec[:], in0=psl[0:64, 0:8], in1=acs[:], op=ALU.subtract), reads=[psl.b, acs.b], writes=[dec.b])
                S.op("act", lambda e: e.activation(out=dec[:], in_=dec[:], func=AF.Exp), reads=[dec.b], writes=[dec.b])
                psb_ = PSF()
                for h in range(8):
                    S.op("pe", lambda e, h=h, psb_=psb_: e.matmul(psb_[0:64, h * 64:(h + 1) * 64], lhsT=dtab[:, h, :], rhs=tri[:, :], start=True, stop=True),
                         reads=[dtab.b, tri.b], writes=[psb_.b])
                S.op("dve", lambda e, psb_=psb_: e.tensor_tensor(out=LT[:], in0=psb_[0:64, :].rearrange("p (h i) -> p h i", h=8),
                                                                 in1=negm[:, :].unsqueeze(1).to_broadcast([64, 8, 64]), op=ALU.add),
                     reads=[psb_.b, negm.b], writes=[LT.b])
                for h in range(8):
                    S.op("act", lambda e, h=h: e.activation(out=LT[:, h, :], in_=LT[:, h, :], func=AF.Exp, bias=nacs[:, h:h + 1], scale=1.0),
                         reads=[LT.b, nacs.b], writes=[LT.b])
                psc = PSF()
                for g in range(2):
                    S.op("pe", lambda e, g=g, psc=psc, cs=cs: e.matmul(psc[0:64, g * 64:(g + 1) * 64], lhsT=xc[:, 4 + g, cs], rhs=xc[:, 6 + g, cs], start=True, stop=True),
                         reads=[xc.b], writes=[psc.b])
                S.op("dve", lambda e, psc=psc: e.tensor_tensor(out=MT[:].rearrange("p (g r) i -> p g r i", g=2), in0=LT[:].rearrange("p (g r) i -> p g r i", g=2),
                                                               in1=psc[0:64, 0:128].rearrange("p (g i) -> p g i", g=2).unsqueeze(2).to_broadcast([64, 2, 4, 64]),
                                                               op=ALU.mult), reads=[psc.b, LT.b], writes=[MT.b])
                S.op("dve", lambda e: e.tensor_tensor(out=xdt[:], in0=xstok[0:64, :].rearrange("p (h q) -> p h q", h=8),
                                                      in1=dtt[0:64, :].unsqueeze(2).to_broadcast([64, 8, 64]), op=ALU.mult), reads=[xstok.b, dtt.b], writes=[xdt.b])
                S.op("dve", lambda e: e.tensor_tensor(out=xdd[:], in0=xdt[:], in1=dec[:, :].unsqueeze(2).to_broadcast([64, 8, 64]), op=ALU.mult),
                     reads=[xdt.b, dec.b], writes=[xdd.b])
                psy, pso, psn = PSF(), PSF(), PSF()
                for h in range(8):
                    S.op("pe", lambda e, h=h, psy=psy: e.matmul(psy[0:64, h * 64:(h + 1) * 64], lhsT=MT[:, h, :], rhs=xdt[:, h, :], start=True, stop=True),
                         reads=[MT.b, xdt.b], writes=[psy.b])
                for h in range(8):
                    S.op("pe", lambda e, h=h, pso=pso, cs=cs: e.matmul(pso[0:64, h * 64:(h + 1) * 64], lhsT=xc[:, 6 + h // 4, cs], rhs=ST[:, h, :], start=True, stop=True),
                         reads=[xc.b, ST.b], writes=[pso.b])
                for h in range(8):
                    S.op("pe", lambda e, h=h, psn=psn: e.matmul(psn[:, h * 64:(h + 1) * 64], lhsT=btok[:, 0, (h // 4) * 128:(h // 4 + 1) * 128], rhs=xdd[:, h, :],
                                                                start=True, stop=True), reads=[btok.b, xdd.b], writes=[psn.b])
                S.op("act", lambda e, psy=psy: e.copy(out=ydg[:], in_=psy[0:64, :]), reads=[psy.b], writes=[ydg.b])
                S.op("dve", lambda e, pso=pso: e.tensor_tensor(out=yt[0:64, :].rearrange("p (h q) -> p h q", h=8), in0=pso[0:64, :].rearrange("p (h q) -> p h q", h=8),
                                                               in1=eacs[:, :].unsqueeze(2).to_broadcast([64, 8, 64]), op=ALU.mult), reads=[pso.b, eacs.b], writes=[yt.b])
                S.op("dve", lambda e: e.tensor_tensor(out=yt[0:64, :], in0=yt[0:64, :], in1=ydg[:], op=ALU.add), reads=[yt.b, ydg.b], writes=[yt.b])
                S.op("pool", lambda e: e.tensor_tensor(out=ST[:], in0=ST[:], in1=cdec[:, :].unsqueeze(2).to_broadcast([128, 8, 64]), op=ALU.mult),
                     reads=[ST.b, cdec.b], writes=[ST.b])
                S.op("dve", lambda e, psn=psn: e.tensor_tensor(out=ST[:], in0=psn[:, :].rearrange("p (h q) -> p h q", h=8), in1=ST[:], op=ALU.add),
                     reads=[psn.b, ST.b], writes=[ST.b])
                S.op("dve", lambda e: e.tensor_tensor(out=xdt[:], in0=xstok[0:64, :].rearrange("p (h q) -> p h q", h=8),
                                                      in1=dsk[0:64, :].unsqueeze(2).to_broadcast([64, 8, 64]), op=ALU.mult), reads=[xstok.b, dsk.b, xdt.b], writes=[xdt.b])
                S.op("dve", lambda e: e.tensor_tensor(out=yt[0:64, :], in0=yt[0:64, :], in1=xdt[:].rearrange("p h q -> p (h q)"), op=ALU.add),
                     reads=[yt.b, xdt.b], writes=[yt.b])
                S.op("dve", lambda e: e.tensor_tensor(out=yt[0:64, :], in0=yt[0:64, :], in1=gate[0:64, :], op=ALU.mult), reads=[yt.b, gate.b], writes=[yt.b])
                S.op("dve", lambda e: e.memset(st_[0:64, 0:2], 0.0), writes=[st_.b])
                for g in range(2):
                    S.op("act", lambda e, g=g: e.activation(out=ydg[:, g * 256:(g + 1) * 256], in_=yt[0:64, g * 256:(g + 1) * 256], func=AF.Square,
                                                            accum_out=st_[0:64, g:g + 1]), reads=[yt.b, st_.b], writes=[ydg.b, st_.b])
                S.op("act", lambda e: e.activation(out=st_[0:64, 2:4], in_=st_[0:64, 0:2], func=AF.Sqrt, scale=1.0 / 256, bias=1e-6), reads=[st_.b], writes=[st_.b])
                S.op("dve", lambda e: e.reciprocal(out=st_[0:64, 2:4], in_=st_[0:64, 2:4]), reads=[st_.b], writes=[st_.b])
                S.op("dve", lambda e: e.tensor_tensor(out=yt[0:64, :].rearrange("p (g q) -> p g q", g=2), in0=yt[0:64, :].rearrange("p (g q) -> p g q", g=2),
                                                      in1=st_[0:64, 2:4].unsqueeze(2).to_broadcast([64, 2, 256]), op=ALU.mult), reads=[yt.b, st_.b], writes=[yt.b])
                S.op("dve", lambda e: e.tensor_tensor(out=cout[0:64, :], in0=yt[0:64, :], in1=ndg[0:64, :], op=ALU.mult), reads=[yt.b, ndg.b], writes=[cout.b])
                pb = PSB()
                for c in range(4):
                    S.op("pe", lambda e, c=c, pb=pb: e.transpose(out=pb[:, c * 64:(c + 1) * 64], in_=cout[0:64, c * 128:(c + 1) * 128], identity=identb[0:64, 0:64]),
                         reads=[cout.b, identb.b], writes=[pb.b])
                S.op("act", lambda e, pb=pb, cs=cs: e.copy(out=cyT[:, 4:8, cs], in_=pb[:, 0:256].rearrange("p (c t) -> p c t", c=4)), reads=[pb.b], writes=[cyT.b])
            if t["last"]:
                for half in range(2):
                    ps = PSF()
                    for hh in range(4):
                        h = half * 4 + hh
                        S.op("pe", lambda e, h=h, hh=hh, ps=ps: e.transpose(out=ps[0:64, hh * 128:(hh + 1) * 128], in_=ST[:, h, :], identity=identf[:, :]),
                             reads=[ST.b, identf.b], writes=[ps.b])
                    S.op("act", lambda e, ps=ps, half=half: e.copy(out=stin[:, half * 4:half * 4 + 4, :], in_=ps[0:64, :].rearrange("p (h n) -> p h n", h=4)),
                         reads=[ps.b], writes=[stin.b])
                dst = G["o_ssdp"] if t["prompt"] else G["o_ssds"][t["s"]]
                S.op("sp", lambda e, dst=dst: e.dma_start(out=dst.rearrange("h p n -> p h n"), in_=stin[:]), reads=[stin.b], writes=[S.db("ossd")], chan="st_stin")
            for n in range(2):
                ps = PSF()
                mm_tok(ps, n * 512, 512, wo, cyT, Tn)
                S.op("dve", lambda e, n=n, ps=ps, x=x, Tn=Tn: e.tensor_tensor(out=x[:Tn, n * 512:(n + 1) * 512], in0=ps[:Tn, :], in1=x[:Tn, n * 512:(n + 1) * 512], op=ALU.add),
                     reads=[ps.b, x.b], writes=[x.b])
            S.op("sp", lambda e, x=x, t=t, Tn=Tn: e.dma_start(out=XS[t["tok0"]:t["tok0"] + Tn, :], in_=x[:Tn]),
                 reads=[x.b], writes=[S.db("XS%d" % ti)], chan="st_" + x.b.name)
        S.emit()


_W_NAMES = ["norm_mix_g", "norm_cross_g", "norm_ffn_g", "norm_final_g", "w_in_ab", "conv_a_w", "conv_a_b", "ln_a_g", "ln_a_b",
            "lam_q1", "lam_k1", "lam_q2", "lam_k2", "subln_g", "w_out_ab", "w_in_cd", "ln_c_g", "ln_c_b", "gm_w_s", "gm_b_s",
            "conv_d_w", "conv_d_b", "dt_bias", "a_log", "d_skip", "norm_d_g", "w_out_cd", "w_xq", "w_xk", "w_xv", "w_xo", "w_pq",
            "sub_keys", "expert_u", "expert_v"]


def _layout_weights(inp):
    f = lambda a: np.ascontiguousarray(np.asarray(a, dtype=np.float32))
    W = {}
    W["norm_mix_g"] = f(inp["norm_mix_g"]); W["norm_cross_g"] = f(inp["norm_cross_g"]); W["norm_ffn_g"] = f(inp["norm_ffn_g"])
    W["norm_final_g"] = f(inp["norm_final_g"]).reshape(1, D)
    W["w_in_ab"] = f(inp["w_in_ab"][0]); W["conv_a_w"] = f(inp["conv_a_w"][0])
    for k in ("conv_a_b", "ln_a_g", "ln_a_b", "lam_q1", "lam_k1", "lam_q2", "lam_k2", "subln_g", "ln_c_g", "ln_c_b", "conv_d_b",
              "dt_bias", "a_log", "d_skip", "norm_d_g"):
        W[k] = f(inp[k][0]).reshape(1, -1)
    W["w_out_ab"] = f(inp["w_out_ab"][0]); W["w_in_cd"] = f(inp["w_in_cd"][0]); W["gm_w_s"] = f(inp["gm_w_s"][0]); W["gm_b_s"] = f(inp["gm_b_s"][0])
    W["conv_d_w"] = f(inp["conv_d_w"][0]); W["w_out_cd"] = f(inp["w_out_cd"][0])
    for k in ("w_xq", "w_xk", "w_xv", "w_xo", "w_pq", "sub_keys", "expert_u", "expert_v"):
        W[k] = f(inp[k])
    return W


def run_cores(inp, n_cores, NPT, NSS, PL, prompt_of_core, trace=False, stop_after=None, split=False):
    f = lambda a: np.ascontiguousarray(np.asarray(a, dtype=np.float32))
    W = _layout_weights(inp)
    in_maps = []
    for c in range(n_cores):
        b = prompt_of_core(c)
        sl = slice(c * NSS, (c + 1) * NSS)
        m = dict(W)
        m["x_prompt"] = f(inp["x_prompt"][b])
        m["x_sample"] = f(inp["x_sample"][sl]).reshape(NSS * 64, D)
        m["cache_attn_k"] = f(inp["cache_attn_k"][0, sl]).reshape(NSS, PL, 512)
        m["cache_attn_v"] = f(inp["cache_attn_v"][0, sl]).reshape(NSS, PL, 512)
        m["state_conv_a"] = f(inp["state_conv_a"][0, sl])
        m["state_ssd"] = f(inp["state_ssd"][0, sl])
        m["state_conv_ssm"] = f(inp["state_conv_ssm"][0, sl])
        m["cache_mem_k"] = f(inp["cache_mem_k"][:, sl]).reshape(2, NSS, 256, D)
        m["cache_mem_v"] = f(inp["cache_mem_v"][:, sl]).reshape(2, NSS, 256, D)
        m["mem_prompt"] = f(inp["mem_prompt"][b])
        if split:
            rank = c // 4
            nown = NPT // 2
            m["rowidx"] = np.ascontiguousarray(((rank * nown + np.arange(nown))[None, :] * 128 + np.arange(128)[:, None]).astype(np.int32))
        in_maps.append(m)
    nc = build(NPT, NSS, PL, stop_after=stop_after, split=split)
    res = run_bass_kernel_spmd(nc, in_maps, core_ids=list(range(n_cores)), **({"trace": True} if trace else {}))
    return res


def assemble(R, nb, n_cores, NPT, NSS, split=False):
    L = NPT * 128
    pc = list(range(nb))
    cat = lambda key, cores: np.stack([np.asarray(R[c][key]) for c in cores])
    if split:
        y_prompt = np.stack([np.concatenate([np.asarray(R[b]["y_p"]), np.asarray(R[b + 4]["y_p"])], axis=0) for b in pc]).reshape(nb, L, D)
    else:
        y_prompt = cat("y_p", pc).reshape(nb, L, D)
    y_sample = cat("y_s", range(n_cores)).reshape(n_cores * NSS, 64, D)
    kp = cat("o_kp", pc).reshape(1, nb, L, 4, 128)
    vp = cat("o_vp", pc).reshape(1, nb, L, 4, 128)
    cap = cat("o_cap", pc).reshape(1, nb, 30, 512)
    ssdp = cat("o_ssdp", pc).reshape(1, nb, 8, 64, 128)
    csp = cat("o_csp", pc).reshape(1, nb, 3, 1024)
    mk = np.stack([np.asarray(R[c]["o_mk"]) for c in pc], axis=1).reshape(2, nb, 256, 4, 256)
    mv = np.stack([np.asarray(R[c]["o_mv"]) for c in pc], axis=1).reshape(2, nb, 256, 4, 256)
    ns = n_cores * NSS
    ks = cat("o_ks", range(n_cores)).reshape(1, ns, 64, 4, 128)
    vs = cat("o_vs", range(n_cores)).reshape(1, ns, 64, 4, 128)
    cas = cat("o_cas", range(n_cores)).reshape(1, ns, 30, 512)
    gvs = cat("o_gvs", range(n_cores)).reshape(1, ns, 64, 4, 128)
    ssds = cat("o_ssds", range(n_cores)).reshape(1, ns, 8, 64, 128)
    css = cat("o_css", range(n_cores)).reshape(1, ns, 3, 1024)
    outs = (y_prompt, y_sample, kp, vp, cap, ssdp, csp, mk, mv, ks, vs, cas, gvs, ssds, css)
    return tuple(np.ascontiguousarray(o, dtype=np.float32) for o in outs)


def kernel(x_prompt, x_sample, cache_attn_k, cache_attn_v, state_conv_a, state_ssd, state_conv_ssm,
           cache_mem_k, cache_mem_v, mem_prompt,
           norm_mix_g, norm_cross_g, norm_ffn_g, norm_final_g,
           w_in_ab, conv_a_w, conv_a_b, ln_a_g, ln_a_b, lam_q1, lam_k1, lam_q2, lam_k2, subln_g, w_out_ab,
           w_in_cd, ln_c_g, ln_c_b, gm_w_s, gm_b_s, conv_d_w, conv_d_b, dt_bias, a_log, d_skip, norm_d_g, w_out_cd,
           w_xq, w_xk, w_xv, w_xo, w_pq, sub_keys, expert_u, expert_v):
    inp = dict(x_prompt=x_prompt, x_sample=x_sample, cache_attn_k=cache_attn_k, cache_attn_v=cache_attn_v, state_conv_a=state_conv_a,
               state_ssd=state_ssd, state_conv_ssm=state_conv_ssm, cache_mem_k=cache_mem_k, cache_mem_v=cache_mem_v, mem_prompt=mem_prompt,
               norm_mix_g=norm_mix_g, norm_cross_g=norm_cross_g, norm_ffn_g=norm_ffn_g, norm_final_g=norm_final_g,
               w_in_ab=w_in_ab, conv_a_w=conv_a_w, conv_a_b=conv_a_b, ln_a_g=ln_a_g, ln_a_b=ln_a_b, lam_q1=lam_q1, lam_k1=lam_k1,
               lam_q2=lam_q2, lam_k2=lam_k2, subln_g=subln_g, w_out_ab=w_out_ab, w_in_cd=w_in_cd, ln_c_g=ln_c_g, ln_c_b=ln_c_b,
               gm_w_s=gm_w_s, gm_b_s=gm_b_s, conv_d_w=conv_d_w, conv_d_b=conv_d_b, dt_bias=dt_bias, a_log=a_log, d_skip=d_skip,
               norm_d_g=norm_d_g, w_out_cd=w_out_cd, w_xq=w_xq, w_xk=w_xk, w_xv=w_xv, w_xo=w_xo, w_pq=w_pq, sub_keys=sub_keys,
               expert_u=expert_u, expert_v=expert_v)
    nb = x_prompt.shape[0]
    L = x_prompt.shape[1]
    n_cores = 8
    NSS = x_sample.shape[0] // n_cores
    PL = cache_attn_k.shape[2]
    split = (nb == 4)
    res = run_cores(inp, n_cores, L // 128, NSS, PL, lambda c: c % nb, split=split)
    return assemble(res.results, nb, n_cores, L // 128, NSS, split=split)
```

```python
import numpy as np
from contextlib import ExitStack
import concourse.bass as bass
import concourse.mybir as mybir
from concourse.bass_utils import run_bass_kernel_spmd

F32 = mybir.dt.float32
BF16 = mybir.dt.bfloat16
I32 = mybir.dt.int32
U32 = mybir.dt.uint32
AF = mybir.ActivationFunctionType
ALU = mybir.AluOpType
AX = mybir.AxisListType
D = 1024
NEG = -1.0e30


class Buf:
    __slots__ = ("name", "w", "r")

    def __init__(self, name):
        self.name = name
        self.w = None
        self.r = {}


class Sched:
    ENGS = ("pe", "act", "dve", "pool", "sp")

    def __init__(self, nc, es):
        self.nc = nc
        self.sems = {}
        self.esem = {}
        for e in ("pe", "act", "dve", "pool"):
            s = es.enter_context(nc.semaphore("es_" + e))
            self.sems["e_" + e] = s
            self.esem[e] = "e_" + e
        self.ecnt = {e: 0 for e in self.esem}
        self.nchan = 28
        self.ccnt = {}
        for pre in ("c", "g"):
            for i in range(self.nchan):
                self.sems["%s%d" % (pre, i)] = es.enter_context(nc.semaphore("%ss%d" % (pre, i)))
                self.ccnt["%s%d" % (pre, i)] = 0
        self.known = {e: {} for e in self.ENGS}
        self.reset_phase()

    def reset_phase(self):
        self.ops = {e: [] for e in self.ENGS}
        self.chanmap = {}
        self.dbufs = {}

    def chan(self, eng, name):
        pre = "g" if eng == "pool" else "c"
        key = (pre, name)
        if key not in self.chanmap:
            n = sum(1 for (p, _) in self.chanmap if p == pre)
            assert n < self.nchan, "too many dma channels"
            self.chanmap[key] = "%s%d" % (pre, n)
        return self.chanmap[key]

    def db(self, name):
        if name not in self.dbufs:
            self.dbufs[name] = Buf(name)
        return self.dbufs[name]

    def capture(self, f):
        self.cap = []
        f()
        c, self.cap = self.cap, None
        return c

    def replay(self, cap, n):
        for _ in range(min(n, len(cap))):
            a = cap.pop(0)
            self.op(*a[0], **a[1])

    def op(self, eng, fn, reads=(), writes=(), chan=None, amt=16):
        if getattr(self, "cap", None) is not None:
            self.cap.append(((eng, fn), dict(reads=list(reads), writes=list(writes), chan=chan, amt=amt)))
            return
        need = {}
        own = self.esem.get(eng) if chan is None else None

        def add(sig, raw):
            if sig is None:
                return
            k, v = sig
            if k == own and not raw:
                return
            if need.get(k, 0) < v:
                need[k] = v

        for b in reads:
            add(b.w, True)
        for b in writes:
            add(b.w, False)
            for k, v in b.r.items():
                add((k, v), False)
        kn = self.known[eng]
        waits = []
        for k, v in need.items():
            if kn.get(k, 0) < v:
                waits.append((k, v))
                kn[k] = v
        if chan is not None:
            k = self.chan(eng, chan)
            self.ccnt[k] += amt
            sig = (k, self.ccnt[k])
        else:
            k = self.esem[eng]
            self.ecnt[eng] += 1
            sig = (k, self.ecnt[eng])
            amt = 1
        self.ops[eng].append((waits, fn, k, amt))
        for b in reads:
            if b.r.get(sig[0], 0) < sig[1]:
                b.r[sig[0]] = sig[1]
        for b in writes:
            b.w = sig
            b.r = {}

    def drain(self):
        waits = []
        kn = self.known["sp"]
        for k, v in list(self.ccnt.items()) + [(self.esem[e], self.ecnt[e]) for e in self.esem]:
            if v > 0 and kn.get(k, 0) < v:
                waits.append((k, v))
                kn[k] = v
        self.ops["sp"].append((waits, None, None, 0))

    def emit(self):
        self.drain()
        nc = self.nc
        with nc.Block() as blk:
            def run(name):
                def f(e):
                    for waits, fn, k, amt in self.ops[name]:
                        for wk, wv in waits:
                            e.wait_ge(self.sems[wk], wv)
                        if fn is not None:
                            fn(e).then_inc(self.sems[k], amt)
                return f
            blk.tensor(run("pe"))
            blk.scalar(run("act"))
            blk.vector(run("dve"))
            blk.gpsimd(run("pool"))
            blk.sync(run("sp"))
        for e in self.ENGS:
            for k, v in list(self.ccnt.items()) + [(self.esem[x], self.ecnt[x]) for x in self.esem]:
                self.known[e][k] = v
        self.reset_phase()


class T:
    def __init__(self, ap, name, nsub=1):
        self.ap = ap
        self.b = Buf(name)
        self.sub = [Buf(name + str(i)) for i in range(nsub)] if nsub > 1 else None

    def __getitem__(self, k):
        return self.ap[k]


def build(NPT, NSS, PL, lam_inits=(0.8 - 0.6, 0.0), stop_after=None, split=False):
    nc = bass.Bass("TRN2", target_bir_lowering=False)
    NTILE = NPT + NSS
    NTOK = NPT * 128 + NSS * 64
    NPK = PL // 128
    LAM0 = lam_inits[0]

    def din(name, shape, dt=F32):
        return nc.dram_tensor(name, list(shape), dt, kind="ExternalInput").ap()

    def dout(name, shape, dt=F32):
        return nc.dram_tensor(name, list(shape), dt, kind="ExternalOutput").ap()

    def dscr(name, shape, dt=F32):
        return nc.dram_tensor(name, list(shape), dt, kind="Internal").ap()

    xp = din("x_prompt", [NPT * 128, D])
    xs = din("x_sample", [NSS * 64, D])
    ck = din("cache_attn_k", [NSS, PL, 512])
    cv = din("cache_attn_v", [NSS, PL, 512])
    sca = din("state_conv_a", [NSS, 30, 512])
    sssd = din("state_ssd", [NSS, 8, 64, 128])
    scs = din("state_conv_ssm", [NSS, 3, 1024])
    cmk = din("cache_mem_k", [2, NSS, 256, D])
    cmv = din("cache_mem_v", [2, NSS, 256, D])
    memp = din("mem_prompt", [256, D])
    norm_mix_g = din("norm_mix_g", [2, D])
    norm_cross_g = din("norm_cross_g", [2, D])
    norm_ffn_g = din("norm_ffn_g", [2, D])
    norm_final_g = din("norm_final_g", [1, D])
    w_in_ab = din("w_in_ab", [D, 2560])
    conv_a_w = din("conv_a_w", [31, 512])
    conv_a_b = din("conv_a_b", [1, 512])
    ln_a_g = din("ln_a_g", [1, 512])
    ln_a_b = din("ln_a_b", [1, 512])
    lam_q1 = din("lam_q1", [1, 64])
    lam_k1 = din("lam_k1", [1, 64])
    lam_q2 = din("lam_q2", [1, 64])
    lam_k2 = din("lam_k2", [1, 64])
    subln_g = din("subln_g", [1, 128])
    w_out_ab = din("w_out_ab", [D, D])
    w_in_cd = din("w_in_cd", [D, 2568])
    ln_c_g = din("ln_c_g", [1, 512])
    ln_c_b = din("ln_c_b", [1, 512])
    gm_w_s = din("gm_w_s", [4, 128, 128])
    gm_b_s = din("gm_b_s", [4, 128])
    conv_d_w = din("conv_d_w", [4, 1024])
    conv_d_b = din("conv_d_b", [1, 1024])
    dt_bias = din("dt_bias", [1, 8])
    a_log = din("a_log", [1, 8])
    d_skip = din("d_skip", [1, 8])
    norm_d_g = din("norm_d_g", [1, 512])
    w_out_cd = din("w_out_cd", [D, D])
    w_xq = din("w_xq", [2, D, D])
    w_xk = din("w_xk", [2, D, D])
    w_xv = din("w_xv", [2, D, D])
    w_xo = din("w_xo", [2, D, D])
    w_pq = din("w_pq", [2, D, 2048])
    sub_keys = din("sub_keys", [2, 2, 128, 128])
    expert_u = din("expert_u", [2, 16384, D])
    expert_v = din("expert_v", [2, 16384, D])

    NOWN = NPT // 2 if split else NPT
    CHT = 4
    y_p = dout("y_p", [NOWN * 128, D])
    if split:
        rowidx = din("rowidx", [128, NOWN], I32)
        SND = dscr("SND", [NOWN * 128, D])
        XOWN = dscr("XOWN", [NOWN * 128, D])
        RCV = [dscr("RCV%d" % j, [2 * CHT * 128, D]) for j in range(NOWN // CHT)]
    y_s = dout("y_s", [NSS * 64, D])
    o_kp = dout("o_kp", [NPT * 128, 512])
    o_vp = dout("o_vp", [NPT * 128, 512])
    o_cap = dout("o_cap", [30, 512])
    o_ssdp = dout("o_ssdp", [8, 64, 128])
    o_csp = dout("o_csp", [3, 1024])
    o_mk = dout("o_mk", [2, 256, D])
    o_mv = dout("o_mv", [2, 256, D])
    o_ks = dout("o_ks", [NSS * 64, 512])
    o_vs = dout("o_vs", [NSS * 64, 512])
    o_cas = dout("o_cas", [NSS, 30, 512])
    o_gvs = dout("o_gvs", [NSS * 64, 512])
    o_ssds = dout("o_ssds", [NSS, 8, 64, 128])
    o_css = dout("o_css", [NSS, 3, 1024])

    XS = dscr("XS", [NTOK, D])
    QTs = dscr("QTs", [NTILE, 128, 512], BF16)
    CATs = dscr("CATs", [NTILE, 128, 512], BF16)
    KTs = dscr("KTs", [128, 4, NTOK], BF16)
    VXs = dscr("VXs", [NTOK, 516], BF16)

    tiles = []
    for i in range(NPT):
        tiles.append(dict(seq=0, prompt=True, T=128, tok0=i * 128, i=i, first=(i == 0), last=(i == NPT - 1), s=-1))
    for s in range(NSS):
        tiles.append(dict(seq=1 + s, prompt=False, T=64, tok0=NPT * 128 + s * 64, i=0, first=True, last=True, s=s))

    def xin_rows(t):
        if t["prompt"]:
            return xp[t["tok0"]:t["tok0"] + 128, :]
        return xs[t["s"] * 64:(t["s"] + 1) * 64, :]

    with ExitStack() as ges:
        S = Sched(nc, ges)
        psf = [T(ges.enter_context(nc.psum_tensor("psf%d" % i, [128, 512], F32)), "psf%d" % i) for i in range(6)]
        psb = [T(ges.enter_context(nc.psum_tensor("psb%d" % i, [128, 1024], BF16)), "psb%d" % i) for i in range(2)]
        identf = T(ges.enter_context(nc.sbuf_tensor("identf", [128, 128], F32)), "identf")
        identb = T(ges.enter_context(nc.sbuf_tensor("identb", [128, 128], BF16)), "identb")
        onesb = T(ges.enter_context(nc.sbuf_tensor("onesb", [128, 128], BF16)), "onesb")
        onesf = T(ges.enter_context(nc.sbuf_tensor("onesf", [128, 128], F32)), "onesf")

        S.op("pool", lambda e: e.memset(identf[:], 1.0), writes=[identf.b])
        S.op("pool", lambda e: e.affine_select(out=identf[:], in_=identf[:], pattern=[[-1, 128]], compare_op=ALU.is_equal,
                                               fill=0.0, base=0, channel_multiplier=1), reads=[identf.b], writes=[identf.b])
        S.op("dve", lambda e: e.tensor_copy(out=identb[:], in_=identf[:]), reads=[identf.b], writes=[identb.b])
        S.op("dve", lambda e: e.memset(onesb[:], 1.0), writes=[onesb.b])
        S.op("dve", lambda e: e.memset(onesf[:], 1.0), writes=[onesf.b])

        rr = {"psf": 0, "psb": 0}

        def PSF():
            rr["psf"] = (rr["psf"] + 1) % 6
            return psf[rr["psf"]]

        def PSB():
            rr["psb"] = (rr["psb"] + 1) % 2
            return psb[rr["psb"]]

        def phase_tiles(es, prefix):
            cnt = [0]

            def sb(shape, dt=F32, name=None, nsub=1):
                cnt[0] += 1
                nm = "%s_%s%d" % (prefix, name or "t", cnt[0])
                return T(es.enter_context(nc.sbuf_tensor(nm, list(shape), dt)), nm, nsub)
            return sb

        def load_w(sb, w_ap, ncols, name):
            t = sb([128, 8, ncols], BF16, name)
            src = w_ap.rearrange("(c p) n -> p c n", p=128)
            for c in range(8):
                for n0 in range(0, ncols, 1024):
                    n1 = min(ncols, n0 + 1024)
                    S.op("pool", lambda e, c=c, n0=n0, n1=n1: e.dma_start(out=t[:, c, n0:n1], in_=src[:, c, n0:n1]),
                         writes=[t.b], chan=t.b.name)
            return t

        def load_cols(sb, v_ap, nchunk, name):
            t = sb([128, nchunk], F32, name)
            with nc.allow_non_contiguous_dma("tiny param column load"):
                pass
            S.op("sp", lambda e: e.dma_start(out=t[:], in_=v_ap.rearrange("o (c p) -> p (o c)", p=128),
                                             allow_slow_non_contiguous=True), writes=[t.b], chan=t.b.name)
            return t

        def load_rep(sb, v_ap, n, name):
            t = sb([128, n], F32, name)
            S.op("sp", lambda e: e.dma_start(out=t[:], in_=v_ap.partition_broadcast(128)), writes=[t.b], chan=t.b.name)
            return t

        def make_rms(sb):
            st = dict(junk=sb([128, D], F32, "rjunk"), ss=sb([128, 2], F32, "rss"), xh=sb([128, D], BF16, "rxh"))

            def rms_T(x, Tn, gcol, hT, h32=None, grep=None):
                junk, ss, xh = st["junk"], st["ss"], st["xh"]
                S.op("dve", lambda e: e.memset(ss[:Tn, 0:1], 0.0), writes=[ss.b])
                S.op("act", lambda e: e.activation(out=junk[:Tn], in_=x[:Tn], func=AF.Square, accum_out=ss[:Tn, 0:1]),
                     reads=[x.b, ss.b], writes=[junk.b, ss.b])
                S.op("act", lambda e: e.activation(out=ss[:Tn, 1:2], in_=ss[:Tn, 0:1], func=AF.Sqrt, scale=1.0 / D, bias=1e-6),
                     reads=[ss.b], writes=[ss.b])
                S.op("dve", lambda e: e.reciprocal(out=ss[:Tn, 1:2], in_=ss[:Tn, 1:2]), reads=[ss.b], writes=[ss.b])
                S.op("dve", lambda e: e.tensor_scalar(out=xh[:Tn], in0=x[:Tn], scalar1=ss[:Tn, 1:2], scalar2=None, op0=ALU.mult),
                     reads=[x.b, ss.b], writes=[xh.b])
                if h32 is not None:
                    S.op("dve", lambda e: e.scalar_tensor_tensor(out=h32[:Tn], in0=x[:Tn], scalar=ss[:Tn, 1:2], in1=grep[:Tn],
                                                                  op0=ALU.mult, op1=ALU.mult),
                         reads=[x.b, ss.b, grep.b], writes=[h32.b])
                pb = PSB()
                for c in range(8):
                    S.op("pe", lambda e, c=c: e.transpose(out=pb[:, c * Tn:(c + 1) * Tn], in_=xh[:Tn, c * 128:(c + 1) * 128],
                                                          identity=identb[:Tn, :Tn]),
                         reads=[xh.b, identb.b], writes=[pb.b])
                pv = pb[:, 0:8 * Tn].rearrange("p (c t) -> p c t", c=8)
                S.op("dve", lambda e: e.tensor_tensor(out=hT[:, :, :Tn], in0=pv, in1=gcol[:, :].unsqueeze(2).to_broadcast([128, 8, Tn]),
                                                      op=ALU.mult),
                     reads=[pb.b, gcol.b], writes=[hT.b])
            rms_T.junk = st["junk"]
            return rms_T

        def mm_feat(ps, col0, nchunk, w, hT, Tn, wb=None):
            for j in range(nchunk):
                for c in range(8):
                    S.op("pe", lambda e, j=j, c=c: e.matmul(ps[:, j * Tn:(j + 1) * Tn], lhsT=w[:, c, col0 + j * 128:col0 + (j + 1) * 128],
                                                            rhs=hT[:, c, :Tn], start=(c == 0), stop=(c == 7)),
                         reads=[w.b, hT.b], writes=[ps.b])

        def mm_tok(ps, col0, ncol, w, hT, Tn, nk=8):
            for c in range(nk):
                S.op("pe", lambda e, c=c: e.matmul(ps[:Tn, 0:ncol], lhsT=hT[:, c, :Tn], rhs=w[:, c, col0:col0 + ncol],
                                                   start=(c == 0), stop=(c == nk - 1)),
                     reads=[w.b, hT.b], writes=[ps.b])

        def gelu_tanh(e_act, e_dve, out, x, shape_ap, tmp, rb, wb_, Tn=None):
            pass

        def QTs_view(scr, ti, Tn):
            return scr[ti].rearrange("p (c t) -> p c t", c=4)[:, :, 0:Tn]

        with ExitStack() as es:
            sb = phase_tiles(es, "p0")
            mem32 = sb([128, 2, D], F32, "mem32")
            memb = sb([128, 2, D], BF16, "memb")
            memT = sb([128, 8, 256], BF16, "memT")
            S.op("sp", lambda e: e.dma_start(out=mem32[:], in_=memp.rearrange("(m p) d -> p m d", p=128)), writes=[mem32.b], chan="mem32")
            S.op("dve", lambda e: e.tensor_copy(out=memb[:], in_=mem32[:]), reads=[mem32.b], writes=[memb.b])
            for mc in range(2):
                pb = PSB()
                for c in range(8):
                    S.op("pe", lambda e, c=c, mc=mc, pb=pb: e.transpose(out=pb[:, c * 128:(c + 1) * 128], in_=memb[:, mc, c * 128:(c + 1) * 128],
                                                                        identity=identb[:]),
                         reads=[memb.b, identb.b], writes=[pb.b])
                S.op("act", lambda e, mc=mc, pb=pb: e.copy(out=memT[:, :, mc * 128:(mc + 1) * 128],
                                                           in_=pb[:, :].rearrange("p (c t) -> p c t", c=8)),
                     reads=[pb.b], writes=[memT.b])
            stg = [sb([128, D], F32, "stg") for _ in range(2)]
            k = 0
            for l in range(2):
                for (wsrc, odst) in ((w_xk, o_mk), (w_xv, o_mv)):
                    w = load_w(sb, wsrc[l], D, "wkv")
                    for mc in range(2):
                        st = stg[k % 2]
                        k += 1
                        for n in range(2):
                            ps = PSF()
                            for c in range(8):
                                S.op("pe", lambda e, c=c, n=n, mc=mc, ps=ps, w=w: e.matmul(
                                    ps[:, :], lhsT=memT[:, c, mc * 128:(mc + 1) * 128], rhs=w[:, c, n * 512:(n + 1) * 512],
                                    start=(c == 0), stop=(c == 7)), reads=[memT.b, w.b], writes=[ps.b])
                            S.op("act", lambda e, n=n, ps=ps, st=st: e.copy(out=st[:, n * 512:(n + 1) * 512], in_=ps[:, :]),
                                 reads=[ps.b], writes=[st.b])
                        S.op("sp", lambda e, l=l, mc=mc, st=st, odst=odst: e.dma_start(out=odst[l, mc * 128:(mc + 1) * 128, :], in_=st[:]),
                             reads=[st.b], writes=[S.db("omem")], chan="st_" + st.b.name)
            S.emit()

        with ExitStack() as es:
            sb = phase_tiles(es, "p1")
            w = load_w(sb, w_in_ab, 2560, "win")
            gcol = load_cols(sb, norm_mix_g[0:1, :], 8, "gcol")
            cw = sb([128, 4, 31], F32, "cw")
            for c in range(4):
                S.op("sp", lambda e, c=c: e.dma_start(out=cw[:, c, :], in_=conv_a_w[:, c * 128:(c + 1) * 128].rearrange("k p -> p k"),
                                                      allow_slow_non_contiguous=True), writes=[cw.b], chan="cw")
            cb = load_cols(sb, conv_a_b, 4, "cb")
            lg = load_cols(sb, ln_a_g, 4, "lg")
            lb = load_cols(sb, ln_a_b, 4, "lb")
            rms_T = make_rms(sb)
            xt = [sb([128, D], F32, "x") for _ in range(2)]
            hT = sb([128, 8, 128], BF16, "hT")
            sg = sb([128, 4, 128], F32, "sg")
            cbuf = sb([128, 4, 158], F32, "cbuf")
            acc = [sb([128, 128], F32, "acc%d" % c) for c in range(4)]
            sq = sb([128, 4, 128], F32, "sq")
            mv_ = sb([128, 3, 128], F32, "mv")
            xn = sb([128, 4, 128], F32, "xn")
            caT = [sb([128, 4, 128], BF16, "caT") for _ in range(2)]
            qT = [sb([128, 4, 128], BF16, "qT") for _ in range(2)]
            kT = [sb([128, 4, 128], BF16, "kT") for _ in range(2)]
            kvt = [sb([128, 2, 512], F32, "kvt") for _ in range(2)]
            vx = [sb([128, 4, 129], BF16, "vx") for _ in range(2)]
            st30 = sb([32, 512], F32, "st30")
            cao = sb([32, 512], F32, "cao")
            onesm = sb([128, 128], F32, "onesm")
            S.op("dve", lambda e: e.memset(onesm[:], 1.0 / 512), writes=[onesm.b])
            for v in vx:
                S.op("dve", lambda e, v=v: e.memset(v[:], 1.0), writes=[v.b])
            for ti, t in enumerate(tiles):
                Tn = t["T"]
                x = xt[ti % 2]
                S.op("sp", lambda e, x=x, t=t, Tn=Tn: e.dma_start(out=x[:Tn], in_=xin_rows(t)), writes=[x.b], chan=x.b.name)
                rms_T(x, Tn, gcol, hT)
                if t["first"]:
                    if t["prompt"]:
                        S.op("pool", lambda e: e.memset(cbuf[:, :, 0:30], 0.0), writes=[cbuf.b])
                    else:
                        S.op("sp", lambda e, t=t: e.dma_start(out=st30[0:30, :], in_=sca[t["s"]]), writes=[st30.b], chan="st30")
                        ps = PSF()
                        for c in range(4):
                            S.op("pe", lambda e, c=c, ps=ps: e.transpose(out=ps[:, c * 30:(c + 1) * 30], in_=st30[0:30, c * 128:(c + 1) * 128],
                                                                         identity=identf[0:30, 0:30]),
                                 reads=[st30.b, identf.b], writes=[ps.b])
                        S.op("act", lambda e, ps=ps: e.copy(out=cbuf[:, :, 0:30], in_=ps[:, 0:120].rearrange("p (c t) -> p c t", c=4)),
                             reads=[ps.b], writes=[cbuf.b])
                else:
                    S.op("pool", lambda e: e.tensor_copy(out=cbuf[:, :, 0:30], in_=cbuf[:, :, 128:158]), reads=[cbuf.b], writes=[cbuf.b])
                pa, pg = PSF(), PSF()
                mm_feat(pa, 0, 4, w, hT, Tn)
                mm_feat(pg, 512, 4, w, hT, Tn)
                S.op("act", lambda e, pg=pg, Tn=Tn: e.activation(out=sg[:, :, :Tn], in_=pg[:, 0:4 * Tn].rearrange("p (c t) -> p c t", c=4),
                                                                 func=AF.Sigmoid), reads=[pg.b], writes=[sg.b])
                S.op("dve", lambda e, pa=pa, Tn=Tn: e.tensor_tensor(out=cbuf[:, :, 30:30 + Tn], in0=pa[:, 0:4 * Tn].rearrange("p (c t) -> p c t", c=4),
                                                                    in1=sg[:, :, :Tn], op=ALU.mult), reads=[pa.b, sg.b], writes=[cbuf.b])
                if t["last"]:
                    ps = PSF()
                    for c in range(4):
                        S.op("pe", lambda e, c=c, ps=ps, Tn=Tn: e.transpose(out=ps[0:30, c * 128:(c + 1) * 128], in_=cbuf[:, c, Tn:Tn + 30],
                                                                            identity=identf[:, :]),
                             reads=[cbuf.b, identf.b], writes=[ps.b])
                    S.op("act", lambda e, ps=ps: e.copy(out=cao[0:30, :], in_=ps[0:30, :]), reads=[ps.b], writes=[cao.b])
                    dst = o_cap if t["prompt"] else o_cas[t["s"]]
                    S.op("sp", lambda e, dst=dst: e.dma_start(out=dst, in_=cao[0:30, :]), reads=[cao.b], writes=[S.db("ocap")], chan="st_cao")
                for k in range(31):
                    for c in range(4):
                        eng = "dve"
                        if k == 0:
                            S.op(eng, lambda e, c=c, Tn=Tn: e.tensor_scalar(out=acc[c][:, :Tn], in0=cbuf[:, c, 0:Tn], scalar1=cw[:, c, 0:1],
                                                                            scalar2=cb[:, c:c + 1], op0=ALU.mult, op1=ALU.add),
                                 reads=[cbuf.b, cw.b, cb.b], writes=[acc[c].b])
                        else:
                            S.op(eng, lambda e, c=c, k=k, Tn=Tn: e.scalar_tensor_tensor(out=acc[c][:, :Tn], in0=cbuf[:, c, k:k + Tn],
                                                                                        scalar=cw[:, c, k:k + 1], in1=acc[c][:, :Tn],
                                                                                        op0=ALU.mult, op1=ALU.add),
                                 reads=[cbuf.b, cw.b, acc[c].b], writes=[acc[c].b])
                pm, pq = PSF(), PSF()
                for c in range(4):
                    S.op("act", lambda e, c=c, Tn=Tn: e.activation(out=sq[:, c, :Tn], in_=acc[c][:, :Tn], func=AF.Square),
                         reads=[acc[c].b], writes=[sq.b])
                for c in range(4):
                    S.op("pe", lambda e, c=c, pm=pm, Tn=Tn: e.matmul(pm[:, :Tn], lhsT=onesm[:, :], rhs=acc[c][:, :Tn], start=(c == 0), stop=(c == 3)),
                         reads=[onesm.b, acc[c].b], writes=[pm.b])
                for c in range(4):
                    S.op("pe", lambda e, c=c, pq=pq, Tn=Tn: e.matmul(pq[:, :Tn], lhsT=onesm[:, :], rhs=sq[:, c, :Tn], start=(c == 0), stop=(c == 3)),
                         reads=[onesm.b, sq.b], writes=[pq.b])
                S.op("act", lambda e, pm=pm, Tn=Tn: e.copy(out=mv_[:, 0, :Tn], in_=pm[:, :Tn]), reads=[pm.b], writes=[mv_.b])
                S.op("dve", lambda e, Tn=Tn: e.tensor_tensor(out=mv_[:, 1, :Tn], in0=mv_[:, 0, :Tn], in1=mv_[:, 0, :Tn], op=ALU.mult),
                     reads=[mv_.b], writes=[mv_.b])
                S.op("dve", lambda e, pq=pq, Tn=Tn: e.tensor_tensor(out=mv_[:, 1, :Tn], in0=pq[:, :Tn], in1=mv_[:, 1, :Tn], op=ALU.subtract),
                     reads=[pq.b, mv_.b], writes=[mv_.b])
                S.op("act", lambda e, Tn=Tn: e.activation(out=mv_[:, 2, :Tn], in_=mv_[:, 1, :Tn], func=AF.Sqrt, bias=1e-5, scale=1.0),
                     reads=[mv_.b], writes=[mv_.b])
                S.op("dve", lambda e, Tn=Tn: e.reciprocal(out=mv_[:, 2, :Tn], in_=mv_[:, 2, :Tn]), reads=[mv_.b], writes=[mv_.b])
                for c in range(4):
                    S.op("dve", lambda e, c=c, Tn=Tn: e.tensor_tensor(out=xn[:, c, :Tn], in0=acc[c][:, :Tn], in1=mv_[:, 0, :Tn], op=ALU.subtract),
                         reads=[acc[c].b, mv_.b], writes=[xn.b])
                S.op("dve", lambda e, Tn=Tn: e.tensor_tensor(out=xn[:, :, :Tn], in0=xn[:, :, :Tn],
                                                             in1=mv_[:, 2, :Tn].unsqueeze(1).to_broadcast([128, 4, Tn]), op=ALU.mult),
                     reads=[xn.b, mv_.b], writes=[xn.b])
                ca = caT[ti % 2]
                for c in range(4):
                    S.op("act", lambda e, c=c, ca=ca, Tn=Tn: e.activation(out=ca[:, c, :Tn], in_=xn[:, c, :Tn], func=AF.Silu,
                                                                          scale=lg[:, c:c + 1], bias=lb[:, c:c + 1]),
                         reads=[xn.b, lg.b, lb.b], writes=[ca.b])
                S.op("sp", lambda e, ca=ca, ti=ti, Tn=Tn: e.dma_start(out=QTs_view(CATs, ti, Tn), in_=ca[:, :, :Tn]),
                     reads=[ca.b], writes=[S.db("CATs%d" % ti)], chan="st_" + ca.b.name)
                pq2, pk2 = PSF(), PSF()
                mm_feat(pq2, 1024, 4, w, hT, Tn)
                mm_feat(pk2, 1536, 4, w, hT, Tn)
                q_, k_ = qT[ti % 2], kT[ti % 2]
                S.op("act", lambda e, pq2=pq2, q_=q_, Tn=Tn: e.copy(out=q_[:, :, :Tn], in_=pq2[:, 0:4 * Tn].rearrange("p (c t) -> p c t", c=4)),
                     reads=[pq2.b], writes=[q_.b])
                S.op("dve", lambda e, pk2=pk2, k_=k_, Tn=Tn: e.tensor_copy(out=k_[:, :, :Tn], in_=pk2[:, 0:4 * Tn].rearrange("p (c t) -> p c t", c=4)),
                     reads=[pk2.b], writes=[k_.b])
                S.op("sp", lambda e, q_=q_, ti=ti, Tn=Tn: e.dma_start(out=QTs_view(QTs, ti, Tn), in_=q_[:, :, :Tn]),
                     reads=[q_.b], writes=[S.db("QTs%d" % ti)], chan="st_" + q_.b.name)
                S.op("sp", lambda e, k_=k_, t=t, Tn=Tn: e.dma_start(out=KTs[:, :, t["tok0"]:t["tok0"] + Tn], in_=k_[:, :, :Tn]),
                     reads=[k_.b], writes=[S.db("KTs%d" % ti)], chan="st_" + k_.b.name)
                kv = kvt[ti % 2]
                for n in range(2):
                    ps = PSF()
                    mm_tok(ps, 1536 + n * 512, 512, w, hT, Tn)
                    S.op("act" if n == 0 else "dve",
                         (lambda e, ps=ps, kv=kv, n=n, Tn=Tn: e.copy(out=kv[:Tn, n, :], in_=ps[:Tn, :])) if n == 0 else
                         (lambda e, ps=ps, kv=kv, n=n, Tn=Tn: e.tensor_copy(out=kv[:Tn, n, :], in_=ps[:Tn, :])),
                         reads=[ps.b], writes=[kv.b])
                okd, ovd = (o_kp, o_vp) if t["prompt"] else (o_ks, o_vs)
                r0 = t["tok0"] if t["prompt"] else t["s"] * 64
                S.op("sp", lambda e, kv=kv, okd=okd, r0=r0, Tn=Tn: e.dma_start(out=okd[r0:r0 + Tn, :], in_=kv[:Tn, 0, :]),
                     reads=[kv.b], writes=[S.db("okv")], chan="st_" + kv.b.name)
                S.op("sp", lambda e, kv=kv, ovd=ovd, r0=r0, Tn=Tn: e.dma_start(out=ovd[r0:r0 + Tn, :], in_=kv[:Tn, 1, :]),
                     reads=[kv.b], writes=[S.db("okv")], chan="st_" + kv.b.name)
                v_ = vx[ti % 2]
                S.op("pool", lambda e, kv=kv, v_=v_, Tn=Tn: e.tensor_copy(out=v_[:Tn, :, 0:128], in_=kv[:Tn, 1, :].rearrange("p (h e) -> p h e", h=4)),
                     reads=[kv.b], writes=[v_.b])
                S.op("sp", lambda e, v_=v_, t=t, Tn=Tn: e.dma_start(out=VXs[t["tok0"]:t["tok0"] + Tn, :], in_=v_[:Tn].rearrange("p h e -> p (h e)")),
                     reads=[v_.b], writes=[S.db("VXs%d" % ti)], chan="st_" + v_.b.name)
            S.emit()

        with ExitStack() as es:
            sb = phase_tiles(es, "p2")
            NKT = max(NPT, NPK + 1)
            wo = load_w(sb, w_out_ab, D, "wo")
            KT = sb([128, 4, NKT * 128], BF16, "KT")
            VS = sb([128, NKT, 516], BF16, "VS")
            kcs = [sb([128, 4, 512], F32, "kcs") for _ in range(2)]
            xt = [sb([128, D], F32, "x") for _ in range(2)]
            qT = [sb([128, 4, 128], BF16, "qT") for _ in range(2)]
            caT = [sb([128, 4, 128], BF16, "caT") for _ in range(2)]
            PT = [sb([128, 4, 128], BF16, "PT") for _ in range(3)]
            ah = sb([128, 128], F32, "ah")
            rr_ = sb([128, 8], F32, "rr")
            junk = sb([128, 128], F32, "junk")
            otok = sb([128, 512], BF16, "otok")
            oT = sb([128, 4, 128], BF16, "oT")
            lam = sb([128, 8], F32, "lam")
            lv = [load_rep(sb, a, 64, "lv") for a in (lam_q1, lam_k1, lam_q2, lam_k2)]
            gs = load_rep(sb, subln_g, 128, "gs")
            lj = sb([128, 64], F32, "lj")
            S.op("dve", lambda e: e.memset(lam[:], 0.0), writes=[lam.b])
            S.op("dve", lambda e: e.scalar_tensor_tensor(out=lj[:], in0=lv[0][:], scalar=1.0, in1=lv[1][:], op0=ALU.mult, op1=ALU.mult,
                                                         accum_out=lam[:, 0:1]), reads=[lv[0].b, lv[1].b, lam.b], writes=[lj.b, lam.b])
            S.op("dve", lambda e: e.scalar_tensor_tensor(out=lj[:], in0=lv[2][:], scalar=1.0, in1=lv[3][:], op0=ALU.mult, op1=ALU.mult,
                                                         accum_out=lam[:, 1:2]), reads=[lv[2].b, lv[3].b, lam.b, lj.b], writes=[lj.b, lam.b])
            S.op("act", lambda e: e.activation(out=lam[:, 2:4], in_=lam[:, 0:2], func=AF.Exp), reads=[lam.b], writes=[lam.b])
            S.op("dve", lambda e: e.tensor_tensor(out=lam[:, 4:5], in0=lam[:, 3:4], in1=lam[:, 2:3], op=ALU.subtract), reads=[lam.b], writes=[lam.b])
            S.op("dve", lambda e: e.tensor_scalar(out=lam[:, 5:6], in0=lam[:, 4:5], scalar1=-LAM0, scalar2=None, op0=ALU.add), reads=[lam.b], writes=[lam.b])
            S.op("dve", lambda e: e.tensor_scalar(out=gs[:], in0=gs[:], scalar1=(1.0 - LAM0), scalar2=None, op0=ALU.mult), reads=[gs.b], writes=[gs.b])
            psO = [psf[0], psf[1]]
            psS = [psf[2], psf[3]]
            psX = [psf[4], psf[5]]
            cnt = {"s": 0, "p": 0}
            seqs = [(0, True)] + [(1 + s, False) for s in range(NSS)]
            for (sq_, isp) in seqs:
                stiles = [(ti, t) for ti, t in enumerate(tiles) if t["seq"] == sq_]
                if isp:
                    nkt_total = NPT
                    for h in range(4):
                        S.op("sp", lambda e, h=h: e.dma_start(out=KT[:, h, 0:NPT * 128], in_=KTs[:, h, 0:NPT * 128]),
                             reads=[S.db("KTs%d" % i) for i in range(NPT)], writes=[KT.b], chan="KT")
                    for j0 in range(0, NPT, 8):
                        j1 = min(NPT, j0 + 8)
                        S.op("sp", lambda e, j0=j0, j1=j1: e.dma_start(out=VS[:, j0:j1, :], in_=VXs[j0 * 128:j1 * 128, :].rearrange("(j p) f -> p j f", p=128)),
                             reads=[S.db("VXs%d" % i) for i in range(j0, j1)], writes=[VS.b], chan="VS")
                    klens = [128] * NPT
                else:
                    s = sq_ - 1
                    ti0 = NPT + s
                    tok0 = NPT * 128 + s * 64
                    S.op("dve", lambda e: e.memset(VS[:, 0:NPK, :], 1.0), writes=[VS.b])
                    for j in range(NPK):
                        S.op("pool", lambda e, s=s, j=j: e.dma_start(out=VS[:, j, :].rearrange("p (h e) -> p h e", h=4)[:, :, 0:128],
                                                                     in_=cv[s, j * 128:(j + 1) * 128, :].rearrange("p (h e) -> p h e", h=4)),
                             writes=[VS.b], chan="VS")
                    S.op("sp", lambda e, tok0=tok0: e.dma_start(out=VS[0:64, NPK, :], in_=VXs[tok0:tok0 + 64, :]),
                         reads=[S.db("VXs%d" % ti0)], writes=[VS.b], chan="VS")
                    S.op("sp", lambda e, tok0=tok0: e.dma_start(out=KT[:, :, PL:PL + 64], in_=KTs[:, :, tok0:tok0 + 64]),
                         reads=[S.db("KTs%d" % ti0)], writes=[KT.b], chan="KT")
                    for j in range(NPK):
                        kc = kcs[j % 2]
                        S.op("sp", lambda e, kc=kc, j=j, s=s: e.dma_start(out=kc[:, 0, :], in_=ck[s, j * 128:(j + 1) * 128, :]),
                             writes=[kc.b], chan=kc.b.name)
                        ps = psX[j % 2]
                        for h in range(4):
                            S.op("pe", lambda e, h=h, kc=kc, ps=ps: e.transpose(out=ps[:, h * 128:(h + 1) * 128], in_=kc[:, 0, h * 128:(h + 1) * 128],
                                                                                identity=identf[:, :]),
                                 reads=[kc.b, identf.b], writes=[ps.b])
                        S.op("act", lambda e, ps=ps, j=j: e.copy(out=KT[:, :, j * 128:(j + 1) * 128], in_=ps[:, :].rearrange("p (h k) -> p h k", h=4)),
                             reads=[ps.b], writes=[KT.b])
                    klens = [128] * NPK + [64]
                for (ti, t) in stiles:
                    Tn = t["T"]
                    x, q_, ca = xt[ti % 2], qT[ti % 2], caT[ti % 2]
                    S.op("sp", lambda e, x=x, t=t, Tn=Tn: e.dma_start(out=x[:Tn], in_=xin_rows(t)), writes=[x.b], chan=x.b.name)
                    S.op("sp", lambda e, q_=q_, ti=ti, Tn=Tn: e.dma_start(out=q_[:, :, :Tn], in_=QTs_view(QTs, ti, Tn)),
                         reads=[S.db("QTs%d" % ti)], writes=[q_.b], chan=q_.b.name)
                    S.op("sp", lambda e, ca=ca, ti=ti, Tn=Tn: e.dma_start(out=ca[:, :, :Tn], in_=QTs_view(CATs, ti, Tn)),
                         reads=[S.db("CATs%d" % ti)], writes=[ca.b], chan=ca.b.name)
                    nk = (t["i"] + 1) if isp else (NPK + 1)
                    for h in range(4):
                        for jb in range(0, nk, 4):
                            js = list(range(jb, min(nk, jb + 4)))
                            for tt in range(2):
                                p0 = 64 * tt
                                pS = psS[cnt["s"] % 2]
                                cnt["s"] += 1
                                P = PT[cnt["p"] % 3]
                                cnt["p"] += 1
                                for jj, j in enumerate(js):
                                    kl = klens[j]
                                    S.op("pe", lambda e, jj=jj, j=j, kl=kl, pS=pS, h=h, p0=p0, q_=q_, Tn=Tn: e.matmul(
                                        pS[:kl, jj * 128:jj * 128 + Tn], lhsT=KT[p0:p0 + 64, h, j * 128:j * 128 + kl], rhs=q_[p0:p0 + 64, h, :Tn],
                                        start=True, stop=True), reads=[KT.b, q_.b], writes=[pS.b])
                                nj = len(js)
                                S.op("act", lambda e, pS=pS, P=P, nj=nj, Tn=Tn: e.activation(
                                    out=P[:, 0:nj, :Tn], in_=pS[:, 0:nj * 128].rearrange("p (j t) -> p j t", j=nj)[:, :, :Tn], func=AF.Exp, scale=0.125),
                                    reads=[pS.b], writes=[P.b])
                                if isp and js[-1] == t["i"]:
                                    jj = len(js) - 1
                                    S.op("dve", lambda e, P=P, jj=jj: e.memset(P[64:128, jj, 0:64], 0.0), reads=[P.b], writes=[P.b])
                                for jj, j in enumerate(js):
                                    kl = klens[j]
                                    S.op("pe", lambda e, jj=jj, j=j, kl=kl, P=P, h=h, tt=tt, Tn=Tn, nk=nk: e.matmul(
                                        psO[tt][:Tn, 0:129], lhsT=P[:kl, jj, :Tn], rhs=VS[:kl, j, h * 129:(h + 1) * 129],
                                        start=(j == 0), stop=(j == nk - 1)), reads=[P.b, VS.b], writes=[psO[tt].b])
                        S.op("dve", lambda e, Tn=Tn: e.reciprocal(out=rr_[:Tn, 0:1], in_=psO[0][:Tn, 128:129]), reads=[psO[0].b], writes=[rr_.b])
                        S.op("dve", lambda e, Tn=Tn: e.reciprocal(out=rr_[:Tn, 1:2], in_=psO[1][:Tn, 128:129]), reads=[psO[1].b, rr_.b], writes=[rr_.b])
                        S.op("dve", lambda e, Tn=Tn: e.tensor_tensor(out=rr_[:Tn, 2:3], in0=rr_[:Tn, 1:2], in1=lam[:Tn, 5:6], op=ALU.mult),
                             reads=[rr_.b, lam.b], writes=[rr_.b])
                        S.op("dve", lambda e, Tn=Tn: e.tensor_scalar(out=ah[:Tn], in0=psO[0][:Tn, 0:128], scalar1=rr_[:Tn, 0:1], scalar2=None, op0=ALU.mult),
                             reads=[psO[0].b, rr_.b], writes=[ah.b])
                        S.op("dve", lambda e, Tn=Tn: e.scalar_tensor_tensor(out=ah[:Tn], in0=psO[1][:Tn, 0:128], scalar=rr_[:Tn, 2:3], in1=ah[:Tn],
                                                                            op0=ALU.mult, op1=ALU.add), reads=[psO[1].b, rr_.b, ah.b], writes=[ah.b])
                        S.op("dve", lambda e, Tn=Tn: e.memset(rr_[:Tn, 3:4], 0.0), reads=[rr_.b], writes=[rr_.b])
                        S.op("act", lambda e, Tn=Tn: e.activation(out=junk[:Tn], in_=ah[:Tn], func=AF.Square, accum_out=rr_[:Tn, 3:4]),
                             reads=[ah.b, rr_.b], writes=[junk.b, rr_.b])
                        S.op("act", lambda e, Tn=Tn: e.activation(out=rr_[:Tn, 4:5], in_=rr_[:Tn, 3:4], func=AF.Sqrt, scale=1.0 / 128, bias=1e-6),
                             reads=[rr_.b], writes=[rr_.b])
                        S.op("dve", lambda e, Tn=Tn: e.reciprocal(out=rr_[:Tn, 4:5], in_=rr_[:Tn, 4:5]), reads=[rr_.b], writes=[rr_.b])
                        S.op("dve", lambda e, Tn=Tn, h=h: e.scalar_tensor_tensor(out=otok[:Tn, h * 128:(h + 1) * 128], in0=ah[:Tn], scalar=rr_[:Tn, 4:5],
                                                                                 in1=gs[:Tn], op0=ALU.mult, op1=ALU.mult),
                             reads=[ah.b, rr_.b, gs.b], writes=[otok.b])
                    pb = PSB()
                    for h in range(4):
                        S.op("pe", lambda e, h=h, pb=pb, Tn=Tn: e.transpose(out=pb[:, h * Tn:(h + 1) * Tn], in_=otok[:Tn, h * 128:(h + 1) * 128],
                                                                            identity=identb[:Tn, :Tn]), reads=[otok.b, identb.b], writes=[pb.b])
                    S.op("act", lambda e, pb=pb, Tn=Tn: e.copy(out=oT[:, :, :Tn], in_=pb[:, 0:4 * Tn].rearrange("p (h t) -> p h t", h=4)),
                         reads=[pb.b], writes=[oT.b])
                    for n in range(2):
                        ps = psX[n]
                        for c in range(8):
                            src = ca if c < 4 else oT
                            S.op("pe", lambda e, c=c, n=n, ps=ps, src=src, Tn=Tn: e.matmul(ps[:Tn, :], lhsT=src[:, c % 4, :Tn],
                                                                                          rhs=wo[:, c, n * 512:(n + 1) * 512],
                                                                                          start=(c == 0), stop=(c == 7)),
                                 reads=[src.b, wo.b], writes=[ps.b])
                        S.op("dve", lambda e, n=n, ps=ps, x=x, Tn=Tn: e.tensor_tensor(out=x[:Tn, n * 512:(n + 1) * 512], in0=ps[:Tn, :],
                                                                                     in1=x[:Tn, n * 512:(n + 1) * 512], op=ALU.add),
                             reads=[ps.b, x.b], writes=[x.b])
                    S.op("sp", lambda e, x=x, t=t, Tn=Tn: e.dma_start(out=XS[t["tok0"]:t["tok0"] + Tn, :], in_=x[:Tn]),
                         reads=[x.b], writes=[S.db("XS%d" % ti)], chan="st_" + x.b.name)
            S.emit()

        def cross_peer(l, final, peer=True):
            with ExitStack() as es:
                sb = phase_tiles(es, "p3%d" % l)
                wq = load_w(sb, w_xq[l], D, "wq")
                wo_ = load_w(sb, w_xo[l], D, "wo")
                wp = load_w(sb, w_pq[l], 2048, "wp")
                gcx = load_cols(sb, norm_cross_g[l:l + 1, :], 8, "gcx")
                gcf = load_cols(sb, norm_ffn_g[l:l + 1, :], 8, "gcf")
                grf = load_rep(sb, norm_ffn_g[l:l + 1, :], D, "grf")
                if final:
                    grz = load_rep(sb, norm_final_g, D, "grz")
                skT = sb([128, 2, 128], F32, "skT")
                sk32 = sb([128, 2, 128], F32, "sk32")
                S.op("sp", lambda e: e.dma_start(out=sk32[:], in_=sub_keys[l].rearrange("c k d -> k c d")), writes=[sk32.b], chan="sk32")
                for c in range(2):
                    ps = PSF()
                    S.op("pe", lambda e, c=c, ps=ps: e.transpose(out=ps[:, 0:128], in_=sk32[:, c, :], identity=identf[:, :]),
                         reads=[sk32.b, identf.b], writes=[ps.b])
                    S.op("act", lambda e, c=c, ps=ps: e.copy(out=skT[:, c, :], in_=ps[:, 0:128]), reads=[ps.b], writes=[skT.b])
                rms_T = make_rms(sb)
                iot = sb([128, 16], F32, "iot")
                S.op("pool", lambda e: e.iota(iot[:], pattern=[[1, 16]], base=0, channel_multiplier=0, allow_small_or_imprecise_dtypes=True),
                     writes=[iot.b])
                mb = sb([128, 2, D], BF16, "mb")
                mkT = sb([128, 8, 256], BF16, "mkT")
                mvb = sb([128, 2, D], BF16, "mvb")
                xt = [sb([128, D], F32, "x") for _ in range(2)]
                hT = sb([128, 8, 128], BF16, "hT")
                h32s = [sb([128, D], F32, "h32") for _ in range(2)]
                qTx = sb([128, 8, 128], BF16, "qTx")
                PTx = sb([128, 8, 128], BF16, "PTx")
                rinv = sb([128, 4, 128], F32, "rinv")
                oTx = sb([128, 8, 128], BF16, "oTx")
                pqT = sb([128, 16, 128], F32, "pqT")
                sc = sb([128, 16, 128], F32, "sc")
                sc2 = sb([128, 16, 128], F32, "sc2")
                ts = sb([128, 16, 16], F32, "ts")
                tiu = sb([128, 16, 16], U32, "tiu")
                tif = sb([128, 16, 16], F32, "tif")
                cand = sb([128, 8, 256], F32, "cand")
                cand2 = T(sc2.ap.rearrange("p (h a) k -> p h (a k)", h=8), "cand2v")
                cand2.b = sc2.b
                bs = sb([128, 8, 16], F32, "bs")
                bpu = sb([128, 8, 16], U32, "bpu")
                bpf = sb([128, 8, 16], F32, "bpf")
                fa = sb([128, 8, 16], F32, "fa")
                fb_ = sb([128, 8, 16], F32, "fb")
                ia = sb([128, 8, 16], I32, "ia")
                oh = T(sc.ap.rearrange("p (h a) (b c) -> p h a b c", h=8, b=8).rearrange("p h a b c -> p h (a b) c"), "ohv")
                oh.b = sc.b
                i0f = sb([128, 8, 16], F32, "i0f")
                i1f = sb([128, 8, 16], F32, "i1f")
                idxs = [sb([128, 128], I32, "idx") for _ in range(2)]
                gates = [sb([128, 8, 16], F32, "gate") for _ in range(2)]
                gsum = sb([128, 8], F32, "gsum")
                act_ = sb([128, 128], F32, "act")
                g1 = sb([128, 128], F32, "g1")
                g2 = sb([128, 128], F32, "g2")
                wgt = sb([128, 128], F32, "wgt")
                NG = 8
                gb = [sb([128, D], F32, "gb") for _ in range(NG)]
                pj = rms_T.junk
                accs = [sb([128, D], F32, "pacc") for _ in range(2)]
                for idx in idxs:
                    S.op("dve", lambda e, idx=idx: e.memset(idx[:], 0), writes=[idx.b])
                fss = sb([128, 2], F32, "fss")
                gcnt = [0]
                cur_seq = [None]
                if split:
                    ridx = sb([128, NOWN], I32, "ridx")
                    S.op("sp", lambda e: e.dma_start(out=ridx[:], in_=rowidx), writes=[ridx.b], chan="ridx")
                    own_tiles = [(k, dict(tiles[0], slot=k)) for k in range(NOWN)] + [(ti, t) for ti, t in enumerate(tiles) if not t["prompt"]]
                else:
                    own_tiles = [(ti, dict(t, slot=t["i"])) for ti, t in enumerate(tiles)]
                def AB(n):
                    ti, t = own_tiles[n]
                    h32, idx, gate = h32s[n % 2], idxs[n % 2], gates[n % 2]
                    Tn = t["T"]
                    if cur_seq[0] != t["seq"]:
                        cur_seq[0] = t["seq"]
                        ksrc = o_mk[l] if t["prompt"] else cmk[l, t["s"]]
                        vsrc = o_mv[l] if t["prompt"] else cmv[l, t["s"]]
                        S.op("pool", lambda e, ksrc=ksrc: e.dma_start(out=mb[:], in_=ksrc.rearrange("(m p) d -> p m d", p=128)),
                             reads=[S.db("omem")], writes=[mb.b], chan="mb")
                        for mc in range(2):
                            pb = PSB()
                            for c in range(8):
                                S.op("pe", lambda e, c=c, mc=mc, pb=pb: e.transpose(out=pb[:, c * 128:(c + 1) * 128], in_=mb[:, mc, c * 128:(c + 1) * 128],
                                                                                    identity=identb[:]), reads=[mb.b, identb.b], writes=[pb.b])
                            S.op("act", lambda e, mc=mc, pb=pb: e.copy(out=mkT[:, :, mc * 128:(mc + 1) * 128],
                                                                       in_=pb[:, :].rearrange("p (c t) -> p c t", c=8)), reads=[pb.b], writes=[mkT.b])
                        S.op("pool", lambda e, vsrc=vsrc: e.dma_start(out=mvb[:], in_=vsrc.rearrange("(m p) d -> p m d", p=128)),
                             reads=[S.db("omem")], writes=[mvb.b], chan="mvb")
                    x = xt[n % 2]
                    if split and t["prompt"]:
                        S.op("sp", lambda e, x=x, t=t: e.dma_start(out=x[:, :], in_=XOWN[t["slot"] * 128:(t["slot"] + 1) * 128, :]),
                             reads=[S.db("XOWN%d" % t["slot"])], writes=[x.b], chan=x.b.name)
                    else:
                        S.op("sp", lambda e, x=x, t=t, Tn=Tn: e.dma_start(out=x[:Tn], in_=XS[t["tok0"]:t["tok0"] + Tn, :]),
                             reads=[S.db("XS%d" % ti)], writes=[x.b], chan=x.b.name)
                    rms_T(x, Tn, gcx, hT)
                    for half in range(2):
                        ps = PSF()
                        mm_feat(ps, half * 512, 4, wq, hT, Tn)
                        S.op("act", lambda e, ps=ps, half=half, Tn=Tn: e.copy(out=qTx[:, half * 4:half * 4 + 4, :Tn],
                                                                             in_=ps[:, 0:4 * Tn].rearrange("p (c t) -> p c t", c=4)),
                             reads=[ps.b], writes=[qTx.b])
                    pss = [PSF(), PSF()]
                    for h in range(4):
                        for mc in range(2):
                            ps = pss[h // 2]
                            o0 = ((h % 2) * 2 + mc) * Tn
                            for dc in range(2):
                                S.op("pe", lambda e, h=h, mc=mc, dc=dc, ps=ps, o0=o0, Tn=Tn: e.matmul(
                                    ps[:, o0:o0 + Tn], lhsT=mkT[:, 2 * h + dc, mc * 128:(mc + 1) * 128], rhs=qTx[:, 2 * h + dc, :Tn],
                                    start=(dc == 0), stop=(dc == 1)), reads=[mkT.b, qTx.b], writes=[ps.b])
                    for hh in range(2):
                        S.op("act", lambda e, hh=hh, Tn=Tn, pss=pss: e.activation(out=PTx[:, hh * 4:hh * 4 + 4, :Tn],
                                                                         in_=pss[hh][:, 0:4 * Tn].rearrange("p (c t) -> p c t", c=4),
                                                                         func=AF.Exp, scale=1.0 / 16), reads=[pss[hh].b], writes=[PTx.b])
                    psr = PSF()
                    for h in range(4):
                        for mc in range(2):
                            S.op("pe", lambda e, h=h, mc=mc, Tn=Tn, psr=psr: e.matmul(psr[:, h * Tn:(h + 1) * Tn], lhsT=onesb[:, :], rhs=PTx[:, h * 2 + mc, :Tn],
                                                                            start=(mc == 0), stop=(mc == 1)), reads=[onesb.b, PTx.b], writes=[psr.b])
                    S.op("dve", lambda e, Tn=Tn, psr=psr: e.reciprocal(out=rinv[:, :, :Tn], in_=psr[:, 0:4 * Tn].rearrange("p (h t) -> p h t", h=4)),
                         reads=[psr.b], writes=[rinv.b])
                    for half in range(2):
                        ps = PSF()
                        for jj in range(4):
                            j = half * 4 + jj
                            h = j // 2
                            for mc in range(2):
                                S.op("pe", lambda e, j=j, jj=jj, h=h, mc=mc, ps=ps, Tn=Tn: e.matmul(
                                    ps[:, jj * Tn:(jj + 1) * Tn], lhsT=mvb[:, mc, j * 128:(j + 1) * 128], rhs=PTx[:, h * 2 + mc, :Tn],
                                    start=(mc == 0), stop=(mc == 1)), reads=[mvb.b, PTx.b], writes=[ps.b])
                        for hh in range(2):
                            h = half * 2 + hh
                            S.op("dve", lambda e, ps=ps, hh=hh, h=h, Tn=Tn: e.tensor_tensor(
                                out=oTx[:, 2 * h:2 * h + 2, :Tn], in0=ps[:, hh * 2 * Tn:(hh * 2 + 2) * Tn].rearrange("p (c t) -> p c t", c=2),
                                in1=rinv[:, h, :Tn].unsqueeze(1).to_broadcast([128, 2, Tn]), op=ALU.mult),
                                reads=[ps.b, rinv.b], writes=[oTx.b])
                    for n in range(2):
                        ps = PSF()
                        mm_tok(ps, n * 512, 512, wo_, oTx, Tn)
                        S.op("dve", lambda e, n=n, ps=ps, x=x, Tn=Tn: e.tensor_tensor(out=x[:Tn, n * 512:(n + 1) * 512], in0=ps[:Tn, :],
                                                                                     in1=x[:Tn, n * 512:(n + 1) * 512], op=ALU.add),
                             reads=[ps.b, x.b], writes=[x.b])
                    if not peer:
                        S.op("sp", lambda e, x=x, t=t, Tn=Tn: e.dma_start(out=XS[t["tok0"]:t["tok0"] + Tn, :], in_=x[:Tn]),
                             reads=[x.b], writes=[S.db("XS%d" % ti)], chan="st_" + x.b.name)
                        return
                    rms_T(x, Tn, gcf, hT, h32=h32, grep=grf)
                    for q4 in range(4):
                        ps = PSF()
                        mm_feat(ps, q4 * 512, 4, wp, hT, Tn)
                        S.op("act", lambda e, ps=ps, q4=q4, Tn=Tn: e.copy(out=pqT[:, q4 * 4:q4 * 4 + 4, :Tn],
                                                                         in_=ps[:, 0:4 * Tn].rearrange("p (c t) -> p c t", c=4)),
                             reads=[ps.b], writes=[pqT.b])
                    for q4 in range(4):
                        ps = PSF()
                        for jj in range(4):
                            j = q4 * 4 + jj
                            S.op("pe", lambda e, j=j, jj=jj, ps=ps, Tn=Tn: e.matmul(ps[:Tn, jj * 128:(jj + 1) * 128], lhsT=pqT[:, j, :Tn],
                                                                                   rhs=skT[:, j % 2, :], start=True, stop=True),
                                 reads=[pqT.b, skT.b], writes=[ps.b])
                        S.op("act", lambda e, ps=ps, q4=q4, Tn=Tn: e.copy(out=sc[:Tn, q4 * 4:q4 * 4 + 4, :],
                                                                         in_=ps[:Tn, :].rearrange("p (c k) -> p c k", c=4)),
                             reads=[ps.b], writes=[sc.b])
                    for j in range(16):
                        S.op("dve", lambda e, j=j, Tn=Tn: e.max(out=ts[:Tn, j, 0:8], in_=sc[:Tn, j, :]), reads=[sc.b], writes=[ts.b])
                    for j in range(16):
                        S.op("dve", lambda e, j=j, Tn=Tn: e.max_index(out=tiu[:Tn, j, 0:8], in_max=ts[:Tn, j, 0:8], in_values=sc[:Tn, j, :]),
                             reads=[sc.b, ts.b], writes=[tiu.b])
                    for j in range(16):
                        S.op("dve", lambda e, j=j, Tn=Tn: e.match_replace(out=sc2[:Tn, j, :], in_to_replace=ts[:Tn, j, 0:8], in_values=sc[:Tn, j, :],
                                                                         imm_value=NEG), reads=[sc.b, ts.b], writes=[sc2.b])
                    for j in range(16):
                        S.op("dve", lambda e, j=j, Tn=Tn: e.max(out=ts[:Tn, j, 8:16], in_=sc2[:Tn, j, :]), reads=[sc2.b], writes=[ts.b])
                    for j in range(16):
                        S.op("dve", lambda e, j=j, Tn=Tn: e.max_index(out=tiu[:Tn, j, 8:16], in_max=ts[:Tn, j, 8:16], in_values=sc2[:Tn, j, :]),
                             reads=[sc2.b, ts.b], writes=[tiu.b])
                    S.op("dve", lambda e, Tn=Tn: e.tensor_copy(out=tif[:Tn], in_=tiu[:Tn]), reads=[tiu.b], writes=[tif.b])
                    tsv = ts[:Tn].rearrange("p (h c) k -> p h c k", c=2)
                    tfv = tif[:Tn].rearrange("p (h c) k -> p h c k", c=2)
                    S.op("dve", lambda e, Tn=Tn, tsv=tsv: e.tensor_tensor(
                        out=cand[:Tn].rearrange("p h (a b) -> p h a b", a=16),
                        in0=tsv[:, :, 0, :].unsqueeze(3).to_broadcast([Tn, 8, 16, 16]),
                        in1=tsv[:, :, 1, :].unsqueeze(2).to_broadcast([Tn, 8, 16, 16]), op=ALU.add), reads=[ts.b], writes=[cand.b])
                    for h in range(8):
                        S.op("dve", lambda e, h=h, Tn=Tn: e.max(out=bs[:Tn, h, 0:8], in_=cand[:Tn, h, :]), reads=[cand.b], writes=[bs.b])
                    for h in range(8):
                        S.op("dve", lambda e, h=h, Tn=Tn: e.max_index(out=bpu[:Tn, h, 0:8], in_max=bs[:Tn, h, 0:8], in_values=cand[:Tn, h, :]),
                             reads=[cand.b, bs.b], writes=[bpu.b])
                    for h in range(8):
                        S.op("dve", lambda e, h=h, Tn=Tn: e.match_replace(out=cand2[:Tn, h, :], in_to_replace=bs[:Tn, h, 0:8], in_values=cand[:Tn, h, :],
                                                                         imm_value=NEG), reads=[cand.b, bs.b], writes=[cand2.b])
                    for h in range(8):
                        S.op("dve", lambda e, h=h, Tn=Tn: e.max(out=bs[:Tn, h, 8:16], in_=cand2[:Tn, h, :]), reads=[cand2.b], writes=[bs.b])
                    for h in range(8):
                        S.op("dve", lambda e, h=h, Tn=Tn: e.max_index(out=bpu[:Tn, h, 8:16], in_max=bs[:Tn, h, 8:16], in_values=cand2[:Tn, h, :]),
                             reads=[cand2.b, bs.b], writes=[bpu.b])
                    S.op("dve", lambda e, Tn=Tn: e.tensor_tensor(out=gate[:Tn], in0=bs[:Tn], in1=bs[:Tn, :, 0:1].to_broadcast([Tn, 8, 16]), op=ALU.subtract),
                         reads=[bs.b], writes=[gate.b])
                    S.op("act", lambda e, Tn=Tn: e.activation(out=gate[:Tn], in_=gate[:Tn], func=AF.Exp), reads=[gate.b], writes=[gate.b])
                    S.op("dve", lambda e, Tn=Tn: e.tensor_reduce(out=gsum[:Tn], in_=gate[:Tn], axis=AX.X, op=ALU.add), reads=[gate.b], writes=[gsum.b])
                    S.op("dve", lambda e, Tn=Tn: e.reciprocal(out=gsum[:Tn], in_=gsum[:Tn]), reads=[gsum.b], writes=[gsum.b])
                    S.op("dve", lambda e, Tn=Tn: e.tensor_tensor(out=gate[:Tn], in0=gate[:Tn], in1=gsum[:Tn].unsqueeze(2).to_broadcast([Tn, 8, 16]), op=ALU.mult),
                         reads=[gate.b, gsum.b], writes=[gate.b])
                    S.op("dve", lambda e, Tn=Tn: e.tensor_copy(out=bpf[:Tn], in_=bpu[:Tn]), reads=[bpu.b], writes=[bpf.b])
                    S.op("dve", lambda e, Tn=Tn: e.tensor_scalar(out=fb_[:Tn], in0=bpf[:Tn], scalar1=0.0625, scalar2=-0.46875, op0=ALU.mult, op1=ALU.add),
                         reads=[bpf.b], writes=[fb_.b])
                    S.op("dve", lambda e, Tn=Tn: e.tensor_copy(out=ia[:Tn], in_=fb_[:Tn]), reads=[fb_.b], writes=[ia.b])
                    S.op("dve", lambda e, Tn=Tn: e.tensor_copy(out=fa[:Tn], in_=ia[:Tn]), reads=[ia.b], writes=[fa.b])
                    S.op("dve", lambda e, Tn=Tn: e.scalar_tensor_tensor(out=fb_[:Tn], in0=fa[:Tn], scalar=-16.0, in1=bpf[:Tn], op0=ALU.mult, op1=ALU.add),
                         reads=[fa.b, bpf.b, fb_.b], writes=[fb_.b])
                    for (pos, cc, dst) in ((fa, 0, i0f), (fb_, 1, i1f)):
                        S.op("dve", lambda e, pos=pos, Tn=Tn: e.tensor_tensor(
                            out=oh[:Tn], in0=iot[:Tn, :].unsqueeze(1).unsqueeze(1).to_broadcast([Tn, 8, 16, 16]),
                            in1=pos[:Tn].unsqueeze(3).to_broadcast([Tn, 8, 16, 16]), op=ALU.is_equal), reads=[iot.b, pos.b], writes=[oh.b])
                        S.op("dve", lambda e, cc=cc, Tn=Tn, tfv=tfv: e.tensor_tensor(
                            out=oh[:Tn], in0=oh[:Tn], in1=tfv[:, :, cc, :].unsqueeze(2).to_broadcast([Tn, 8, 16, 16]), op=ALU.mult),
                            reads=[oh.b, tif.b], writes=[oh.b])
                        S.op("dve", lambda e, dst=dst, Tn=Tn: e.tensor_reduce(out=dst[:Tn], in_=oh[:Tn], axis=AX.X, op=ALU.add), reads=[oh.b], writes=[dst.b])
                    S.op("dve", lambda e, Tn=Tn: e.scalar_tensor_tensor(out=i0f[:Tn], in0=i0f[:Tn], scalar=128.0, in1=i1f[:Tn], op0=ALU.mult, op1=ALU.add),
                         reads=[i0f.b, i1f.b], writes=[i0f.b])
                    S.op("dve", lambda e, Tn=Tn: e.tensor_copy(out=idx[:Tn, :], in_=i0f[:Tn].rearrange("p h k -> p (h k)")), reads=[i0f.b], writes=[idx.b])
                def CDEF(n, cap):
                    ti, t = own_tiles[n]
                    h32, idx, gate = h32s[n % 2], idxs[n % 2], gates[n % 2]
                    Tn = t["T"]
                    x = xt[n % 2]
                    per = (len(cap) + 179) // 180
                    S.op("dve", lambda e: e.memset(act_[:], 0.0), writes=[act_.b])
                    for s_ in range(128):
                        g = gb[gcnt[0] % NG]
                        gcnt[0] += 1
                        S.op("pool", lambda e, g=g, s_=s_: e.indirect_dma_start(out=g[:, :], out_offset=None, in_=expert_u.rearrange("l e d -> (l e) d"),
                                                                                in_offset=bass.IndirectOffsetOnAxis(ap=idx[:, s_:s_ + 1], axis=0),
                                                                                element_offset=l * 16384 * D),
                             reads=[idx.b], writes=[g.b], chan=g.b.name)
                        S.op("dve", lambda e, g=g, s_=s_, Tn=Tn: e.scalar_tensor_tensor(out=pj[:Tn], in0=g[:Tn], scalar=1.0, in1=h32[:Tn],
                                                                                        op0=ALU.mult, op1=ALU.mult, accum_out=act_[:Tn, s_:s_ + 1]),
                             reads=[g.b, h32.b], writes=[pj.b, act_.b])
                        S.replay(cap, per)
                    S.op("dve", lambda e, Tn=Tn: e.tensor_tensor(out=g1[:Tn], in0=act_[:Tn], in1=act_[:Tn], op=ALU.mult), reads=[act_.b], writes=[g1.b])
                    S.op("dve", lambda e, Tn=Tn: e.tensor_scalar(out=g1[:Tn], in0=g1[:Tn], scalar1=0.044715, scalar2=1.0, op0=ALU.mult, op1=ALU.add),
                         reads=[g1.b], writes=[g1.b])
                    S.op("dve", lambda e, Tn=Tn: e.tensor_tensor(out=g1[:Tn], in0=g1[:Tn], in1=act_[:Tn], op=ALU.mult), reads=[g1.b, act_.b], writes=[g1.b])
                    S.op("act", lambda e, Tn=Tn: e.activation(out=g2[:Tn], in_=g1[:Tn], func=AF.Sigmoid, scale=1.5957691216057308),
                         reads=[g1.b], writes=[g2.b])
                    S.op("dve", lambda e, Tn=Tn: e.tensor_tensor(out=g2[:Tn], in0=g2[:Tn], in1=act_[:Tn], op=ALU.mult), reads=[g2.b, act_.b], writes=[g2.b])
                    S.op("dve", lambda e, Tn=Tn: e.tensor_tensor(out=wgt[:Tn], in0=g2[:Tn], in1=gate[:Tn].rearrange("p h k -> p (h k)"), op=ALU.mult),
                         reads=[g2.b, gate.b], writes=[wgt.b])
                    for s_ in range(128):
                        g = gb[gcnt[0] % NG]
                        gcnt[0] += 1
                        S.op("pool", lambda e, g=g, s_=s_: e.indirect_dma_start(out=g[:, :], out_offset=None, in_=expert_v.rearrange("l e d -> (l e) d"),
                                                                                in_offset=bass.IndirectOffsetOnAxis(ap=idx[:, s_:s_ + 1], axis=0),
                                                                                element_offset=l * 16384 * D),
                             reads=[idx.b], writes=[g.b], chan=g.b.name)
                        a_ = accs[s_ % 2]
                        if s_ < 2:
                            S.op("dve", lambda e, g=g, s_=s_, a_=a_, Tn=Tn: e.tensor_scalar(out=a_[:Tn], in0=g[:Tn], scalar1=wgt[:Tn, s_:s_ + 1], scalar2=None,
                                                                                          op0=ALU.mult), reads=[g.b, wgt.b], writes=[a_.b])
                        else:
                            S.op("dve", lambda e, g=g, s_=s_, a_=a_, Tn=Tn: e.scalar_tensor_tensor(out=a_[:Tn], in0=g[:Tn], scalar=wgt[:Tn, s_:s_ + 1], in1=a_[:Tn],
                                                                                                  op0=ALU.mult, op1=ALU.add),
                                 reads=[g.b, wgt.b, a_.b], writes=[a_.b])
                        S.replay(cap, per)
                    S.op("dve", lambda e, x=x, Tn=Tn: e.tensor_tensor(out=x[:Tn], in0=x[:Tn], in1=accs[0][:Tn], op=ALU.add), reads=[x.b, accs[0].b], writes=[x.b])
                    S.op("dve", lambda e, x=x, Tn=Tn: e.tensor_tensor(out=x[:Tn], in0=x[:Tn], in1=accs[1][:Tn], op=ALU.add), reads=[x.b, accs[1].b], writes=[x.b])
                    if final:
                        ss = fss
                        S.op("dve", lambda e, ss=ss, Tn=Tn: e.memset(ss[:Tn, 0:1], 0.0), writes=[ss.b])
                        S.op("act", lambda e, ss=ss, x=x, Tn=Tn: e.activation(out=pj[:Tn], in_=x[:Tn], func=AF.Square, accum_out=ss[:Tn, 0:1]),
                             reads=[x.b, ss.b], writes=[pj.b, ss.b])
                        S.op("act", lambda e, ss=ss, Tn=Tn: e.activation(out=ss[:Tn, 1:2], in_=ss[:Tn, 0:1], func=AF.Sqrt, scale=1.0 / D, bias=1e-6),
                             reads=[ss.b], writes=[ss.b])
                        S.op("dve", lambda e, ss=ss, Tn=Tn: e.reciprocal(out=ss[:Tn, 1:2], in_=ss[:Tn, 1:2]), reads=[ss.b], writes=[ss.b])
                        S.op("dve", lambda e, ss=ss, x=x, Tn=Tn: e.scalar_tensor_tensor(out=x[:Tn], in0=x[:Tn], scalar=ss[:Tn, 1:2], in1=grz[:Tn],
                                                                                       op0=ALU.mult, op1=ALU.mult), reads=[x.b, ss.b, grz.b], writes=[x.b])
                        dst = y_p[t["slot"] * 128:(t["slot"] + 1) * 128, :] if t["prompt"] else y_s[t["s"] * 64:(t["s"] + 1) * 64, :]
                        S.op("sp", lambda e, x=x, dst=dst, Tn=Tn: e.dma_start(out=dst, in_=x[:Tn]), reads=[x.b], writes=[S.db("yout")], chan="st_" + x.b.name)
                    elif split and t["prompt"]:
                        S.op("sp", lambda e, x=x, t=t: e.dma_start(out=SND[t["slot"] * 128:(t["slot"] + 1) * 128, :], in_=x[:, :]),
                             reads=[x.b], writes=[S.db("SND")], chan="st_" + x.b.name)
                    else:
                        S.op("sp", lambda e, x=x, t=t, Tn=Tn: e.dma_start(out=XS[t["tok0"]:t["tok0"] + Tn, :], in_=x[:Tn]),
                             reads=[x.b], writes=[S.db("XS%d" % ti)], chan="st_" + x.b.name)

                if split:
                    for k in range(NOWN):
                        g = gb[k % NG]
                        S.op("pool", lambda e, g=g, k=k: e.indirect_dma_start(out=g[:, :], out_offset=None, in_=XS,
                                                                              in_offset=bass.IndirectOffsetOnAxis(ap=ridx[:, k:k + 1], axis=0)),
                             reads=[ridx.b], writes=[g.b], chan=g.b.name)
                        S.op("sp", lambda e, g=g, k=k: e.dma_start(out=XOWN[k * 128:(k + 1) * 128, :], in_=g[:, :]),
                             reads=[g.b], writes=[S.db("XOWN%d" % k)], chan="st_" + g.b.name)
                cap = S.capture(lambda: AB(0))
                S.replay(cap, len(cap))
                for n in range(len(own_tiles)):
                    cap = S.capture(lambda: AB(n + 1)) if n + 1 < len(own_tiles) else []
                    if peer:
                        CDEF(n, cap)
                    S.replay(cap, len(cap))
                S.emit()
                if split and not final:
                    for j in range(NOWN // CHT):
                        S.op("pool", lambda e, j=j: e.collective_compute("AllGather", ALU.bypass, replica_groups=[[0, 4], [1, 5], [2, 6], [3, 7]],
                                                                         ins=[SND[j * CHT * 128:(j + 1) * CHT * 128, :]], outs=[RCV[j]]),
                             writes=[S.db("RCV%d" % j)], chan="cc", amt=1)
                    S.emit()

        def dump_xs():
            S.op("sp", lambda e: e.dma_start(out=y_p[:, :], in_=XS[0:NPT * 128, :]), writes=[S.db("yout")], chan="dbg")
            S.op("sp", lambda e: e.dma_start(out=y_s[:, :], in_=XS[NPT * 128:NTOK, :]), writes=[S.db("yout")], chan="dbg")
            S.emit()
        if stop_after == 2:
            dump_xs()
            return nc
        if stop_after == 25:
            cross_peer(0, False, peer=False)
            dump_xs()
            return nc
        cross_peer(0, False)
        if stop_after == 3:
            dump_xs()
            return nc
        P4(nc, S, locals())
        if stop_after == 4:
            dump_xs()
            return nc
        cross_peer(1, True)
    return nc


def P4(nc, S, G):
    tiles = G["tiles"]; PSF = G["PSF"]; PSB = G["PSB"]; phase_tiles = G["phase_tiles"]; load_w = G["load_w"]
    load_cols = G["load_cols"]; load_rep = G["load_rep"]; make_rms = G["make_rms"]; mm_feat = G["mm_feat"]; mm_tok = G["mm_tok"]
    identf = G["identf"]; identb = G["identb"]; onesf = G["onesf"]; XS = G["XS"]
    NPT = G["NPT"]; NSS = G["NSS"]
    with ExitStack() as es:
        sb = phase_tiles(es, "p4")
        w = load_w(sb, G["w_in_cd"], 2568, "win")
        wo = load_w(sb, G["w_out_cd"], D, "wo")
        gcol = load_cols(sb, G["norm_mix_g"][1:2, :], 8, "gcol")
        lcg = load_rep(sb, G["ln_c_g"], 512, "lcg")
        lcb = load_rep(sb, G["ln_c_b"], 512, "lcb")
        ndg = load_rep(sb, G["norm_d_g"], 512, "ndg")
        dtb = load_rep(sb, G["dt_bias"], 8, "dtb")
        alog = load_rep(sb, G["a_log"], 8, "alog")
        dsk = load_rep(sb, G["d_skip"], 8, "dsk")
        cdw = sb([128, 8, 4], F32, "cdw")
        for c in range(8):
            S.op("sp", lambda e, c=c: e.dma_start(out=cdw[:, c, :], in_=G["conv_d_w"][:, c * 128:(c + 1) * 128].rearrange("k p -> p k"),
                                                  allow_slow_non_contiguous=True), writes=[cdw.b], chan="cdw")
        cdb = load_cols(sb, G["conv_d_b"], 8, "cdb")
        ws32 = sb([128, 4, 128], F32, "ws32")
        wsT = sb([128, 4, 128], BF16, "wsT")
        bsc = sb([128, 4], F32, "bsc")
        S.op("sp", lambda e: e.dma_start(out=ws32[:], in_=G["gm_w_s"].rearrange("g i j -> i g j")), writes=[ws32.b], chan="ws32")
        S.op("sp", lambda e: e.dma_start(out=bsc[:], in_=G["gm_b_s"].rearrange("g i -> i g"), allow_slow_non_contiguous=True), writes=[bsc.b], chan="bsc")
        for g in range(4):
            S.op("pool", lambda e, g=g: e.affine_select(out=ws32[:, g, :], in_=ws32[:, g, :], pattern=[[-1, 128]], compare_op=ALU.is_ge, fill=0.0,
                                                        base=0, channel_multiplier=1), reads=[ws32.b], writes=[ws32.b])
        ps = PSF()
        for g in range(4):
            S.op("pe", lambda e, g=g, ps=ps: e.transpose(out=ps[:, g * 128:(g + 1) * 128], in_=ws32[:, g, :], identity=identf[:, :]),
                 reads=[ws32.b, identf.b], writes=[ps.b])
        S.op("act", lambda e, ps=ps: e.copy(out=wsT[:], in_=ps[:, :].rearrange("p (g i) -> p g i", g=4)), reads=[ps.b], writes=[wsT.b])
        tri = sb([64, 64], F32, "tri")
        negm = sb([64, 64], F32, "negm")
        ones64 = sb([64, 128], F32, "ones64")
        S.op("pool", lambda e: e.memset(tri[:], 1.0), writes=[tri.b])
        S.op("pool", lambda e: e.affine_select(out=tri[:], in_=tri[:], pattern=[[1, 64]], compare_op=ALU.is_ge, fill=0.0, base=0, channel_multiplier=-1),
             reads=[tri.b], writes=[tri.b])
        S.op("pool", lambda e: e.memset(negm[:], 0.0), writes=[negm.b])
        S.op("pool", lambda e: e.affine_select(out=negm[:], in_=negm[:], pattern=[[1, 64]], compare_op=ALU.is_ge, fill=NEG, base=0, channel_multiplier=-1),
             reads=[negm.b], writes=[negm.b])
        S.op("pool", lambda e: e.memset(ones64[:], 1.0), writes=[ones64.b])
        Arep = sb([128, 8], F32, "Arep")
        S.op("act", lambda e: e.activation(out=Arep[:], in_=alog[:], func=AF.Exp), reads=[alog.b], writes=[Arep.b])
        S.op("dve", lambda e: e.tensor_scalar(out=Arep[:], in0=Arep[:], scalar1=-1.0, scalar2=None, op0=ALU.mult), reads=[Arep.b], writes=[Arep.b])
        rms_T = make_rms(sb)
        xt = [sb([128, D], F32, "x") for _ in range(2)]
        hT = sb([128, 8, 128], BF16, "hT")
        cg = sb([128, 1024], F32, "cg")
        t1 = sb([128, 1024], F32, "t1")
        t2 = sb([128, 1024], F32, "t2")
        st_ = sb([128, 8], F32, "st")
        vvn = sb([128, 512], F32, "vvn")
        vvb = sb([128, 512], BF16, "vvb")
        cout = sb([128, 512], BF16, "cout")
        cyT = sb([128, 8, 128], BF16, "cyT")
        cbuf = sb([128, 8, 131], F32, "cbuf")
        xc = sb([128, 8, 128], F32, "xc")
        cacc = [sb([128, 128], F32, "cacc%d" % c) for c in range(8)]
        xstok = sb([128, 512], F32, "xstok")
        btok = sb([64, 2, 256], F32, "btok")
        gate = sb([128, 512], F32, "gate")
        dtt = sb([128, 8], F32, "dtt")
        dta = sb([128, 8], F32, "dta")
        ST = sb([128, 8, 64], F32, "ST")
        stin = sb([64, 8, 128], F32, "stin")
        c3 = sb([8, 1024], F32, "c3")
        yt = sb([128, 512], F32, "yt")
        acs = sb([64, 8], F32, "acs")
        nacs = sb([64, 8], F32, "nacs")
        eacs = sb([64, 8], F32, "eacs")
        dec = sb([64, 8], F32, "dec")
        cdec = sb([128, 8], F32, "cdec")
        LT = sb([64, 8, 64], F32, "LT")
        MT = sb([64, 8, 64], F32, "MT")
        xdt = sb([64, 8, 64], F32, "xdt")
        xdd = sb([64, 8, 64], F32, "xdd")
        ydg = sb([64, 512], F32, "ydg")
        dtab = sb([64, 8, 64], F32, "dtab")
        for ti, t in enumerate(tiles):
            Tn = t["T"]
            x = xt[ti % 2]
            if G["split"] and t["prompt"]:
                NOWN, CHT = G["NOWN"], G["CHT"]
                rk, kk = t["i"] // NOWN, t["i"] % NOWN
                src = G["RCV"][kk // CHT][rk * CHT * 128 + (kk % CHT) * 128: rk * CHT * 128 + (kk % CHT + 1) * 128, :]
                S.op("sp", lambda e, x=x, src=src: e.dma_start(out=x[:, :], in_=src), writes=[x.b], chan=x.b.name)
            else:
                S.op("sp", lambda e, x=x, t=t, Tn=Tn: e.dma_start(out=x[:Tn], in_=XS[t["tok0"]:t["tok0"] + Tn, :]),
                     reads=[S.db("XS%d" % ti)], writes=[x.b], chan=x.b.name)
            rms_T(x, Tn, gcol, hT)
            for n in range(2):
                ps = PSF()
                mm_tok(ps, n * 512, 512, w, hT, Tn)
                sl = slice(n * 512, (n + 1) * 512)
                S.op("act", lambda e, ps=ps, sl=sl, Tn=Tn: e.activation(out=t1[:Tn, sl], in_=ps[:Tn, :], func=AF.Square), reads=[ps.b], writes=[t1.b])
                S.op("dve", lambda e, sl=sl, Tn=Tn: e.tensor_scalar(out=t1[:Tn, sl], in0=t1[:Tn, sl], scalar1=0.044715, scalar2=1.0, op0=ALU.mult, op1=ALU.add),
                     reads=[t1.b], writes=[t1.b])
                S.op("dve", lambda e, ps=ps, sl=sl, Tn=Tn: e.tensor_tensor(out=t1[:Tn, sl], in0=ps[:Tn, :], in1=t1[:Tn, sl], op=ALU.mult),
                     reads=[ps.b, t1.b], writes=[t1.b])
                S.op("act", lambda e, sl=sl, Tn=Tn: e.activation(out=t2[:Tn, sl], in_=t1[:Tn, sl], func=AF.Sigmoid, scale=1.5957691216057308),
                     reads=[t1.b], writes=[t2.b])
                S.op("dve", lambda e, ps=ps, sl=sl, Tn=Tn: e.tensor_tensor(out=cg[:Tn, sl], in0=ps[:Tn, :], in1=t2[:Tn, sl], op=ALU.mult),
                     reads=[ps.b, t2.b], writes=[cg.b])
            S.op("dve", lambda e, Tn=Tn: e.memset(st_[:Tn, 0:2], 0.0), writes=[st_.b])
            S.op("act", lambda e, Tn=Tn: e.activation(out=t1[:Tn, 0:512], in_=cg[:Tn, 512:1024], func=AF.Copy, accum_out=st_[:Tn, 0:1]),
                 reads=[cg.b, st_.b], writes=[t1.b, st_.b])
            S.op("act", lambda e, Tn=Tn: e.activation(out=t1[:Tn, 512:1024], in_=cg[:Tn, 512:1024], func=AF.Square, accum_out=st_[:Tn, 1:2]),
                 reads=[cg.b, st_.b], writes=[t1.b, st_.b])
            S.op("dve", lambda e, Tn=Tn: e.tensor_scalar(out=st_[:Tn, 2:4], in0=st_[:Tn, 0:2], scalar1=1.0 / 512, scalar2=None, op0=ALU.mult),
                 reads=[st_.b], writes=[st_.b])
            S.op("dve", lambda e, Tn=Tn: e.tensor_tensor(out=st_[:Tn, 4:5], in0=st_[:Tn, 2:3], in1=st_[:Tn, 2:3], op=ALU.mult), reads=[st_.b], writes=[st_.b])
            S.op("dve", lambda e, Tn=Tn: e.tensor_tensor(out=st_[:Tn, 4:5], in0=st_[:Tn, 3:4], in1=st_[:Tn, 4:5], op=ALU.subtract), reads=[st_.b], writes=[st_.b])
            S.op("act", lambda e, Tn=Tn: e.activation(out=st_[:Tn, 5:6], in_=st_[:Tn, 4:5], func=AF.Sqrt, bias=1e-5, scale=1.0), reads=[st_.b], writes=[st_.b])
            S.op("dve", lambda e, Tn=Tn: e.reciprocal(out=st_[:Tn, 5:6], in_=st_[:Tn, 5:6]), reads=[st_.b], writes=[st_.b])
            S.op("dve", lambda e, Tn=Tn: e.tensor_scalar(out=vvn[:Tn], in0=cg[:Tn, 512:1024], scalar1=st_[:Tn, 2:3], scalar2=st_[:Tn, 5:6],
                                                         op0=ALU.subtract, op1=ALU.mult), reads=[cg.b, st_.b], writes=[vvn.b])
            S.op("dve", lambda e, Tn=Tn: e.tensor_tensor(out=vvn[:Tn], in0=vvn[:Tn], in1=lcg[:Tn], op=ALU.mult), reads=[vvn.b, lcg.b], writes=[vvn.b])
            S.op("dve", lambda e, Tn=Tn: e.tensor_tensor(out=vvn[:Tn], in0=vvn[:Tn], in1=lcb[:Tn], op=ALU.add), reads=[vvn.b, lcb.b], writes=[vvn.b])
            if not t["prompt"]:
                S.op("sp", lambda e, t=t: e.dma_start(out=G["o_gvs"][t["s"] * 64:(t["s"] + 1) * 64, :], in_=vvn[:64]), reads=[vvn.b],
                     writes=[S.db("ogvs")], chan="st_vvn")
            S.op("act", lambda e, Tn=Tn: e.copy(out=vvb[:Tn], in_=vvn[:Tn]), reads=[vvn.b], writes=[vvb.b])
            ps = PSF()
            for g in range(4):
                S.op("pe", lambda e, g=g, ps=ps, Tn=Tn: e.matmul(ps[:Tn, g * 128:(g + 1) * 128], lhsT=wsT[:Tn, g, :Tn], rhs=vvb[:Tn, g * 128:(g + 1) * 128],
                                                                start=True, stop=True), reads=[wsT.b, vvb.b], writes=[ps.b])
            S.op("dve", lambda e, ps=ps, Tn=Tn: e.tensor_tensor(out=t1[:Tn, 0:512].rearrange("p (g d) -> p g d", g=4),
                                                               in0=ps[:Tn, :].rearrange("p (g d) -> p g d", g=4),
                                                               in1=bsc[:Tn, :].unsqueeze(2).to_broadcast([Tn, 4, 128]), op=ALU.add),
                 reads=[ps.b, bsc.b], writes=[t1.b])
            S.op("dve", lambda e, Tn=Tn: e.tensor_tensor(out=cout[:Tn], in0=t1[:Tn, 0:512], in1=cg[:Tn, 0:512], op=ALU.mult), reads=[t1.b, cg.b], writes=[cout.b])
            pb = PSB()
            for c in range(4):
                S.op("pe", lambda e, c=c, pb=pb, Tn=Tn: e.transpose(out=pb[:, c * Tn:(c + 1) * Tn], in_=cout[:Tn, c * 128:(c + 1) * 128], identity=identb[:Tn, :Tn]),
                     reads=[cout.b, identb.b], writes=[pb.b])
            S.op("act", lambda e, pb=pb, Tn=Tn: e.copy(out=cyT[:, 0:4, :Tn], in_=pb[:, 0:4 * Tn].rearrange("p (c t) -> p c t", c=4)), reads=[pb.b], writes=[cyT.b])
            if t["first"]:
                if t["prompt"]:
                    S.op("pool", lambda e: e.memset(cbuf[:, :, 0:3], 0.0), writes=[cbuf.b])
                    S.op("pool", lambda e: e.memset(ST[:], 0.0), writes=[ST.b])
                else:
                    S.op("sp", lambda e, t=t: e.dma_start(out=c3[0:3, :], in_=G["scs"][t["s"]]), writes=[c3.b], chan="c3")
                    ps = PSF()
                    for c in range(8):
                        S.op("pe", lambda e, c=c, ps=ps: e.transpose(out=ps[:, c * 3:(c + 1) * 3], in_=c3[0:3, c * 128:(c + 1) * 128], identity=identf[0:3, 0:3]),
                             reads=[c3.b, identf.b], writes=[ps.b])
                    S.op("act", lambda e, ps=ps: e.copy(out=cbuf[:, :, 0:3], in_=ps[:, 0:24].rearrange("p (c t) -> p c t", c=8)), reads=[ps.b], writes=[cbuf.b])
                    S.op("sp", lambda e, t=t: e.dma_start(out=stin[:], in_=G["sssd"][t["s"]].rearrange("h p n -> p h n")), writes=[stin.b], chan="stin")
                    ps = PSF()
                    for h in range(8):
                        S.op("pe", lambda e, h=h, ps=ps: e.transpose(out=ps[:, h * 64:(h + 1) * 64], in_=stin[:, h, :], identity=identf[0:64, 0:64]),
                             reads=[stin.b, identf.b], writes=[ps.b])
                    S.op("act", lambda e, ps=ps: e.copy(out=ST[:], in_=ps[:, :].rearrange("p (h q) -> p h q", h=8)), reads=[ps.b], writes=[ST.b])
            else:
                S.op("pool", lambda e: e.tensor_copy(out=cbuf[:, :, 0:3], in_=cbuf[:, :, 128:131]), reads=[cbuf.b], writes=[cbuf.b])
            for half in range(2):
                ps = PSF()
                mm_feat(ps, 1536 + half * 512, 4, w, hT, Tn)
                S.op("act", lambda e, ps=ps, half=half, Tn=Tn: e.copy(out=cbuf[:, half * 4:half * 4 + 4, 3:3 + Tn],
                                                                     in_=ps[:, 0:4 * Tn].rearrange("p (c t) -> p c t", c=4)), reads=[ps.b], writes=[cbuf.b])
            if t["last"]:
                for half in range(2):
                    ps = PSF()
                    for cc in range(4):
                        c = half * 4 + cc
                        S.op("pe", lambda e, c=c, cc=cc, ps=ps, Tn=Tn: e.transpose(out=ps[0:3, cc * 128:(cc + 1) * 128], in_=cbuf[:, c, Tn:Tn + 3], identity=identf[:, :]),
                             reads=[cbuf.b, identf.b], writes=[ps.b])
                    S.op("act", lambda e, ps=ps, half=half: e.copy(out=c3[0:3, half * 512:(half + 1) * 512], in_=ps[0:3, :]), reads=[ps.b], writes=[c3.b])
                dst = G["o_csp"] if t["prompt"] else G["o_css"][t["s"]]
                S.op("sp", lambda e, dst=dst: e.dma_start(out=dst, in_=c3[0:3, :]), reads=[c3.b], writes=[S.db("ocs")], chan="st_c3")
            for k in range(4):
                for c in range(8):
                    eng = "dve"
                    if k == 0:
                        S.op(eng, lambda e, c=c, Tn=Tn: e.tensor_scalar(out=cacc[c][:, :Tn], in0=cbuf[:, c, 0:Tn], scalar1=cdw[:, c, 0:1], scalar2=None, op0=ALU.mult),
                             reads=[cbuf.b, cdw.b], writes=[cacc[c].b])
                    else:
                        S.op(eng, lambda e, c=c, k=k, Tn=Tn: e.scalar_tensor_tensor(out=cacc[c][:, :Tn], in0=cbuf[:, c, k:k + Tn], scalar=cdw[:, c, k:k + 1],
                                                                                    in1=cacc[c][:, :Tn], op0=ALU.mult, op1=ALU.add),
                             reads=[cbuf.b, cdw.b, cacc[c].b], writes=[cacc[c].b])
            for c in range(8):
                S.op("act", lambda e, c=c, Tn=Tn: e.activation(out=xc[:, c, :Tn], in_=cacc[c][:, :Tn], func=AF.Silu, bias=cdb[:, c:c + 1], scale=1.0),
                     reads=[cacc[c].b, cdb.b], writes=[xc.b])
            for ch in range(Tn // 64):
                c0 = ch * 64
                cs = slice(c0, c0 + 64)
                ps = PSF()
                for c in range(4):
                    S.op("pe", lambda e, c=c, ps=ps, cs=cs: e.transpose(out=ps[0:64, c * 128:(c + 1) * 128], in_=xc[:, c, cs], identity=identf[:, :]),
                         reads=[xc.b, identf.b], writes=[ps.b])
                S.op("act", lambda e, ps=ps: e.copy(out=xstok[0:64, :], in_=ps[0:64, :]), reads=[ps.b], writes=[xstok.b])
                ps = PSF()
                for g in range(2):
                    S.op("pe", lambda e, g=g, ps=ps, cs=cs: e.transpose(out=ps[0:64, g * 128:(g + 1) * 128], in_=xc[:, 4 + g, cs], identity=identf[:, :]),
                         reads=[xc.b, identf.b], writes=[ps.b])
                S.op("act", lambda e, ps=ps: e.copy(out=btok[:, 0, :], in_=ps[0:64, 0:256]), reads=[ps.b], writes=[btok.b])
                ps = PSF()
                for c in range(8):
                    S.op("pe", lambda e, c=c, ps=ps, cs=cs: e.matmul(ps[0:64, :], lhsT=hT[:, c, cs], rhs=w[:, c, 1024:1536], start=(c == 0), stop=(c == 7)),
                         reads=[hT.b, w.b], writes=[ps.b])
                S.op("act", lambda e, ps=ps: e.activation(out=gate[0:64, :], in_=ps[0:64, :], func=AF.Silu), reads=[ps.b], writes=[gate.b])
                ps = PSF()
                for c in range(8):
                    S.op("pe", lambda e, c=c, ps=ps, cs=cs: e.matmul(ps[0:64, 0:8], lhsT=hT[:, c, cs], rhs=w[:, c, 2560:2568], start=(c == 0), stop=(c == 7)),
                         reads=[hT.b, w.b], writes=[ps.b])
                S.op("dve", lambda e, ps=ps: e.tensor_tensor(out=dtt[0:64, :], in0=ps[0:64, 0:8], in1=dtb[0:64, :], op=ALU.add), reads=[ps.b, dtb.b], writes=[dtt.b])
                S.op("act", lambda e: e.activation(out=dtt[0:64, :], in_=dtt[0:64, :], func=AF.Exp), reads=[dtt.b], writes=[dtt.b])
                S.op("act", lambda e: e.activation(out=dtt[0:64, :], in_=dtt[0:64, :], func=AF.Ln, bias=1.0, scale=1.0), reads=[dtt.b], writes=[dtt.b])
                S.op("dve", lambda e: e.tensor_tensor(out=dta[0:64, :], in0=dtt[0:64, :], in1=Arep[0:64, :], op=ALU.mult), reads=[dtt.b, Arep.b], writes=[dta.b])
                S.op("dve", lambda e: e.tensor_copy(out=dtab[:], in_=dta[0:64, :].unsqueeze(2).to_broadcast([64, 8, 64])), reads=[dta.b], writes=[dtab.b])
                psa = PSF()
                S.op("pe", lambda e, psa=psa: e.matmul(psa[0:64, 0:8], lhsT=tri[:, :], rhs=dta[0:64, :], start=True, stop=True), reads=[tri.b, dta.b], writes=[psa.b])
                S.op("act", lambda e, psa=psa: e.copy(out=acs[:], in_=psa[0:64, 0:8]), reads=[psa.b], writes=[acs.b])
                S.op("dve", lambda e, psa=psa: e.tensor_scalar(out=nacs[:], in0=psa[0:64, 0:8], scalar1=-1.0, scalar2=None, op0=ALU.mult), reads=[psa.b], writes=[nacs.b])
                S.op("act", lambda e: e.activation(out=eacs[:], in_=acs[:], func=AF.Exp), reads=[acs.b], writes=[eacs.b])
                psl = PSF()
                S.op("pe", lambda e, psl=psl: e.matmul(psl[:, 0:8], lhsT=ones64[:, :], rhs=dta[0:64, :], start=True, stop=True), reads=[ones64.b, dta.b], writes=[psl.b])
                S.op("act", lambda e, psl=psl: e.activation(out=cdec[:], in_=psl[:, 0:8], func=AF.Exp), reads=[psl.b], writes=[cdec.b])
                S.op("dve", lambda e, psl=psl: e.tensor_tensor(out=dec[:], in0=psl[0:64, 0:8], in1=acs[:], op=ALU.subtract), reads=[psl.b, acs.b], writes=[dec.b])
                S.op("act", lambda e: e.activation(out=dec[:], in_=dec[:], func=AF.Exp), reads=[dec.b], writes=[dec.b])
                psb_ = PSF()
                for h in range(8):
                    S.op("pe", lambda e, h=h, psb_=psb_: e.matmul(psb_[0:64, h * 64:(h + 1) * 64], lhsT=dtab[:, h, :], rhs=tri[:, :], start=True, stop=True),
                         reads=[dtab.b, tri.b], writes=[psb_.b])
                S.op("dve", lambda e, psb_=psb_: e.tensor_tensor(out=LT[:], in0=psb_[0:64, :].rearrange("p (h i) -> p h i", h=8),
                                                                 in1=negm[:, :].unsqueeze(1).to_broadcast([64, 8, 64]), op=ALU.add),
                     reads=[psb_.b, negm.b], writes=[LT.b])
                for h in range(8):
                    S.op("act", lambda e, h=h: e.activation(out=LT[:, h, :], in_=LT[:, h, :], func=AF.Exp, bias=nacs[:, h:h + 1], scale=1.0),
                         reads=[LT.b, nacs.b], writes=[LT.b])
                psc = PSF()
                for g in range(2):
                    S.op("pe", lambda e, g=g, psc=psc, cs=cs: e.matmul(psc[0:64, g * 64:(g + 1) * 64], lhsT=xc[:, 4 + g, cs], rhs=xc[:, 6 + g, cs], start=True, stop=True),
                         reads=[xc.b], writes=[psc.b])
                S.op("dve", lambda e, psc=psc: e.tensor_tensor(out=MT[:].rearrange("p (g r) i -> p g r i", g=2), in0=LT[:].rearrange("p (g r) i -> p g r i", g=2),
                                                               in1=psc[0:64, 0:128].rearrange("p (g i) -> p g i", g=2).unsqueeze(2).to_broadcast([64, 2, 4, 64]),
                                                               op=ALU.mult), reads=[psc.b, LT.b], writes=[MT.b])
                S.op("dve", lambda e: e.tensor_tensor(out=xdt[:], in0=xstok[0:64, :].rearrange("p (h q) -> p h q", h=8),
                                                      in1=dtt[0:64, :].unsqueeze(2).to_broadcast([64, 8, 64]), op=ALU.mult), reads=[xstok.b, dtt.b], writes=[xdt.b])
                S.op("dve", lambda e: e.tensor_tensor(out=xdd[:], in0=xdt[:], in1=dec[:, :].unsqueeze(2).to_broadcast([64, 8, 64]), op=ALU.mult),
                     reads=[xdt.b, dec.b], writes=[xdd.b])
                psy, pso, psn = PSF(), PSF(), PSF()
                for h in range(8):
                    S.op("pe", lambda e, h=h, psy=psy: e.matmul(psy[0:64, h * 64:(h + 1) * 64], lhsT=MT[:, h, :], rhs=xdt[:, h, :], start=True, stop=True),
                         reads=[MT.b, xdt.b], writes=[psy.b])
                for h in range(8):
                    S.op("pe", lambda e, h=h, pso=pso, cs=cs: e.matmul(pso[0:64, h * 64:(h + 1) * 64], lhsT=xc[:, 6 + h // 4, cs], rhs=ST[:, h, :], start=True, stop=True),
                         reads=[xc.b, ST.b], writes=[pso.b])
                for h in range(8):
                    S.op("pe", lambda e, h=h, psn=psn: e.matmul(psn[:, h * 64:(h + 1) * 64], lhsT=btok[:, 0, (h // 4) * 128:(h // 4 + 1) * 128], rhs=xdd[:, h, :],
                                                                start=True, stop=True), reads=[btok.b, xdd.b], writes=[psn.b])
                S.op("act", lambda e, psy=psy: e.copy(out=ydg[:], in_=psy[0:64, :]), reads=[psy.b], writes=[ydg.b])
                S.op("dve", lambda e, pso=pso: e.tensor_tensor(out=yt[0:64, :].rearrange("p (h q) -> p h q", h=8), in0=pso[0:64, :].rearrange("p (h q) -> p h q", h=8),
                                                               in1=eacs[:, :].unsqueeze(2).to_broadcast([64, 8, 64]), op=ALU.mult), reads=[pso.b, eacs.b], writes=[yt.b])
                S.op("dve", lambda e: e.tensor_tensor(out=yt[0:64, :], in0=yt[0:64, :], in1=ydg[:], op=ALU.add), reads=[yt.b, ydg.b], writes=[yt.b])
                S.op("pool", lambda e: e.tensor_tensor(out=ST[:], in0=ST[:], in1=cdec[:, :].unsqueeze(2).to_broadcast([128, 8, 64]), op=ALU.mult),
                     reads=[ST.b, cdec.b], writes=[ST.b])
                S.op("dve", lambda e, psn=psn: e.tensor_tensor(out=ST[:], in0=psn[:, :].rearrange("p (h q) -> p h q", h=8), in1=ST[:], op=ALU.add),
                     reads=[psn.b, ST.b], writes=[ST.b])
                S.op("dve", lambda e: e.tensor_tensor(out=xdt[:], in0=xstok[0:64, :].rearrange("p (h q) -> p h q", h=8),
                                                      in1=dsk[0:64, :].unsqueeze(2).to_broadcast([64, 8, 64]), op=ALU.mult), reads=[xstok.b, dsk.b, xdt.b], writes=[xdt.b])
                S.op("dve", lambda e: e.tensor_tensor(out=yt[0:64, :], in0=yt[0:64, :], in1=xdt[:].rearrange("p h q -> p (h q)"), op=ALU.add),
                     reads=[yt.b, xdt.b], writes=[yt.b])
                S.op("dve", lambda e: e.tensor_tensor(out=yt[0:64, :], in0=yt[0:64, :], in1=gate[0:64, :], op=ALU.mult), reads=[yt.b, gate.b], writes=[yt.b])
                S.op("dve", lambda e: e.memset(st_[0:64, 0:2], 0.0), writes=[st_.b])
                for g in range(2):
                    S.op("act", lambda e, g=g: e.activation(out=ydg[:, g * 256:(g + 1) * 256], in_=yt[0:64, g * 256:(g + 1) * 256], func=AF.Square,
                                                            accum_out=st_[0:64, g:g + 1]), reads=[yt.b, st_.b], writes=[ydg.b, st_.b])
                S.op("act", lambda e: e.activation(out=st_[0:64, 2:4], in_=st_[0:64, 0:2], func=AF.Sqrt, scale=1.0 / 256, bias=1e-6), reads=[st_.b], writes=[st_.b])
                S.op("dve", lambda e: e.reciprocal(out=st_[0:64, 2:4], in_=st_[0:64, 2:4]), reads=[st_.b], writes=[st_.b])
                S.op("dve", lambda e: e.tensor_tensor(out=yt[0:64, :].rearrange("p (g q) -> p g q", g=2), in0=yt[0:64, :].rearrange("p (g q) -> p g q", g=2),
                                                      in1=st_[0:64, 2:4].unsqueeze(2).to_broadcast([64, 2, 256]), op=ALU.mult), reads=[yt.b, st_.b], writes=[yt.b])
                S.op("dve", lambda e: e.tensor_tensor(out=cout[0:64, :], in0=yt[0:64, :], in1=ndg[0:64, :], op=ALU.mult), reads=[yt.b, ndg.b], writes=[cout.b])
                pb = PSB()
                for c in range(4):
                    S.op("pe", lambda e, c=c, pb=pb: e.transpose(out=pb[:, c * 64:(c + 1) * 64], in_=cout[0:64, c * 128:(c + 1) * 128], identity=identb[0:64, 0:64]),
                         reads=[cout.b, identb.b], writes=[pb.b])
                S.op("act", lambda e, pb=pb, cs=cs: e.copy(out=cyT[:, 4:8, cs], in_=pb[:, 0:256].rearrange("p (c t) -> p c t", c=4)), reads=[pb.b], writes=[cyT.b])
            if t["last"]:
                for half in range(2):
                    ps = PSF()
                    for hh in range(4):
                        h = half * 4 + hh
                        S.op("pe", lambda e, h=h, hh=hh, ps=ps: e.transpose(out=ps[0:64, hh * 128:(hh + 1) * 128], in_=ST[:, h, :], identity=identf[:, :]),
                             reads=[ST.b, identf.b], writes=[ps.b])
                    S.op("act", lambda e, ps=ps, half=half: e.copy(out=stin[:, half * 4:half * 4 + 4, :], in_=ps[0:64, :].rearrange("p (h n) -> p h n", h=4)),
                         reads=[ps.b], writes=[stin.b])
                dst = G["o_ssdp"] if t["prompt"] else G["o_ssds"][t["s"]]
                S.op("sp", lambda e, dst=dst: e.dma_start(out=dst.rearrange("h p n -> p h n"), in_=stin[:]), reads=[stin.b], writes=[S.db("ossd")], chan="st_stin")
            for n in range(2):
                ps = PSF()
                mm_tok(ps, n * 512, 512, wo, cyT, Tn)
                S.op("dve", lambda e, n=n, ps=ps, x=x, Tn=Tn: e.tensor_tensor(out=x[:Tn, n * 512:(n + 1) * 512], in0=ps[:Tn, :], in1=x[:Tn, n * 512:(n + 1) * 512], op=ALU.add),
                     reads=[ps.b, x.b], writes=[x.b])
            S.op("sp", lambda e, x=x, t=t, Tn=Tn: e.dma_start(out=XS[t["tok0"]:t["tok0"] + Tn, :], in_=x[:Tn]),
                 reads=[x.b], writes=[S.db("XS%d" % ti)], chan="st_" + x.b.name)
        S.emit()


_W_NAMES = ["norm_mix_g", "norm_cross_g", "norm_ffn_g", "norm_final_g", "w_in_ab", "conv_a_w", "conv_a_b", "ln_a_g", "ln_a_b",
            "lam_q1", "lam_k1", "lam_q2", "lam_k2", "subln_g", "w_out_ab", "w_in_cd", "ln_c_g", "ln_c_b", "gm_w_s", "gm_b_s",
            "conv_d_w", "conv_d_b", "dt_bias", "a_log", "d_skip", "norm_d_g", "w_out_cd", "w_xq", "w_xk", "w_xv", "w_xo", "w_pq",
            "sub_keys", "expert_u", "expert_v"]


def _layout_weights(inp):
    f = lambda a: np.ascontiguousarray(np.asarray(a, dtype=np.float32))
    W = {}
    W["norm_mix_g"] = f(inp["norm_mix_g"]); W["norm_cross_g"] = f(inp["norm_cross_g"]); W["norm_ffn_g"] = f(inp["norm_ffn_g"])
    W["norm_final_g"] = f(inp["norm_final_g"]).reshape(1, D)
    W["w_in_ab"] = f(inp["w_in_ab"][0]); W["conv_a_w"] = f(inp["conv_a_w"][0])
    for k in ("conv_a_b", "ln_a_g", "ln_a_b", "lam_q1", "lam_k1", "lam_q2", "lam_k2", "subln_g", "ln_c_g", "ln_c_b", "conv_d_b",
              "dt_bias", "a_log", "d_skip", "norm_d_g"):
        W[k] = f(inp[k][0]).reshape(1, -1)
    W["w_out_ab"] = f(inp["w_out_ab"][0]); W["w_in_cd"] = f(inp["w_in_cd"][0]); W["gm_w_s"] = f(inp["gm_w_s"][0]); W["gm_b_s"] = f(inp["gm_b_s"][0])
    W["conv_d_w"] = f(inp["conv_d_w"][0]); W["w_out_cd"] = f(inp["w_out_cd"][0])
    for k in ("w_xq", "w_xk", "w_xv", "w_xo", "w_pq", "sub_keys", "expert_u", "expert_v"):
        W[k] = f(inp[k])
    return W


def run_cores(inp, n_cores, NPT, NSS, PL, prompt_of_core, trace=False, stop_after=None, split=False):
    f = lambda a: np.ascontiguousarray(np.asarray(a, dtype=np.float32))
    W = _layout_weights(inp)
    in_maps = []
    for c in range(n_cores):
        b = prompt_of_core(c)
        sl = slice(c * NSS, (c + 1) * NSS)
        m = dict(W)
        m["x_prompt"] = f(inp["x_prompt"][b])
        m["x_sample"] = f(inp["x_sample"][sl]).reshape(NSS * 64, D)
        m["cache_attn_k"] = f(inp["cache_attn_k"][0, sl]).reshape(NSS, PL, 512)
        m["cache_attn_v"] = f(inp["cache_attn_v"][0, sl]).reshape(NSS, PL, 512)
        m["state_conv_a"] = f(inp["state_conv_a"][0, sl])
        m["state_ssd"] = f(inp["state_ssd"][0, sl])
        m["state_conv_ssm"] = f(inp["state_conv_ssm"][0, sl])
        m["cache_mem_k"] = f(inp["cache_mem_k"][:, sl]).reshape(2, NSS, 256, D)
        m["cache_mem_v"] = f(inp["cache_mem_v"][:, sl]).reshape(2, NSS, 256, D)
        m["mem_prompt"] = f(inp["mem_prompt"][b])
        if split:
            rank = c // 4
            nown = NPT // 2
            m["rowidx"] = np.ascontiguousarray(((rank * nown + np.arange(nown))[None, :] * 128 + np.arange(128)[:, None]).astype(np.int32))
        in_maps.append(m)
    nc = build(NPT, NSS, PL, stop_after=stop_after, split=split)
    res = run_bass_kernel_spmd(nc, in_maps, core_ids=list(range(n_cores)), **({"trace": True} if trace else {}))
    return res


def assemble(R, nb, n_cores, NPT, NSS, split=False):
    L = NPT * 128
    pc = list(range(nb))
    cat = lambda key, cores: np.stack([np.asarray(R[c][key]) for c in cores])
    if split:
        y_prompt = np.stack([np.concatenate([np.asarray(R[b]["y_p"]), np.asarray(R[b + 4]["y_p"])], axis=0) for b in pc]).reshape(nb, L, D)
    else:
        y_prompt = cat("y_p", pc).reshape(nb, L, D)
    y_sample = cat("y_s", range(n_cores)).reshape(n_cores * NSS, 64, D)
    kp = cat("o_kp", pc).reshape(1, nb, L, 4, 128)
    vp = cat("o_vp", pc).reshape(1, nb, L, 4, 128)
    cap = cat("o_cap", pc).reshape(1, nb, 30, 512)
    ssdp = cat("o_ssdp", pc).reshape(1, nb, 8, 64, 128)
    csp = cat("o_csp", pc).reshape(1, nb, 3, 1024)
    mk = np.stack([np.asarray(R[c]["o_mk"]) for c in pc], axis=1).reshape(2, nb, 256, 4, 256)
    mv = np.stack([np.asarray(R[c]["o_mv"]) for c in pc], axis=1).reshape(2, nb, 256, 4, 256)
    ns = n_cores * NSS
    ks = cat("o_ks", range(n_cores)).reshape(1, ns, 64, 4, 128)
    vs = cat("o_vs", range(n_cores)).reshape(1, ns, 64, 4, 128)
    cas = cat("o_cas", range(n_cores)).reshape(1, ns, 30, 512)
    gvs = cat("o_gvs", range(n_cores)).reshape(1, ns, 64, 4, 128)
    ssds = cat("o_ssds", range(n_cores)).reshape(1, ns, 8, 64, 128)
    css = cat("o_css", range(n_cores)).reshape(1, ns, 3, 1024)
    outs = (y_prompt, y_sample, kp, vp, cap, ssdp, csp, mk, mv, ks, vs, cas, gvs, ssds, css)
    return tuple(np.ascontiguousarray(o, dtype=np.float32) for o in outs)


def kernel(x_prompt, x_sample, cache_attn_k, cache_attn_v, state_conv_a, state_ssd, state_conv_ssm,
           cache_mem_k, cache_mem_v, mem_prompt,
           norm_mix_g, norm_cross_g, norm_ffn_g, norm_final_g,
           w_in_ab, conv_a_w, conv_a_b, ln_a_g, ln_a_b, lam_q1, lam_k1, lam_q2, lam_k2, subln_g, w_out_ab,
           w_in_cd, ln_c_g, ln_c_b, gm_w_s, gm_b_s, conv_d_w, conv_d_b, dt_bias, a_log, d_skip, norm_d_g, w_out_cd,
           w_xq, w_xk, w_xv, w_xo, w_pq, sub_keys, expert_u, expert_v):
    inp = dict(x_prompt=x_prompt, x_sample=x_sample, cache_attn_k=cache_attn_k, cache_attn_v=cache_attn_v, state_conv_a=state_conv_a,
               state_ssd=state_ssd, state_conv_ssm=state_conv_ssm, cache_mem_k=cache_mem_k, cache_mem_v=cache_mem_v, mem_prompt=mem_prompt,
               norm_mix_g=norm_mix_g, norm_cross_g=norm_cross_g, norm_ffn_g=norm_ffn_g, norm_final_g=norm_final_g,
               w_in_ab=w_in_ab, conv_a_w=conv_a_w, conv_a_b=conv_a_b, ln_a_g=ln_a_g, ln_a_b=ln_a_b, lam_q1=lam_q1, lam_k1=lam_k1,
               lam_q2=lam_q2, lam_k2=lam_k2, subln_g=subln_g, w_out_ab=w_out_ab, w_in_cd=w_in_cd, ln_c_g=ln_c_g, ln_c_b=ln_c_b,
               gm_w_s=gm_w_s, gm_b_s=gm_b_s, conv_d_w=conv_d_w, conv_d_b=conv_d_b, dt_bias=dt_bias, a_log=a_log, d_skip=d_skip,
               norm_d_g=norm_d_g, w_out_cd=w_out_cd, w_xq=w_xq, w_xk=w_xk, w_xv=w_xv, w_xo=w_xo, w_pq=w_pq, sub_keys=sub_keys,
               expert_u=expert_u, expert_v=expert_v)
    nb = x_prompt.shape[0]
    L = x_prompt.shape[1]
    n_cores = 8
    NSS = x_sample.shape[0] // n_cores
    PL = cache_attn_k.shape[2]
    split = (nb == 4)
    res = run_cores(inp, n_cores, L // 128, NSS, PL, lambda c: c % nb, split=split)
    return assemble(res.results, nb, n_cores, L // 128, NSS, split=split)
```

```python
import numpy as np
from contextlib import ExitStack
import concourse.bass as bass
import concourse.mybir as mybir
from concourse.bass_utils import run_bass_kernel_spmd

F32 = mybir.dt.float32
BF16 = mybir.dt.bfloat16
I32 = mybir.dt.int32
U32 = mybir.dt.uint32
AF = mybir.ActivationFunctionType
ALU = mybir.AluOpType
AX = mybir.AxisListType
D = 1024
NEG = -1.0e30


class Buf:
    __slots__ = ("name", "w", "r")

    def __init__(self, name):
        self.name = name
        self.w = None
        self.r = {}


class Sched:
    ENGS = ("pe", "act", "dve", "pool", "sp")

    def __init__(self, nc, es):
        self.nc = nc
        self.sems = {}
        self.esem = {}
        for e in ("pe", "act", "dve", "pool"):
            s = es.enter_context(nc.semaphore("es_" + e))
            self.sems["e_" + e] = s
            self.esem[e] = "e_" + e
        self.ecnt = {e: 0 for e in self.esem}
        self.nchan = 28
        self.ccnt = {}
        for pre in ("c", "g"):
            for i in range(self.nchan):
                self.sems["%s%d" % (pre, i)] = es.enter_context(nc.semaphore("%ss%d" % (pre, i)))
                self.ccnt["%s%d" % (pre, i)] = 0
        self.known = {e: {} for e in self.ENGS}
        self.reset_phase()

    def reset_phase(self):
        self.ops = {e: [] for e in self.ENGS}
        self.chanmap = {}
        self.dbufs = {}

    def chan(self, eng, name):
        pre = "g" if eng == "pool" else "c"
        key = (pre, name)
        if key not in self.chanmap:
            n = sum(1 for (p, _) in self.chanmap if p == pre)
            assert n < self.nchan, "too many dma channels"
            self.chanmap[key] = "%s%d" % (pre, n)
        return self.chanmap[key]

    def db(self, name):
        if name not in self.dbufs:
            self.dbufs[name] = Buf(name)
        return self.dbufs[name]

    def capture(self, f):
        self.cap = []
        f()
        c, self.cap = self.cap, None
        return c

    def replay(self, cap, n):
        for _ in range(min(n, len(cap))):
            a = cap.pop(0)
            self.op(*a[0], **a[1])

    def op(self, eng, fn, reads=(), writes=(), chan=None, amt=16):
        if getattr(self, "cap", None) is not None:
            self.cap.append(((eng, fn), dict(reads=list(reads), writes=list(writes), chan=chan, amt=amt)))
            return
        need = {}
        own = self.esem.get(eng) if chan is None else None

        def add(sig, raw):
            if sig is None:
                return
            k, v = sig
            if k == own and not raw:
                return
            if need.get(k, 0) < v:
                need[k] = v

        for b in reads:
            add(b.w, True)
        for b in writes:
            add(b.w, False)
            for k, v in b.r.items():
                add((k, v), False)
        kn = self.known[eng]
        waits = []
        for k, v in need.items():
            if kn.get(k, 0) < v:
                waits.append((k, v))
                kn[k] = v
        if chan is not None:
            k = self.chan(eng, chan)
            self.ccnt[k] += amt
            sig = (k, self.ccnt[k])
        else:
            k = self.esem[eng]
            self.ecnt[eng] += 1
            sig = (k, self.ecnt[eng])
            amt = 1
        self.ops[eng].append((waits, fn, k, amt))
        for b in reads:
            if b.r.get(sig[0], 0) < sig[1]:
                b.r[sig[0]] = sig[1]
        for b in writes:
            b.w = sig
            b.r = {}

    def drain(self):
        waits = []
        kn = self.known["sp"]
        for k, v in list(self.ccnt.items()) + [(self.esem[e], self.ecnt[e]) for e in self.esem]:
            if v > 0 and kn.get(k, 0) < v:
                waits.append((k, v))
                kn[k] = v
        self.ops["sp"].append((waits, None, None, 0))

    def emit(self):
        self.drain()
        nc = self.nc
        with nc.Block() as blk:
            def run(name):
                def f(e):
                    for waits, fn, k, amt in self.ops[name]:
                        for wk, wv in waits:
                            e.wait_ge(self.sems[wk], wv)
                        if fn is not None:
                            fn(e).then_inc(self.sems[k], amt)
                return f
            blk.tensor(run("pe"))
            blk.scalar(run("act"))
            blk.vector(run("dve"))
            blk.gpsimd(run("pool"))
            blk.sync(run("sp"))
        for e in self.ENGS:
            for k, v in list(self.ccnt.items()) + [(self.esem[x], self.ecnt[x]) for x in self.esem]:
                self.known[e][k] = v
        self.reset_phase()


class T:
    def __init__(self, ap, name, nsub=1):
        self.ap = ap
        self.b = Buf(name)
        self.sub = [Buf(name + str(i)) for i in range(nsub)] if nsub > 1 else None

    def __getitem__(self, k):
        return self.ap[k]


def build(NPT, NSS, PL, lam_inits=(0.8 - 0.6, 0.0), stop_after=None, split=False):
    nc = bass.Bass("TRN2", target_bir_lowering=False)
    NTILE = NPT + NSS
    NTOK = NPT * 128 + NSS * 64
    NPK = PL // 128
    LAM0 = lam_inits[0]

    def din(name, shape, dt=F32):
        return nc.dram_tensor(name, list(shape), dt, kind="ExternalInput").ap()

    def dout(name, shape, dt=F32):
        return nc.dram_tensor(name, list(shape), dt, kind="ExternalOutput").ap()

    def dscr(name, shape, dt=F32):
        return nc.dram_tensor(name, list(shape), dt, kind="Internal").ap()

    xp = din("x_prompt", [NPT * 128, D])
    xs = din("x_sample", [NSS * 64, D])
    ck = din("cache_attn_k", [NSS, PL, 512])
    cv = din("cache_attn_v", [NSS, PL, 512])
    sca = din("state_conv_a", [NSS, 30, 512])
    sssd = din("state_ssd", [NSS, 8, 64, 128])
    scs = din("state_conv_ssm", [NSS, 3, 1024])
    cmk = din("cache_mem_k", [2, NSS, 256, D])
    cmv = din("cache_mem_v", [2, NSS, 256, D])
    memp = din("mem_prompt", [256, D])
    norm_mix_g = din("norm_mix_g", [2, D])
    norm_cross_g = din("norm_cross_g", [2, D])
    norm_ffn_g = din("norm_ffn_g", [2, D])
    norm_final_g = din("norm_final_g", [1, D])
    w_in_ab = din("w_in_ab", [D, 2560])
    conv_a_w = din("conv_a_w", [31, 512])
    conv_a_b = din("conv_a_b", [1, 512])
    ln_a_g = din("ln_a_g", [1, 512])
    ln_a_b = din("ln_a_b", [1, 512])
    lam_q1 = din("lam_q1", [1, 64])
    lam_k1 = din("lam_k1", [1, 64])
    lam_q2 = din("lam_q2", [1, 64])
    lam_k2 = din("lam_k2", [1, 64])
    subln_g = din("subln_g", [1, 128])
    w_out_ab = din("w_out_ab", [D, D])
    w_in_cd = din("w_in_cd", [D, 2568])
    ln_c_g = din("ln_c_g", [1, 512])
    ln_c_b = din("ln_c_b", [1, 512])
    gm_w_s = din("gm_w_s", [4, 128, 128])
    gm_b_s = din("gm_b_s", [4, 128])
    conv_d_w = din("conv_d_w", [4, 1024])
    conv_d_b = din("conv_d_b", [1, 1024])
    dt_bias = din("dt_bias", [1, 8])
    a_log = din("a_log", [1, 8])
    d_skip = din("d_skip", [1, 8])
    norm_d_g = din("norm_d_g", [1, 512])
    w_out_cd = din("w_out_cd", [D, D])
    w_xq = din("w_xq", [2, D, D])
    w_xk = din("w_xk", [2, D, D])
    w_xv = din("w_xv", [2, D, D])
    w_xo = din("w_xo", [2, D, D])
    w_pq = din("w_pq", [2, D, 2048])
    sub_keys = din("sub_keys", [2, 2, 128, 128])
    expert_u = din("expert_u", [2, 16384, D])
    expert_v = din("expert_v", [2, 16384, D])

    NOWN = NPT // 2 if split else NPT
    CHT = 4
    y_p = dout("y_p", [NOWN * 128, D])
    if split:
        rowidx = din("rowidx", [128, NOWN], I32)
        SND = dscr("SND", [NOWN * 128, D])
        XOWN = dscr("XOWN", [NOWN * 128, D])
        RCV = [dscr("RCV%d" % j, [2 * CHT * 128, D]) for j in range(NOWN // CHT)]
    y_s = dout("y_s", [NSS * 64, D])
    o_kp = dout("o_kp", [NPT * 128, 512])
    o_vp = dout("o_vp", [NPT * 128, 512])
    o_cap = dout("o_cap", [30, 512])
    o_ssdp = dout("o_ssdp", [8, 64, 128])
    o_csp = dout("o_csp", [3, 1024])
    o_mk = dout("o_mk", [2, 256, D])
    o_mv = dout("o_mv", [2, 256, D])
    o_ks = dout("o_ks", [NSS * 64, 512])
    o_vs = dout("o_vs", [NSS * 64, 512])
    o_cas = dout("o_cas", [NSS, 30, 512])
    o_gvs = dout("o_gvs", [NSS * 64, 512])
    o_ssds = dout("o_ssds", [NSS, 8, 64, 128])
    o_css = dout("o_css", [NSS, 3, 1024])

    XS = dscr("XS", [NTOK, D])
    QTs = dscr("QTs", [NTILE, 128, 512], BF16)
    CATs = dscr("CATs", [NTILE, 128, 512], BF16)
    KTs = dscr("KTs", [128, 4, NTOK], BF16)
    VXs = dscr("VXs", [NTOK, 516], BF16)

    tiles = []
    for i in range(NPT):
        tiles.append(dict(seq=0, prompt=True, T=128, tok0=i * 128, i=i, first=(i == 0), last=(i == NPT - 1), s=-1))
    for s in range(NSS):
        tiles.append(dict(seq=1 + s, prompt=False, T=64, tok0=NPT * 128 + s * 64, i=0, first=True, last=True, s=s))

    def xin_rows(t):
        if t["prompt"]:
            return xp[t["tok0"]:t["tok0"] + 128, :]
        return xs[t["s"] * 64:(t["s"] + 1) * 64, :]

    with ExitStack() as ges:
        S = Sched(nc, ges)
        psf = [T(ges.enter_context(nc.psum_tensor("psf%d" % i, [128, 512], F32)), "psf%d" % i) for i in range(6)]
        psb = [T(ges.enter_context(nc.psum_tensor("psb%d" % i, [128, 1024], BF16)), "psb%d" % i) for i in range(2)]
        identf = T(ges.enter_context(nc.sbuf_tensor("identf", [128, 128], F32)), "identf")
        identb = T(ges.enter_context(nc.sbuf_tensor("identb", [128, 128], BF16)), "identb")
        onesb = T(ges.enter_context(nc.sbuf_tensor("onesb", [128, 128], BF16)), "onesb")
        onesf = T(ges.enter_context(nc.sbuf_tensor("onesf", [128, 128], F32)), "onesf")

        S.op("pool", lambda e: e.memset(identf[:], 1.0), writes=[identf.b])
        S.op("pool", lambda e: e.affine_select(out=identf[:], in_=identf[:], pattern=[[-1, 128]], compare_op=ALU.is_equal,
                                               fill=0.0, base=0, channel_multiplier=1), reads=[identf.b], writes=[identf.b])
        S.op("dve", lambda e: e.tensor_copy(out=identb[:], in_=identf[:]), reads=[identf.b], writes=[identb.b])
        S.op("dve", lambda e: e.memset(onesb[:], 1.0), writes=[onesb.b])
        S.op("dve", lambda e: e.memset(onesf[:], 1.0), writes=[onesf.b])

        rr = {"psf": 0, "psb": 0, "npsf": 6}

        def PSF():
            rr["psf"] = (rr["psf"] + 1) % rr["npsf"]
            return psf[rr["psf"]]

        def PSB():
            rr["psb"] = (rr["psb"] + 1) % 2
            return psb[rr["psb"]]

        def phase_tiles(es, prefix):
            cnt = [0]

            def sb(shape, dt=F32, name=None, nsub=1):
                cnt[0] += 1
                nm = "%s_%s%d" % (prefix, name or "t", cnt[0])
                return T(es.enter_context(nc.sbuf_tensor(nm, list(shape), dt)), nm, nsub)
            return sb

        def load_w(sb, w_ap, ncols, name):
            t = sb([128, 8, ncols], BF16, name)
            src = w_ap.rearrange("(c p) n -> p c n", p=128)
            for c in range(8):
                for n0 in range(0, ncols, 1024):
                    n1 = min(ncols, n0 + 1024)
                    S.op("pool", lambda e, c=c, n0=n0, n1=n1: e.dma_start(out=t[:, c, n0:n1], in_=src[:, c, n0:n1]),
                         writes=[t.b], chan=t.b.name)
            return t

        def load_cols(sb, v_ap, nchunk, name):
            t = sb([128, nchunk], F32, name)
            with nc.allow_non_contiguous_dma("tiny param column load"):
                pass
            S.op("sp", lambda e: e.dma_start(out=t[:], in_=v_ap.rearrange("o (c p) -> p (o c)", p=128),
                                             allow_slow_non_contiguous=True), writes=[t.b], chan=t.b.name)
            return t

        def load_rep(sb, v_ap, n, name):
            t = sb([128, n], F32, name)
            S.op("sp", lambda e: e.dma_start(out=t[:], in_=v_ap.partition_broadcast(128)), writes=[t.b], chan=t.b.name)
            return t

        def make_rms(sb):
            st = dict(junk=sb([128, D], F32, "rjunk"), ss=sb([128, 2], F32, "rss"), xh=sb([128, D], BF16, "rxh"))

            def rms_T(x, Tn, gcol, hT, h32=None, grep=None):
                junk, ss, xh = st["junk"], st["ss"], st["xh"]
                S.op("dve", lambda e: e.memset(ss[:Tn, 0:1], 0.0), writes=[ss.b])
                S.op("act", lambda e: e.activation(out=junk[:Tn], in_=x[:Tn], func=AF.Square, accum_out=ss[:Tn, 0:1]),
                     reads=[x.b, ss.b], writes=[junk.b, ss.b])
                S.op("act", lambda e: e.activation(out=ss[:Tn, 1:2], in_=ss[:Tn, 0:1], func=AF.Sqrt, scale=1.0 / D, bias=1e-6),
                     reads=[ss.b], writes=[ss.b])
                S.op("dve", lambda e: e.reciprocal(out=ss[:Tn, 1:2], in_=ss[:Tn, 1:2]), reads=[ss.b], writes=[ss.b])
                S.op("dve", lambda e: e.tensor_scalar(out=xh[:Tn], in0=x[:Tn], scalar1=ss[:Tn, 1:2], scalar2=None, op0=ALU.mult),
                     reads=[x.b, ss.b], writes=[xh.b])
                if h32 is not None:
                    S.op("dve", lambda e: e.scalar_tensor_tensor(out=h32[:Tn], in0=x[:Tn], scalar=ss[:Tn, 1:2], in1=grep[:Tn],
                                                                  op0=ALU.mult, op1=ALU.mult),
                         reads=[x.b, ss.b, grep.b], writes=[h32.b])
                pb = PSB()
                for c in range(8):
                    S.op("pe", lambda e, c=c: e.transpose(out=pb[:, c * Tn:(c + 1) * Tn], in_=xh[:Tn, c * 128:(c + 1) * 128],
                                                          identity=identb[:Tn, :Tn]),
                         reads=[xh.b, identb.b], writes=[pb.b])
                pv = pb[:, 0:8 * Tn].rearrange("p (c t) -> p c t", c=8)
                S.op("dve", lambda e: e.tensor_tensor(out=hT[:, :, :Tn], in0=pv, in1=gcol[:, :].unsqueeze(2).to_broadcast([128, 8, Tn]),
                                                      op=ALU.mult),
                     reads=[pb.b, gcol.b], writes=[hT.b])
            rms_T.junk = st["junk"]
            return rms_T

        def mm_feat(ps, col0, nchunk, w, hT, Tn, wb=None):
            for j in range(nchunk):
                for c in range(8):
                    S.op("pe", lambda e, j=j, c=c: e.matmul(ps[:, j * Tn:(j + 1) * Tn], lhsT=w[:, c, col0 + j * 128:col0 + (j + 1) * 128],
                                                            rhs=hT[:, c, :Tn], start=(c == 0), stop=(c == 7)),
                         reads=[w.b, hT.b], writes=[ps.b])

        def mm_tok(ps, col0, ncol, w, hT, Tn, nk=8):
            for c in range(nk):
                S.op("pe", lambda e, c=c: e.matmul(ps[:Tn, 0:ncol], lhsT=hT[:, c, :Tn], rhs=w[:, c, col0:col0 + ncol],
                                                   start=(c == 0), stop=(c == nk - 1)),
                     reads=[w.b, hT.b], writes=[ps.b])

        def gelu_tanh(e_act, e_dve, out, x, shape_ap, tmp, rb, wb_, Tn=None):
            pass

        def QTs_view(scr, ti, Tn):
            return scr[ti].rearrange("p (c t) -> p c t", c=4)[:, :, 0:Tn]

        with ExitStack() as es:
            sb = phase_tiles(es, "p0")
            mem32 = sb([128, 2, D], F32, "mem32")
            memb = sb([128, 2, D], BF16, "memb")
            memT = sb([128, 8, 256], BF16, "memT")
            S.op("sp", lambda e: e.dma_start(out=mem32[:], in_=memp.rearrange("(m p) d -> p m d", p=128)), writes=[mem32.b], chan="mem32")
            S.op("dve", lambda e: e.tensor_copy(out=memb[:], in_=mem32[:]), reads=[mem32.b], writes=[memb.b])
            for mc in range(2):
                pb = PSB()
                for c in range(8):
                    S.op("pe", lambda e, c=c, mc=mc, pb=pb: e.transpose(out=pb[:, c * 128:(c + 1) * 128], in_=memb[:, mc, c * 128:(c + 1) * 128],
                                                                        identity=identb[:]),
                         reads=[memb.b, identb.b], writes=[pb.b])
                S.op("act", lambda e, mc=mc, pb=pb: e.copy(out=memT[:, :, mc * 128:(mc + 1) * 128],
                                                           in_=pb[:, :].rearrange("p (c t) -> p c t", c=8)),
                     reads=[pb.b], writes=[memT.b])
            stg = [sb([128, D], F32, "stg") for _ in range(2)]
            k = 0
            for l in range(2):
                for (wsrc, odst) in ((w_xk, o_mk), (w_xv, o_mv)):
                    w = load_w(sb, wsrc[l], D, "wkv")
                    for mc in range(2):
                        st = stg[k % 2]
                        k += 1
                        for n in range(2):
                            ps = PSF()
                            for c in range(8):
                                S.op("pe", lambda e, c=c, n=n, mc=mc, ps=ps, w=w: e.matmul(
                                    ps[:, :], lhsT=memT[:, c, mc * 128:(mc + 1) * 128], rhs=w[:, c, n * 512:(n + 1) * 512],
                                    start=(c == 0), stop=(c == 7)), reads=[memT.b, w.b], writes=[ps.b])
                            S.op("act", lambda e, n=n, ps=ps, st=st: e.copy(out=st[:, n * 512:(n + 1) * 512], in_=ps[:, :]),
                                 reads=[ps.b], writes=[st.b])
                        S.op("sp", lambda e, l=l, mc=mc, st=st, odst=odst: e.dma_start(out=odst[l, mc * 128:(mc + 1) * 128, :], in_=st[:]),
                             reads=[st.b], writes=[S.db("omem")], chan="st_" + st.b.name)
            S.emit()

        with ExitStack() as es:
            sb = phase_tiles(es, "p1")
            w = load_w(sb, w_in_ab, 2560, "win")
            gcol = load_cols(sb, norm_mix_g[0:1, :], 8, "gcol")
            cw = sb([128, 4, 31], F32, "cw")
            for c in range(4):
                S.op("sp", lambda e, c=c: e.dma_start(out=cw[:, c, :], in_=conv_a_w[:, c * 128:(c + 1) * 128].rearrange("k p -> p k"),
                                                      allow_slow_non_contiguous=True), writes=[cw.b], chan="cw")
            cb = load_cols(sb, conv_a_b, 4, "cb")
            lg = load_cols(sb, ln_a_g, 4, "lg")
            lb = load_cols(sb, ln_a_b, 4, "lb")
            rms_T = make_rms(sb)
            xt = [sb([128, D], F32, "x") for _ in range(2)]
            hT = sb([128, 8, 128], BF16, "hT")
            sg = sb([128, 4, 128], F32, "sg")
            cbuf = sb([128, 4, 158], F32, "cbuf")
            acc = [sb([128, 128], F32, "acc%d" % c) for c in range(4)]
            sq = sb([128, 4, 128], F32, "sq")
            mv_ = sb([128, 3, 128], F32, "mv")
            xn = sb([128, 4, 128], F32, "xn")
            caT = [sb([128, 4, 128], BF16, "caT") for _ in range(2)]
            qT = [sb([128, 4, 128], BF16, "qT") for _ in range(2)]
            kT = [sb([128, 4, 128], BF16, "kT") for _ in range(2)]
            kvt = [sb([128, 2, 512], F32, "kvt") for _ in range(2)]
            vx = [sb([128, 4, 129], BF16, "vx") for _ in range(2)]
            st30 = sb([32, 512], F32, "st30")
            cao = sb([32, 512], F32, "cao")
            onesm = sb([128, 128], F32, "onesm")
            S.op("dve", lambda e: e.memset(onesm[:], 1.0 / 512), writes=[onesm.b])
            for v in vx:
                S.op("dve", lambda e, v=v: e.memset(v[:], 1.0), writes=[v.b])
            for ti, t in enumerate(tiles):
                Tn = t["T"]
                x = xt[ti % 2]
                S.op("sp", lambda e, x=x, t=t, Tn=Tn: e.dma_start(out=x[:Tn], in_=xin_rows(t)), writes=[x.b], chan=x.b.name)
                rms_T(x, Tn, gcol, hT)
                if t["first"]:
                    if t["prompt"]:
                        S.op("pool", lambda e: e.memset(cbuf[:, :, 0:30], 0.0), writes=[cbuf.b])
                    else:
                        S.op("sp", lambda e, t=t: e.dma_start(out=st30[0:30, :], in_=sca[t["s"]]), writes=[st30.b], chan="st30")
                        ps = PSF()
                        for c in range(4):
                            S.op("pe", lambda e, c=c, ps=ps: e.transpose(out=ps[:, c * 30:(c + 1) * 30], in_=st30[0:30, c * 128:(c + 1) * 128],
                                                                         identity=identf[0:30, 0:30]),
                                 reads=[st30.b, identf.b], writes=[ps.b])
                        S.op("act", lambda e, ps=ps: e.copy(out=cbuf[:, :, 0:30], in_=ps[:, 0:120].rearrange("p (c t) -> p c t", c=4)),
                             reads=[ps.b], writes=[cbuf.b])
                else:
                    S.op("pool", lambda e: e.tensor_copy(out=cbuf[:, :, 0:30], in_=cbuf[:, :, 128:158]), reads=[cbuf.b], writes=[cbuf.b])
                pa, pg = PSF(), PSF()
                mm_feat(pa, 0, 4, w, hT, Tn)
                mm_feat(pg, 512, 4, w, hT, Tn)
                S.op("act", lambda e, pg=pg, Tn=Tn: e.activation(out=sg[:, :, :Tn], in_=pg[:, 0:4 * Tn].rearrange("p (c t) -> p c t", c=4),
                                                                 func=AF.Sigmoid), reads=[pg.b], writes=[sg.b])
                S.op("dve", lambda e, pa=pa, Tn=Tn: e.tensor_tensor(out=cbuf[:, :, 30:30 + Tn], in0=pa[:, 0:4 * Tn].rearrange("p (c t) -> p c t", c=4),
                                                                    in1=sg[:, :, :Tn], op=ALU.mult), reads=[pa.b, sg.b], writes=[cbuf.b])
                if t["last"]:
                    ps = PSF()
                    for c in range(4):
                        S.op("pe", lambda e, c=c, ps=ps, Tn=Tn: e.transpose(out=ps[0:30, c * 128:(c + 1) * 128], in_=cbuf[:, c, Tn:Tn + 30],
                                                                            identity=identf[:, :]),
                             reads=[cbuf.b, identf.b], writes=[ps.b])
                    S.op("act", lambda e, ps=ps: e.copy(out=cao[0:30, :], in_=ps[0:30, :]), reads=[ps.b], writes=[cao.b])
                    dst = o_cap if t["prompt"] else o_cas[t["s"]]
                    S.op("sp", lambda e, dst=dst: e.dma_start(out=dst, in_=cao[0:30, :]), reads=[cao.b], writes=[S.db("ocap")], chan="st_cao")
                for k in range(31):
                    for c in range(4):
                        eng = "dve"
                        if k == 0:
                            S.op(eng, lambda e, c=c, Tn=Tn: e.tensor_scalar(out=acc[c][:, :Tn], in0=cbuf[:, c, 0:Tn], scalar1=cw[:, c, 0:1],
                                                                            scalar2=cb[:, c:c + 1], op0=ALU.mult, op1=ALU.add),
                                 reads=[cbuf.b, cw.b, cb.b], writes=[acc[c].b])
                        else:
                            S.op(eng, lambda e, c=c, k=k, Tn=Tn: e.scalar_tensor_tensor(out=acc[c][:, :Tn], in0=cbuf[:, c, k:k + Tn],
                                                                                        scalar=cw[:, c, k:k + 1], in1=acc[c][:, :Tn],
                                                                                        op0=ALU.mult, op1=ALU.add),
                                 reads=[cbuf.b, cw.b, acc[c].b], writes=[acc[c].b])
                pm, pq = PSF(), PSF()
                for c in range(4):
                    S.op("act", lambda e, c=c, Tn=Tn: e.activation(out=sq[:, c, :Tn], in_=acc[c][:, :Tn], func=AF.Square),
                         reads=[acc[c].b], writes=[sq.b])
                for c in range(4):
                    S.op("pe", lambda e, c=c, pm=pm, Tn=Tn: e.matmul(pm[:, :Tn], lhsT=onesm[:, :], rhs=acc[c][:, :Tn], start=(c == 0), stop=(c == 3)),
                         reads=[onesm.b, acc[c].b], writes=[pm.b])
                for c in range(4):
                    S.op("pe", lambda e, c=c, pq=pq, Tn=Tn: e.matmul(pq[:, :Tn], lhsT=onesm[:, :], rhs=sq[:, c, :Tn], start=(c == 0), stop=(c == 3)),
                         reads=[onesm.b, sq.b], writes=[pq.b])
                S.op("act", lambda e, pm=pm, Tn=Tn: e.copy(out=mv_[:, 0, :Tn], in_=pm[:, :Tn]), reads=[pm.b], writes=[mv_.b])
                S.op("dve", lambda e, Tn=Tn: e.tensor_tensor(out=mv_[:, 1, :Tn], in0=mv_[:, 0, :Tn], in1=mv_[:, 0, :Tn], op=ALU.mult),
                     reads=[mv_.b], writes=[mv_.b])
                S.op("dve", lambda e, pq=pq, Tn=Tn: e.tensor_tensor(out=mv_[:, 1, :Tn], in0=pq[:, :Tn], in1=mv_[:, 1, :Tn], op=ALU.subtract),
                     reads=[pq.b, mv_.b], writes=[mv_.b])
                S.op("act", lambda e, Tn=Tn: e.activation(out=mv_[:, 2, :Tn], in_=mv_[:, 1, :Tn], func=AF.Sqrt, bias=1e-5, scale=1.0),
                     reads=[mv_.b], writes=[mv_.b])
                S.op("dve", lambda e, Tn=Tn: e.reciprocal(out=mv_[:, 2, :Tn], in_=mv_[:, 2, :Tn]), reads=[mv_.b], writes=[mv_.b])
                for c in range(4):
                    S.op("dve", lambda e, c=c, Tn=Tn: e.tensor_tensor(out=xn[:, c, :Tn], in0=acc[c][:, :Tn], in1=mv_[:, 0, :Tn], op=ALU.subtract),
                         reads=[acc[c].b, mv_.b], writes=[xn.b])
                S.op("dve", lambda e, Tn=Tn: e.tensor_tensor(out=xn[:, :, :Tn], in0=xn[:, :, :Tn],
                                                             in1=mv_[:, 2, :Tn].unsqueeze(1).to_broadcast([128, 4, Tn]), op=ALU.mult),
                     reads=[xn.b, mv_.b], writes=[xn.b])
                ca = caT[ti % 2]
                for c in range(4):
                    S.op("act", lambda e, c=c, ca=ca, Tn=Tn: e.activation(out=ca[:, c, :Tn], in_=xn[:, c, :Tn], func=AF.Silu,
                                                                          scale=lg[:, c:c + 1], bias=lb[:, c:c + 1]),
                         reads=[xn.b, lg.b, lb.b], writes=[ca.b])
                S.op("sp", lambda e, ca=ca, ti=ti, Tn=Tn: e.dma_start(out=QTs_view(CATs, ti, Tn), in_=ca[:, :, :Tn]),
                     reads=[ca.b], writes=[S.db("CATs%d" % ti)], chan="st_" + ca.b.name)
                pq2, pk2 = PSF(), PSF()
                mm_feat(pq2, 1024, 4, w, hT, Tn)
                mm_feat(pk2, 1536, 4, w, hT, Tn)
                q_, k_ = qT[ti % 2], kT[ti % 2]
                S.op("act", lambda e, pq2=pq2, q_=q_, Tn=Tn: e.copy(out=q_[:, :, :Tn], in_=pq2[:, 0:4 * Tn].rearrange("p (c t) -> p c t", c=4)),
                     reads=[pq2.b], writes=[q_.b])
                S.op("dve", lambda e, pk2=pk2, k_=k_, Tn=Tn: e.tensor_copy(out=k_[:, :, :Tn], in_=pk2[:, 0:4 * Tn].rearrange("p (c t) -> p c t", c=4)),
                     reads=[pk2.b], writes=[k_.b])
                S.op("sp", lambda e, q_=q_, ti=ti, Tn=Tn: e.dma_start(out=QTs_view(QTs, ti, Tn), in_=q_[:, :, :Tn]),
                     reads=[q_.b], writes=[S.db("QTs%d" % ti)], chan="st_" + q_.b.name)
                S.op("sp", lambda e, k_=k_, t=t, Tn=Tn: e.dma_start(out=KTs[:, :, t["tok0"]:t["tok0"] + Tn], in_=k_[:, :, :Tn]),
                     reads=[k_.b], writes=[S.db("KTs%d" % ti)], chan="st_" + k_.b.name)
                kv = kvt[ti % 2]
                for n in range(2):
                    ps = PSF()
                    mm_tok(ps, 1536 + n * 512, 512, w, hT, Tn)
                    S.op("act" if n == 0 else "dve",
                         (lambda e, ps=ps, kv=kv, n=n, Tn=Tn: e.copy(out=kv[:Tn, n, :], in_=ps[:Tn, :])) if n == 0 else
                         (lambda e, ps=ps, kv=kv, n=n, Tn=Tn: e.tensor_copy(out=kv[:Tn, n, :], in_=ps[:Tn, :])),
                         reads=[ps.b], writes=[kv.b])
                okd, ovd = (o_kp, o_vp) if t["prompt"] else (o_ks, o_vs)
                r0 = t["tok0"] if t["prompt"] else t["s"] * 64
                S.op("sp", lambda e, kv=kv, okd=okd, r0=r0, Tn=Tn: e.dma_start(out=okd[r0:r0 + Tn, :], in_=kv[:Tn, 0, :]),
                     reads=[kv.b], writes=[S.db("okv")], chan="st_" + kv.b.name)
                S.op("sp", lambda e, kv=kv, ovd=ovd, r0=r0, Tn=Tn: e.dma_start(out=ovd[r0:r0 + Tn, :], in_=kv[:Tn, 1, :]),
                     reads=[kv.b], writes=[S.db("okv")], chan="st_" + kv.b.name)
                v_ = vx[ti % 2]
                S.op("pool", lambda e, kv=kv, v_=v_, Tn=Tn: e.tensor_copy(out=v_[:Tn, :, 0:128], in_=kv[:Tn, 1, :].rearrange("p (h e) -> p h e", h=4)),
                     reads=[kv.b], writes=[v_.b])
                S.op("sp", lambda e, v_=v_, t=t, Tn=Tn: e.dma_start(out=VXs[t["tok0"]:t["tok0"] + Tn, :], in_=v_[:Tn].rearrange("p h e -> p (h e)")),
                     reads=[v_.b], writes=[S.db("VXs%d" % ti)], chan="st_" + v_.b.name)
            S.emit()

        with ExitStack() as es:
            sb = phase_tiles(es, "p2")
            NKT = max(NPT, NPK + 1)
            wo = load_w(sb, w_out_ab, D, "wo")
            KT = sb([128, 4, NKT * 128], BF16, "KT")
            VS = sb([128, NKT, 516], BF16, "VS")
            kcs = [sb([128, 4, 512], F32, "kcs") for _ in range(2)]
            xt = [sb([128, D], F32, "x") for _ in range(2)]
            qT = [sb([128, 4, 128], BF16, "qT") for _ in range(2)]
            caT = [sb([128, 4, 128], BF16, "caT") for _ in range(2)]
            PT = [sb([128, 4, 128], BF16, "PT") for _ in range(3)]
            ah = sb([128, 128], F32, "ah")
            rr_ = sb([128, 8], F32, "rr")
            junk = sb([128, 128], F32, "junk")
            otok = sb([128, 512], BF16, "otok")
            oT = sb([128, 4, 128], BF16, "oT")
            lam = sb([128, 8], F32, "lam")
            lv = [load_rep(sb, a, 64, "lv") for a in (lam_q1, lam_k1, lam_q2, lam_k2)]
            gs = load_rep(sb, subln_g, 128, "gs")
            lj = sb([128, 64], F32, "lj")
            S.op("dve", lambda e: e.memset(lam[:], 0.0), writes=[lam.b])
            S.op("dve", lambda e: e.scalar_tensor_tensor(out=lj[:], in0=lv[0][:], scalar=1.0, in1=lv[1][:], op0=ALU.mult, op1=ALU.mult,
                                                         accum_out=lam[:, 0:1]), reads=[lv[0].b, lv[1].b, lam.b], writes=[lj.b, lam.b])
            S.op("dve", lambda e: e.scalar_tensor_tensor(out=lj[:], in0=lv[2][:], scalar=1.0, in1=lv[3][:], op0=ALU.mult, op1=ALU.mult,
                                                         accum_out=lam[:, 1:2]), reads=[lv[2].b, lv[3].b, lam.b, lj.b], writes=[lj.b, lam.b])
            S.op("act", lambda e: e.activation(out=lam[:, 2:4], in_=lam[:, 0:2], func=AF.Exp), reads=[lam.b], writes=[lam.b])
            S.op("dve", lambda e: e.tensor_tensor(out=lam[:, 4:5], in0=lam[:, 3:4], in1=lam[:, 2:3], op=ALU.subtract), reads=[lam.b], writes=[lam.b])
            S.op("dve", lambda e: e.tensor_scalar(out=lam[:, 5:6], in0=lam[:, 4:5], scalar1=-LAM0, scalar2=None, op0=ALU.add), reads=[lam.b], writes=[lam.b])
            S.op("dve", lambda e: e.tensor_scalar(out=gs[:], in0=gs[:], scalar1=(1.0 - LAM0), scalar2=None, op0=ALU.mult), reads=[gs.b], writes=[gs.b])
            psO = [psf[0], psf[1]]
            psS = [psf[2], psf[3]]
            psX = [psf[4], psf[5]]
            cnt = {"s": 0, "p": 0}
            seqs = [(0, True)] + [(1 + s, False) for s in range(NSS)]
            for (sq_, isp) in seqs:
                stiles = [(ti, t) for ti, t in enumerate(tiles) if t["seq"] == sq_]
                if isp:
                    nkt_total = NPT
                    for h in range(4):
                        S.op("sp", lambda e, h=h: e.dma_start(out=KT[:, h, 0:NPT * 128], in_=KTs[:, h, 0:NPT * 128]),
                             reads=[S.db("KTs%d" % i) for i in range(NPT)], writes=[KT.b], chan="KT")
                    for j0 in range(0, NPT, 8):
                        j1 = min(NPT, j0 + 8)
                        S.op("sp", lambda e, j0=j0, j1=j1: e.dma_start(out=VS[:, j0:j1, :], in_=VXs[j0 * 128:j1 * 128, :].rearrange("(j p) f -> p j f", p=128)),
                             reads=[S.db("VXs%d" % i) for i in range(j0, j1)], writes=[VS.b], chan="VS")
                    klens = [128] * NPT
                else:
                    s = sq_ - 1
                    ti0 = NPT + s
                    tok0 = NPT * 128 + s * 64
                    S.op("dve", lambda e: e.memset(VS[:, 0:NPK, :], 1.0), writes=[VS.b])
                    for j in range(NPK):
                        S.op("pool", lambda e, s=s, j=j: e.dma_start(out=VS[:, j, :].rearrange("p (h e) -> p h e", h=4)[:, :, 0:128],
                                                                     in_=cv[s, j * 128:(j + 1) * 128, :].rearrange("p (h e) -> p h e", h=4)),
                             writes=[VS.b], chan="VS")
                    S.op("sp", lambda e, tok0=tok0: e.dma_start(out=VS[0:64, NPK, :], in_=VXs[tok0:tok0 + 64, :]),
                         reads=[S.db("VXs%d" % ti0)], writes=[VS.b], chan="VS")
                    S.op("sp", lambda e, tok0=tok0: e.dma_start(out=KT[:, :, PL:PL + 64], in_=KTs[:, :, tok0:tok0 + 64]),
                         reads=[S.db("KTs%d" % ti0)], writes=[KT.b], chan="KT")
                    for j in range(NPK):
                        kc = kcs[j % 2]
                        S.op("sp", lambda e, kc=kc, j=j, s=s: e.dma_start(out=kc[:, 0, :], in_=ck[s, j * 128:(j + 1) * 128, :]),
                             writes=[kc.b], chan=kc.b.name)
                        ps = psX[j % 2]
                        for h in range(4):
                            S.op("pe", lambda e, h=h, kc=kc, ps=ps: e.transpose(out=ps[:, h * 128:(h + 1) * 128], in_=kc[:, 0, h * 128:(h + 1) * 128],
                                                                                identity=identf[:, :]),
                                 reads=[kc.b, identf.b], writes=[ps.b])
                        S.op("act", lambda e, ps=ps, j=j: e.copy(out=KT[:, :, j * 128:(j + 1) * 128], in_=ps[:, :].rearrange("p (h k) -> p h k", h=4)),
                             reads=[ps.b], writes=[KT.b])
                    klens = [128] * NPK + [64]
                for (ti, t) in stiles:
                    Tn = t["T"]
                    x, q_, ca = xt[ti % 2], qT[ti % 2], caT[ti % 2]
                    S.op("sp", lambda e, x=x, t=t, Tn=Tn: e.dma_start(out=x[:Tn], in_=xin_rows(t)), writes=[x.b], chan=x.b.name)
                    S.op("sp", lambda e, q_=q_, ti=ti, Tn=Tn: e.dma_start(out=q_[:, :, :Tn], in_=QTs_view(QTs, ti, Tn)),
                         reads=[S.db("QTs%d" % ti)], writes=[q_.b], chan=q_.b.name)
                    S.op("sp", lambda e, ca=ca, ti=ti, Tn=Tn: e.dma_start(out=ca[:, :, :Tn], in_=QTs_view(CATs, ti, Tn)),
                         reads=[S.db("CATs%d" % ti)], writes=[ca.b], chan=ca.b.name)
                    nk = (t["i"] + 1) if isp else (NPK + 1)
                    for h in range(4):
                        for jb in range(0, nk, 4):
                            js = list(range(jb, min(nk, jb + 4)))
                            for tt in range(2):
                                p0 = 64 * tt
                                pS = psS[cnt["s"] % 2]
                                cnt["s"] += 1
                                P = PT[cnt["p"] % 3]
                                cnt["p"] += 1
                                for jj, j in enumerate(js):
                                    kl = klens[j]
                                    S.op("pe", lambda e, jj=jj, j=j, kl=kl, pS=pS, h=h, p0=p0, q_=q_, Tn=Tn: e.matmul(
                                        pS[:kl, jj * 128:jj * 128 + Tn], lhsT=KT[p0:p0 + 64, h, j * 128:j * 128 + kl], rhs=q_[p0:p0 + 64, h, :Tn],
                                        start=True, stop=True), reads=[KT.b, q_.b], writes=[pS.b])
                                nj = len(js)
                                S.op("act", lambda e, pS=pS, P=P, nj=nj, Tn=Tn: e.activation(
                                    out=P[:, 0:nj, :Tn], in_=pS[:, 0:nj * 128].rearrange("p (j t) -> p j t", j=nj)[:, :, :Tn], func=AF.Exp, scale=0.125),
                                    reads=[pS.b], writes=[P.b])
                                if isp and js[-1] == t["i"]:
                                    jj = len(js) - 1
                                    S.op("dve", lambda e, P=P, jj=jj: e.memset(P[64:128, jj, 0:64], 0.0), reads=[P.b], writes=[P.b])
                                for jj, j in enumerate(js):
                                    kl = klens[j]
                                    S.op("pe", lambda e, jj=jj, j=j, kl=kl, P=P, h=h, tt=tt, Tn=Tn, nk=nk: e.matmul(
                                        psO[tt][:Tn, 0:129], lhsT=P[:kl, jj, :Tn], rhs=VS[:kl, j, h * 129:(h + 1) * 129],
                                        start=(j == 0), stop=(j == nk - 1)), reads=[P.b, VS.b], writes=[psO[tt].b])
                        S.op("dve", lambda e, Tn=Tn: e.reciprocal(out=rr_[:Tn, 0:1], in_=psO[0][:Tn, 128:129]), reads=[psO[0].b], writes=[rr_.b])
                        S.op("dve", lambda e, Tn=Tn: e.reciprocal(out=rr_[:Tn, 1:2], in_=psO[1][:Tn, 128:129]), reads=[psO[1].b, rr_.b], writes=[rr_.b])
                        S.op("dve", lambda e, Tn=Tn: e.tensor_tensor(out=rr_[:Tn, 2:3], in0=rr_[:Tn, 1:2], in1=lam[:Tn, 5:6], op=ALU.mult),
                             reads=[rr_.b, lam.b], writes=[rr_.b])
                        S.op("dve", lambda e, Tn=Tn: e.tensor_scalar(out=ah[:Tn], in0=psO[0][:Tn, 0:128], scalar1=rr_[:Tn, 0:1], scalar2=None, op0=ALU.mult),
                             reads=[psO[0].b, rr_.b], writes=[ah.b])
                        S.op("dve", lambda e, Tn=Tn: e.scalar_tensor_tensor(out=ah[:Tn], in0=psO[1][:Tn, 0:128], scalar=rr_[:Tn, 2:3], in1=ah[:Tn],
                                                                            op0=ALU.mult, op1=ALU.add), reads=[psO[1].b, rr_.b, ah.b], writes=[ah.b])
                        S.op("dve", lambda e, Tn=Tn: e.memset(rr_[:Tn, 3:4], 0.0), reads=[rr_.b], writes=[rr_.b])
                        S.op("act", lambda e, Tn=Tn: e.activation(out=junk[:Tn], in_=ah[:Tn], func=AF.Square, accum_out=rr_[:Tn, 3:4]),
                             reads=[ah.b, rr_.b], writes=[junk.b, rr_.b])
                        S.op("act", lambda e, Tn=Tn: e.activation(out=rr_[:Tn, 4:5], in_=rr_[:Tn, 3:4], func=AF.Sqrt, scale=1.0 / 128, bias=1e-6),
                             reads=[rr_.b], writes=[rr_.b])
                        S.op("dve", lambda e, Tn=Tn: e.reciprocal(out=rr_[:Tn, 4:5], in_=rr_[:Tn, 4:5]), reads=[rr_.b], writes=[rr_.b])
                        S.op("dve", lambda e, Tn=Tn, h=h: e.scalar_tensor_tensor(out=otok[:Tn, h * 128:(h + 1) * 128], in0=ah[:Tn], scalar=rr_[:Tn, 4:5],
                                                                                 in1=gs[:Tn], op0=ALU.mult, op1=ALU.mult),
                             reads=[ah.b, rr_.b, gs.b], writes=[otok.b])
                    pb = PSB()
                    for h in range(4):
                        S.op("pe", lambda e, h=h, pb=pb, Tn=Tn: e.transpose(out=pb[:, h * Tn:(h + 1) * Tn], in_=otok[:Tn, h * 128:(h + 1) * 128],
                                                                            identity=identb[:Tn, :Tn]), reads=[otok.b, identb.b], writes=[pb.b])
                    S.op("act", lambda e, pb=pb, Tn=Tn: e.copy(out=oT[:, :, :Tn], in_=pb[:, 0:4 * Tn].rearrange("p (h t) -> p h t", h=4)),
                         reads=[pb.b], writes=[oT.b])
                    for n in range(2):
                        ps = psX[n]
                        for c in range(8):
                            src = ca if c < 4 else oT
                            S.op("pe", lambda e, c=c, n=n, ps=ps, src=src, Tn=Tn: e.matmul(ps[:Tn, :], lhsT=src[:, c % 4, :Tn],
                                                                                          rhs=wo[:, c, n * 512:(n + 1) * 512],
                                                                                          start=(c == 0), stop=(c == 7)),
                                 reads=[src.b, wo.b], writes=[ps.b])
                        S.op("dve", lambda e, n=n, ps=ps, x=x, Tn=Tn: e.tensor_tensor(out=x[:Tn, n * 512:(n + 1) * 512], in0=ps[:Tn, :],
                                                                                     in1=x[:Tn, n * 512:(n + 1) * 512], op=ALU.add),
                             reads=[ps.b, x.b], writes=[x.b])
                    S.op("sp", lambda e, x=x, t=t, Tn=Tn: e.dma_start(out=XS[t["tok0"]:t["tok0"] + Tn, :], in_=x[:Tn]),
                         reads=[x.b], writes=[S.db("XS%d" % ti)], chan="st_" + x.b.name)
            S.emit()

        def cross_peer(l, final, peer=True):
            with ExitStack() as es:
                sb = phase_tiles(es, "p3%d" % l)
                wq = load_w(sb, w_xq[l], D, "wq")
                wo_ = load_w(sb, w_xo[l], D, "wo")
                wp = load_w(sb, w_pq[l], 2048, "wp")
                gcx = load_cols(sb, norm_cross_g[l:l + 1, :], 8, "gcx")
                gcf = load_cols(sb, norm_ffn_g[l:l + 1, :], 8, "gcf")
                grf = load_rep(sb, norm_ffn_g[l:l + 1, :], D, "grf")
                if final:
                    grz = load_rep(sb, norm_final_g, D, "grz")
                skT = sb([128, 2, 128], F32, "skT")
                sk32 = sb([128, 2, 128], F32, "sk32")
                S.op("sp", lambda e: e.dma_start(out=sk32[:], in_=sub_keys[l].rearrange("c k d -> k c d")), writes=[sk32.b], chan="sk32")
                for c in range(2):
                    ps = PSF()
                    S.op("pe", lambda e, c=c, ps=ps: e.transpose(out=ps[:, 0:128], in_=sk32[:, c, :], identity=identf[:, :]),
                         reads=[sk32.b, identf.b], writes=[ps.b])
                    S.op("act", lambda e, c=c, ps=ps: e.copy(out=skT[:, c, :], in_=ps[:, 0:128]), reads=[ps.b], writes=[skT.b])
                rms_T = make_rms(sb)
                iot = sb([128, 16], F32, "iot")
                S.op("pool", lambda e: e.iota(iot[:], pattern=[[1, 16]], base=0, channel_multiplier=0, allow_small_or_imprecise_dtypes=True),
                     writes=[iot.b])
                mb = sb([128, 2, D], BF16, "mb")
                mkT = sb([128, 8, 256], BF16, "mkT")
                mvb = sb([128, 2, D], BF16, "mvb")
                xt = [sb([128, D], F32, "x") for _ in range(2)]
                hT = sb([128, 8, 128], BF16, "hT")
                h32s = [sb([128, D], F32, "h32") for _ in range(2)]
                qTx = sb([128, 8, 128], BF16, "qTx")
                PTx = sb([128, 8, 128], BF16, "PTx")
                rinv = sb([128, 4, 128], F32, "rinv")
                oTx = sb([128, 8, 128], BF16, "oTx")
                pqT = sb([128, 16, 128], F32, "pqT")
                sc = sb([128, 16, 128], F32, "sc")
                sc2 = sb([128, 16, 128], F32, "sc2")
                ts = sb([128, 16, 16], F32, "ts")
                tiu = sb([128, 16, 16], U32, "tiu")
                tif = sb([128, 16, 16], F32, "tif")
                cand = sb([128, 8, 256], F32, "cand")
                cand2 = T(sc2.ap.rearrange("p (h a) k -> p h (a k)", h=8), "cand2v")
                cand2.b = sc2.b
                bs = sb([128, 8, 16], F32, "bs")
                bpu = sb([128, 8, 16], U32, "bpu")
                bpf = sb([128, 8, 16], F32, "bpf")
                fa = sb([128, 8, 16], F32, "fa")
                fb_ = sb([128, 8, 16], F32, "fb")
                ia = sb([128, 8, 16], I32, "ia")
                oh = T(sc.ap.rearrange("p (h a) (b c) -> p h a b c", h=8, b=8).rearrange("p h a b c -> p h (a b) c"), "ohv")
                oh.b = sc.b
                i0f = sb([128, 8, 16], F32, "i0f")
                i1f = sb([128, 8, 16], F32, "i1f")
                idxs = [sb([128, 128], I32, "idx") for _ in range(2)]
                gates = [sb([128, 8, 16], F32, "gate") for _ in range(2)]
                gsum = sb([128, 8], F32, "gsum")
                act_ = sb([128, 128], F32, "act")
                g1 = sb([128, 128], F32, "g1")
                g2 = sb([128, 128], F32, "g2")
                wgt = sb([128, 128], F32, "wgt")
                NG = 8
                gb = [sb([128, D], F32, "gb") for _ in range(NG)]
                pj = rms_T.junk
                rr["npsf"] = 4
                psV = [psf[4], psf[5]]
                dgs = [sb([128, 128], BF16, "dg") for _ in range(4)]
                for idx in idxs:
                    S.op("dve", lambda e, idx=idx: e.memset(idx[:], 0), writes=[idx.b])
                fss = sb([128, 2], F32, "fss")
                gcnt = [0]
                cur_seq = [None]
                if split:
                    ridx = sb([128, NOWN], I32, "ridx")
                    S.op("sp", lambda e: e.dma_start(out=ridx[:], in_=rowidx), writes=[ridx.b], chan="ridx")
                    own_tiles = [(k, dict(tiles[0], slot=k)) for k in range(NOWN)] + [(ti, t) for ti, t in enumerate(tiles) if not t["prompt"]]
                else:
                    own_tiles = [(ti, dict(t, slot=t["i"])) for ti, t in enumerate(tiles)]
                def AB(n):
                    ti, t = own_tiles[n]
                    h32, idx, gate = h32s[n % 2], idxs[n % 2], gates[n % 2]
                    Tn = t["T"]
                    if cur_seq[0] != t["seq"]:
                        cur_seq[0] = t["seq"]
                        ksrc = o_mk[l] if t["prompt"] else cmk[l, t["s"]]
                        vsrc = o_mv[l] if t["prompt"] else cmv[l, t["s"]]
                        S.op("pool", lambda e, ksrc=ksrc: e.dma_start(out=mb[:], in_=ksrc.rearrange("(m p) d -> p m d", p=128)),
                             reads=[S.db("omem")], writes=[mb.b], chan="mb")
                        for mc in range(2):
                            pb = PSB()
                            for c in range(8):
                                S.op("pe", lambda e, c=c, mc=mc, pb=pb: e.transpose(out=pb[:, c * 128:(c + 1) * 128], in_=mb[:, mc, c * 128:(c + 1) * 128],
                                                                                    identity=identb[:]), reads=[mb.b, identb.b], writes=[pb.b])
                            S.op("act", lambda e, mc=mc, pb=pb: e.copy(out=mkT[:, :, mc * 128:(mc + 1) * 128],
                                                                       in_=pb[:, :].rearrange("p (c t) -> p c t", c=8)), reads=[pb.b], writes=[mkT.b])
                        S.op("pool", lambda e, vsrc=vsrc: e.dma_start(out=mvb[:], in_=vsrc.rearrange("(m p) d -> p m d", p=128)),
                             reads=[S.db("omem")], writes=[mvb.b], chan="mvb")
                    x = xt[n % 2]
                    if split and t["prompt"]:
                        S.op("sp", lambda e, x=x, t=t: e.dma_start(out=x[:, :], in_=XOWN[t["slot"] * 128:(t["slot"] + 1) * 128, :]),
                             reads=[S.db("XOWN%d" % t["slot"])], writes=[x.b], chan=x.b.name)
                    else:
                        S.op("sp", lambda e, x=x, t=t, Tn=Tn: e.dma_start(out=x[:Tn], in_=XS[t["tok0"]:t["tok0"] + Tn, :]),
                             reads=[S.db("XS%d" % ti)], writes=[x.b], chan=x.b.name)
                    rms_T(x, Tn, gcx, hT)
                    for half in range(2):
                        ps = PSF()
                        mm_feat(ps, half * 512, 4, wq, hT, Tn)
                        S.op("act", lambda e, ps=ps, half=half, Tn=Tn: e.copy(out=qTx[:, half * 4:half * 4 + 4, :Tn],
                                                                             in_=ps[:, 0:4 * Tn].rearrange("p (c t) -> p c t", c=4)),
                             reads=[ps.b], writes=[qTx.b])
                    pss = [PSF(), PSF()]
                    for h in range(4):
                        for mc in range(2):
                            ps = pss[h // 2]
                            o0 = ((h % 2) * 2 + mc) * Tn
                            for dc in range(2):
                                S.op("pe", lambda e, h=h, mc=mc, dc=dc, ps=ps, o0=o0, Tn=Tn: e.matmul(
                                    ps[:, o0:o0 + Tn], lhsT=mkT[:, 2 * h + dc, mc * 128:(mc + 1) * 128], rhs=qTx[:, 2 * h + dc, :Tn],
                                    start=(dc == 0), stop=(dc == 1)), reads=[mkT.b, qTx.b], writes=[ps.b])
                    for hh in range(2):
                        S.op("act", lambda e, hh=hh, Tn=Tn, pss=pss: e.activation(out=PTx[:, hh * 4:hh * 4 + 4, :Tn],
                                                                         in_=pss[hh][:, 0:4 * Tn].rearrange("p (c t) -> p c t", c=4),
                                                                         func=AF.Exp, scale=1.0 / 16), reads=[pss[hh].b], writes=[PTx.b])
                    psr = PSF()
                    for h in range(4):
                        for mc in range(2):
                            S.op("pe", lambda e, h=h, mc=mc, Tn=Tn, psr=psr: e.matmul(psr[:, h * Tn:(h + 1) * Tn], lhsT=onesb[:, :], rhs=PTx[:, h * 2 + mc, :Tn],
                                                                            start=(mc == 0), stop=(mc == 1)), reads=[onesb.b, PTx.b], writes=[psr.b])
                    S.op("dve", lambda e, Tn=Tn, psr=psr: e.reciprocal(out=rinv[:, :, :Tn], in_=psr[:, 0:4 * Tn].rearrange("p (h t) -> p h t", h=4)),
                         reads=[psr.b], writes=[rinv.b])
                    for half in range(2):
                        ps = PSF()
                        for jj in range(4):
                            j = half * 4 + jj
                            h = j // 2
                            for mc in range(2):
                                S.op("pe", lambda e, j=j, jj=jj, h=h, mc=mc, ps=ps, Tn=Tn: e.matmul(
                                    ps[:, jj * Tn:(jj + 1) * Tn], lhsT=mvb[:, mc, j * 128:(j + 1) * 128], rhs=PTx[:, h * 2 + mc, :Tn],
                                    start=(mc == 0), stop=(mc == 1)), reads=[mvb.b, PTx.b], writes=[ps.b])
                        for hh in range(2):
                            h = half * 2 + hh
                            S.op("dve", lambda e, ps=ps, hh=hh, h=h, Tn=Tn: e.tensor_tensor(
                                out=oTx[:, 2 * h:2 * h + 2, :Tn], in0=ps[:, hh * 2 * Tn:(hh * 2 + 2) * Tn].rearrange("p (c t) -> p c t", c=2),
                                in1=rinv[:, h, :Tn].unsqueeze(1).to_broadcast([128, 2, Tn]), op=ALU.mult),
                                reads=[ps.b, rinv.b], writes=[oTx.b])
                    for n in range(2):
                        ps = PSF()
                        mm_tok(ps, n * 512, 512, wo_, oTx, Tn)
                        S.op("dve", lambda e, n=n, ps=ps, x=x, Tn=Tn: e.tensor_tensor(out=x[:Tn, n * 512:(n + 1) * 512], in0=ps[:Tn, :],
                                                                                     in1=x[:Tn, n * 512:(n + 1) * 512], op=ALU.add),
                             reads=[ps.b, x.b], writes=[x.b])
                    if not peer:
                        S.op("sp", lambda e, x=x, t=t, Tn=Tn: e.dma_start(out=XS[t["tok0"]:t["tok0"] + Tn, :], in_=x[:Tn]),
                             reads=[x.b], writes=[S.db("XS%d" % ti)], chan="st_" + x.b.name)
                        return
                    rms_T(x, Tn, gcf, hT, h32=h32, grep=grf)
                    for q4 in range(4):
                        ps = PSF()
                        mm_feat(ps, q4 * 512, 4, wp, hT, Tn)
                        S.op("act", lambda e, ps=ps, q4=q4, Tn=Tn: e.copy(out=pqT[:, q4 * 4:q4 * 4 + 4, :Tn],
                                                                         in_=ps[:, 0:4 * Tn].rearrange("p (c t) -> p c t", c=4)),
                             reads=[ps.b], writes=[pqT.b])
                    for q4 in range(4):
                        ps = PSF()
                        for jj in range(4):
                            j = q4 * 4 + jj
                            S.op("pe", lambda e, j=j, jj=jj, ps=ps, Tn=Tn: e.matmul(ps[:Tn, jj * 128:(jj + 1) * 128], lhsT=pqT[:, j, :Tn],
                                                                                   rhs=skT[:, j % 2, :], start=True, stop=True),
                                 reads=[pqT.b, skT.b], writes=[ps.b])
                        S.op("act", lambda e, ps=ps, q4=q4, Tn=Tn: e.copy(out=sc[:Tn, q4 * 4:q4 * 4 + 4, :],
                                                                         in_=ps[:Tn, :].rearrange("p (c k) -> p c k", c=4)),
                             reads=[ps.b], writes=[sc.b])
                    for j in range(16):
                        S.op("dve", lambda e, j=j, Tn=Tn: e.max(out=ts[:Tn, j, 0:8], in_=sc[:Tn, j, :]), reads=[sc.b], writes=[ts.b])
                    for j in range(16):
                        S.op("dve", lambda e, j=j, Tn=Tn: e.max_index(out=tiu[:Tn, j, 0:8], in_max=ts[:Tn, j, 0:8], in_values=sc[:Tn, j, :]),
                             reads=[sc.b, ts.b], writes=[tiu.b])
                    for j in range(16):
                        S.op("dve", lambda e, j=j, Tn=Tn: e.match_replace(out=sc2[:Tn, j, :], in_to_replace=ts[:Tn, j, 0:8], in_values=sc[:Tn, j, :],
                                                                         imm_value=NEG), reads=[sc.b, ts.b], writes=[sc2.b])
                    for j in range(16):
                        S.op("dve", lambda e, j=j, Tn=Tn: e.max(out=ts[:Tn, j, 8:16], in_=sc2[:Tn, j, :]), reads=[sc2.b], writes=[ts.b])
                    for j in range(16):
                        S.op("dve", lambda e, j=j, Tn=Tn: e.max_index(out=tiu[:Tn, j, 8:16], in_max=ts[:Tn, j, 8:16], in_values=sc2[:Tn, j, :]),
                             reads=[sc2.b, ts.b], writes=[tiu.b])
                    S.op("dve", lambda e, Tn=Tn: e.tensor_copy(out=tif[:Tn], in_=tiu[:Tn]), reads=[tiu.b], writes=[tif.b])
                    tsv = ts[:Tn].rearrange("p (h c) k -> p h c k", c=2)
                    tfv = tif[:Tn].rearrange("p (h c) k -> p h c k", c=2)
                    S.op("dve", lambda e, Tn=Tn, tsv=tsv: e.tensor_tensor(
                        out=cand[:Tn].rearrange("p h (a b) -> p h a b", a=16),
                        in0=tsv[:, :, 0, :].unsqueeze(3).to_broadcast([Tn, 8, 16, 16]),
                        in1=tsv[:, :, 1, :].unsqueeze(2).to_broadcast([Tn, 8, 16, 16]), op=ALU.add), reads=[ts.b], writes=[cand.b])
                    for h in range(8):
                        S.op("dve", lambda e, h=h, Tn=Tn: e.max(out=bs[:Tn, h, 0:8], in_=cand[:Tn, h, :]), reads=[cand.b], writes=[bs.b])
                    for h in range(8):
                        S.op("dve", lambda e, h=h, Tn=Tn: e.max_index(out=bpu[:Tn, h, 0:8], in_max=bs[:Tn, h, 0:8], in_values=cand[:Tn, h, :]),
                             reads=[cand.b, bs.b], writes=[bpu.b])
                    for h in range(8):
                        S.op("dve", lambda e, h=h, Tn=Tn: e.match_replace(out=cand2[:Tn, h, :], in_to_replace=bs[:Tn, h, 0:8], in_values=cand[:Tn, h, :],
                                                                         imm_value=NEG), reads=[cand.b, bs.b], writes=[cand2.b])
                    for h in range(8):
                        S.op("dve", lambda e, h=h, Tn=Tn: e.max(out=bs[:Tn, h, 8:16], in_=cand2[:Tn, h, :]), reads=[cand2.b], writes=[bs.b])
                    for h in range(8):
                        S.op("dve", lambda e, h=h, Tn=Tn: e.max_index(out=bpu[:Tn, h, 8:16], in_max=bs[:Tn, h, 8:16], in_values=cand2[:Tn, h, :]),
                             reads=[cand2.b, bs.b], writes=[bpu.b])
                    S.op("dve", lambda e, Tn=Tn: e.tensor_tensor(out=gate[:Tn], in0=bs[:Tn], in1=bs[:Tn, :, 0:1].to_broadcast([Tn, 8, 16]), op=ALU.subtract),
                         reads=[bs.b], writes=[gate.b])
                    S.op("act", lambda e, Tn=Tn: e.activation(out=gate[:Tn], in_=gate[:Tn], func=AF.Exp), reads=[gate.b], writes=[gate.b])
                    S.op("dve", lambda e, Tn=Tn: e.tensor_reduce(out=gsum[:Tn], in_=gate[:Tn], axis=AX.X, op=ALU.add), reads=[gate.b], writes=[gsum.b])
                    S.op("dve", lambda e, Tn=Tn: e.reciprocal(out=gsum[:Tn], in_=gsum[:Tn]), reads=[gsum.b], writes=[gsum.b])
                    S.op("dve", lambda e, Tn=Tn: e.tensor_tensor(out=gate[:Tn], in0=gate[:Tn], in1=gsum[:Tn].unsqueeze(2).to_broadcast([Tn, 8, 16]), op=ALU.mult),
                         reads=[gate.b, gsum.b], writes=[gate.b])
                    S.op("dve", lambda e, Tn=Tn: e.tensor_copy(out=bpf[:Tn], in_=bpu[:Tn]), reads=[bpu.b], writes=[bpf.b])
                    S.op("dve", lambda e, Tn=Tn: e.tensor_scalar(out=fb_[:Tn], in0=bpf[:Tn], scalar1=0.0625, scalar2=-0.46875, op0=ALU.mult, op1=ALU.add),
                         reads=[bpf.b], writes=[fb_.b])
                    S.op("dve", lambda e, Tn=Tn: e.tensor_copy(out=ia[:Tn], in_=fb_[:Tn]), reads=[fb_.b], writes=[ia.b])
                    S.op("dve", lambda e, Tn=Tn: e.tensor_copy(out=fa[:Tn], in_=ia[:Tn]), reads=[ia.b], writes=[fa.b])
                    S.op("dve", lambda e, Tn=Tn: e.scalar_tensor_tensor(out=fb_[:Tn], in0=fa[:Tn], scalar=-16.0, in1=bpf[:Tn], op0=ALU.mult, op1=ALU.add),
                         reads=[fa.b, bpf.b, fb_.b], writes=[fb_.b])
                    for (pos, cc, dst) in ((fa, 0, i0f), (fb_, 1, i1f)):
                        S.op("dve", lambda e, pos=pos, Tn=Tn: e.tensor_tensor(
                            out=oh[:Tn], in0=iot[:Tn, :].unsqueeze(1).unsqueeze(1).to_broadcast([Tn, 8, 16, 16]),
                            in1=pos[:Tn].unsqueeze(3).to_broadcast([Tn, 8, 16, 16]), op=ALU.is_equal), reads=[iot.b, pos.b], writes=[oh.b])
                        S.op("dve", lambda e, cc=cc, Tn=Tn, tfv=tfv: e.tensor_tensor(
                            out=oh[:Tn], in0=oh[:Tn], in1=tfv[:, :, cc, :].unsqueeze(2).to_broadcast([Tn, 8, 16, 16]), op=ALU.mult),
                            reads=[oh.b, tif.b], writes=[oh.b])
                        S.op("dve", lambda e, dst=dst, Tn=Tn: e.tensor_reduce(out=dst[:Tn], in_=oh[:Tn], axis=AX.X, op=ALU.add), reads=[oh.b], writes=[dst.b])
                    S.op("dve", lambda e, Tn=Tn: e.scalar_tensor_tensor(out=i0f[:Tn], in0=i0f[:Tn], scalar=128.0, in1=i1f[:Tn], op0=ALU.mult, op1=ALU.add),
                         reads=[i0f.b, i1f.b], writes=[i0f.b])
                    S.op("dve", lambda e, Tn=Tn: e.tensor_copy(out=idx[:Tn, :], in_=i0f[:Tn].rearrange("p h k -> p (h k)")), reads=[i0f.b], writes=[idx.b])
                def CDEF(n, cap):
                    ti, t = own_tiles[n]
                    h32, idx, gate = h32s[n % 2], idxs[n % 2], gates[n % 2]
                    Tn = t["T"]
                    x = xt[n % 2]
                    per = (len(cap) + 179) // 180
                    S.op("dve", lambda e: e.memset(act_[:], 0.0), writes=[act_.b])
                    for s_ in range(128):
                        g = gb[gcnt[0] % NG]
                        gcnt[0] += 1
                        S.op("pool", lambda e, g=g, s_=s_: e.indirect_dma_start(out=g[:, :], out_offset=None, in_=expert_u.rearrange("l e d -> (l e) d"),
                                                                                in_offset=bass.IndirectOffsetOnAxis(ap=idx[:, s_:s_ + 1], axis=0),
                                                                                element_offset=l * 16384 * D),
                             reads=[idx.b], writes=[g.b], chan=g.b.name)
                        S.op("dve", lambda e, g=g, s_=s_, Tn=Tn: e.scalar_tensor_tensor(out=pj[:Tn], in0=g[:Tn], scalar=1.0, in1=h32[:Tn],
                                                                                        op0=ALU.mult, op1=ALU.mult, accum_out=act_[:Tn, s_:s_ + 1]),
                             reads=[g.b, h32.b], writes=[pj.b, act_.b])
                        S.replay(cap, per)
                    S.op("dve", lambda e, Tn=Tn: e.tensor_tensor(out=g1[:Tn], in0=act_[:Tn], in1=act_[:Tn], op=ALU.mult), reads=[act_.b], writes=[g1.b])
                    S.op("dve", lambda e, Tn=Tn: e.tensor_scalar(out=g1[:Tn], in0=g1[:Tn], scalar1=0.044715, scalar2=1.0, op0=ALU.mult, op1=ALU.add),
                         reads=[g1.b], writes=[g1.b])
                    S.op("dve", lambda e, Tn=Tn: e.tensor_tensor(out=g1[:Tn], in0=g1[:Tn], in1=act_[:Tn], op=ALU.mult), reads=[g1.b, act_.b], writes=[g1.b])
                    S.op("act", lambda e, Tn=Tn: e.activation(out=g2[:Tn], in_=g1[:Tn], func=AF.Sigmoid, scale=1.5957691216057308),
                         reads=[g1.b], writes=[g2.b])
                    S.op("dve", lambda e, Tn=Tn: e.tensor_tensor(out=g2[:Tn], in0=g2[:Tn], in1=act_[:Tn], op=ALU.mult), reads=[g2.b, act_.b], writes=[g2.b])
                    S.op("dve", lambda e, Tn=Tn: e.tensor_tensor(out=wgt[:Tn], in0=g2[:Tn], in1=gate[:Tn].rearrange("p h k -> p (h k)"), op=ALU.mult),
                         reads=[g2.b, gate.b], writes=[wgt.b])
                    for s_ in range(128):
                        g = gb[gcnt[0] % NG]
                        gcnt[0] += 1
                        gv = g.ap.bitcast(BF16)
                        S.op("pool", lambda e, gv=gv, s_=s_: e.indirect_dma_start(out=gv[:, 0:D], out_offset=None, in_=expert_v.rearrange("l e d -> (l e) d"),
                                                                                in_offset=bass.IndirectOffsetOnAxis(ap=idx[:, s_:s_ + 1], axis=0),
                                                                                element_offset=l * 16384 * D),
                             reads=[idx.b], writes=[g.b], chan=g.b.name)
                        dg = dgs[s_ % 4]
                        S.op("act", lambda e, dg=dg, s_=s_, Tn=Tn: e.activation(out=dg[:Tn, :Tn], in_=identb[:Tn, :Tn], func=AF.Copy, scale=wgt[:Tn, s_:s_ + 1]),
                             reads=[identb.b, wgt.b], writes=[dg.b])
                        for nn in range(2):
                            S.op("pe", lambda e, dg=dg, gv=gv, s_=s_, nn=nn, Tn=Tn: e.matmul(psV[nn][:Tn, :], lhsT=dg[:Tn, :Tn], rhs=gv[:Tn, nn * 512:(nn + 1) * 512],
                                                                                          start=(s_ == 0), stop=(s_ == 127)),
                                 reads=[dg.b, g.b], writes=[psV[nn].b])
                        S.replay(cap, per)
                    for nn in range(2):
                        S.op("dve", lambda e, x=x, nn=nn, Tn=Tn: e.tensor_tensor(out=x[:Tn, nn * 512:(nn + 1) * 512], in0=psV[nn][:Tn, :], in1=x[:Tn, nn * 512:(nn + 1) * 512],
                                                                                op=ALU.add), reads=[psV[nn].b, x.b], writes=[x.b])
                    if final:
                        ss = fss
                        S.op("dve", lambda e, ss=ss, Tn=Tn: e.memset(ss[:Tn, 0:1], 0.0), writes=[ss.b])
                        S.op("act", lambda e, ss=ss, x=x, Tn=Tn: e.activation(out=pj[:Tn], in_=x[:Tn], func=AF.Square, accum_out=ss[:Tn, 0:1]),
                             reads=[x.b, ss.b], writes=[pj.b, ss.b])
                        S.op("act", lambda e, ss=ss, Tn=Tn: e.activation(out=ss[:Tn, 1:2], in_=ss[:Tn, 0:1], func=AF.Sqrt, scale=1.0 / D, bias=1e-6),
                             reads=[ss.b], writes=[ss.b])
                        S.op("dve", lambda e, ss=ss, Tn=Tn: e.reciprocal(out=ss[:Tn, 1:2], in_=ss[:Tn, 1:2]), reads=[ss.b], writes=[ss.b])
                        S.op("dve", lambda e, ss=ss, x=x, Tn=Tn: e.scalar_tensor_tensor(out=x[:Tn], in0=x[:Tn], scalar=ss[:Tn, 1:2], in1=grz[:Tn],
                                                                                       op0=ALU.mult, op1=ALU.mult), reads=[x.b, ss.b, grz.b], writes=[x.b])
                        dst = y_p[t["slot"] * 128:(t["slot"] + 1) * 128, :] if t["prompt"] else y_s[t["s"] * 64:(t["s"] + 1) * 64, :]
                        S.op("sp", lambda e, x=x, dst=dst, Tn=Tn: e.dma_start(out=dst, in_=x[:Tn]), reads=[x.b], writes=[S.db("yout")], chan="st_" + x.b.name)
                    elif split and t["prompt"]:
                        S.op("sp", lambda e, x=x, t=t: e.dma_start(out=SND[t["slot"] * 128:(t["slot"] + 1) * 128, :], in_=x[:, :]),
                             reads=[x.b], writes=[S.db("SND")], chan="st_" + x.b.name)
                    else:
                        S.op("sp", lambda e, x=x, t=t, Tn=Tn: e.dma_start(out=XS[t["tok0"]:t["tok0"] + Tn, :], in_=x[:Tn]),
                             reads=[x.b], writes=[S.db("XS%d" % ti)], chan="st_" + x.b.name)

                if split:
                    for k in range(NOWN):
                        g = gb[k % NG]
                        S.op("pool", lambda e, g=g, k=k: e.indirect_dma_start(out=g[:, :], out_offset=None, in_=XS,
                                                                              in_offset=bass.IndirectOffsetOnAxis(ap=ridx[:, k:k + 1], axis=0)),
                             reads=[ridx.b], writes=[g.b], chan=g.b.name)
                        S.op("sp", lambda e, g=g, k=k: e.dma_start(out=XOWN[k * 128:(k + 1) * 128, :], in_=g[:, :]),
                             reads=[g.b], writes=[S.db("XOWN%d" % k)], chan="st_" + g.b.name)
                cap = S.capture(lambda: AB(0))
                S.replay(cap, len(cap))
                for n in range(len(own_tiles)):
                    cap = S.capture(lambda: AB(n + 1)) if n + 1 < len(own_tiles) else []
                    if peer:
                        CDEF(n, cap)
                    S.replay(cap, len(cap))
                S.emit()
                rr["npsf"] = 6
                if split and not final:
                    for j in range(NOWN // CHT):
                        S.op("pool", lambda e, j=j: e.collective_compute("AllGather", ALU.bypass, replica_groups=[[0, 4], [1, 5], [2, 6], [3, 7]],
                                                                         ins=[SND[j * CHT * 128:(j + 1) * CHT * 128, :]], outs=[RCV[j]]),
                             writes=[S.db("RCV%d" % j)], chan="cc", amt=1)
                    S.emit()

        def dump_xs():
            S.op("sp", lambda e: e.dma_start(out=y_p[:, :], in_=XS[0:NPT * 128, :]), writes=[S.db("yout")], chan="dbg")
            S.op("sp", lambda e: e.dma_start(out=y_s[:, :], in_=XS[NPT * 128:NTOK, :]), writes=[S.db("yout")], chan="dbg")
            S.emit()
        if stop_after == 2:
            dump_xs()
            return nc
        if stop_after == 25:
            cross_peer(0, False, peer=False)
            dump_xs()
            return nc
        cross_peer(0, False)
        if stop_after == 3:
            dump_xs()
            return nc
        P4(nc, S, locals())
        if stop_after == 4:
            dump_xs()
            return nc
        cross_peer(1, True)
    return nc


def P4(nc, S, G):
    tiles = G["tiles"]; PSF = G["PSF"]; PSB = G["PSB"]; phase_tiles = G["phase_tiles"]; load_w = G["load_w"]
    load_cols = G["load_cols"]; load_rep = G["load_rep"]; make_rms = G["make_rms"]; mm_feat = G["mm_feat"]; mm_tok = G["mm_tok"]
    identf = G["identf"]; identb = G["identb"]; onesf = G["onesf"]; XS = G["XS"]
    NPT = G["NPT"]; NSS = G["NSS"]
    with ExitStack() as es:
        sb = phase_tiles(es, "p4")
        w = load_w(sb, G["w_in_cd"], 2568, "win")
        wo = load_w(sb, G["w_out_cd"], D, "wo")
        gcol = load_cols(sb, G["norm_mix_g"][1:2, :], 8, "gcol")
        lcg = load_rep(sb, G["ln_c_g"], 512, "lcg")
        lcb = load_rep(sb, G["ln_c_b"], 512, "lcb")
        ndg = load_rep(sb, G["norm_d_g"], 512, "ndg")
        dtb = load_rep(sb, G["dt_bias"], 8, "dtb")
        alog = load_rep(sb, G["a_log"], 8, "alog")
        dsk = load_rep(sb, G["d_skip"], 8, "dsk")
        cdw = sb([128, 8, 4], F32, "cdw")
        for c in range(8):
            S.op("sp", lambda e, c=c: e.dma_start(out=cdw[:, c, :], in_=G["conv_d_w"][:, c * 128:(c + 1) * 128].rearrange("k p -> p k"),
                                                  allow_slow_non_contiguous=True), writes=[cdw.b], chan="cdw")
        cdb = load_cols(sb, G["conv_d_b"], 8, "cdb")
        ws32 = sb([128, 4, 128], F32, "ws32")
        wsT = sb([128, 4, 128], BF16, "wsT")
        bsc = sb([128, 4], F32, "bsc")
        S.op("sp", lambda e: e.dma_start(out=ws32[:], in_=G["gm_w_s"].rearrange("g i j -> i g j")), writes=[ws32.b], chan="ws32")
        S.op("sp", lambda e: e.dma_start(out=bsc[:], in_=G["gm_b_s"].rearrange("g i -> i g"), allow_slow_non_contiguous=True), writes=[bsc.b], chan="bsc")
        for g in range(4):
            S.op("pool", lambda e, g=g: e.affine_select(out=ws32[:, g, :], in_=ws32[:, g, :], pattern=[[-1, 128]], compare_op=ALU.is_ge, fill=0.0,
                                                        base=0, channel_multiplier=1), reads=[ws32.b], writes=[ws32.b])
        ps = PSF()
        for g in range(4):
            S.op("pe", lambda e, g=g, ps=ps: e.transpose(out=ps[:, g * 128:(g + 1) * 128], in_=ws32[:, g, :], identity=identf[:, :]),
                 reads=[ws32.b, identf.b], writes=[ps.b])
        S.op("act", lambda e, ps=ps: e.copy(out=wsT[:], in_=ps[:, :].rearrange("p (g i) -> p g i", g=4)), reads=[ps.b], writes=[wsT.b])
        tri = sb([64, 64], F32, "tri")
        negm = sb([64, 64], F32, "negm")
        ones64 = sb([64, 128], F32, "ones64")
        S.op("pool", lambda e: e.memset(tri[:], 1.0), writes=[tri.b])
        S.op("pool", lambda e: e.affine_select(out=tri[:], in_=tri[:], pattern=[[1, 64]], compare_op=ALU.is_ge, fill=0.0, base=0, channel_multiplier=-1),
             reads=[tri.b], writes=[tri.b])
        S.op("pool", lambda e: e.memset(negm[:], 0.0), writes=[negm.b])
        S.op("pool", lambda e: e.affine_select(out=negm[:], in_=negm[:], pattern=[[1, 64]], compare_op=ALU.is_ge, fill=NEG, base=0, channel_multiplier=-1),
             reads=[negm.b], writes=[negm.b])
        S.op("pool", lambda e: e.memset(ones64[:], 1.0), writes=[ones64.b])
        Arep = sb([128, 8], F32, "Arep")
        S.op("act", lambda e: e.activation(out=Arep[:], in_=alog[:], func=AF.Exp), reads=[alog.b], writes=[Arep.b])
        S.op("dve", lambda e: e.tensor_scalar(out=Arep[:], in0=Arep[:], scalar1=-1.0, scalar2=None, op0=ALU.mult), reads=[Arep.b], writes=[Arep.b])
        rms_T = make_rms(sb)
        xt = [sb([128, D], F32, "x") for _ in range(2)]
        hT = sb([128, 8, 128], BF16, "hT")
        cg = sb([128, 1024], F32, "cg")
        t1 = sb([128, 1024], F32, "t1")
        t2 = sb([128, 1024], F32, "t2")
        st_ = sb([128, 8], F32, "st")
        vvn = sb([128, 512], F32, "vvn")
        vvb = sb([128, 512], BF16, "vvb")
        cout = sb([128, 512], BF16, "cout")
        cyT = sb([128, 8, 128], BF16, "cyT")
        cbuf = sb([128, 8, 131], F32, "cbuf")
        xc = sb([128, 8, 128], F32, "xc")
        cacc = [sb([128, 128], F32, "cacc%d" % c) for c in range(8)]
        xstok = sb([128, 512], F32, "xstok")
        btok = sb([64, 2, 256], F32, "btok")
        gate = sb([128, 512], F32, "gate")
        dtt = sb([128, 8], F32, "dtt")
        dta = sb([128, 8], F32, "dta")
        ST = sb([128, 8, 64], F32, "ST")
        stin = sb([64, 8, 128], F32, "stin")
        c3 = sb([8, 1024], F32, "c3")
        yt = sb([128, 512], F32, "yt")
        acs = sb([64, 8], F32, "acs")
        nacs = sb([64, 8], F32, "nacs")
        eacs = sb([64, 8], F32, "eacs")
        dec = sb([64, 8], F32, "dec")
        cdec = sb([128, 8], F32, "cdec")
        LT = sb([64, 8, 64], F32, "LT")
        MT = sb([64, 8, 64], F32, "MT")
        xdt = sb([64, 8, 64], F32, "xdt")
        xdd = sb([64, 8, 64], F32, "xdd")
        ydg = sb([64, 512], F32, "ydg")
        dtab = sb([64, 8, 64], F32, "dtab")
        for ti, t in enumerate(tiles):
            Tn = t["T"]
            x = xt[ti % 2]
            if G["split"] and t["prompt"]:
                NOWN, CHT = G["NOWN"], G["CHT"]
                rk, kk = t["i"] // NOWN, t["i"] % NOWN
                src = G["RCV"][kk // CHT][rk * CHT * 128 + (kk % CHT) * 128: rk * CHT * 128 + (kk % CHT + 1) * 128, :]
                S.op("sp", lambda e, x=x, src=src: e.dma_start(out=x[:, :], in_=src), writes=[x.b], chan=x.b.name)
            else:
                S.op("sp", lambda e, x=x, t=t, Tn=Tn: e.dma_start(out=x[:Tn], in_=XS[t["tok0"]:t["tok0"] + Tn, :]),
                     reads=[S.db("XS%d" % ti)], writes=[x.b], chan=x.b.name)
            rms_T(x, Tn, gcol, hT)
            for n in range(2):
                ps = PSF()
                mm_tok(ps, n * 512, 512, w, hT, Tn)
                sl = slice(n * 512, (n + 1) * 512)
                S.op("act", lambda e, ps=ps, sl=sl, Tn=Tn: e.activation(out=t1[:Tn, sl], in_=ps[:Tn, :], func=AF.Square), reads=[ps.b], writes=[t1.b])
                S.op("dve", lambda e, sl=sl, Tn=Tn: e.tensor_scalar(out=t1[:Tn, sl], in0=t1[:Tn, sl], scalar1=0.044715, scalar2=1.0, op0=ALU.mult, op1=ALU.add),
                     reads=[t1.b], writes=[t1.b])
                S.op("dve", lambda e, ps=ps, sl=sl, Tn=Tn: e.tensor_tensor(out=t1[:Tn, sl], in0=ps[:Tn, :], in1=t1[:Tn, sl], op=ALU.mult),
                     reads=[ps.b, t1.b], writes=[t1.b])
                S.op("act", lambda e, sl=sl, Tn=Tn: e.activation(out=t2[:Tn, sl], in_=t1[:Tn, sl], func=AF.Sigmoid, scale=1.5957691216057308),
                     reads=[t1.b], writes=[t2.b])
                S.op("dve", lambda e, ps=ps, sl=sl, Tn=Tn: e.tensor_tensor(out=cg[:Tn, sl], in0=ps[:Tn, :], in1=t2[:Tn, sl], op=ALU.mult),
                     reads=[ps.b, t2.b], writes=[cg.b])
            S.op("dve", lambda e, Tn=Tn: e.memset(st_[:Tn, 0:2], 0.0), writes=[st_.b])
            S.op("act", lambda e, Tn=Tn: e.activation(out=t1[:Tn, 0:512], in_=cg[:Tn, 512:1024], func=AF.Copy, accum_out=st_[:Tn, 0:1]),
                 reads=[cg.b, st_.b], writes=[t1.b, st_.b])
            S.op("act", lambda e, Tn=Tn: e.activation(out=t1[:Tn, 512:1024], in_=cg[:Tn, 512:1024], func=AF.Square, accum_out=st_[:Tn, 1:2]),
                 reads=[cg.b, st_.b], writes=[t1.b, st_.b])
            S.op("dve", lambda e, Tn=Tn: e.tensor_scalar(out=st_[:Tn, 2:4], in0=st_[:Tn, 0:2], scalar1=1.0 / 512, scalar2=None, op0=ALU.mult),
                 reads=[st_.b], writes=[st_.b])
            S.op("dve", lambda e, Tn=Tn: e.tensor_tensor(out=st_[:Tn, 4:5], in0=st_[:Tn, 2:3], in1=st_[:Tn, 2:3], op=ALU.mult), reads=[st_.b], writes=[st_.b])
            S.op("dve", lambda e, Tn=Tn: e.tensor_tensor(out=st_[:Tn, 4:5], in0=st_[:Tn, 3:4], in1=st_[:Tn, 4:5], op=ALU.subtract), reads=[st_.b], writes=[st_.b])
            S.op("act", lambda e, Tn=Tn: e.activation(out=st_[:Tn, 5:6], in_=st_[:Tn, 4:5], func=AF.Sqrt, bias=1e-5, scale=1.0), reads=[st_.b], writes=[st_.b])
            S.op("dve", lambda e, Tn=Tn: e.reciprocal(out=st_[:Tn, 5:6], in_=st_[:Tn, 5:6]), reads=[st_.b], writes=[st_.b])
            S.op("dve", lambda e, Tn=Tn: e.tensor_scalar(out=vvn[:Tn], in0=cg[:Tn, 512:1024], scalar1=st_[:Tn, 2:3], scalar2=st_[:Tn, 5:6],
                                                         op0=ALU.subtract, op1=ALU.mult), reads=[cg.b, st_.b], writes=[vvn.b])
            S.op("dve", lambda e, Tn=Tn: e.tensor_tensor(out=vvn[:Tn], in0=vvn[:Tn], in1=lcg[:Tn], op=ALU.mult), reads=[vvn.b, lcg.b], writes=[vvn.b])
            S.op("dve", lambda e, Tn=Tn: e.tensor_tensor(out=vvn[:Tn], in0=vvn[:Tn], in1=lcb[:Tn], op=ALU.add), reads=[vvn.b, lcb.b], writes=[vvn.b])
            if not t["prompt"]:
                S.op("sp", lambda e, t=t: e.dma_start(out=G["o_gvs"][t["s"] * 64:(t["s"] + 1) * 64, :], in_=vvn[:64]), reads=[vvn.b],
                     writes=[S.db("ogvs")], chan="st_vvn")
            S.op("act", lambda e, Tn=Tn: e.copy(out=vvb[:Tn], in_=vvn[:Tn]), reads=[vvn.b], writes=[vvb.b])
            ps = PSF()
            for g in range(4):
                S.op("pe", lambda e, g=g, ps=ps, Tn=Tn: e.matmul(ps[:Tn, g * 128:(g + 1) * 128], lhsT=wsT[:Tn, g, :Tn], rhs=vvb[:Tn, g * 128:(g + 1) * 128],
                                                                start=True, stop=True), reads=[wsT.b, vvb.b], writes=[ps.b])
            S.op("dve", lambda e, ps=ps, Tn=Tn: e.tensor_tensor(out=t1[:Tn, 0:512].rearrange("p (g d) -> p g d", g=4),
                                                               in0=ps[:Tn, :].rearrange("p (g d) -> p g d", g=4),
                                                               in1=bsc[:Tn, :].unsqueeze(2).to_broadcast([Tn, 4, 128]), op=ALU.add),
                 reads=[ps.b, bsc.b], writes=[t1.b])
            S.op("dve", lambda e, Tn=Tn: e.tensor_tensor(out=cout[:Tn], in0=t1[:Tn, 0:512], in1=cg[:Tn, 0:512], op=ALU.mult), reads=[t1.b, cg.b], writes=[cout.b])
            pb = PSB()
            for c in range(4):
                S.op("pe", lambda e, c=c, pb=pb, Tn=Tn: e.transpose(out=pb[:, c * Tn:(c + 1) * Tn], in_=cout[:Tn, c * 128:(c + 1) * 128], identity=identb[:Tn, :Tn]),
                     reads=[cout.b, identb.b], writes=[pb.b])
            S.op("act", lambda e, pb=pb, Tn=Tn: e.copy(out=cyT[:, 0:4, :Tn], in_=pb[:, 0:4 * Tn].rearrange("p (c t) -> p c t", c=4)), reads=[pb.b], writes=[cyT.b])
            if t["first"]:
                if t["prompt"]:
                    S.op("pool", lambda e: e.memset(cbuf[:, :, 0:3], 0.0), writes=[cbuf.b])
                    S.op("pool", lambda e: e.memset(ST[:], 0.0), writes=[ST.b])
                else:
                    S.op("sp", lambda e, t=t: e.dma_start(out=c3[0:3, :], in_=G["scs"][t["s"]]), writes=[c3.b], chan="c3")
                    ps = PSF()
                    for c in range(8):
                        S.op("pe", lambda e, c=c, ps=ps: e.transpose(out=ps[:, c * 3:(c + 1) * 3], in_=c3[0:3, c * 128:(c + 1) * 128], identity=identf[0:3, 0:3]),
                             reads=[c3.b, identf.b], writes=[ps.b])
                    S.op("act", lambda e, ps=ps: e.copy(out=cbuf[:, :, 0:3], in_=ps[:, 0:24].rearrange("p (c t) -> p c t", c=8)), reads=[ps.b], writes=[cbuf.b])
                    S.op("sp", lambda e, t=t: e.dma_start(out=stin[:], in_=G["sssd"][t["s"]].rearrange("h p n -> p h n")), writes=[stin.b], chan="stin")
                    ps = PSF()
                    for h in range(8):
                        S.op("pe", lambda e, h=h, ps=ps: e.transpose(out=ps[:, h * 64:(h + 1) * 64], in_=stin[:, h, :], identity=identf[0:64, 0:64]),
                             reads=[stin.b, identf.b], writes=[ps.b])
                    S.op("act", lambda e, ps=ps: e.copy(out=ST[:], in_=ps[:, :].rearrange("p (h q) -> p h q", h=8)), reads=[ps.b], writes=[ST.b])
            else:
                S.op("pool", lambda e: e.tensor_copy(out=cbuf[:, :, 0:3], in_=cbuf[:, :, 128:131]), reads=[cbuf.b], writes=[cbuf.b])
            for half in range(2):
                ps = PSF()
                mm_feat(ps, 1536 + half * 512, 4, w, hT, Tn)
                S.op("act", lambda e, ps=ps, half=half, Tn=Tn: e.copy(out=cbuf[:, half * 4:half * 4 + 4, 3:3 + Tn],
                                                                     in_=ps[:, 0:4 * Tn].rearrange("p (c t) -> p c t", c=4)), reads=[ps.b], writes=[cbuf.b])
            if t["last"]:
                for half in range(2):
                    ps = PSF()
                    for cc in range(4):
                        c = half * 4 + cc
                        S.op("pe", lambda e, c=c, cc=cc, ps=ps, Tn=Tn: e.transpose(out=ps[0:3, cc * 128:(cc + 1) * 128], in_=cbuf[:, c, Tn:Tn + 3], identity=identf[:, :]),
                             reads=[cbuf.b, identf.b], writes=[ps.b])
                    S.op("act", lambda e, ps=ps, half=half: e.copy(out=c3[0:3, half * 512:(half + 1) * 512], in_=ps[0:3, :]), reads=[ps.b], writes=[c3.b])
                dst = G["o_csp"] if t["prompt"] else G["o_css"][t["s"]]
                S.op("sp", lambda e, dst=dst: e.dma_start(out=dst, in_=c3[0:3, :]), reads=[c3.b], writes=[S.db("ocs")], chan="st_c3")
            for k in range(4):
                for c in range(8):
                    eng = "dve"
                    if k == 0:
                        S.op(eng, lambda e, c=c, Tn=Tn: e.tensor_scalar(out=cacc[c][:, :Tn], in0=cbuf[:, c, 0:Tn], scalar1=cdw[:, c, 0:1], scalar2=None, op0=ALU.mult),
                             reads=[cbuf.b, cdw.b], writes=[cacc[c].b])
                    else:
                        S.op(eng, lambda e, c=c, k=k, Tn=Tn: e.scalar_tensor_tensor(out=cacc[c][:, :Tn], in0=cbuf[:, c, k:k + Tn], scalar=cdw[:, c, k:k + 1],
                                                                                    in1=cacc[c][:, :Tn], op0=ALU.mult, op1=ALU.add),
                             reads=[cbuf.b, cdw.b, cacc[c].b], writes=[cacc[c].b])
            for c in range(8):
                S.op("act", lambda e, c=c, Tn=Tn: e.activation(out=xc[:, c, :Tn], in_=cacc[c][:, :Tn], func=AF.Silu, bias=cdb[:, c:c + 1], scale=1.0),
                     reads=[cacc[c].b, cdb.b], writes=[xc.b])
            for ch in range(Tn // 64):
                c0 = ch * 64
                cs = slice(c0, c0 + 64)
                ps = PSF()
                for c in range(4):
                    S.op("pe", lambda e, c=c, ps=ps, cs=cs: e.transpose(out=ps[0:64, c * 128:(c + 1) * 128], in_=xc[:, c, cs], identity=identf[:, :]),
                         reads=[xc.b, identf.b], writes=[ps.b])
                S.op("act", lambda e, ps=ps: e.copy(out=xstok[0:64, :], in_=ps[0:64, :]), reads=[ps.b], writes=[xstok.b])
                ps = PSF()
                for g in range(2):
                    S.op("pe", lambda e, g=g, ps=ps, cs=cs: e.transpose(out=ps[0:64, g * 128:(g + 1) * 128], in_=xc[:, 4 + g, cs], identity=identf[:, :]),
                         reads=[xc.b, identf.b], writes=[ps.b])
                S.op("act", lambda e, ps=ps: e.copy(out=btok[:, 0, :], in_=ps[0:64, 0:256]), reads=[ps.b], writes=[btok.b])
                ps = PSF()
                for c in range(8):
                    S.op("pe", lambda e, c=c, ps=ps, cs=cs: e.matmul(ps[0:64, :], lhsT=hT[:, c, cs], rhs=w[:, c, 1024:1536], start=(c == 0), stop=(c == 7)),
                         reads=[hT.b, w.b], writes=[ps.b])
                S.op("act", lambda e, ps=ps: e.activation(out=gate[0:64, :], in_=ps[0:64, :], func=AF.Silu), reads=[ps.b], writes=[gate.b])
                ps = PSF()
                for c in range(8):
                    S.op("pe", lambda e, c=c, ps=ps, cs=cs: e.matmul(ps[0:64, 0:8], lhsT=hT[:, c, cs], rhs=w[:, c, 2560:2568], start=(c == 0), stop=(c == 7)),
                         reads=[hT.b, w.b], writes=[ps.b])
                S.op("dve", lambda e, ps=ps: e.tensor_tensor(out=dtt[0:64, :], in0=ps[0:64, 0:8], in1=dtb[0:64, :], op=ALU.add), reads=[ps.b, dtb.b], writes=[dtt.b])
                S.op("act", lambda e: e.activation(out=dtt[0:64, :], in_=dtt[0:64, :], func=AF.Exp), reads=[dtt.b], writes=[dtt.b])
                S.op("act", lambda e: e.activation(out=dtt[0:64, :], in_=dtt[0:64, :], func=AF.Ln, bias=1.0, scale=1.0), reads=[dtt.b], writes=[dtt.b])
                S.op("dve", lambda e: e.tensor_tensor(out=dta[0:64, :], in0=dtt[0:64, :], in1=Arep[0:64, :], op=ALU.mult), reads=[dtt.b, Arep.b], writes=[dta.b])
                S.op("dve", lambda e: e.tensor_copy(out=dtab[:], in_=dta[0:64, :].unsqueeze(2).to_broadcast([64, 8, 64])), reads=[dta.b], writes=[dtab.b])
                psa = PSF()
                S.op("pe", lambda e, psa=psa: e.matmul(psa[0:64, 0:8], lhsT=tri[:, :], rhs=dta[0:64, :], start=True, stop=True), reads=[tri.b, dta.b], writes=[psa.b])
                S.op("act", lambda e, psa=psa: e.copy(out=acs[:], in_=psa[0:64, 0:8]), reads=[psa.b], writes=[acs.b])
                S.op("dve", lambda e, psa=psa: e.tensor_scalar(out=nacs[:], in0=psa[0:64, 0:8], scalar1=-1.0, scalar2=None, op0=ALU.mult), reads=[psa.b], writes=[nacs.b])
                S.op("act", lambda e: e.activation(out=eacs[:], in_=acs[:], func=AF.Exp), reads=[acs.b], writes=[eacs.b])
                psl = PSF()
                S.op("pe", lambda e, psl=psl: e.matmul(psl[:, 0:8], lhsT=ones64[:, :], rhs=dta[0:64, :], start=True, stop=True), reads=[ones64.b, dta.b], writes=[psl.b])
                S.op("act", lambda e, psl=psl: e.activation(out=cdec[:], in_=psl[:, 0:8], func=AF.Exp), reads=[psl.b], writes=[cdec.b])
                S.op("dve", lambda e, psl=psl: e.tensor_tensor(out=dec[:], in0=psl[0:64, 0:8], in1=acs[:], op=ALU.subtract), reads=[psl.b, acs.b], writes=[dec.b])
                S.op("act", lambda e: e.activation(out=dec[:], in_=dec[:], func=AF.Exp), reads=[dec.b], writes=[dec.b])
                psb_ = PSF()
                for h in range(8):
                    S.op("pe", lambda e, h=h, psb_=psb_: e.matmul(psb_[0:64, h * 64:(h + 1) * 64], lhsT=dtab[:, h, :], rhs=tri[:, :], start=True, stop=True),
                         reads=[dtab.b, tri.b], writes=[psb_.b])
                S.op("dve", lambda e, psb_=psb_: e.tensor_tensor(out=LT[:], in0=psb_[0:64, :].rearrange("p (h i) -> p h i", h=8),
                                                                 in1=negm[:, :].unsqueeze(1).to_broadcast([64, 8, 64]), op=ALU.add),
                     reads=[psb_.b, negm.b], writes=[LT.b])
                for h in range(8):
                    S.op("act", lambda e, h=h: e.activation(out=LT[:, h, :], in_=LT[:, h, :], func=AF.Exp, bias=nacs[:, h:h + 1], scale=1.0),
                         reads=[LT.b, nacs.b], writes=[LT.b])
                psc = PSF()
                for g in range(2):
                    S.op("pe", lambda e, g=g, psc=psc, cs=cs: e.matmul(psc[0:64, g * 64:(g + 1) * 64], lhsT=xc[:, 4 + g, cs], rhs=xc[:, 6 + g, cs], start=True, stop=True),
                         reads=[xc.b], writes=[psc.b])
                S.op("dve", lambda e, psc=psc: e.tensor_tensor(out=MT[:].rearrange("p (g r) i -> p g r i", g=2), in0=LT[:].rearrange("p (g r) i -> p g r i", g=2),
                                                               in1=psc[0:64, 0:128].rearrange("p (g i) -> p g i", g=2).unsqueeze(2).to_broadcast([64, 2, 4, 64]),
                                                               op=ALU.mult), reads=[psc.b, LT.b], writes=[MT.b])
                S.op("dve", lambda e: e.tensor_tensor(out=xdt[:], in0=xstok[0:64, :].rearrange("p (h q) -> p h q", h=8),
                                                      in1=dtt[0:64, :].unsqueeze(2).to_broadcast([64, 8, 64]), op=ALU.mult), reads=[xstok.b, dtt.b], writes=[xdt.b])
                S.op("dve", lambda e: e.tensor_tensor(out=xdd[:], in0=xdt[:], in1=dec[:, :].unsqueeze(2).to_broadcast([64, 8, 64]), op=ALU.mult),
                     reads=[xdt.b, dec.b], writes=[xdd.b])
                psy, pso, psn = PSF(), PSF(), PSF()
                for h in range(8):
                    S.op("pe", lambda e, h=h, psy=psy: e.matmul(psy[0:64, h * 64:(h + 1) * 64], lhsT=MT[:, h, :], rhs=xdt[:, h, :], start=True, stop=True),
                         reads=[MT.b, xdt.b], writes=[psy.b])
                for h in range(8):
                    S.op("pe", lambda e, h=h, pso=pso, cs=cs: e.matmul(pso[0:64, h * 64:(h + 1) * 64], lhsT=xc[:, 6 + h // 4, cs], rhs=ST[:, h, :], start=True, stop=True),
                         reads=[xc.b, ST.b], writes=[pso.b])
                for h in range(8):
                    S.op("pe", lambda e, h=h, psn=psn: e.matmul(psn[:, h * 64:(h + 1) * 64], lhsT=btok[:, 0, (h // 4) * 128:(h // 4 + 1) * 128], rhs=xdd[:, h, :],
                                                                start=True, stop=True), reads=[btok.b, xdd.b], writes=[psn.b])
                S.op("act", lambda e, psy=psy: e.copy(out=ydg[:], in_=psy[0:64, :]), reads=[psy.b], writes=[ydg.b])
                S.op("dve", lambda e, pso=pso: e.tensor_tensor(out=yt[0:64, :].rearrange("p (h q) -> p h q", h=8), in0=pso[0:64, :].rearrange("p (h q) -> p h q", h=8),
                                                               in1=eacs[:, :].unsqueeze(2).to_broadcast([64, 8, 64]), op=ALU.mult), reads=[pso.b, eacs.b], writes=[yt.b])
                S.op("dve", lambda e: e.tensor_tensor(out=yt[0:64, :], in0=yt[0:64, :], in1=ydg[:], op=ALU.add), reads=[yt.b, ydg.b], writes=[yt.b])
                S.op("pool", lambda e: e.tensor_tensor(out=ST[:], in0=ST[:], in1=cdec[:, :].unsqueeze(2).to_broadcast([128, 8, 64]), op=ALU.mult),
                     reads=[ST.b, cdec.b], writes=[ST.b])
                S.op("dve", lambda e, psn=psn: e.tensor_tensor(out=ST[:], in0=psn[:, :].rearrange("p (h q) -> p h q", h=8), in1=ST[:], op=ALU.add),
                     reads=[psn.b, ST.b], writes=[ST.b])
                S.op("dve", lambda e: e.tensor_tensor(out=xdt[:], in0=xstok[0:64, :].rearrange("p (h q) -> p h q", h=8),
                                                      in1=dsk[0:64, :].unsqueeze(2).to_broadcast([64, 8, 64]), op=ALU.mult), reads=[xstok.b, dsk.b, xdt.b], writes=[xdt.b])
                S.op("dve", lambda e: e.tensor_tensor(out=yt[0:64, :], in0=yt[0:64, :], in1=xdt[:].rearrange("p h q -> p (h q)"), op=ALU.add),
                     reads=[yt.b, xdt.b], writes=[yt.b])
                S.op("dve", lambda e: e.tensor_tensor(out=yt[0:64, :], in0=yt[0:64, :], in1=gate[0:64, :], op=ALU.mult), reads=[yt.b, gate.b], writes=[yt.b])
                S.op("dve", lambda e: e.memset(st_[0:64, 0:2], 0.0), writes=[st_.b])
                for g in range(2):
                    S.op("act", lambda e, g=g: e.activation(out=ydg[:, g * 256:(g + 1) * 256], in_=yt[0:64, g * 256:(g + 1) * 256], func=AF.Square,
                                                            accum_out=st_[0:64, g:g + 1]), reads=[yt.b, st_.b], writes=[ydg.b, st_.b])
                S.op("act", lambda e: e.activation(out=st_[0:64, 2:4], in_=st_[0:64, 0:2], func=AF.Sqrt, scale=1.0 / 256, bias=1e-6), reads=[st_.b], writes=[st_.b])
                S.op("dve", lambda e: e.reciprocal(out=st_[0:64, 2:4], in_=st_[0:64, 2:4]), reads=[st_.b], writes=[st_.b])
                S.op("dve", lambda e: e.tensor_tensor(out=yt[0:64, :].rearrange("p (g q) -> p g q", g=2), in0=yt[0:64, :].rearrange("p (g q) -> p g q", g=2),
                                                      in1=st_[0:64, 2:4].unsqueeze(2).to_broadcast([64, 2, 256]), op=ALU.mult), reads=[yt.b, st_.b], writes=[yt.b])
                S.op("dve", lambda e: e.tensor_tensor(out=cout[0:64, :], in0=yt[0:64, :], in1=ndg[0:64, :], op=ALU.mult), reads=[yt.b, ndg.b], writes=[cout.b])
                pb = PSB()
                for c in range(4):
                    S.op("pe", lambda e, c=c, pb=pb: e.transpose(out=pb[:, c * 64:(c + 1) * 64], in_=cout[0:64, c * 128:(c + 1) * 128], identity=identb[0:64, 0:64]),
                         reads=[cout.b, identb.b], writes=[pb.b])
                S.op("act", lambda e, pb=pb, cs=cs: e.copy(out=cyT[:, 4:8, cs], in_=pb[:, 0:256].rearrange("p (c t) -> p c t", c=4)), reads=[pb.b], writes=[cyT.b])
            if t["last"]:
                for half in range(2):
                    ps = PSF()
                    for hh in range(4):
                        h = half * 4 + hh
                        S.op("pe", lambda e, h=h, hh=hh, ps=ps: e.transpose(out=ps[0:64, hh * 128:(hh + 1) * 128], in_=ST[:, h, :], identity=identf[:, :]),
                             reads=[ST.b, identf.b], writes=[ps.b])
                    S.op("act", lambda e, ps=ps, half=half: e.copy(out=stin[:, half * 4:half * 4 + 4, :], in_=ps[0:64, :].rearrange("p (h n) -> p h n", h=4)),
                         reads=[ps.b], writes=[stin.b])
                dst = G["o_ssdp"] if t["prompt"] else G["o_ssds"][t["s"]]
                S.op("sp", lambda e, dst=dst: e.dma_start(out=dst.rearrange("h p n -> p h n"), in_=stin[:]), reads=[stin.b], writes=[S.db("ossd")], chan="st_stin")
            for n in range(2):
                ps = PSF()
                mm_tok(ps, n * 512, 512, wo, cyT, Tn)
                S.op("dve", lambda e, n=n, ps=ps, x=x, Tn=Tn: e.tensor_tensor(out=x[:Tn, n * 512:(n + 1) * 512], in0=ps[:Tn, :], in1=x[:Tn, n * 512:(n + 1) * 512], op=ALU.add),
                     reads=[ps.b, x.b], writes=[x.b])
            S.op("sp", lambda e, x=x, t=t, Tn=Tn: e.dma_start(out=XS[t["tok0"]:t["tok0"] + Tn, :], in_=x[:Tn]),
                 reads=[x.b], writes=[S.db("XS%d" % ti)], chan="st_" + x.b.name)
        S.emit()


_W_NAMES = ["norm_mix_g", "norm_cross_g", "norm_ffn_g", "norm_final_g", "w_in_ab", "conv_a_w", "conv_a_b", "ln_a_g", "ln_a_b",
            "lam_q1", "lam_k1", "lam_q2", "lam_k2", "subln_g", "w_out_ab", "w_in_cd", "ln_c_g", "ln_c_b", "gm_w_s", "gm_b_s",
            "conv_d_w", "conv_d_b", "dt_bias", "a_log", "d_skip", "norm_d_g", "w_out_cd", "w_xq", "w_xk", "w_xv", "w_xo", "w_pq",
            "sub_keys", "expert_u", "expert_v"]


def _layout_weights(inp):
    f = lambda a: np.ascontiguousarray(np.asarray(a, dtype=np.float32))
    W = {}
    W["norm_mix_g"] = f(inp["norm_mix_g"]); W["norm_cross_g"] = f(inp["norm_cross_g"]); W["norm_ffn_g"] = f(inp["norm_ffn_g"])
    W["norm_final_g"] = f(inp["norm_final_g"]).reshape(1, D)
    W["w_in_ab"] = f(inp["w_in_ab"][0]); W["conv_a_w"] = f(inp["conv_a_w"][0])
    for k in ("conv_a_b", "ln_a_g", "ln_a_b", "lam_q1", "lam_k1", "lam_q2", "lam_k2", "subln_g", "ln_c_g", "ln_c_b", "conv_d_b",
              "dt_bias", "a_log", "d_skip", "norm_d_g"):
        W[k] = f(inp[k][0]).reshape(1, -1)
    W["w_out_ab"] = f(inp["w_out_ab"][0]); W["w_in_cd"] = f(inp["w_in_cd"][0]); W["gm_w_s"] = f(inp["gm_w_s"][0]); W["gm_b_s"] = f(inp["gm_b_s"][0])
    W["conv_d_w"] = f(inp["conv_d_w"][0]); W["w_out_cd"] = f(inp["w_out_cd"][0])
    for k in ("w_xq", "w_xk", "w_xv", "w_xo", "w_pq", "sub_keys", "expert_u", "expert_v"):
        W[k] = f(inp[k])
    return W


def run_cores(inp, n_cores, NPT, NSS, PL, prompt_of_core, trace=False, stop_after=None, split=False):
    f = lambda a: np.ascontiguousarray(np.asarray(a, dtype=np.float32))
    W = _layout_weights(inp)
    in_maps = []
    for c in range(n_cores):
        b = prompt_of_core(c)
        sl = slice(c * NSS, (c + 1) * NSS)
        m = dict(W)
        m["x_prompt"] = f(inp["x_prompt"][b])
        m["x_sample"] = f(inp["x_sample"][sl]).reshape(NSS * 64, D)
        m["cache_attn_k"] = f(inp["cache_attn_k"][0, sl]).reshape(NSS, PL, 512)
        m["cache_attn_v"] = f(inp["cache_attn_v"][0, sl]).reshape(NSS, PL, 512)
        m["state_conv_a"] = f(inp["state_conv_a"][0, sl])
        m["state_ssd"] = f(inp["state_ssd"][0, sl])
        m["state_conv_ssm"] = f(inp["state_conv_ssm"][0, sl])
        m["cache_mem_k"] = f(inp["cache_mem_k"][:, sl]).reshape(2, NSS, 256, D)
        m["cache_mem_v"] = f(inp["cache_mem_v"][:, sl]).reshape(2, NSS, 256, D)
        m["mem_prompt"] = f(inp["mem_prompt"][b])
        if split:
            rank = c // 4
            nown = NPT // 2
            m["rowidx"] = np.ascontiguousarray(((rank * nown + np.arange(nown))[None, :] * 128 + np.arange(128)[:, None]).astype(np.int32))
        in_maps.append(m)
    nc = build(NPT, NSS, PL, stop_after=stop_after, split=split)
    res = run_bass_kernel_spmd(nc, in_maps, core_ids=list(range(n_cores)), **({"trace": True} if trace else {}))
    return res


def assemble(R, nb, n_cores, NPT, NSS, split=False):
    L = NPT * 128
    pc = list(range(nb))
    cat = lambda key, cores: np.stack([np.asarray(R[c][key]) for c in cores])
    if split:
        y_prompt = np.stack([np.concatenate([np.asarray(R[b]["y_p"]), np.asarray(R[b + 4]["y_p"])], axis=0) for b in pc]).reshape(nb, L, D)
    else:
        y_prompt = cat("y_p", pc).reshape(nb, L, D)
    y_sample = cat("y_s", range(n_cores)).reshape(n_cores * NSS, 64, D)
    kp = cat("o_kp", pc).reshape(1, nb, L, 4, 128)
    vp = cat("o_vp", pc).reshape(1, nb, L, 4, 128)
    cap = cat("o_cap", pc).reshape(1, nb, 30, 512)
    ssdp = cat("o_ssdp", pc).reshape(1, nb, 8, 64, 128)
    csp = cat("o_csp", pc).reshape(1, nb, 3, 1024)
    mk = np.stack([np.asarray(R[c]["o_mk"]) for c in pc], axis=1).reshape(2, nb, 256, 4, 256)
    mv = np.stack([np.asarray(R[c]["o_mv"]) for c in pc], axis=1).reshape(2, nb, 256, 4, 256)
    ns = n_cores * NSS
    ks = cat("o_ks", range(n_cores)).reshape(1, ns, 64, 4, 128)
    vs = cat("o_vs", range(n_cores)).reshape(1, ns, 64, 4, 128)
    cas = cat("o_cas", range(n_cores)).reshape(1, ns, 30, 512)
    gvs = cat("o_gvs", range(n_cores)).reshape(1, ns, 64, 4, 128)
    ssds = cat("o_ssds", range(n_cores)).reshape(1, ns, 8, 64, 128)
    css = cat("o_css", range(n_cores)).reshape(1, ns, 3, 1024)
    outs = (y_prompt, y_sample, kp, vp, cap, ssdp, csp, mk, mv, ks, vs, cas, gvs, ssds, css)
    return tuple(np.ascontiguousarray(o, dtype=np.float32) for o in outs)


def kernel(x_prompt, x_sample, cache_attn_k, cache_attn_v, state_conv_a, state_ssd, state_conv_ssm,
           cache_mem_k, cache_mem_v, mem_prompt,
           norm_mix_g, norm_cross_g, norm_ffn_g, norm_final_g,
           w_in_ab, conv_a_w, conv_a_b, ln_a_g, ln_a_b, lam_q1, lam_k1, lam_q2, lam_k2, subln_g, w_out_ab,
           w_in_cd, ln_c_g, ln_c_b, gm_w_s, gm_b_s, conv_d_w, conv_d_b, dt_bias, a_log, d_skip, norm_d_g, w_out_cd,
           w_xq, w_xk, w_xv, w_xo, w_pq, sub_keys, expert_u, expert_v):
    inp = dict(x_prompt=x_prompt, x_sample=x_sample, cache_attn_k=cache_attn_k, cache_attn_v=cache_attn_v, state_conv_a=state_conv_a,
               state_ssd=state_ssd, state_conv_ssm=state_conv_ssm, cache_mem_k=cache_mem_k, cache_mem_v=cache_mem_v, mem_prompt=mem_prompt,
               norm_mix_g=norm_mix_g, norm_cross_g=norm_cross_g, norm_ffn_g=norm_ffn_g, norm_final_g=norm_final_g,
               w_in_ab=w_in_ab, conv_a_w=conv_a_w, conv_a_b=conv_a_b, ln_a_g=ln_a_g, ln_a_b=ln_a_b, lam_q1=lam_q1, lam_k1=lam_k1,
               lam_q2=lam_q2, lam_k2=lam_k2, subln_g=subln_g, w_out_ab=w_out_ab, w_in_cd=w_in_cd, ln_c_g=ln_c_g, ln_c_b=ln_c_b,
               gm_w_s=gm_w_s, gm_b_s=gm_b_s, conv_d_w=conv_d_w, conv_d_b=conv_d_b, dt_bias=dt_bias, a_log=a_log, d_skip=d_skip,
               norm_d_g=norm_d_g, w_out_cd=w_out_cd, w_xq=w_xq, w_xk=w_xk, w_xv=w_xv, w_xo=w_xo, w_pq=w_pq, sub_keys=sub_keys,
               expert_u=expert_u, expert_v=expert_v)
    nb = x_prompt.shape[0]
    L = x_prompt.shape[1]
    n_cores = 8
    NSS = x_sample.shape[0] // n_cores
    PL = cache_attn_k.shape[2]
    split = (nb == 4)
    res = run_cores(inp, n_cores, L // 128, NSS, PL, lambda c: c % nb, split=split)
    return assemble(res.results, nb, n_cores, L // 128, NSS, split=split)
```

```python
import numpy as np
from contextlib import ExitStack
import concourse.bass as bass
import concourse.mybir as mybir
from concourse.bass_utils import run_bass_kernel_spmd

F32 = mybir.dt.float32
BF16 = mybir.dt.bfloat16
I32 = mybir.dt.int32
U32 = mybir.dt.uint32
AF = mybir.ActivationFunctionType
ALU = mybir.AluOpType
AX = mybir.AxisListType
D = 1024
NEG = -1.0e30


class Buf:
    __slots__ = ("name", "w", "r")

    def __init__(self, name):
        self.name = name
        self.w = None
        self.r = {}


class Sched:
    ENGS = ("pe", "act", "dve", "pool", "sp")

    def __init__(self, nc, es):
        self.nc = nc
        self.sems = {}
        self.esem = {}
        for e in ("pe", "act", "dve", "pool"):
            s = es.enter_context(nc.semaphore("es_" + e))
            self.sems["e_" + e] = s
            self.esem[e] = "e_" + e
        self.ecnt = {e: 0 for e in self.esem}
        self.nchan = 28
        self.ccnt = {}
        for pre in ("c", "g"):
            for i in range(self.nchan):
                self.sems["%s%d" % (pre, i)] = es.enter_context(nc.semaphore("%ss%d" % (pre, i)))
                self.ccnt["%s%d" % (pre, i)] = 0
        self.known = {e: {} for e in self.ENGS}
        self.reset_phase()

    def reset_phase(self):
        self.ops = {e: [] for e in self.ENGS}
        self.chanmap = {}
        self.dbufs = {}

    def chan(self, eng, name):
        pre = "g" if eng == "pool" else "c"
        key = (pre, name)
        if key not in self.chanmap:
            n = sum(1 for (p, _) in self.chanmap if p == pre)
            assert n < self.nchan, "too many dma channels"
            self.chanmap[key] = "%s%d" % (pre, n)
        return self.chanmap[key]

    def db(self, name):
        if name not in self.dbufs:
            self.dbufs[name] = Buf(name)
        return self.dbufs[name]

    def capture(self, f):
        self.cap = []
        f()
        c, self.cap = self.cap, None
        return c

    def replay(self, cap, n):
        for _ in range(min(n, len(cap))):
            a = cap.pop(0)
            self.op(*a[0], **a[1])

    def op(self, eng, fn, reads=(), writes=(), chan=None, amt=16):
        if getattr(self, "cap", None) is not None:
            self.cap.append(((eng, fn), dict(reads=list(reads), writes=list(writes), chan=chan, amt=amt)))
            return
        inter = getattr(self, "inter", None)
        if inter:
            self._n = getattr(self, "_n", 0) + 1
            if self._n % 2 == 0:
                self.inter = None
                self.replay(inter, 1)
                self.inter = inter
        need = {}
        own = self.esem.get(eng) if chan is None else None

        def add(sig, raw):
            if sig is None:
                return
            k, v = sig
            if k == own and not raw:
                return
            if need.get(k, 0) < v:
                need[k] = v

        for b in reads:
            add(b.w, True)
        for b in writes:
            add(b.w, False)
            for k, v in b.r.items():
                add((k, v), False)
        kn = self.known[eng]
        waits = []
        for k, v in need.items():
            if kn.get(k, 0) < v:
                waits.append((k, v))
                kn[k] = v
        if chan is not None:
            k = self.chan(eng, chan)
            self.ccnt[k] += amt
            sig = (k, self.ccnt[k])
        else:
            k = self.esem[eng]
            self.ecnt[eng] += 1
            sig = (k, self.ecnt[eng])
            amt = 1
        self.ops[eng].append((waits, fn, k, amt))
        for b in reads:
            if b.r.get(sig[0], 0) < sig[1]:
                b.r[sig[0]] = sig[1]
        for b in writes:
            b.w = sig
            b.r = {}

    def drain(self):
        waits = []
        kn = self.known["sp"]
        for k, v in list(self.ccnt.items()) + [(self.esem[e], self.ecnt[e]) for e in self.esem]:
            if v > 0 and kn.get(k, 0) < v:
                waits.append((k, v))
                kn[k] = v
        self.ops["sp"].append((waits, None, None, 0))

    def emit(self):
        self.drain()
        nc = self.nc
        with nc.Block() as blk:
            def run(name):
                def f(e):
                    for waits, fn, k, amt in self.ops[name]:
                        for wk, wv in waits:
                            e.wait_ge(self.sems[wk], wv)
                        if fn is not None:
                            fn(e).then_inc(self.sems[k], amt)
                return f
            blk.tensor(run("pe"))
            blk.scalar(run("act"))
            blk.vector(run("dve"))
            blk.gpsimd(run("pool"))
            blk.sync(run("sp"))
        for e in self.ENGS:
            for k, v in list(self.ccnt.items()) + [(self.esem[x], self.ecnt[x]) for x in self.esem]:
                self.known[e][k] = v
        self.reset_phase()


class T:
    def __init__(self, ap, name, nsub=1):
        self.ap = ap
        self.b = Buf(name)
        self.sub = [Buf(name + str(i)) for i in range(nsub)] if nsub > 1 else None

    def __getitem__(self, k):
        return self.ap[k]


def build(NPT, NSS, PL, lam_inits=(0.8 - 0.6, 0.0), stop_after=None, split=False):
    nc = bass.Bass("TRN2", target_bir_lowering=False)
    NTILE = NPT + NSS
    NTOK = NPT * 128 + NSS * 64
    NPK = PL // 128
    LAM0 = lam_inits[0]

    def din(name, shape, dt=F32):
        return nc.dram_tensor(name, list(shape), dt, kind="ExternalInput").ap()

    def dout(name, shape, dt=F32):
        return nc.dram_tensor(name, list(shape), dt, kind="ExternalOutput").ap()

    def dscr(name, shape, dt=F32):
        return nc.dram_tensor(name, list(shape), dt, kind="Internal").ap()

    xp = din("x_prompt", [NPT * 128, D])
    xs = din("x_sample", [NSS * 64, D])
    ck = din("cache_attn_k", [NSS, PL, 512])
    cv = din("cache_attn_v", [NSS, PL, 512])
    sca = din("state_conv_a", [NSS, 30, 512])
    sssd = din("state_ssd", [NSS, 8, 64, 128])
    scs = din("state_conv_ssm", [NSS, 3, 1024])
    cmk = din("cache_mem_k", [2, NSS, 256, D])
    cmv = din("cache_mem_v", [2, NSS, 256, D])
    memp = din("mem_prompt", [256, D])
    norm_mix_g = din("norm_mix_g", [2, D])
    norm_cross_g = din("norm_cross_g", [2, D])
    norm_ffn_g = din("norm_ffn_g", [2, D])
    norm_final_g = din("norm_final_g", [1, D])
    w_in_ab = din("w_in_ab", [D, 2560])
    conv_a_w = din("conv_a_w", [31, 512])
    conv_a_b = din("conv_a_b", [1, 512])
    ln_a_g = din("ln_a_g", [1, 512])
    ln_a_b = din("ln_a_b", [1, 512])
    lam_q1 = din("lam_q1", [1, 64])
    lam_k1 = din("lam_k1", [1, 64])
    lam_q2 = din("lam_q2", [1, 64])
    lam_k2 = din("lam_k2", [1, 64])
    subln_g = din("subln_g", [1, 128])
    w_out_ab = din("w_out_ab", [D, D])
    w_in_cd = din("w_in_cd", [D, 2568])
    ln_c_g = din("ln_c_g", [1, 512])
    ln_c_b = din("ln_c_b", [1, 512])
    gm_w_s = din("gm_w_s", [4, 128, 128])
    gm_b_s = din("gm_b_s", [4, 128])
    conv_d_w = din("conv_d_w", [4, 1024])
    conv_d_b = din("conv_d_b", [1, 1024])
    dt_bias = din("dt_bias", [1, 8])
    a_log = din("a_log", [1, 8])
    d_skip = din("d_skip", [1, 8])
    norm_d_g = din("norm_d_g", [1, 512])
    w_out_cd = din("w_out_cd", [D, D])
    w_xq = din("w_xq", [2, D, D])
    w_xk = din("w_xk", [2, D, D])
    w_xv = din("w_xv", [2, D, D])
    w_xo = din("w_xo", [2, D, D])
    w_pq = din("w_pq", [2, D, 2048])
    sub_keys = din("sub_keys", [2, 2, 128, 128])
    expert_u = din("expert_u", [2, 16384, D])
    expert_v = din("expert_v", [2, 16384, D])

    NOWN = NPT // 2 if split else NPT
    CHT = 4
    y_p = dout("y_p", [NOWN * 128, D])
    if split:
        rowidx = din("rowidx", [128, NOWN], I32)
        SND = dscr("SND", [NOWN * 128, D])
        XOWN = dscr("XOWN", [NOWN * 128, D])
        RCV = [dscr("RCV%d" % j, [2 * CHT * 128, D]) for j in range(NOWN // CHT)]
    y_s = dout("y_s", [NSS * 64, D])
    o_kp = dout("o_kp", [NPT * 128, 512])
    o_vp = dout("o_vp", [NPT * 128, 512])
    o_cap = dout("o_cap", [30, 512])
    o_ssdp = dout("o_ssdp", [8, 64, 128])
    o_csp = dout("o_csp", [3, 1024])
    o_mk = dout("o_mk", [2, 256, D])
    o_mv = dout("o_mv", [2, 256, D])
    o_ks = dout("o_ks", [NSS * 64, 512])
    o_vs = dout("o_vs", [NSS * 64, 512])
    o_cas = dout("o_cas", [NSS, 30, 512])
    o_gvs = dout("o_gvs", [NSS * 64, 512])
    o_ssds = dout("o_ssds", [NSS, 8, 64, 128])
    o_css = dout("o_css", [NSS, 3, 1024])

    XS = dscr("XS", [NTOK, D])
    QTs = dscr("QTs", [NTILE, 128, 512], BF16)
    CATs = dscr("CATs", [NTILE, 128, 512], BF16)
    KTs = dscr("KTs", [128, 4, NTOK], BF16)
    VXs = dscr("VXs", [NTOK, 516], BF16)

    tiles = []
    for i in range(NPT):
        tiles.append(dict(seq=0, prompt=True, T=128, tok0=i * 128, i=i, first=(i == 0), last=(i == NPT - 1), s=-1))
    for s in range(NSS):
        tiles.append(dict(seq=1 + s, prompt=False, T=64, tok0=NPT * 128 + s * 64, i=0, first=True, last=True, s=s))

    def xin_rows(t):
        if t["prompt"]:
            return xp[t["tok0"]:t["tok0"] + 128, :]
        return xs[t["s"] * 64:(t["s"] + 1) * 64, :]

    with ExitStack() as ges:
        S = Sched(nc, ges)
        psf = [T(ges.enter_context(nc.psum_tensor("psf%d" % i, [128, 512], F32)), "psf%d" % i) for i in range(6)]
        psb = [T(ges.enter_context(nc.psum_tensor("psb%d" % i, [128, 1024], BF16)), "psb%d" % i) for i in range(2)]
        identf = T(ges.enter_context(nc.sbuf_tensor("identf", [128, 128], F32)), "identf")
        identb = T(ges.enter_context(nc.sbuf_tensor("identb", [128, 128], BF16)), "identb")
        onesb = T(ges.enter_context(nc.sbuf_tensor("onesb", [128, 128], BF16)), "onesb")
        onesf = T(ges.enter_context(nc.sbuf_tensor("onesf", [128, 128], F32)), "onesf")

        S.op("pool", lambda e: e.memset(identf[:], 1.0), writes=[identf.b])
        S.op("pool", lambda e: e.affine_select(out=identf[:], in_=identf[:], pattern=[[-1, 128]], compare_op=ALU.is_equal,
                                               fill=0.0, base=0, channel_multiplier=1), reads=[identf.b], writes=[identf.b])
        S.op("dve", lambda e: e.tensor_copy(out=identb[:], in_=identf[:]), reads=[identf.b], writes=[identb.b])
        S.op("dve", lambda e: e.memset(onesb[:], 1.0), writes=[onesb.b])
        S.op("dve", lambda e: e.memset(onesf[:], 1.0), writes=[onesf.b])

        rr = {"psf": 0, "psb": 0, "npsf": 6}

        def PSF():
            rr["psf"] = (rr["psf"] + 1) % rr["npsf"]
            return psf[rr["psf"]]

        def PSB():
            rr["psb"] = (rr["psb"] + 1) % 2
            return psb[rr["psb"]]

        def phase_tiles(es, prefix):
            cnt = [0]

            def sb(shape, dt=F32, name=None, nsub=1):
                cnt[0] += 1
                nm = "%s_%s%d" % (prefix, name or "t", cnt[0])
                return T(es.enter_context(nc.sbuf_tensor(nm, list(shape), dt)), nm, nsub)
            return sb

        def load_w(sb, w_ap, ncols, name):
            t = sb([128, 8, ncols], BF16, name)
            src = w_ap.rearrange("(c p) n -> p c n", p=128)
            for c in range(8):
                for n0 in range(0, ncols, 1024):
                    n1 = min(ncols, n0 + 1024)
                    S.op("pool", lambda e, c=c, n0=n0, n1=n1: e.dma_start(out=t[:, c, n0:n1], in_=src[:, c, n0:n1]),
                         writes=[t.b], chan=t.b.name)
            return t

        def load_cols(sb, v_ap, nchunk, name):
            t = sb([128, nchunk], F32, name)
            with nc.allow_non_contiguous_dma("tiny param column load"):
                pass
            S.op("sp", lambda e: e.dma_start(out=t[:], in_=v_ap.rearrange("o (c p) -> p (o c)", p=128),
                                             allow_slow_non_contiguous=True), writes=[t.b], chan=t.b.name)
            return t

        def load_rep(sb, v_ap, n, name):
            t = sb([128, n], F32, name)
            S.op("sp", lambda e: e.dma_start(out=t[:], in_=v_ap.partition_broadcast(128)), writes=[t.b], chan=t.b.name)
            return t

        def make_rms(sb):
            st = dict(junk=sb([128, D], F32, "rjunk"), ss=sb([128, 2], F32, "rss"), xh=sb([128, D], BF16, "rxh"))

            def rms_T(x, Tn, gcol, hT, h32=None, grep=None):
                junk, ss, xh = st["junk"], st["ss"], st["xh"]
                S.op("dve", lambda e: e.memset(ss[:Tn, 0:1], 0.0), writes=[ss.b])
                S.op("act", lambda e: e.activation(out=junk[:Tn], in_=x[:Tn], func=AF.Square, accum_out=ss[:Tn, 0:1]),
                     reads=[x.b, ss.b], writes=[junk.b, ss.b])
                S.op("act", lambda e: e.activation(out=ss[:Tn, 1:2], in_=ss[:Tn, 0:1], func=AF.Sqrt, scale=1.0 / D, bias=1e-6),
                     reads=[ss.b], writes=[ss.b])
                S.op("dve", lambda e: e.reciprocal(out=ss[:Tn, 1:2], in_=ss[:Tn, 1:2]), reads=[ss.b], writes=[ss.b])
                S.op("dve", lambda e: e.tensor_scalar(out=xh[:Tn], in0=x[:Tn], scalar1=ss[:Tn, 1:2], scalar2=None, op0=ALU.mult),
                     reads=[x.b, ss.b], writes=[xh.b])
                if h32 is not None:
                    S.op("dve", lambda e: e.scalar_tensor_tensor(out=h32[:Tn], in0=x[:Tn], scalar=ss[:Tn, 1:2], in1=grep[:Tn],
                                                                  op0=ALU.mult, op1=ALU.mult),
                         reads=[x.b, ss.b, grep.b], writes=[h32.b])
                pb = PSB()
                for c in range(8):
                    S.op("pe", lambda e, c=c: e.transpose(out=pb[:, c * Tn:(c + 1) * Tn], in_=xh[:Tn, c * 128:(c + 1) * 128],
                                                          identity=identb[:Tn, :Tn]),
                         reads=[xh.b, identb.b], writes=[pb.b])
                pv = pb[:, 0:8 * Tn].rearrange("p (c t) -> p c t", c=8)
                S.op("dve", lambda e: e.tensor_tensor(out=hT[:, :, :Tn], in0=pv, in1=gcol[:, :].unsqueeze(2).to_broadcast([128, 8, Tn]),
                                                      op=ALU.mult),
                     reads=[pb.b, gcol.b], writes=[hT.b])
            rms_T.junk = st["junk"]
            return rms_T

        def mm_feat(ps, col0, nchunk, w, hT, Tn, wb=None):
            for j in range(nchunk):
                for c in range(8):
                    S.op("pe", lambda e, j=j, c=c: e.matmul(ps[:, j * Tn:(j + 1) * Tn], lhsT=w[:, c, col0 + j * 128:col0 + (j + 1) * 128],
                                                            rhs=hT[:, c, :Tn], start=(c == 0), stop=(c == 7)),
                         reads=[w.b, hT.b], writes=[ps.b])

        def mm_tok(ps, col0, ncol, w, hT, Tn, nk=8):
            for c in range(nk):
                S.op("pe", lambda e, c=c: e.matmul(ps[:Tn, 0:ncol], lhsT=hT[:, c, :Tn], rhs=w[:, c, col0:col0 + ncol],
                                                   start=(c == 0), stop=(c == nk - 1)),
                     reads=[w.b, hT.b], writes=[ps.b])

        def gelu_tanh(e_act, e_dve, out, x, shape_ap, tmp, rb, wb_, Tn=None):
            pass

        def QTs_view(scr, ti, Tn):
            return scr[ti].rearrange("p (c t) -> p c t", c=4)[:, :, 0:Tn]

        with ExitStack() as es:
            sb = phase_tiles(es, "p0")
            mem32 = sb([128, 2, D], F32, "mem32")
            memb = sb([128, 2, D], BF16, "memb")
            memT = sb([128, 8, 256], BF16, "memT")
            S.op("sp", lambda e: e.dma_start(out=mem32[:], in_=memp.rearrange("(m p) d -> p m d", p=128)), writes=[mem32.b], chan="mem32")
            S.op("dve", lambda e: e.tensor_copy(out=memb[:], in_=mem32[:]), reads=[mem32.b], writes=[memb.b])
            for mc in range(2):
                pb = PSB()
                for c in range(8):
                    S.op("pe", lambda e, c=c, mc=mc, pb=pb: e.transpose(out=pb[:, c * 128:(c + 1) * 128], in_=memb[:, mc, c * 128:(c + 1) * 128],
                                                                        identity=identb[:]),
                         reads=[memb.b, identb.b], writes=[pb.b])
                S.op("act", lambda e, mc=mc, pb=pb: e.copy(out=memT[:, :, mc * 128:(mc + 1) * 128],
                                                           in_=pb[:, :].rearrange("p (c t) -> p c t", c=8)),
                     reads=[pb.b], writes=[memT.b])
            stg = [sb([128, D], F32, "stg") for _ in range(2)]
            k = 0
            for l in range(2):
                for (wsrc, odst) in ((w_xk, o_mk), (w_xv, o_mv)):
                    w = load_w(sb, wsrc[l], D, "wkv")
                    for mc in range(2):
                        st = stg[k % 2]
                        k += 1
                        for n in range(2):
                            ps = PSF()
                            for c in range(8):
                                S.op("pe", lambda e, c=c, n=n, mc=mc, ps=ps, w=w: e.matmul(
                                    ps[:, :], lhsT=memT[:, c, mc * 128:(mc + 1) * 128], rhs=w[:, c, n * 512:(n + 1) * 512],
                                    start=(c == 0), stop=(c == 7)), reads=[memT.b, w.b], writes=[ps.b])
                            S.op("act", lambda e, n=n, ps=ps, st=st: e.copy(out=st[:, n * 512:(n + 1) * 512], in_=ps[:, :]),
                                 reads=[ps.b], writes=[st.b])
                        S.op("sp", lambda e, l=l, mc=mc, st=st, odst=odst: e.dma_start(out=odst[l, mc * 128:(mc + 1) * 128, :], in_=st[:]),
                             reads=[st.b], writes=[S.db("omem")], chan="st_" + st.b.name)
            S.emit()

        with ExitStack() as es:
            sb = phase_tiles(es, "p1")
            w = load_w(sb, w_in_ab, 2560, "win")
            gcol = load_cols(sb, norm_mix_g[0:1, :], 8, "gcol")
            cw = sb([128, 4, 31], F32, "cw")
            for c in range(4):
                S.op("sp", lambda e, c=c: e.dma_start(out=cw[:, c, :], in_=conv_a_w[:, c * 128:(c + 1) * 128].rearrange("k p -> p k"),
                                                      allow_slow_non_contiguous=True), writes=[cw.b], chan="cw")
            cb = load_cols(sb, conv_a_b, 4, "cb")
            lg = load_cols(sb, ln_a_g, 4, "lg")
            lb = load_cols(sb, ln_a_b, 4, "lb")
            rms_T = make_rms(sb)
            xt = [sb([128, D], F32, "x") for _ in range(2)]
            hT = sb([128, 8, 128], BF16, "hT")
            sg = sb([128, 4, 128], F32, "sg")
            cbuf = sb([128, 4, 158], F32, "cbuf")
            acc = [sb([128, 128], F32, "acc%d" % c) for c in range(4)]
            sq = sb([128, 4, 128], F32, "sq")
            mv_ = sb([128, 3, 128], F32, "mv")
            xn = sb([128, 4, 128], F32, "xn")
            caT = [sb([128, 4, 128], BF16, "caT") for _ in range(2)]
            qT = [sb([128, 4, 128], BF16, "qT") for _ in range(2)]
            kT = [sb([128, 4, 128], BF16, "kT") for _ in range(2)]
            kvt = [sb([128, 2, 512], F32, "kvt") for _ in range(2)]
            vx = [sb([128, 4, 129], BF16, "vx") for _ in range(2)]
            st30 = sb([32, 512], F32, "st30")
            cao = sb([32, 512], F32, "cao")
            onesm = sb([128, 128], F32, "onesm")
            S.op("dve", lambda e: e.memset(onesm[:], 1.0 / 512), writes=[onesm.b])
            for v in vx:
                S.op("dve", lambda e, v=v: e.memset(v[:], 1.0), writes=[v.b])
            for ti, t in enumerate(tiles):
                Tn = t["T"]
                x = xt[ti % 2]
                S.op("sp", lambda e, x=x, t=t, Tn=Tn: e.dma_start(out=x[:Tn], in_=xin_rows(t)), writes=[x.b], chan=x.b.name)
                rms_T(x, Tn, gcol, hT)
                if t["first"]:
                    if t["prompt"]:
                        S.op("pool", lambda e: e.memset(cbuf[:, :, 0:30], 0.0), writes=[cbuf.b])
                    else:
                        S.op("sp", lambda e, t=t: e.dma_start(out=st30[0:30, :], in_=sca[t["s"]]), writes=[st30.b], chan="st30")
                        ps = PSF()
                        for c in range(4):
                            S.op("pe", lambda e, c=c, ps=ps: e.transpose(out=ps[:, c * 30:(c + 1) * 30], in_=st30[0:30, c * 128:(c + 1) * 128],
                                                                         identity=identf[0:30, 0:30]),
                                 reads=[st30.b, identf.b], writes=[ps.b])
                        S.op("act", lambda e, ps=ps: e.copy(out=cbuf[:, :, 0:30], in_=ps[:, 0:120].rearrange("p (c t) -> p c t", c=4)),
                             reads=[ps.b], writes=[cbuf.b])
                else:
                    S.op("pool", lambda e: e.tensor_copy(out=cbuf[:, :, 0:30], in_=cbuf[:, :, 128:158]), reads=[cbuf.b], writes=[cbuf.b])
                pa, pg = PSF(), PSF()
                mm_feat(pa, 0, 4, w, hT, Tn)
                mm_feat(pg, 512, 4, w, hT, Tn)
                S.op("act", lambda e, pg=pg, Tn=Tn: e.activation(out=sg[:, :, :Tn], in_=pg[:, 0:4 * Tn].rearrange("p (c t) -> p c t", c=4),
                                                                 func=AF.Sigmoid), reads=[pg.b], writes=[sg.b])
                S.op("dve", lambda e, pa=pa, Tn=Tn: e.tensor_tensor(out=cbuf[:, :, 30:30 + Tn], in0=pa[:, 0:4 * Tn].rearrange("p (c t) -> p c t", c=4),
                                                                    in1=sg[:, :, :Tn], op=ALU.mult), reads=[pa.b, sg.b], writes=[cbuf.b])
                if t["last"]:
                    ps = PSF()
                    for c in range(4):
                        S.op("pe", lambda e, c=c, ps=ps, Tn=Tn: e.transpose(out=ps[0:30, c * 128:(c + 1) * 128], in_=cbuf[:, c, Tn:Tn + 30],
                                                                            identity=identf[:, :]),
                             reads=[cbuf.b, identf.b], writes=[ps.b])
                    S.op("act", lambda e, ps=ps: e.copy(out=cao[0:30, :], in_=ps[0:30, :]), reads=[ps.b], writes=[cao.b])
                    dst = o_cap if t["prompt"] else o_cas[t["s"]]
                    S.op("sp", lambda e, dst=dst: e.dma_start(out=dst, in_=cao[0:30, :]), reads=[cao.b], writes=[S.db("ocap")], chan="st_cao")
                for k in range(31):
                    for c in range(4):
                        eng = "dve"
                        if k == 0:
                            S.op(eng, lambda e, c=c, Tn=Tn: e.tensor_scalar(out=acc[c][:, :Tn], in0=cbuf[:, c, 0:Tn], scalar1=cw[:, c, 0:1],
                                                                            scalar2=cb[:, c:c + 1], op0=ALU.mult, op1=ALU.add),
                                 reads=[cbuf.b, cw.b, cb.b], writes=[acc[c].b])
                        else:
                            S.op(eng, lambda e, c=c, k=k, Tn=Tn: e.scalar_tensor_tensor(out=acc[c][:, :Tn], in0=cbuf[:, c, k:k + Tn],
                                                                                        scalar=cw[:, c, k:k + 1], in1=acc[c][:, :Tn],
                                                                                        op0=ALU.mult, op1=ALU.add),
                                 reads=[cbuf.b, cw.b, acc[c].b], writes=[acc[c].b])
                pm, pq = PSF(), PSF()
                for c in range(4):
                    S.op("act", lambda e, c=c, Tn=Tn: e.activation(out=sq[:, c, :Tn], in_=acc[c][:, :Tn], func=AF.Square),
                         reads=[acc[c].b], writes=[sq.b])
                for c in range(4):
                    S.op("pe", lambda e, c=c, pm=pm, Tn=Tn: e.matmul(pm[:, :Tn], lhsT=onesm[:, :], rhs=acc[c][:, :Tn], start=(c == 0), stop=(c == 3)),
                         reads=[onesm.b, acc[c].b], writes=[pm.b])
                for c in range(4):
                    S.op("pe", lambda e, c=c, pq=pq, Tn=Tn: e.matmul(pq[:, :Tn], lhsT=onesm[:, :], rhs=sq[:, c, :Tn], start=(c == 0), stop=(c == 3)),
                         reads=[onesm.b, sq.b], writes=[pq.b])
                S.op("act", lambda e, pm=pm, Tn=Tn: e.copy(out=mv_[:, 0, :Tn], in_=pm[:, :Tn]), reads=[pm.b], writes=[mv_.b])
                S.op("dve", lambda e, Tn=Tn: e.tensor_tensor(out=mv_[:, 1, :Tn], in0=mv_[:, 0, :Tn], in1=mv_[:, 0, :Tn], op=ALU.mult),
                     reads=[mv_.b], writes=[mv_.b])
                S.op("dve", lambda e, pq=pq, Tn=Tn: e.tensor_tensor(out=mv_[:, 1, :Tn], in0=pq[:, :Tn], in1=mv_[:, 1, :Tn], op=ALU.subtract),
                     reads=[pq.b, mv_.b], writes=[mv_.b])
                S.op("act", lambda e, Tn=Tn: e.activation(out=mv_[:, 2, :Tn], in_=mv_[:, 1, :Tn], func=AF.Sqrt, bias=1e-5, scale=1.0),
                     reads=[mv_.b], writes=[mv_.b])
                S.op("dve", lambda e, Tn=Tn: e.reciprocal(out=mv_[:, 2, :Tn], in_=mv_[:, 2, :Tn]), reads=[mv_.b], writes=[mv_.b])
                for c in range(4):
                    S.op("dve", lambda e, c=c, Tn=Tn: e.tensor_tensor(out=xn[:, c, :Tn], in0=acc[c][:, :Tn], in1=mv_[:, 0, :Tn], op=ALU.subtract),
                         reads=[acc[c].b, mv_.b], writes=[xn.b])
                S.op("dve", lambda e, Tn=Tn: e.tensor_tensor(out=xn[:, :, :Tn], in0=xn[:, :, :Tn],
                                                             in1=mv_[:, 2, :Tn].unsqueeze(1).to_broadcast([128, 4, Tn]), op=ALU.mult),
                     reads=[xn.b, mv_.b], writes=[xn.b])
                ca = caT[ti % 2]
                for c in range(4):
                    S.op("act", lambda e, c=c, ca=ca, Tn=Tn: e.activation(out=ca[:, c, :Tn], in_=xn[:, c, :Tn], func=AF.Silu,
                                                                          scale=lg[:, c:c + 1], bias=lb[:, c:c + 1]),
                         reads=[xn.b, lg.b, lb.b], writes=[ca.b])
                S.op("sp", lambda e, ca=ca, ti=ti, Tn=Tn: e.dma_start(out=QTs_view(CATs, ti, Tn), in_=ca[:, :, :Tn]),
                     reads=[ca.b], writes=[S.db("CATs%d" % ti)], chan="st_" + ca.b.name)
                pq2, pk2 = PSF(), PSF()
                mm_feat(pq2, 1024, 4, w, hT, Tn)
                mm_feat(pk2, 1536, 4, w, hT, Tn)
                q_, k_ = qT[ti % 2], kT[ti % 2]
                S.op("act", lambda e, pq2=pq2, q_=q_, Tn=Tn: e.copy(out=q_[:, :, :Tn], in_=pq2[:, 0:4 * Tn].rearrange("p (c t) -> p c t", c=4)),
                     reads=[pq2.b], writes=[q_.b])
                S.op("dve", lambda e, pk2=pk2, k_=k_, Tn=Tn: e.tensor_copy(out=k_[:, :, :Tn], in_=pk2[:, 0:4 * Tn].rearrange("p (c t) -> p c t", c=4)),
                     reads=[pk2.b], writes=[k_.b])
                S.op("sp", lambda e, q_=q_, ti=ti, Tn=Tn: e.dma_start(out=QTs_view(QTs, ti, Tn), in_=q_[:, :, :Tn]),
                     reads=[q_.b], writes=[S.db("QTs%d" % ti)], chan="st_" + q_.b.name)
                S.op("sp", lambda e, k_=k_, t=t, Tn=Tn: e.dma_start(out=KTs[:, :, t["tok0"]:t["tok0"] + Tn], in_=k_[:, :, :Tn]),
                     reads=[k_.b], writes=[S.db("KTs%d" % ti)], chan="st_" + k_.b.name)
                kv = kvt[ti % 2]
                for n in range(2):
                    ps = PSF()
                    mm_tok(ps, 1536 + n * 512, 512, w, hT, Tn)
                    S.op("act" if n == 0 else "dve",
                         (lambda e, ps=ps, kv=kv, n=n, Tn=Tn: e.copy(out=kv[:Tn, n, :], in_=ps[:Tn, :])) if n == 0 else
                         (lambda e, ps=ps, kv=kv, n=n, Tn=Tn: e.tensor_copy(out=kv[:Tn, n, :], in_=ps[:Tn, :])),
                         reads=[ps.b], writes=[kv.b])
                okd, ovd = (o_kp, o_vp) if t["prompt"] else (o_ks, o_vs)
                r0 = t["tok0"] if t["prompt"] else t["s"] * 64
                S.op("sp", lambda e, kv=kv, okd=okd, r0=r0, Tn=Tn: e.dma_start(out=okd[r0:r0 + Tn, :], in_=kv[:Tn, 0, :]),
                     reads=[kv.b], writes=[S.db("okv")], chan="st_" + kv.b.name)
                S.op("sp", lambda e, kv=kv, ovd=ovd, r0=r0, Tn=Tn: e.dma_start(out=ovd[r0:r0 + Tn, :], in_=kv[:Tn, 1, :]),
                     reads=[kv.b], writes=[S.db("okv")], chan="st_" + kv.b.name)
                v_ = vx[ti % 2]
                S.op("pool", lambda e, kv=kv, v_=v_, Tn=Tn: e.tensor_copy(out=v_[:Tn, :, 0:128], in_=kv[:Tn, 1, :].rearrange("p (h e) -> p h e", h=4)),
                     reads=[kv.b], writes=[v_.b])
                S.op("sp", lambda e, v_=v_, t=t, Tn=Tn: e.dma_start(out=VXs[t["tok0"]:t["tok0"] + Tn, :], in_=v_[:Tn].rearrange("p h e -> p (h e)")),
                     reads=[v_.b], writes=[S.db("VXs%d" % ti)], chan="st_" + v_.b.name)
            S.emit()

        with ExitStack() as es:
            sb = phase_tiles(es, "p2")
            NKT = max(NPT, NPK + 1)
            wo = load_w(sb, w_out_ab, D, "wo")
            KT = sb([128, 4, NKT * 128], BF16, "KT")
            VS = sb([128, NKT, 516], BF16, "VS")
            kcs = [sb([128, 4, 512], F32, "kcs") for _ in range(2)]
            xt = [sb([128, D], F32, "x") for _ in range(2)]
            qT = [sb([128, 4, 2, 128], BF16, "qbd") for _ in range(2)]
            for q_ in qT:
                S.op("pool", lambda e, q_=q_: e.memset(q_[:], 0.0), writes=[q_.b])
            caT = [sb([128, 4, 128], BF16, "caT") for _ in range(2)]
            PT = [sb([128, 4, 128], BF16, "PT") for _ in range(3)]
            ah = sb([128, 128], F32, "ah")
            rr_ = sb([128, 8], F32, "rr")
            junk = sb([128, 128], F32, "junk")
            otok = sb([128, 512], BF16, "otok")
            oT = sb([128, 4, 128], BF16, "oT")
            lam = sb([128, 8], F32, "lam")
            lv = [load_rep(sb, a, 64, "lv") for a in (lam_q1, lam_k1, lam_q2, lam_k2)]
            gs = load_rep(sb, subln_g, 128, "gs")
            lj = sb([128, 64], F32, "lj")
            S.op("dve", lambda e: e.memset(lam[:], 0.0), writes=[lam.b])
            S.op("dve", lambda e: e.scalar_tensor_tensor(out=lj[:], in0=lv[0][:], scalar=1.0, in1=lv[1][:], op0=ALU.mult, op1=ALU.mult,
                                                         accum_out=lam[:, 0:1]), reads=[lv[0].b, lv[1].b, lam.b], writes=[lj.b, lam.b])
            S.op("dve", lambda e: e.scalar_tensor_tensor(out=lj[:], in0=lv[2][:], scalar=1.0, in1=lv[3][:], op0=ALU.mult, op1=ALU.mult,
                                                         accum_out=lam[:, 1:2]), reads=[lv[2].b, lv[3].b, lam.b, lj.b], writes=[lj.b, lam.b])
            S.op("act", lambda e: e.activation(out=lam[:, 2:4], in_=lam[:, 0:2], func=AF.Exp), reads=[lam.b], writes=[lam.b])
            S.op("dve", lambda e: e.tensor_tensor(out=lam[:, 4:5], in0=lam[:, 3:4], in1=lam[:, 2:3], op=ALU.subtract), reads=[lam.b], writes=[lam.b])
            S.op("dve", lambda e: e.tensor_scalar(out=lam[:, 5:6], in0=lam[:, 4:5], scalar1=-LAM0, scalar2=None, op0=ALU.add), reads=[lam.b], writes=[lam.b])
            S.op("dve", lambda e: e.tensor_scalar(out=gs[:], in0=gs[:], scalar1=(1.0 - LAM0), scalar2=None, op0=ALU.mult), reads=[gs.b], writes=[gs.b])
            psO = [psf[0], psf[1]]
            psS = [psf[2], psf[3]]
            psX = [psf[4], psf[5]]
            cnt = {"s": 0, "p": 0}
            seqs = [(0, True)] + [(1 + s, False) for s in range(NSS)]
            for (sq_, isp) in seqs:
                stiles = [(ti, t) for ti, t in enumerate(tiles) if t["seq"] == sq_]
                if isp:
                    nkt_total = NPT
                    for h in range(4):
                        S.op("sp", lambda e, h=h: e.dma_start(out=KT[:, h, 0:NPT * 128], in_=KTs[:, h, 0:NPT * 128]),
                             reads=[S.db("KTs%d" % i) for i in range(NPT)], writes=[KT.b], chan="KT")
                    for j0 in range(0, NPT, 8):
                        j1 = min(NPT, j0 + 8)
                        S.op("sp", lambda e, j0=j0, j1=j1: e.dma_start(out=VS[:, j0:j1, :], in_=VXs[j0 * 128:j1 * 128, :].rearrange("(j p) f -> p j f", p=128)),
                             reads=[S.db("VXs%d" % i) for i in range(j0, j1)], writes=[VS.b], chan="VS")
                    klens = [128] * NPT
                else:
                    s = sq_ - 1
                    ti0 = NPT + s
                    tok0 = NPT * 128 + s * 64
                    S.op("dve", lambda e: e.memset(VS[:, 0:NPK, :], 1.0), writes=[VS.b])
                    for j in range(NPK):
                        S.op("pool", lambda e, s=s, j=j: e.dma_start(out=VS[:, j, :].rearrange("p (h e) -> p h e", h=4)[:, :, 0:128],
                                                                     in_=cv[s, j * 128:(j + 1) * 128, :].rearrange("p (h e) -> p h e", h=4)),
                             writes=[VS.b], chan="VS")
                    S.op("sp", lambda e, tok0=tok0: e.dma_start(out=VS[0:64, NPK, :], in_=VXs[tok0:tok0 + 64, :]),
                         reads=[S.db("VXs%d" % ti0)], writes=[VS.b], chan="VS")
                    S.op("sp", lambda e, tok0=tok0: e.dma_start(out=KT[:, :, PL:PL + 64], in_=KTs[:, :, tok0:tok0 + 64]),
                         reads=[S.db("KTs%d" % ti0)], writes=[KT.b], chan="KT")
                    for j in range(NPK):
                        kc = kcs[j % 2]
                        S.op("sp", lambda e, kc=kc, j=j, s=s: e.dma_start(out=kc[:, 0, :], in_=ck[s, j * 128:(j + 1) * 128, :]),
                             writes=[kc.b], chan=kc.b.name)
                        ps = psX[j % 2]
                        for h in range(4):
                            S.op("pe", lambda e, h=h, kc=kc, ps=ps: e.transpose(out=ps[:, h * 128:(h + 1) * 128], in_=kc[:, 0, h * 128:(h + 1) * 128],
                                                                                identity=identf[:, :]),
                                 reads=[kc.b, identf.b], writes=[ps.b])
                        S.op("act", lambda e, ps=ps, j=j: e.copy(out=KT[:, :, j * 128:(j + 1) * 128], in_=ps[:, :].rearrange("p (h k) -> p h k", h=4)),
                             reads=[ps.b], writes=[KT.b])
                    klens = [128] * NPK + [64]
                for (ti, t) in stiles:
                    Tn = t["T"]
                    x, q_, ca = xt[ti % 2], qT[ti % 2], caT[ti % 2]
                    S.op("sp", lambda e, x=x, t=t, Tn=Tn: e.dma_start(out=x[:Tn], in_=xin_rows(t)), writes=[x.b], chan=x.b.name)
                    for tt in range(2):
                        S.op("sp", lambda e, q_=q_, ti=ti, Tn=Tn, tt=tt: e.dma_start(out=q_[64 * tt:64 * tt + 64, :, tt, :Tn],
                                                                                   in_=QTs_view(QTs, ti, Tn)[64 * tt:64 * tt + 64]),
                             reads=[S.db("QTs%d" % ti)], writes=[q_.b], chan=q_.b.name)
                    S.op("sp", lambda e, ca=ca, ti=ti, Tn=Tn: e.dma_start(out=ca[:, :, :Tn], in_=QTs_view(CATs, ti, Tn)),
                         reads=[S.db("CATs%d" % ti)], writes=[ca.b], chan=ca.b.name)
                    nk = (t["i"] + 1) if isp else (NPK + 1)
                    for h in range(4):
                        for jb in range(0, nk, 2):
                            js = list(range(jb, min(nk, jb + 2)))
                            nj = len(js)
                            pS = psS[cnt["s"] % 2]
                            cnt["s"] += 1
                            P = PT[cnt["p"] % 3]
                            cnt["p"] += 1
                            for jj, j in enumerate(js):
                                kl = klens[j]
                                S.op("pe", lambda e, jj=jj, j=j, kl=kl, pS=pS, h=h, q_=q_, Tn=Tn: e.matmul(
                                    pS[:kl, jj * 256:(jj + 1) * 256].rearrange("p (t q) -> p t q", t=2)[:, :, :Tn],
                                    lhsT=KT[:, h, j * 128:j * 128 + kl], rhs=q_[:, h, :, :Tn],
                                    start=True, stop=True), reads=[KT.b, q_.b], writes=[pS.b])
                            S.op("act", lambda e, pS=pS, P=P, nj=nj, Tn=Tn: e.activation(
                                out=P[:, 0:2 * nj, :Tn], in_=pS[:, 0:nj * 256].rearrange("p (j t) -> p j t", j=2 * nj)[:, :, :Tn], func=AF.Exp, scale=0.125),
                                reads=[pS.b], writes=[P.b])
                            if isp and js[-1] == t["i"]:
                                jj = len(js) - 1
                                S.op("dve", lambda e, P=P, jj=jj: e.memset(P[64:128, 2 * jj:2 * jj + 2, 0:64], 0.0), reads=[P.b], writes=[P.b])
                            for jj, j in enumerate(js):
                                kl = klens[j]
                                for tt in range(2):
                                    S.op("pe", lambda e, jj=jj, j=j, kl=kl, P=P, h=h, tt=tt, Tn=Tn, nk=nk: e.matmul(
                                        psO[tt][:Tn, 0:129], lhsT=P[:kl, 2 * jj + tt, :Tn], rhs=VS[:kl, j, h * 129:(h + 1) * 129],
                                        start=(j == 0), stop=(j == nk - 1)), reads=[P.b, VS.b], writes=[psO[tt].b])
                        S.op("dve", lambda e, Tn=Tn: e.reciprocal(out=rr_[:Tn, 0:1], in_=psO[0][:Tn, 128:129]), reads=[psO[0].b], writes=[rr_.b])
                        S.op("dve", lambda e, Tn=Tn: e.reciprocal(out=rr_[:Tn, 1:2], in_=psO[1][:Tn, 128:129]), reads=[psO[1].b, rr_.b], writes=[rr_.b])
                        S.op("dve", lambda e, Tn=Tn: e.tensor_tensor(out=rr_[:Tn, 2:3], in0=rr_[:Tn, 1:2], in1=lam[:Tn, 5:6], op=ALU.mult),
                             reads=[rr_.b, lam.b], writes=[rr_.b])
                        S.op("dve", lambda e, Tn=Tn: e.tensor_scalar(out=ah[:Tn], in0=psO[0][:Tn, 0:128], scalar1=rr_[:Tn, 0:1], scalar2=None, op0=ALU.mult),
                             reads=[psO[0].b, rr_.b], writes=[ah.b])
                        S.op("dve", lambda e, Tn=Tn: e.scalar_tensor_tensor(out=ah[:Tn], in0=psO[1][:Tn, 0:128], scalar=rr_[:Tn, 2:3], in1=ah[:Tn],
                                                                            op0=ALU.mult, op1=ALU.add), reads=[psO[1].b, rr_.b, ah.b], writes=[ah.b])
                        S.op("dve", lambda e, Tn=Tn: e.memset(rr_[:Tn, 3:4], 0.0), reads=[rr_.b], writes=[rr_.b])
                        S.op("act", lambda e, Tn=Tn: e.activation(out=junk[:Tn], in_=ah[:Tn], func=AF.Square, accum_out=rr_[:Tn, 3:4]),
                             reads=[ah.b, rr_.b], writes=[junk.b, rr_.b])
                        S.op("act", lambda e, Tn=Tn: e.activation(out=rr_[:Tn, 4:5], in_=rr_[:Tn, 3:4], func=AF.Sqrt, scale=1.0 / 128, bias=1e-6),
                             reads=[rr_.b], writes=[rr_.b])
                        S.op("dve", lambda e, Tn=Tn: e.reciprocal(out=rr_[:Tn, 4:5], in_=rr_[:Tn, 4:5]), reads=[rr_.b], writes=[rr_.b])
                        S.op("dve", lambda e, Tn=Tn, h=h: e.scalar_tensor_tensor(out=otok[:Tn, h * 128:(h + 1) * 128], in0=ah[:Tn], scalar=rr_[:Tn, 4:5],
                                                                                 in1=gs[:Tn], op0=ALU.mult, op1=ALU.mult),
                             reads=[ah.b, rr_.b, gs.b], writes=[otok.b])
                    pb = PSB()
                    for h in range(4):
                        S.op("pe", lambda e, h=h, pb=pb, Tn=Tn: e.transpose(out=pb[:, h * Tn:(h + 1) * Tn], in_=otok[:Tn, h * 128:(h + 1) * 128],
                                                                            identity=identb[:Tn, :Tn]), reads=[otok.b, identb.b], writes=[pb.b])
                    S.op("act", lambda e, pb=pb, Tn=Tn: e.copy(out=oT[:, :, :Tn], in_=pb[:, 0:4 * Tn].rearrange("p (h t) -> p h t", h=4)),
                         reads=[pb.b], writes=[oT.b])
                    for n in range(2):
                        ps = psX[n]
                        for c in range(8):
                            src = ca if c < 4 else oT
                            S.op("pe", lambda e, c=c, n=n, ps=ps, src=src, Tn=Tn: e.matmul(ps[:Tn, :], lhsT=src[:, c % 4, :Tn],
                                                                                          rhs=wo[:, c, n * 512:(n + 1) * 512],
                                                                                          start=(c == 0), stop=(c == 7)),
                                 reads=[src.b, wo.b], writes=[ps.b])
                        S.op("dve", lambda e, n=n, ps=ps, x=x, Tn=Tn: e.tensor_tensor(out=x[:Tn, n * 512:(n + 1) * 512], in0=ps[:Tn, :],
                                                                                     in1=x[:Tn, n * 512:(n + 1) * 512], op=ALU.add),
                             reads=[ps.b, x.b], writes=[x.b])
                    S.op("sp", lambda e, x=x, t=t, Tn=Tn: e.dma_start(out=XS[t["tok0"]:t["tok0"] + Tn, :], in_=x[:Tn]),
                         reads=[x.b], writes=[S.db("XS%d" % ti)], chan="st_" + x.b.name)
            S.emit()

        def cross_peer(l, final, peer=True):
            with ExitStack() as es:
                sb = phase_tiles(es, "p3%d" % l)
                wq = load_w(sb, w_xq[l], D, "wq")
                wo_ = load_w(sb, w_xo[l], D, "wo")
                wp = load_w(sb, w_pq[l], 2048, "wp")
                gcx = load_cols(sb, norm_cross_g[l:l + 1, :], 8, "gcx")
                gcf = load_cols(sb, norm_ffn_g[l:l + 1, :], 8, "gcf")
                grf = load_rep(sb, norm_ffn_g[l:l + 1, :], D, "grf")
                if final:
                    grz = load_rep(sb, norm_final_g, D, "grz")
                skT = sb([128, 2, 128], F32, "skT")
                sk32 = sb([128, 2, 128], F32, "sk32")
                S.op("sp", lambda e: e.dma_start(out=sk32[:], in_=sub_keys[l].rearrange("c k d -> k c d")), writes=[sk32.b], chan="sk32")
                for c in range(2):
                    ps = PSF()
                    S.op("pe", lambda e, c=c, ps=ps: e.transpose(out=ps[:, 0:128], in_=sk32[:, c, :], identity=identf[:, :]),
                         reads=[sk32.b, identf.b], writes=[ps.b])
                    S.op("act", lambda e, c=c, ps=ps: e.copy(out=skT[:, c, :], in_=ps[:, 0:128]), reads=[ps.b], writes=[skT.b])
                rms_T = make_rms(sb)
                iot = sb([128, 16], F32, "iot")
                S.op("pool", lambda e: e.iota(iot[:], pattern=[[1, 16]], base=0, channel_multiplier=0, allow_small_or_imprecise_dtypes=True),
                     writes=[iot.b])
                mb = sb([128, 2, D], BF16, "mb")
                mkT = sb([128, 8, 256], BF16, "mkT")
                mvb = sb([128, 2, D], BF16, "mvb")
                xt = [sb([128, D], F32, "x") for _ in range(2)]
                hT = sb([128, 8, 128], BF16, "hT")
                h32s = [sb([128, D], F32, "h32") for _ in range(2)]
                qTx = sb([128, 8, 128], BF16, "qTx")
                PTx = sb([128, 8, 128], BF16, "PTx")
                rinv = sb([128, 4, 128], F32, "rinv")
                oTx = sb([128, 8, 128], BF16, "oTx")
                pqT = sb([128, 16, 128], F32, "pqT")
                sc = sb([128, 16, 128], F32, "sc")
                sc2 = sb([128, 16, 128], F32, "sc2")
                ts = sb([128, 16, 16], F32, "ts")
                tiu = sb([128, 16, 16], U32, "tiu")
                tif = sb([128, 16, 16], F32, "tif")
                cand = sb([128, 8, 256], F32, "cand")
                cand2 = T(sc2.ap.rearrange("p (h a) k -> p h (a k)", h=8), "cand2v")
                cand2.b = sc2.b
                bs = sb([128, 8, 16], F32, "bs")
                bpu = sb([128, 8, 16], U32, "bpu")
                bpf = sb([128, 8, 16], F32, "bpf")
                fa = sb([128, 8, 16], F32, "fa")
                fb_ = sb([128, 8, 16], F32, "fb")
                ia = sb([128, 8, 16], I32, "ia")
                oh = T(sc.ap.rearrange("p (h a) (b c) -> p h a b c", h=8, b=8).rearrange("p h a b c -> p h (a b) c"), "ohv")
                oh.b = sc.b
                i0f = sb([128, 8, 16], F32, "i0f")
                i1f = sb([128, 8, 16], F32, "i1f")
                idxs = [sb([128, 128], I32, "idx") for _ in range(2)]
                gates = [sb([128, 8, 16], F32, "gate") for _ in range(2)]
                gsum = sb([128, 8], F32, "gsum")
                act_ = sb([128, 128], F32, "act")
                g1 = sb([128, 128], F32, "g1")
                g2 = sb([128, 128], F32, "g2")
                wgt = sb([128, 128], F32, "wgt")
                NG = 8
                gb = [sb([128, D], F32, "gb") for _ in range(NG)]
                pj = rms_T.junk
                rr["npsf"] = 4
                psV = [psf[4], psf[5]]
                dgs = [sb([128, 128], BF16, "dg") for _ in range(4)]
                for idx in idxs:
                    S.op("dve", lambda e, idx=idx: e.memset(idx[:], 0), writes=[idx.b])
                fss = sb([128, 2], F32, "fss")
                gcnt = [0]
                cur_seq = [None]
                if split:
                    ridx = sb([128, NOWN], I32, "ridx")
                    S.op("sp", lambda e: e.dma_start(out=ridx[:], in_=rowidx), writes=[ridx.b], chan="ridx")
                    own_tiles = [(k, dict(tiles[0], slot=k)) for k in range(NOWN)] + [(ti, t) for ti, t in enumerate(tiles) if not t["prompt"]]
                else:
                    own_tiles = [(ti, dict(t, slot=t["i"])) for ti, t in enumerate(tiles)]
                def AB(n):
                    ti, t = own_tiles[n]
                    h32, idx, gate = h32s[n % 2], idxs[n % 2], gates[n % 2]
                    Tn = t["T"]
                    if cur_seq[0] != t["seq"]:
                        cur_seq[0] = t["seq"]
                        ksrc = o_mk[l] if t["prompt"] else cmk[l, t["s"]]
                        vsrc = o_mv[l] if t["prompt"] else cmv[l, t["s"]]
                        S.op("pool", lambda e, ksrc=ksrc: e.dma_start(out=mb[:], in_=ksrc.rearrange("(m p) d -> p m d", p=128)),
                             reads=[S.db("omem")], writes=[mb.b], chan="mb")
                        for mc in range(2):
                            pb = PSB()
                            for c in range(8):
                                S.op("pe", lambda e, c=c, mc=mc, pb=pb: e.transpose(out=pb[:, c * 128:(c + 1) * 128], in_=mb[:, mc, c * 128:(c + 1) * 128],
                                                                                    identity=identb[:]), reads=[mb.b, identb.b], writes=[pb.b])
                            S.op("act", lambda e, mc=mc, pb=pb: e.copy(out=mkT[:, :, mc * 128:(mc + 1) * 128],
                                                                       in_=pb[:, :].rearrange("p (c t) -> p c t", c=8)), reads=[pb.b], writes=[mkT.b])
                        S.op("pool", lambda e, vsrc=vsrc: e.dma_start(out=mvb[:], in_=vsrc.rearrange("(m p) d -> p m d", p=128)),
                             reads=[S.db("omem")], writes=[mvb.b], chan="mvb")
                    x = xt[n % 2]
                    if split and t["prompt"]:
                        S.op("sp", lambda e, x=x, t=t: e.dma_start(out=x[:, :], in_=XOWN[t["slot"] * 128:(t["slot"] + 1) * 128, :]),
                             reads=[S.db("XOWN%d" % t["slot"])], writes=[x.b], chan=x.b.name)
                    else:
                        S.op("sp", lambda e, x=x, t=t, Tn=Tn: e.dma_start(out=x[:Tn], in_=XS[t["tok0"]:t["tok0"] + Tn, :]),
                             reads=[S.db("XS%d" % ti)], writes=[x.b], chan=x.b.name)
                    rms_T(x, Tn, gcx, hT)
                    for half in range(2):
                        ps = PSF()
                        mm_feat(ps, half * 512, 4, wq, hT, Tn)
                        S.op("act", lambda e, ps=ps, half=half, Tn=Tn: e.copy(out=qTx[:, half * 4:half * 4 + 4, :Tn],
                                                                             in_=ps[:, 0:4 * Tn].rearrange("p (c t) -> p c t", c=4)),
                             reads=[ps.b], writes=[qTx.b])
                    pss = [PSF(), PSF()]
                    for h in range(4):
                        for mc in range(2):
                            ps = pss[h // 2]
                            o0 = ((h % 2) * 2 + mc) * Tn
                            for dc in range(2):
                                S.op("pe", lambda e, h=h, mc=mc, dc=dc, ps=ps, o0=o0, Tn=Tn: e.matmul(
                                    ps[:, o0:o0 + Tn], lhsT=mkT[:, 2 * h + dc, mc * 128:(mc + 1) * 128], rhs=qTx[:, 2 * h + dc, :Tn],
                                    start=(dc == 0), stop=(dc == 1)), reads=[mkT.b, qTx.b], writes=[ps.b])
                    for hh in range(2):
                        S.op("act", lambda e, hh=hh, Tn=Tn, pss=pss: e.activation(out=PTx[:, hh * 4:hh * 4 + 4, :Tn],
                                                                         in_=pss[hh][:, 0:4 * Tn].rearrange("p (c t) -> p c t", c=4),
                                                                         func=AF.Exp, scale=1.0 / 16), reads=[pss[hh].b], writes=[PTx.b])
                    psr = PSF()
                    for h in range(4):
                        for mc in range(2):
                            S.op("pe", lambda e, h=h, mc=mc, Tn=Tn, psr=psr: e.matmul(psr[:, h * Tn:(h + 1) * Tn], lhsT=onesb[:, :], rhs=PTx[:, h * 2 + mc, :Tn],
                                                                            start=(mc == 0), stop=(mc == 1)), reads=[onesb.b, PTx.b], writes=[psr.b])
                    S.op("dve", lambda e, Tn=Tn, psr=psr: e.reciprocal(out=rinv[:, :, :Tn], in_=psr[:, 0:4 * Tn].rearrange("p (h t) -> p h t", h=4)),
                         reads=[psr.b], writes=[rinv.b])
                    for half in range(2):
                        ps = PSF()
                        for jj in range(4):
                            j = half * 4 + jj
                            h = j // 2
                            for mc in range(2):
                                S.op("pe", lambda e, j=j, jj=jj, h=h, mc=mc, ps=ps, Tn=Tn: e.matmul(
                                    ps[:, jj * Tn:(jj + 1) * Tn], lhsT=mvb[:, mc, j * 128:(j + 1) * 128], rhs=PTx[:, h * 2 + mc, :Tn],
                                    start=(mc == 0), stop=(mc == 1)), reads=[mvb.b, PTx.b], writes=[ps.b])
                        for hh in range(2):
                            h = half * 2 + hh
                            S.op("dve", lambda e, ps=ps, hh=hh, h=h, Tn=Tn: e.tensor_tensor(
                                out=oTx[:, 2 * h:2 * h + 2, :Tn], in0=ps[:, hh * 2 * Tn:(hh * 2 + 2) * Tn].rearrange("p (c t) -> p c t", c=2),
                                in1=rinv[:, h, :Tn].unsqueeze(1).to_broadcast([128, 2, Tn]), op=ALU.mult),
                                reads=[ps.b, rinv.b], writes=[oTx.b])
                    for n in range(2):
                        ps = PSF()
                        mm_tok(ps, n * 512, 512, wo_, oTx, Tn)
                        S.op("dve", lambda e, n=n, ps=ps, x=x, Tn=Tn: e.tensor_tensor(out=x[:Tn, n * 512:(n + 1) * 512], in0=ps[:Tn, :],
                                                                                     in1=x[:Tn, n * 512:(n + 1) * 512], op=ALU.add),
                             reads=[ps.b, x.b], writes=[x.b])
                    if not peer:
                        S.op("sp", lambda e, x=x, t=t, Tn=Tn: e.dma_start(out=XS[t["tok0"]:t["tok0"] + Tn, :], in_=x[:Tn]),
                             reads=[x.b], writes=[S.db("XS%d" % ti)], chan="st_" + x.b.name)
                        return
                    rms_T(x, Tn, gcf, hT, h32=h32, grep=grf)
                    for q4 in range(4):
                        ps = PSF()
                        mm_feat(ps, q4 * 512, 4, wp, hT, Tn)
                        S.op("act", lambda e, ps=ps, q4=q4, Tn=Tn: e.copy(out=pqT[:, q4 * 4:q4 * 4 + 4, :Tn],
                                                                         in_=ps[:, 0:4 * Tn].rearrange("p (c t) -> p c t", c=4)),
                             reads=[ps.b], writes=[pqT.b])
                    for q4 in range(4):
                        ps = PSF()
                        for jj in range(4):
                            j = q4 * 4 + jj
                            S.op("pe", lambda e, j=j, jj=jj, ps=ps, Tn=Tn: e.matmul(ps[:Tn, jj * 128:(jj + 1) * 128], lhsT=pqT[:, j, :Tn],
                                                                                   rhs=skT[:, j % 2, :], start=True, stop=True),
                                 reads=[pqT.b, skT.b], writes=[ps.b])
                        S.op("act", lambda e, ps=ps, q4=q4, Tn=Tn: e.copy(out=sc[:Tn, q4 * 4:q4 * 4 + 4, :],
                                                                         in_=ps[:Tn, :].rearrange("p (c k) -> p c k", c=4)),
                             reads=[ps.b], writes=[sc.b])
                    for j in range(16):
                        S.op("dve", lambda e, j=j, Tn=Tn: e.max(out=ts[:Tn, j, 0:8], in_=sc[:Tn, j, :]), reads=[sc.b], writes=[ts.b])
                    for j in range(16):
                        S.op("dve", lambda e, j=j, Tn=Tn: e.max_index(out=tiu[:Tn, j, 0:8], in_max=ts[:Tn, j, 0:8], in_values=sc[:Tn, j, :]),
                             reads=[sc.b, ts.b], writes=[tiu.b])
                    for j in range(16):
                        S.op("dve", lambda e, j=j, Tn=Tn: e.match_replace(out=sc2[:Tn, j, :], in_to_replace=ts[:Tn, j, 0:8], in_values=sc[:Tn, j, :],
                                                                         imm_value=NEG), reads=[sc.b, ts.b], writes=[sc2.b])
                    for j in range(16):
                        S.op("dve", lambda e, j=j, Tn=Tn: e.max(out=ts[:Tn, j, 8:16], in_=sc2[:Tn, j, :]), reads=[sc2.b], writes=[ts.b])
                    for j in range(16):
                        S.op("dve", lambda e, j=j, Tn=Tn: e.max_index(out=tiu[:Tn, j, 8:16], in_max=ts[:Tn, j, 8:16], in_values=sc2[:Tn, j, :]),
                             reads=[sc2.b, ts.b], writes=[tiu.b])
                    S.op("dve", lambda e, Tn=Tn: e.tensor_copy(out=tif[:Tn], in_=tiu[:Tn]), reads=[tiu.b], writes=[tif.b])
                    tsv = ts[:Tn].rearrange("p (h c) k -> p h c k", c=2)
                    tfv = tif[:Tn].rearrange("p (h c) k -> p h c k", c=2)
                    S.op("dve", lambda e, Tn=Tn, tsv=tsv: e.tensor_tensor(
                        out=cand[:Tn].rearrange("p h (a b) -> p h a b", a=16),
                        in0=tsv[:, :, 0, :].unsqueeze(3).to_broadcast([Tn, 8, 16, 16]),
                        in1=tsv[:, :, 1, :].unsqueeze(2).to_broadcast([Tn, 8, 16, 16]), op=ALU.add), reads=[ts.b], writes=[cand.b])
                    for h in range(8):
                        S.op("dve", lambda e, h=h, Tn=Tn: e.max(out=bs[:Tn, h, 0:8], in_=cand[:Tn, h, :]), reads=[cand.b], writes=[bs.b])
                    for h in range(8):
                        S.op("dve", lambda e, h=h, Tn=Tn: e.max_index(out=bpu[:Tn, h, 0:8], in_max=bs[:Tn, h, 0:8], in_values=cand[:Tn, h, :]),
                             reads=[cand.b, bs.b], writes=[bpu.b])
                    for h in range(8):
                        S.op("dve", lambda e, h=h, Tn=Tn: e.match_replace(out=cand2[:Tn, h, :], in_to_replace=bs[:Tn, h, 0:8], in_values=cand[:Tn, h, :],
                                                                         imm_value=NEG), reads=[cand.b, bs.b], writes=[cand2.b])
                    for h in range(8):
                        S.op("dve", lambda e, h=h, Tn=Tn: e.max(out=bs[:Tn, h, 8:16], in_=cand2[:Tn, h, :]), reads=[cand2.b], writes=[bs.b])
                    for h in range(8):
                        S.op("dve", lambda e, h=h, Tn=Tn: e.max_index(out=bpu[:Tn, h, 8:16], in_max=bs[:Tn, h, 8:16], in_values=cand2[:Tn, h, :]),
                             reads=[cand2.b, bs.b], writes=[bpu.b])
                    S.op("dve", lambda e, Tn=Tn: e.tensor_tensor(out=gate[:Tn], in0=bs[:Tn], in1=bs[:Tn, :, 0:1].to_broadcast([Tn, 8, 16]), op=ALU.subtract),
                         reads=[bs.b], writes=[gate.b])
                    S.op("act", lambda e, Tn=Tn: e.activation(out=gate[:Tn], in_=gate[:Tn], func=AF.Exp), reads=[gate.b], writes=[gate.b])
                    S.op("dve", lambda e, Tn=Tn: e.tensor_reduce(out=gsum[:Tn], in_=gate[:Tn], axis=AX.X, op=ALU.add), reads=[gate.b], writes=[gsum.b])
                    S.op("dve", lambda e, Tn=Tn: e.reciprocal(out=gsum[:Tn], in_=gsum[:Tn]), reads=[gsum.b], writes=[gsum.b])
                    S.op("dve", lambda e, Tn=Tn: e.tensor_tensor(out=gate[:Tn], in0=gate[:Tn], in1=gsum[:Tn].unsqueeze(2).to_broadcast([Tn, 8, 16]), op=ALU.mult),
                         reads=[gate.b, gsum.b], writes=[gate.b])
                    S.op("dve", lambda e, Tn=Tn: e.tensor_copy(out=bpf[:Tn], in_=bpu[:Tn]), reads=[bpu.b], writes=[bpf.b])
                    S.op("dve", lambda e, Tn=Tn: e.tensor_scalar(out=fb_[:Tn], in0=bpf[:Tn], scalar1=0.0625, scalar2=-0.46875, op0=ALU.mult, op1=ALU.add),
                         reads=[bpf.b], writes=[fb_.b])
                    S.op("dve", lambda e, Tn=Tn: e.tensor_copy(out=ia[:Tn], in_=fb_[:Tn]), reads=[fb_.b], writes=[ia.b])
                    S.op("dve", lambda e, Tn=Tn: e.tensor_copy(out=fa[:Tn], in_=ia[:Tn]), reads=[ia.b], writes=[fa.b])
                    S.op("dve", lambda e, Tn=Tn: e.scalar_tensor_tensor(out=fb_[:Tn], in0=fa[:Tn], scalar=-16.0, in1=bpf[:Tn], op0=ALU.mult, op1=ALU.add),
                         reads=[fa.b, bpf.b, fb_.b], writes=[fb_.b])
                    for (pos, cc, dst) in ((fa, 0, i0f), (fb_, 1, i1f)):
                        S.op("dve", lambda e, pos=pos, Tn=Tn: e.tensor_tensor(
                            out=oh[:Tn], in0=iot[:Tn, :].unsqueeze(1).unsqueeze(1).to_broadcast([Tn, 8, 16, 16]),
                            in1=pos[:Tn].unsqueeze(3).to_broadcast([Tn, 8, 16, 16]), op=ALU.is_equal), reads=[iot.b, pos.b], writes=[oh.b])
                        S.op("dve", lambda e, cc=cc, Tn=Tn, tfv=tfv: e.tensor_tensor(
                            out=oh[:Tn], in0=oh[:Tn], in1=tfv[:, :, cc, :].unsqueeze(2).to_broadcast([Tn, 8, 16, 16]), op=ALU.mult),
                            reads=[oh.b, tif.b], writes=[oh.b])
                        S.op("dve", lambda e, dst=dst, Tn=Tn: e.tensor_reduce(out=dst[:Tn], in_=oh[:Tn], axis=AX.X, op=ALU.add), reads=[oh.b], writes=[dst.b])
                    S.op("dve", lambda e, Tn=Tn: e.scalar_tensor_tensor(out=i0f[:Tn], in0=i0f[:Tn], scalar=128.0, in1=i1f[:Tn], op0=ALU.mult, op1=ALU.add),
                         reads=[i0f.b, i1f.b], writes=[i0f.b])
                    S.op("dve", lambda e, Tn=Tn: e.tensor_copy(out=idx[:Tn, :], in_=i0f[:Tn].rearrange("p h k -> p (h k)")), reads=[i0f.b], writes=[idx.b])
                def CDEF(n, cap):
                    ti, t = own_tiles[n]
                    h32, idx, gate = h32s[n % 2], idxs[n % 2], gates[n % 2]
                    Tn = t["T"]
                    x = xt[n % 2]
                    per = (len(cap) + 179) // 180
                    S.op("dve", lambda e: e.memset(act_[:], 0.0), writes=[act_.b])
                    for s_ in range(128):
                        g = gb[gcnt[0] % NG]
                        gcnt[0] += 1
                        S.op("pool", lambda e, g=g, s_=s_: e.indirect_dma_start(out=g[:, :], out_offset=None, in_=expert_u.rearrange("l e d -> (l e) d"),
                                                                                in_offset=bass.IndirectOffsetOnAxis(ap=idx[:, s_:s_ + 1], axis=0),
                                                                                element_offset=l * 16384 * D),
                             reads=[idx.b], writes=[g.b], chan=g.b.name)
                        S.op("dve", lambda e, g=g, s_=s_, Tn=Tn: e.scalar_tensor_tensor(out=pj[:Tn], in0=g[:Tn], scalar=1.0, in1=h32[:Tn],
                                                                                        op0=ALU.mult, op1=ALU.mult, accum_out=act_[:Tn, s_:s_ + 1]),
                             reads=[g.b, h32.b], writes=[pj.b, act_.b])
                        S.replay(cap, per)
                    S.op("dve", lambda e, Tn=Tn: e.tensor_tensor(out=g1[:Tn], in0=act_[:Tn], in1=act_[:Tn], op=ALU.mult), reads=[act_.b], writes=[g1.b])
                    S.op("dve", lambda e, Tn=Tn: e.tensor_scalar(out=g1[:Tn], in0=g1[:Tn], scalar1=0.044715, scalar2=1.0, op0=ALU.mult, op1=ALU.add),
                         reads=[g1.b], writes=[g1.b])
                    S.op("dve", lambda e, Tn=Tn: e.tensor_tensor(out=g1[:Tn], in0=g1[:Tn], in1=act_[:Tn], op=ALU.mult), reads=[g1.b, act_.b], writes=[g1.b])
                    S.op("act", lambda e, Tn=Tn: e.activation(out=g2[:Tn], in_=g1[:Tn], func=AF.Sigmoid, scale=1.5957691216057308),
                         reads=[g1.b], writes=[g2.b])
                    S.op("dve", lambda e, Tn=Tn: e.tensor_tensor(out=g2[:Tn], in0=g2[:Tn], in1=act_[:Tn], op=ALU.mult), reads=[g2.b, act_.b], writes=[g2.b])
                    S.op("dve", lambda e, Tn=Tn: e.tensor_tensor(out=wgt[:Tn], in0=g2[:Tn], in1=gate[:Tn].rearrange("p h k -> p (h k)"), op=ALU.mult),
                         reads=[g2.b, gate.b], writes=[wgt.b])
                    for s_ in range(128):
                        g = gb[gcnt[0] % NG]
                        gcnt[0] += 1
                        gv = g.ap.bitcast(BF16)
                        S.op("pool", lambda e, gv=gv, s_=s_: e.indirect_dma_start(out=gv[:, 0:D], out_offset=None, in_=expert_v.rearrange("l e d -> (l e) d"),
                                                                                in_offset=bass.IndirectOffsetOnAxis(ap=idx[:, s_:s_ + 1], axis=0),
                                                                                element_offset=l * 16384 * D),
                             reads=[idx.b], writes=[g.b], chan=g.b.name)
                        dg = dgs[s_ % 4]
                        S.op("act", lambda e, dg=dg, s_=s_, Tn=Tn: e.activation(out=dg[:Tn, :Tn], in_=identb[:Tn, :Tn], func=AF.Copy, scale=wgt[:Tn, s_:s_ + 1]),
                             reads=[identb.b, wgt.b], writes=[dg.b])
                        for nn in range(2):
                            S.op("pe", lambda e, dg=dg, gv=gv, s_=s_, nn=nn, Tn=Tn: e.matmul(psV[nn][:Tn, :], lhsT=dg[:Tn, :Tn], rhs=gv[:Tn, nn * 512:(nn + 1) * 512],
                                                                                          start=(s_ == 0), stop=(s_ == 127)),
                                 reads=[dg.b, g.b], writes=[psV[nn].b])
                        S.replay(cap, per)
                    for nn in range(2):
                        S.op("dve", lambda e, x=x, nn=nn, Tn=Tn: e.tensor_tensor(out=x[:Tn, nn * 512:(nn + 1) * 512], in0=psV[nn][:Tn, :], in1=x[:Tn, nn * 512:(nn + 1) * 512],
                                                                                op=ALU.add), reads=[psV[nn].b, x.b], writes=[x.b])
                    if final:
                        ss = fss
                        S.op("dve", lambda e, ss=ss, Tn=Tn: e.memset(ss[:Tn, 0:1], 0.0), writes=[ss.b])
                        S.op("act", lambda e, ss=ss, x=x, Tn=Tn: e.activation(out=pj[:Tn], in_=x[:Tn], func=AF.Square, accum_out=ss[:Tn, 0:1]),
                             reads=[x.b, ss.b], writes=[pj.b, ss.b])
                        S.op("act", lambda e, ss=ss, Tn=Tn: e.activation(out=ss[:Tn, 1:2], in_=ss[:Tn, 0:1], func=AF.Sqrt, scale=1.0 / D, bias=1e-6),
                             reads=[ss.b], writes=[ss.b])
                        S.op("dve", lambda e, ss=ss, Tn=Tn: e.reciprocal(out=ss[:Tn, 1:2], in_=ss[:Tn, 1:2]), reads=[ss.b], writes=[ss.b])
                        S.op("dve", lambda e, ss=ss, x=x, Tn=Tn: e.scalar_tensor_tensor(out=x[:Tn], in0=x[:Tn], scalar=ss[:Tn, 1:2], in1=grz[:Tn],
                                                                                       op0=ALU.mult, op1=ALU.mult), reads=[x.b, ss.b, grz.b], writes=[x.b])
                        dst = y_p[t["slot"] * 128:(t["slot"] + 1) * 128, :] if t["prompt"] else y_s[t["s"] * 64:(t["s"] + 1) * 64, :]
                        S.op("sp", lambda e, x=x, dst=dst, Tn=Tn: e.dma_start(out=dst, in_=x[:Tn]), reads=[x.b], writes=[S.db("yout")], chan="st_" + x.b.name)
                    elif split and t["prompt"]:
                        S.op("sp", lambda e, x=x, t=t: e.dma_start(out=SND[t["slot"] * 128:(t["slot"] + 1) * 128, :], in_=x[:, :]),
                             reads=[x.b], writes=[S.db("SND")], chan="st_" + x.b.name)
                    else:
                        S.op("sp", lambda e, x=x, t=t, Tn=Tn: e.dma_start(out=XS[t["tok0"]:t["tok0"] + Tn, :], in_=x[:Tn]),
                             reads=[x.b], writes=[S.db("XS%d" % ti)], chan="st_" + x.b.name)

                if split:
                    for k in range(NOWN):
                        g = gb[k % NG]
                        S.op("pool", lambda e, g=g, k=k: e.indirect_dma_start(out=g[:, :], out_offset=None, in_=XS,
                                                                              in_offset=bass.IndirectOffsetOnAxis(ap=ridx[:, k:k + 1], axis=0)),
                             reads=[ridx.b], writes=[g.b], chan=g.b.name)
                        S.op("sp", lambda e, g=g, k=k: e.dma_start(out=XOWN[k * 128:(k + 1) * 128, :], in_=g[:, :]),
                             reads=[g.b], writes=[S.db("XOWN%d" % k)], chan="st_" + g.b.name)
                cap = S.capture(lambda: AB(0))
                S.replay(cap, len(cap))
                for n in range(len(own_tiles)):
                    cap = S.capture(lambda: AB(n + 1)) if n + 1 < len(own_tiles) else []
                    if peer:
                        CDEF(n, cap)
                    S.replay(cap, len(cap))
                S.emit()
                rr["npsf"] = 6
                if split and not final:
                    for j in range(NOWN // CHT):
                        S.op("pool", lambda e, j=j: e.collective_compute("AllGather", ALU.bypass, replica_groups=[[0, 4], [1, 5], [2, 6], [3, 7]],
                                                                         ins=[SND[j * CHT * 128:(j + 1) * CHT * 128, :]], outs=[RCV[j]]),
                             writes=[S.db("RCV%d" % j)], chan="cc", amt=1)
                    S.emit()

        def dump_xs():
            S.op("sp", lambda e: e.dma_start(out=y_p[:, :], in_=XS[0:NPT * 128, :]), writes=[S.db("yout")], chan="dbg")
            S.op("sp", lambda e: e.dma_start(out=y_s[:, :], in_=XS[NPT * 128:NTOK, :]), writes=[S.db("yout")], chan="dbg")
            S.emit()
        if stop_after == 2:
            dump_xs()
            return nc
        if stop_after == 25:
            cross_peer(0, False, peer=False)
            dump_xs()
            return nc
        cross_peer(0, False)
        if stop_after == 3:
            dump_xs()
            return nc
        P4(nc, S, locals())
        if stop_after == 4:
            dump_xs()
            return nc
        cross_peer(1, True)
    return nc


def P4(nc, S, G):
    tiles = G["tiles"]; PSF = G["PSF"]; PSB = G["PSB"]; phase_tiles = G["phase_tiles"]; load_w = G["load_w"]
    load_cols = G["load_cols"]; load_rep = G["load_rep"]; make_rms = G["make_rms"]; mm_feat = G["mm_feat"]; mm_tok = G["mm_tok"]
    identf = G["identf"]; identb = G["identb"]; onesf = G["onesf"]; XS = G["XS"]
    NPT = G["NPT"]; NSS = G["NSS"]
    with ExitStack() as es:
        sb = phase_tiles(es, "p4")
        w = load_w(sb, G["w_in_cd"], 2568, "win")
        wo = load_w(sb, G["w_out_cd"], D, "wo")
        gcol = load_cols(sb, G["norm_mix_g"][1:2, :], 8, "gcol")
        lcg = load_rep(sb, G["ln_c_g"], 512, "lcg")
        lcb = load_rep(sb, G["ln_c_b"], 512, "lcb")
        ndg = load_rep(sb, G["norm_d_g"], 512, "ndg")
        dtb = load_rep(sb, G["dt_bias"], 8, "dtb")
        alog = load_rep(sb, G["a_log"], 8, "alog")
        dsk = load_rep(sb, G["d_skip"], 8, "dsk")
        cdw = sb([128, 8, 4], F32, "cdw")
        for c in range(8):
            S.op("sp", lambda e, c=c: e.dma_start(out=cdw[:, c, :], in_=G["conv_d_w"][:, c * 128:(c + 1) * 128].rearrange("k p -> p k"),
                                                  allow_slow_non_contiguous=True), writes=[cdw.b], chan="cdw")
        cdb = load_cols(sb, G["conv_d_b"], 8, "cdb")
        ws32 = sb([128, 4, 128], F32, "ws32")
        wsT = sb([128, 4, 128], BF16, "wsT")
        bsc = sb([128, 4], F32, "bsc")
        S.op("sp", lambda e: e.dma_start(out=ws32[:], in_=G["gm_w_s"].rearrange("g i j -> i g j")), writes=[ws32.b], chan="ws32")
        S.op("sp", lambda e: e.dma_start(out=bsc[:], in_=G["gm_b_s"].rearrange("g i -> i g"), allow_slow_non_contiguous=True), writes=[bsc.b], chan="bsc")
        for g in range(4):
            S.op("pool", lambda e, g=g: e.affine_select(out=ws32[:, g, :], in_=ws32[:, g, :], pattern=[[-1, 128]], compare_op=ALU.is_ge, fill=0.0,
                                                        base=0, channel_multiplier=1), reads=[ws32.b], writes=[ws32.b])
        ps = PSF()
        for g in range(4):
            S.op("pe", lambda e, g=g, ps=ps: e.transpose(out=ps[:, g * 128:(g + 1) * 128], in_=ws32[:, g, :], identity=identf[:, :]),
                 reads=[ws32.b, identf.b], writes=[ps.b])
        S.op("act", lambda e, ps=ps: e.copy(out=wsT[:], in_=ps[:, :].rearrange("p (g i) -> p g i", g=4)), reads=[ps.b], writes=[wsT.b])
        tri = sb([64, 64], F32, "tri")
        negm = sb([64, 64], F32, "negm")
        ones64 = sb([64, 128], F32, "ones64")
        S.op("pool", lambda e: e.memset(tri[:], 1.0), writes=[tri.b])
        S.op("pool", lambda e: e.affine_select(out=tri[:], in_=tri[:], pattern=[[1, 64]], compare_op=ALU.is_ge, fill=0.0, base=0, channel_multiplier=-1),
             reads=[tri.b], writes=[tri.b])
        S.op("pool", lambda e: e.memset(negm[:], 0.0), writes=[negm.b])
        S.op("pool", lambda e: e.affine_select(out=negm[:], in_=negm[:], pattern=[[1, 64]], compare_op=ALU.is_ge, fill=NEG, base=0, channel_multiplier=-1),
             reads=[negm.b], writes=[negm.b])
        S.op("pool", lambda e: e.memset(ones64[:], 1.0), writes=[ones64.b])
        Arep = sb([128, 8], F32, "Arep")
        S.op("act", lambda e: e.activation(out=Arep[:], in_=alog[:], func=AF.Exp), reads=[alog.b], writes=[Arep.b])
        S.op("dve", lambda e: e.tensor_scalar(out=Arep[:], in0=Arep[:], scalar1=-1.0, scalar2=None, op0=ALU.mult), reads=[Arep.b], writes=[Arep.b])
        rms_T = make_rms(sb)
        xt = [sb([128, D], F32, "x") for _ in range(2)]
        hT = sb([128, 8, 128], BF16, "hT")
        cg = sb([128, 1024], F32, "cg")
        t1 = sb([128, 1024], F32, "t1")
        t2 = sb([128, 1024], F32, "t2")
        st_ = sb([128, 8], F32, "st")
        vvn = sb([128, 512], F32, "vvn")
        vvb = sb([128, 512], BF16, "vvb")
        cout = sb([128, 512], BF16, "cout")
        cyT = sb([128, 8, 128], BF16, "cyT")
        cbuf = sb([128, 8, 131], F32, "cbuf")
        xc = sb([128, 8, 128], F32, "xc")
        cacc = [sb([128, 128], F32, "cacc%d" % c) for c in range(8)]
        xstok = sb([128, 512], F32, "xstok")
        btok = sb([64, 2, 256], F32, "btok")
        gate = sb([128, 512], F32, "gate")
        dtt = sb([128, 8], F32, "dtt")
        dta = sb([128, 8], F32, "dta")
        ST = sb([128, 8, 64], F32, "ST")
        stin = sb([64, 8, 128], F32, "stin")
        c3 = sb([8, 1024], F32, "c3")
        yt = sb([128, 512], F32, "yt")
        acs = sb([64, 8], F32, "acs")
        nacs = sb([64, 8], F32, "nacs")
        eacs = sb([64, 8], F32, "eacs")
        dec = sb([64, 8], F32, "dec")
        cdec = sb([128, 8], F32, "cdec")
        LT = sb([64, 8, 64], F32, "LT")
        MT = sb([64, 8, 64], F32, "MT")
        xdt = sb([64, 8, 64], F32, "xdt")
        xdd = sb([64, 8, 64], F32, "xdd")
        ydg = sb([64, 512], F32, "ydg")
        dtab = sb([64, 8, 64], F32, "dtab")
        st2 = sb([128, 8], F32, "st2")
        cout2 = sb([128, 512], BF16, "cout2")
        for ti, t in enumerate(tiles):
            Tn = t["T"]
            x = xt[ti % 2]
            if G["split"] and t["prompt"]:
                NOWN, CHT = G["NOWN"], G["CHT"]
                rk, kk = t["i"] // NOWN, t["i"] % NOWN
                src = G["RCV"][kk // CHT][rk * CHT * 128 + (kk % CHT) * 128: rk * CHT * 128 + (kk % CHT + 1) * 128, :]
                S.op("sp", lambda e, x=x, src=src: e.dma_start(out=x[:, :], in_=src), writes=[x.b], chan=x.b.name)
            else:
                S.op("sp", lambda e, x=x, t=t, Tn=Tn: e.dma_start(out=x[:Tn], in_=XS[t["tok0"]:t["tok0"] + Tn, :]),
                     reads=[S.db("XS%d" % ti)], writes=[x.b], chan=x.b.name)
            rms_T(x, Tn, gcol, hT)
            def C_branch(x=x, Tn=Tn, t=t):
                for n in range(2):
                    ps = PSF()
                    mm_tok(ps, n * 512, 512, w, hT, Tn)
                    sl = slice(n * 512, (n + 1) * 512)
                    S.op("act", lambda e, ps=ps, sl=sl, Tn=Tn: e.activation(out=t1[:Tn, sl], in_=ps[:Tn, :], func=AF.Square), reads=[ps.b], writes=[t1.b])
                    S.op("dve", lambda e, sl=sl, Tn=Tn: e.tensor_scalar(out=t1[:Tn, sl], in0=t1[:Tn, sl], scalar1=0.044715, scalar2=1.0, op0=ALU.mult, op1=ALU.add),
                         reads=[t1.b], writes=[t1.b])
                    S.op("dve", lambda e, ps=ps, sl=sl, Tn=Tn: e.tensor_tensor(out=t1[:Tn, sl], in0=ps[:Tn, :], in1=t1[:Tn, sl], op=ALU.mult),
                         reads=[ps.b, t1.b], writes=[t1.b])
                    S.op("act", lambda e, sl=sl, Tn=Tn: e.activation(out=t2[:Tn, sl], in_=t1[:Tn, sl], func=AF.Sigmoid, scale=1.5957691216057308),
                         reads=[t1.b], writes=[t2.b])
                    S.op("dve", lambda e, ps=ps, sl=sl, Tn=Tn: e.tensor_tensor(out=cg[:Tn, sl], in0=ps[:Tn, :], in1=t2[:Tn, sl], op=ALU.mult),
                         reads=[ps.b, t2.b], writes=[cg.b])
                S.op("dve", lambda e, Tn=Tn: e.memset(st_[:Tn, 0:2], 0.0), writes=[st_.b])
                S.op("act", lambda e, Tn=Tn: e.activation(out=t1[:Tn, 0:512], in_=cg[:Tn, 512:1024], func=AF.Copy, accum_out=st_[:Tn, 0:1]),
                     reads=[cg.b, st_.b], writes=[t1.b, st_.b])
                S.op("act", lambda e, Tn=Tn: e.activation(out=t1[:Tn, 512:1024], in_=cg[:Tn, 512:1024], func=AF.Square, accum_out=st_[:Tn, 1:2]),
                     reads=[cg.b, st_.b], writes=[t1.b, st_.b])
                S.op("dve", lambda e, Tn=Tn: e.tensor_scalar(out=st_[:Tn, 2:4], in0=st_[:Tn, 0:2], scalar1=1.0 / 512, scalar2=None, op0=ALU.mult),
                     reads=[st_.b], writes=[st_.b])
                S.op("dve", lambda e, Tn=Tn: e.tensor_tensor(out=st_[:Tn, 4:5], in0=st_[:Tn, 2:3], in1=st_[:Tn, 2:3], op=ALU.mult), reads=[st_.b], writes=[st_.b])
                S.op("dve", lambda e, Tn=Tn: e.tensor_tensor(out=st_[:Tn, 4:5], in0=st_[:Tn, 3:4], in1=st_[:Tn, 4:5], op=ALU.subtract), reads=[st_.b], writes=[st_.b])
                S.op("act", lambda e, Tn=Tn: e.activation(out=st_[:Tn, 5:6], in_=st_[:Tn, 4:5], func=AF.Sqrt, bias=1e-5, scale=1.0), reads=[st_.b], writes=[st_.b])
                S.op("dve", lambda e, Tn=Tn: e.reciprocal(out=st_[:Tn, 5:6], in_=st_[:Tn, 5:6]), reads=[st_.b], writes=[st_.b])
                S.op("dve", lambda e, Tn=Tn: e.tensor_scalar(out=vvn[:Tn], in0=cg[:Tn, 512:1024], scalar1=st_[:Tn, 2:3], scalar2=st_[:Tn, 5:6],
                                                             op0=ALU.subtract, op1=ALU.mult), reads=[cg.b, st_.b], writes=[vvn.b])
                S.op("dve", lambda e, Tn=Tn: e.tensor_tensor(out=vvn[:Tn], in0=vvn[:Tn], in1=lcg[:Tn], op=ALU.mult), reads=[vvn.b, lcg.b], writes=[vvn.b])
                S.op("dve", lambda e, Tn=Tn: e.tensor_tensor(out=vvn[:Tn], in0=vvn[:Tn], in1=lcb[:Tn], op=ALU.add), reads=[vvn.b, lcb.b], writes=[vvn.b])
                if not t["prompt"]:
                    S.op("sp", lambda e, t=t: e.dma_start(out=G["o_gvs"][t["s"] * 64:(t["s"] + 1) * 64, :], in_=vvn[:64]), reads=[vvn.b],
                         writes=[S.db("ogvs")], chan="st_vvn")
                S.op("act", lambda e, Tn=Tn: e.copy(out=vvb[:Tn], in_=vvn[:Tn]), reads=[vvn.b], writes=[vvb.b])
                ps = PSF()
                for g in range(4):
                    S.op("pe", lambda e, g=g, ps=ps, Tn=Tn: e.matmul(ps[:Tn, g * 128:(g + 1) * 128], lhsT=wsT[:Tn, g, :Tn], rhs=vvb[:Tn, g * 128:(g + 1) * 128],
                                                                    start=True, stop=True), reads=[wsT.b, vvb.b], writes=[ps.b])
                S.op("dve", lambda e, ps=ps, Tn=Tn: e.tensor_tensor(out=t1[:Tn, 0:512].rearrange("p (g d) -> p g d", g=4),
                                                                   in0=ps[:Tn, :].rearrange("p (g d) -> p g d", g=4),
                                                                   in1=bsc[:Tn, :].unsqueeze(2).to_broadcast([Tn, 4, 128]), op=ALU.add),
                     reads=[ps.b, bsc.b], writes=[t1.b])
                S.op("dve", lambda e, Tn=Tn: e.tensor_tensor(out=cout[:Tn], in0=t1[:Tn, 0:512], in1=cg[:Tn, 0:512], op=ALU.mult), reads=[t1.b, cg.b], writes=[cout.b])
                pb = PSB()
                for c in range(4):
                    S.op("pe", lambda e, c=c, pb=pb, Tn=Tn: e.transpose(out=pb[:, c * Tn:(c + 1) * Tn], in_=cout[:Tn, c * 128:(c + 1) * 128], identity=identb[:Tn, :Tn]),
                         reads=[cout.b, identb.b], writes=[pb.b])
                S.op("act", lambda e, pb=pb, Tn=Tn: e.copy(out=cyT[:, 0:4, :Tn], in_=pb[:, 0:4 * Tn].rearrange("p (c t) -> p c t", c=4)), reads=[pb.b], writes=[cyT.b])

            S.inter = S.capture(C_branch)
            if t["first"]:
                if t["prompt"]:
                    S.op("pool", lambda e: e.memset(cbuf[:, :, 0:3], 0.0), writes=[cbuf.b])
                    S.op("pool", lambda e: e.memset(ST[:], 0.0), writes=[ST.b])
                else:
                    S.op("sp", lambda e, t=t: e.dma_start(out=c3[0:3, :], in_=G["scs"][t["s"]]), writes=[c3.b], chan="c3")
                    ps = PSF()
                    for c in range(8):
                        S.op("pe", lambda e, c=c, ps=ps: e.transpose(out=ps[:, c * 3:(c + 1) * 3], in_=c3[0:3, c * 128:(c + 1) * 128], identity=identf[0:3, 0:3]),
                             reads=[c3.b, identf.b], writes=[ps.b])
                    S.op("act", lambda e, ps=ps: e.copy(out=cbuf[:, :, 0:3], in_=ps[:, 0:24].rearrange("p (c t) -> p c t", c=8)), reads=[ps.b], writes=[cbuf.b])
                    S.op("sp", lambda e, t=t: e.dma_start(out=stin[:], in_=G["sssd"][t["s"]].rearrange("h p n -> p h n")), writes=[stin.b], chan="stin")
                    ps = PSF()
                    for h in range(8):
                        S.op("pe", lambda e, h=h, ps=ps: e.transpose(out=ps[:, h * 64:(h + 1) * 64], in_=stin[:, h, :], identity=identf[0:64, 0:64]),
                             reads=[stin.b, identf.b], writes=[ps.b])
                    S.op("act", lambda e, ps=ps: e.copy(out=ST[:], in_=ps[:, :].rearrange("p (h q) -> p h q", h=8)), reads=[ps.b], writes=[ST.b])
            else:
                S.op("pool", lambda e: e.tensor_copy(out=cbuf[:, :, 0:3], in_=cbuf[:, :, 128:131]), reads=[cbuf.b], writes=[cbuf.b])
            for half in range(2):
                ps = PSF()
                mm_feat(ps, 1536 + half * 512, 4, w, hT, Tn)
                S.op("act", lambda e, ps=ps, half=half, Tn=Tn: e.copy(out=cbuf[:, half * 4:half * 4 + 4, 3:3 + Tn],
                                                                     in_=ps[:, 0:4 * Tn].rearrange("p (c t) -> p c t", c=4)), reads=[ps.b], writes=[cbuf.b])
            if t["last"]:
                for half in range(2):
                    ps = PSF()
                    for cc in range(4):
                        c = half * 4 + cc
                        S.op("pe", lambda e, c=c, cc=cc, ps=ps, Tn=Tn: e.transpose(out=ps[0:3, cc * 128:(cc + 1) * 128], in_=cbuf[:, c, Tn:Tn + 3], identity=identf[:, :]),
                             reads=[cbuf.b, identf.b], writes=[ps.b])
                    S.op("act", lambda e, ps=ps, half=half: e.copy(out=c3[0:3, half * 512:(half + 1) * 512], in_=ps[0:3, :]), reads=[ps.b], writes=[c3.b])
                dst = G["o_csp"] if t["prompt"] else G["o_css"][t["s"]]
                S.op("sp", lambda e, dst=dst: e.dma_start(out=dst, in_=c3[0:3, :]), reads=[c3.b], writes=[S.db("ocs")], chan="st_c3")
            for k in range(4):
                for c in range(8):
                    eng = "dve"
                    if k == 0:
                        S.op(eng, lambda e, c=c, Tn=Tn: e.tensor_scalar(out=cacc[c][:, :Tn], in0=cbuf[:, c, 0:Tn], scalar1=cdw[:, c, 0:1], scalar2=None, op0=ALU.mult),
                             reads=[cbuf.b, cdw.b], writes=[cacc[c].b])
                    else:
                        S.op(eng, lambda e, c=c, k=k, Tn=Tn: e.scalar_tensor_tensor(out=cacc[c][:, :Tn], in0=cbuf[:, c, k:k + Tn], scalar=cdw[:, c, k:k + 1],
                                                                                    in1=cacc[c][:, :Tn], op0=ALU.mult, op1=ALU.add),
                             reads=[cbuf.b, cdw.b, cacc[c].b], writes=[cacc[c].b])
            for c in range(8):
                S.op("act", lambda e, c=c, Tn=Tn: e.activation(out=xc[:, c, :Tn], in_=cacc[c][:, :Tn], func=AF.Silu, bias=cdb[:, c:c + 1], scale=1.0),
                     reads=[cacc[c].b, cdb.b], writes=[xc.b])
            for ch in range(Tn // 64):
                c0 = ch * 64
                cs = slice(c0, c0 + 64)
                ps = PSF()
                for c in range(4):
                    S.op("pe", lambda e, c=c, ps=ps, cs=cs: e.transpose(out=ps[0:64, c * 128:(c + 1) * 128], in_=xc[:, c, cs], identity=identf[:, :]),
                         reads=[xc.b, identf.b], writes=[ps.b])
                S.op("act", lambda e, ps=ps: e.copy(out=xstok[0:64, :], in_=ps[0:64, :]), reads=[ps.b], writes=[xstok.b])
                ps = PSF()
                for g in range(2):
                    S.op("pe", lambda e, g=g, ps=ps, cs=cs: e.transpose(out=ps[0:64, g * 128:(g + 1) * 128], in_=xc[:, 4 + g, cs], identity=identf[:, :]),
                         reads=[xc.b, identf.b], writes=[ps.b])
                S.op("act", lambda e, ps=ps: e.copy(out=btok[:, 0, :], in_=ps[0:64, 0:256]), reads=[ps.b], writes=[btok.b])
                ps = PSF()
                for c in range(8):
                    S.op("pe", lambda e, c=c, ps=ps, cs=cs: e.matmul(ps[0:64, :], lhsT=hT[:, c, cs], rhs=w[:, c, 1024:1536], start=(c == 0), stop=(c == 7)),
                         reads=[hT.b, w.b], writes=[ps.b])
                S.op("act", lambda e, ps=ps: e.activation(out=gate[0:64, :], in_=ps[0:64, :], func=AF.Silu), reads=[ps.b], writes=[gate.b])
                ps = PSF()
                for c in range(8):
                    S.op("pe", lambda e, c=c, ps=ps, cs=cs: e.matmul(ps[0:64, 0:8], lhsT=hT[:, c, cs], rhs=w[:, c, 2560:2568], start=(c == 0), stop=(c == 7)),
                         reads=[hT.b, w.b], writes=[ps.b])
                S.op("dve", lambda e, ps=ps: e.tensor_tensor(out=dtt[0:64, :], in0=ps[0:64, 0:8], in1=dtb[0:64, :], op=ALU.add), reads=[ps.b, dtb.b], writes=[dtt.b])
                S.op("act", lambda e: e.activation(out=dtt[0:64, :], in_=dtt[0:64, :], func=AF.Exp), reads=[dtt.b], writes=[dtt.b])
                S.op("act", lambda e: e.activation(out=dtt[0:64, :], in_=dtt[0:64, :], func=AF.Ln, bias=1.0, scale=1.0), reads=[dtt.b], writes=[dtt.b])
                S.op("dve", lambda e: e.tensor_tensor(out=dta[0:64, :], in0=dtt[0:64, :], in1=Arep[0:64, :], op=ALU.mult), reads=[dtt.b, Arep.b], writes=[dta.b])
                S.op("dve", lambda e: e.tensor_copy(out=dtab[:], in_=dta[0:64, :].unsqueeze(2).to_broadcast([64, 8, 64])), reads=[dta.b], writes=[dtab.b])
                psa = PSF()
                S.op("pe", lambda e, psa=psa: e.matmul(psa[0:64, 0:8], lhsT=tri[:, :], rhs=dta[0:64, :], start=True, stop=True), reads=[tri.b, dta.b], writes=[psa.b])
                S.op("act", lambda e, psa=psa: e.copy(out=acs[:], in_=psa[0:64, 0:8]), reads=[psa.b], writes=[acs.b])
                S.op("dve", lambda e, psa=psa: e.tensor_scalar(out=nacs[:], in0=psa[0:64, 0:8], scalar1=-1.0, scalar2=None, op0=ALU.mult), reads=[psa.b], writes=[nacs.b])
                S.op("act", lambda e: e.activation(out=eacs[:], in_=acs[:], func=AF.Exp), reads=[acs.b], writes=[eacs.b])
                psl = PSF()
                S.op("pe", lambda e, psl=psl: e.matmul(psl[:, 0:8], lhsT=ones64[:, :], rhs=dta[0:64, :], start=True, stop=True), reads=[ones64.b, dta.b], writes=[psl.b])
                S.op("act", lambda e, psl=psl: e.activation(out=cdec[:], in_=psl[:, 0:8], func=AF.Exp), reads=[psl.b], writes=[cdec.b])
                S.op("dve", lambda e, psl=psl: e.tensor_tensor(out=dec[:], in0=psl[0:64, 0:8], in1=acs[:], op=ALU.subtract), reads=[psl.b, acs.b], writes=[dec.b])
                S.op("act", lambda e: e.activation(out=dec[:], in_=dec[:], func=AF.Exp), reads=[dec.b], writes=[dec.b])
                psb_ = PSF()
                for h in range(8):
                    S.op("pe", lambda e, h=h, psb_=psb_: e.matmul(psb_[0:64, h * 64:(h + 1) * 64], lhsT=dtab[:, h, :], rhs=tri[:, :], start=True, stop=True),
                         reads=[dtab.b, tri.b], writes=[psb_.b])
                S.op("dve", lambda e, psb_=psb_: e.tensor_tensor(out=LT[:], in0=psb_[0:64, :].rearrange("p (h i) -> p h i", h=8),
                                                                 in1=negm[:, :].unsqueeze(1).to_broadcast([64, 8, 64]), op=ALU.add),
                     reads=[psb_.b, negm.b], writes=[LT.b])
                for h in range(8):
                    S.op("act", lambda e, h=h: e.activation(out=LT[:, h, :], in_=LT[:, h, :], func=AF.Exp, bias=nacs[:, h:h + 1], scale=1.0),
                         reads=[LT.b, nacs.b], writes=[LT.b])
                psc = PSF()
                for g in range(2):
                    S.op("pe", lambda e, g=g, psc=psc, cs=cs: e.matmul(psc[0:64, g * 64:(g + 1) * 64], lhsT=xc[:, 4 + g, cs], rhs=xc[:, 6 + g, cs], start=True, stop=True),
                         reads=[xc.b], writes=[psc.b])
                S.op("dve", lambda e, psc=psc: e.tensor_tensor(out=MT[:].rearrange("p (g r) i -> p g r i", g=2), in0=LT[:].rearrange("p (g r) i -> p g r i", g=2),
                                                               in1=psc[0:64, 0:128].rearrange("p (g i) -> p g i", g=2).unsqueeze(2).to_broadcast([64, 2, 4, 64]),
                                                               op=ALU.mult), reads=[psc.b, LT.b], writes=[MT.b])
                S.op("dve", lambda e: e.tensor_tensor(out=xdt[:], in0=xstok[0:64, :].rearrange("p (h q) -> p h q", h=8),
                                                      in1=dtt[0:64, :].unsqueeze(2).to_broadcast([64, 8, 64]), op=ALU.mult), reads=[xstok.b, dtt.b], writes=[xdt.b])
                S.op("dve", lambda e: e.tensor_tensor(out=xdd[:], in0=xdt[:], in1=dec[:, :].unsqueeze(2).to_broadcast([64, 8, 64]), op=ALU.mult),
                     reads=[xdt.b, dec.b], writes=[xdd.b])
                psy, pso, psn = PSF(), PSF(), PSF()
                for h in range(8):
                    S.op("pe", lambda e, h=h, psy=psy: e.matmul(psy[0:64, h * 64:(h + 1) * 64], lhsT=MT[:, h, :], rhs=xdt[:, h, :], start=True, stop=True),
                         reads=[MT.b, xdt.b], writes=[psy.b])
                for h in range(8):
                    S.op("pe", lambda e, h=h, pso=pso, cs=cs: e.matmul(pso[0:64, h * 64:(h + 1) * 64], lhsT=xc[:, 6 + h // 4, cs], rhs=ST[:, h, :], start=True, stop=True),
                         reads=[xc.b, ST.b], writes=[pso.b])
                for h in range(8):
                    S.op("pe", lambda e, h=h, psn=psn: e.matmul(psn[:, h * 64:(h + 1) * 64], lhsT=btok[:, 0, (h // 4) * 128:(h // 4 + 1) * 128], rhs=xdd[:, h, :],
                                                                start=True, stop=True), reads=[btok.b, xdd.b], writes=[psn.b])
                S.op("act", lambda e, psy=psy: e.copy(out=ydg[:], in_=psy[0:64, :]), reads=[psy.b], writes=[ydg.b])
                S.op("dve", lambda e, pso=pso: e.tensor_tensor(out=yt[0:64, :].rearrange("p (h q) -> p h q", h=8), in0=pso[0:64, :].rearrange("p (h q) -> p h q", h=8),
                                                               in1=eacs[:, :].unsqueeze(2).to_broadcast([64, 8, 64]), op=ALU.mult), reads=[pso.b, eacs.b], writes=[yt.b])
                S.op("dve", lambda e: e.tensor_tensor(out=yt[0:64, :], in0=yt[0:64, :], in1=ydg[:], op=ALU.add), reads=[yt.b, ydg.b], writes=[yt.b])
                S.op("pool", lambda e: e.tensor_tensor(out=ST[:], in0=ST[:], in1=cdec[:, :].unsqueeze(2).to_broadcast([128, 8, 64]), op=ALU.mult),
                     reads=[ST.b, cdec.b], writes=[ST.b])
                S.op("dve", lambda e, psn=psn: e.tensor_tensor(out=ST[:], in0=psn[:, :].rearrange("p (h q) -> p h q", h=8), in1=ST[:], op=ALU.add),
                     reads=[psn.b, ST.b], writes=[ST.b])
                S.op("dve", lambda e: e.tensor_tensor(out=xdt[:], in0=xstok[0:64, :].rearrange("p (h q) -> p h q", h=8),
                                                      in1=dsk[0:64, :].unsqueeze(2).to_broadcast([64, 8, 64]), op=ALU.mult), reads=[xstok.b, dsk.b, xdt.b], writes=[xdt.b])
                S.op("dve", lambda e: e.tensor_tensor(out=yt[0:64, :], in0=yt[0:64, :], in1=xdt[:].rearrange("p h q -> p (h q)"), op=ALU.add),
                     reads=[yt.b, xdt.b], writes=[yt.b])
                S.op("dve", lambda e: e.tensor_tensor(out=yt[0:64, :], in0=yt[0:64, :], in1=gate[0:64, :], op=ALU.mult), reads=[yt.b, gate.b], writes=[yt.b])
                S.op("dve", lambda e: e.memset(st2[0:64, 0:2], 0.0), writes=[st2.b])
                for g in range(2):
                    S.op("act", lambda e, g=g: e.activation(out=ydg[:, g * 256:(g + 1) * 256], in_=yt[0:64, g * 256:(g + 1) * 256], func=AF.Square,
                                                            accum_out=st2[0:64, g:g + 1]), reads=[yt.b, st2.b], writes=[ydg.b, st2.b])
                S.op("act", lambda e: e.activation(out=st2[0:64, 2:4], in_=st2[0:64, 0:2], func=AF.Sqrt, scale=1.0 / 256, bias=1e-6), reads=[st2.b], writes=[st2.b])
                S.op("dve", lambda e: e.reciprocal(out=st2[0:64, 2:4], in_=st2[0:64, 2:4]), reads=[st2.b], writes=[st2.b])
                S.op("dve", lambda e: e.tensor_tensor(out=yt[0:64, :].rearrange("p (g q) -> p g q", g=2), in0=yt[0:64, :].rearrange("p (g q) -> p g q", g=2),
                                                      in1=st2[0:64, 2:4].unsqueeze(2).to_broadcast([64, 2, 256]), op=ALU.mult), reads=[yt.b, st2.b], writes=[yt.b])
                S.op("dve", lambda e: e.tensor_tensor(out=cout2[0:64, :], in0=yt[0:64, :], in1=ndg[0:64, :], op=ALU.mult), reads=[yt.b, ndg.b], writes=[cout2.b])
                pb = PSB()
                for c in range(4):
                    S.op("pe", lambda e, c=c, pb=pb: e.transpose(out=pb[:, c * 64:(c + 1) * 64], in_=cout2[0:64, c * 128:(c + 1) * 128], identity=identb[0:64, 0:64]),
                         reads=[cout2.b, identb.b], writes=[pb.b])
                S.op("act", lambda e, pb=pb, cs=cs: e.copy(out=cyT[:, 4:8, cs], in_=pb[:, 0:256].rearrange("p (c t) -> p c t", c=4)), reads=[pb.b], writes=[cyT.b])
            if t["last"]:
                for half in range(2):
                    ps = PSF()
                    for hh in range(4):
                        h = half * 4 + hh
                        S.op("pe", lambda e, h=h, hh=hh, ps=ps: e.transpose(out=ps[0:64, hh * 128:(hh + 1) * 128], in_=ST[:, h, :], identity=identf[:, :]),
                             reads=[ST.b, identf.b], writes=[ps.b])
                    S.op("act", lambda e, ps=ps, half=half: e.copy(out=stin[:, half * 4:half * 4 + 4, :], in_=ps[0:64, :].rearrange("p (h n) -> p h n", h=4)),
                         reads=[ps.b], writes=[stin.b])
                dst = G["o_ssdp"] if t["prompt"] else G["o_ssds"][t["s"]]
                S.op("sp", lambda e, dst=dst: e.dma_start(out=dst.rearrange("h p n -> p h n"), in_=stin[:]), reads=[stin.b], writes=[S.db("ossd")], chan="st_stin")
            inter, S.inter = S.inter, None
            S.replay(inter, len(inter))
            for n in range(2):
                ps = PSF()
                mm_tok(ps, n * 512, 512, wo, cyT, Tn)
                S.op("dve", lambda e, n=n, ps=ps, x=x, Tn=Tn: e.tensor_tensor(out=x[:Tn, n * 512:(n + 1) * 512], in0=ps[:Tn, :], in1=x[:Tn, n * 512:(n + 1) * 512], op=ALU.add),
                     reads=[ps.b, x.b], writes=[x.b])
            S.op("sp", lambda e, x=x, t=t, Tn=Tn: e.dma_start(out=XS[t["tok0"]:t["tok0"] + Tn, :], in_=x[:Tn]),
                 reads=[x.b], writes=[S.db("XS%d" % ti)], chan="st_" + x.b.name)
        S.emit()


_W_NAMES = ["norm_mix_g", "norm_cross_g", "norm_ffn_g", "norm_final_g", "w_in_ab", "conv_a_w", "conv_a_b", "ln_a_g", "ln_a_b",
            "lam_q1", "lam_k1", "lam_q2", "lam_k2", "subln_g", "w_out_ab", "w_in_cd", "ln_c_g", "ln_c_b", "gm_w_s", "gm_b_s",
            "conv_d_w", "conv_d_b", "dt_bias", "a_log", "d_skip", "norm_d_g", "w_out_cd", "w_xq", "w_xk", "w_xv", "w_xo", "w_pq",
            "sub_keys", "expert_u", "expert_v"]


def _layout_weights(inp):
    f = lambda a: np.ascontiguousarray(np.asarray(a, dtype=np.float32))
    W = {}
    W["norm_mix_g"] = f(inp["norm_mix_g"]); W["norm_cross_g"] = f(inp["norm_cross_g"]); W["norm_ffn_g"] = f(inp["norm_ffn_g"])
    W["norm_final_g"] = f(inp["norm_final_g"]).reshape(1, D)
    W["w_in_ab"] = f(inp["w_in_ab"][0]); W["conv_a_w"] = f(inp["conv_a_w"][0])
    for k in ("conv_a_b", "ln_a_g", "ln_a_b", "lam_q1", "lam_k1", "lam_q2", "lam_k2", "subln_g", "ln_c_g", "ln_c_b", "conv_d_b",
              "dt_bias", "a_log", "d_skip", "norm_d_g"):
        W[k] = f(inp[k][0]).reshape(1, -1)
    W["w_out_ab"] = f(inp["w_out_ab"][0]); W["w_in_cd"] = f(inp["w_in_cd"][0]); W["gm_w_s"] = f(inp["gm_w_s"][0]); W["gm_b_s"] = f(inp["gm_b_s"][0])
    W["conv_d_w"] = f(inp["conv_d_w"][0]); W["w_out_cd"] = f(inp["w_out_cd"][0])
    for k in ("w_xq", "w_xk", "w_xv", "w_xo", "w_pq", "sub_keys", "expert_u", "expert_v"):
        W[k] = f(inp[k])
    return W


def run_cores(inp, n_cores, NPT, NSS, PL, prompt_of_core, trace=False, stop_after=None, split=False):
    f = lambda a: np.ascontiguousarray(np.asarray(a, dtype=np.float32))
    W = _layout_weights(inp)
    in_maps = []
    for c in range(n_cores):
        b = prompt_of_core(c)
        sl = slice(c * NSS, (c + 1) * NSS)
        m = dict(W)
        m["x_prompt"] = f(inp["x_prompt"][b])
        m["x_sample"] = f(inp["x_sample"][sl]).reshape(NSS * 64, D)
        m["cache_attn_k"] = f(inp["cache_attn_k"][0, sl]).reshape(NSS, PL, 512)
        m["cache_attn_v"] = f(inp["cache_attn_v"][0, sl]).reshape(NSS, PL, 512)
        m["state_conv_a"] = f(inp["state_conv_a"][0, sl])
        m["state_ssd"] = f(inp["state_ssd"][0, sl])
        m["state_conv_ssm"] = f(inp["state_conv_ssm"][0, sl])
        m["cache_mem_k"] = f(inp["cache_mem_k"][:, sl]).reshape(2, NSS, 256, D)
        m["cache_mem_v"] = f(inp["cache_mem_v"][:, sl]).reshape(2, NSS, 256, D)
        m["mem_prompt"] = f(inp["mem_prompt"][b])
        if split:
            rank = c // 4
            nown = NPT // 2
            m["rowidx"] = np.ascontiguousarray(((rank * nown + np.arange(nown))[None, :] * 128 + np.arange(128)[:, None]).astype(np.int32))
        in_maps.append(m)
    nc = build(NPT, NSS, PL, stop_after=stop_after, split=split)
    res = run_bass_kernel_spmd(nc, in_maps, core_ids=list(range(n_cores)), **({"trace": True} if trace else {}))
    return res


def assemble(R, nb, n_cores, NPT, NSS, split=False):
    L = NPT * 128
    pc = list(range(nb))
    cat = lambda key, cores: np.stack([np.asarray(R[c][key]) for c in cores])
    if split:
        y_prompt = np.stack([np.concatenate([np.asarray(R[b]["y_p"]), np.asarray(R[b + 4]["y_p"])], axis=0) for b in pc]).reshape(nb, L, D)
    else:
        y_prompt = cat("y_p", pc).reshape(nb, L, D)
    y_sample = cat("y_s", range(n_cores)).reshape(n_cores * NSS, 64, D)
    kp = cat("o_kp", pc).reshape(1, nb, L, 4, 128)
    vp = cat("o_vp", pc).reshape(1, nb, L, 4, 128)
    cap = cat("o_cap", pc).reshape(1, nb, 30, 512)
    ssdp = cat("o_ssdp", pc).reshape(1, nb, 8, 64, 128)
    csp = cat("o_csp", pc).reshape(1, nb, 3, 1024)
    mk = np.stack([np.asarray(R[c]["o_mk"]) for c in pc], axis=1).reshape(2, nb, 256, 4, 256)
    mv = np.stack([np.asarray(R[c]["o_mv"]) for c in pc], axis=1).reshape(2, nb, 256, 4, 256)
    ns = n_cores * NSS
    ks = cat("o_ks", range(n_cores)).reshape(1, ns, 64, 4, 128)
    vs = cat("o_vs", range(n_cores)).reshape(1, ns, 64, 4, 128)
    cas = cat("o_cas", range(n_cores)).reshape(1, ns, 30, 512)
    gvs = cat("o_gvs", range(n_cores)).reshape(1, ns, 64, 4, 128)
    ssds = cat("o_ssds", range(n_cores)).reshape(1, ns, 8, 64, 128)
    css = cat("o_css", range(n_cores)).reshape(1, ns, 3, 1024)
    outs = (y_prompt, y_sample, kp, vp, cap, ssdp, csp, mk, mv, ks, vs, cas, gvs, ssds, css)
    return tuple(np.ascontiguousarray(o, dtype=np.float32) for o in outs)


def kernel(x_prompt, x_sample, cache_attn_k, cache_attn_v, state_conv_a, state_ssd, state_conv_ssm,
           cache_mem_k, cache_mem_v, mem_prompt,
           norm_mix_g, norm_cross_g, norm_ffn_g, norm_final_g,
           w_in_ab, conv_a_w, conv_a_b, ln_a_g, ln_a_b, lam_q1, lam_k1, lam_q2, lam_k2, subln_g, w_out_ab,
           w_in_cd, ln_c_g, ln_c_b, gm_w_s, gm_b_s, conv_d_w, conv_d_b, dt_bias, a_log, d_skip, norm_d_g, w_out_cd,
           w_xq, w_xk, w_xv, w_xo, w_pq, sub_keys, expert_u, expert_v):
    inp = dict(x_prompt=x_prompt, x_sample=x_sample, cache_attn_k=cache_attn_k, cache_attn_v=cache_attn_v, state_conv_a=state_conv_a,
               state_ssd=state_ssd, state_conv_ssm=state_conv_ssm, cache_mem_k=cache_mem_k, cache_mem_v=cache_mem_v, mem_prompt=mem_prompt,
               norm_mix_g=norm_mix_g, norm_cross_g=norm_cross_g, norm_ffn_g=norm_ffn_g, norm_final_g=norm_final_g,
               w_in_ab=w_in_ab, conv_a_w=conv_a_w, conv_a_b=conv_a_b, ln_a_g=ln_a_g, ln_a_b=ln_a_b, lam_q1=lam_q1, lam_k1=lam_k1,
               lam_q2=lam_q2, lam_k2=lam_k2, subln_g=subln_g, w_out_ab=w_out_ab, w_in_cd=w_in_cd, ln_c_g=ln_c_g, ln_c_b=ln_c_b,
               gm_w_s=gm_w_s, gm_b_s=gm_b_s, conv_d_w=conv_d_w, conv_d_b=conv_d_b, dt_bias=dt_bias, a_log=a_log, d_skip=d_skip,
               norm_d_g=norm_d_g, w_out_cd=w_out_cd, w_xq=w_xq, w_xk=w_xk, w_xv=w_xv, w_xo=w_xo, w_pq=w_pq, sub_keys=sub_keys,
               expert_u=expert_u, expert_v=expert_v)
    nb = x_prompt.shape[0]
    L = x_prompt.shape[1]
    n_cores = 8
    NSS = x_sample.shape[0] // n_cores
    PL = cache_attn_k.shape[2]
    split = (nb == 4)
    res = run_cores(inp, n_cores, L // 128, NSS, PL, lambda c: c % nb, split=split)
    return assemble(res.results, nb, n_cores, L // 128, NSS, split=split)
```
